# Optimizing a Trainium2 kernel written in Bass

```python
import numpy as np
import jax, jax.numpy as jnp
from jax import lax

D_MODEL = 2048
BATCH = 4
SEQ = 2048
DEPTH = 1

MEM_LEN = 256
NSA_HEADS = 16
NSA_GROUPS = 4
NSA_REP = NSA_HEADS // NSA_GROUPS
NSA_DK = 128
NSA_DV = 128
CMP_LEN = 32
CMP_STRIDE = 16
CMP_HIDDEN = 1024
SEL_LEN = 64
SEL_TOPK = 16
WIN = 512
WIN_QB = 128
SEL_QB = 32
RET_HEADS = 8
RET_DK = 128
RET_DV = 256
RET_CHUNK = 128
ROPE_BASE = 10000.0
X_HEADS = 4
X_DH = 128
D_FF = 4 * D_MODEL
EPS = 1e-6
NEG = -1e30
NSA_Q = NSA_HEADS * NSA_DK
NSA_KW = NSA_GROUPS * NSA_DK
NSA_VW = NSA_GROUPS * NSA_DV
NSA_GATE = NSA_HEADS * 3
RET_QK = RET_HEADS * RET_DK
RET_V = RET_HEADS * RET_DV
IN_SIZES = (NSA_Q, NSA_KW, NSA_VW, NSA_KW, NSA_VW, NSA_KW, NSA_VW, NSA_GATE,
            RET_QK, RET_QK, RET_V, RET_V, D_MODEL, D_MODEL)
IN_WIDTH = sum(IN_SIZES)

kernel_name = "hybrid_nsa_retention_gated_block"


def rmsnorm(x, w):
    xf = x.astype(jnp.float32)
    y = xf * lax.rsqrt(jnp.mean(xf * xf, axis=-1, keepdims=True) + EPS)
    return (y * w.astype(jnp.float32)).astype(x.dtype)


def masked_softmax(s, mask):
    p = jax.nn.softmax(jnp.where(mask, s, NEG), axis=-1)
    return p * mask


def split_in(z):
    offs = np.cumsum(np.array(IN_SIZES))[:-1]
    return jnp.split(z, offs, axis=-1)


def nsa_compress(k, pe, w1, w2):
    B, T, G, d = k.shape
    n_cmp = (T - CMP_LEN) // CMP_STRIDE + 1
    idx = np.arange(n_cmp)[:, None] * CMP_STRIDE + np.arange(CMP_LEN)[None, :]
    blk = k[:, idx] + pe[None, None, :, None, :]
    blk = blk.transpose(0, 1, 3, 2, 4).reshape(B, n_cmp, G, CMP_LEN * d)
    return jax.nn.silu(blk @ w1) @ w2


def cmp_attention(q, kc, vc):
    T, n_cmp = q.shape[1], kc.shape[1]
    s = jnp.einsum('btgrd,bcgd->bgrtc', q, kc).astype(jnp.float32) * (NSA_DK ** -0.5)
    t = jnp.arange(T)
    end = jnp.arange(n_cmp) * CMP_STRIDE + CMP_LEN - 1
    mask = end[None, :] <= t[:, None]
    p = masked_softmax(s, mask)
    o = jnp.einsum('bgrtc,bcgd->btgrd', p.astype(vc.dtype), vc)
    return o, p


def select_blocks(p_cmp, T):
    n_cmp = p_cmp.shape[-1]
    n_sel = T // SEL_LEN
    cs = np.arange(n_cmp) * CMP_STRIDE
    js = np.arange(n_sel) * SEL_LEN
    ov = ((cs[:, None] < js[None, :] + SEL_LEN) & (cs[:, None] + CMP_LEN > js[None, :])).astype(np.float32)
    imp = jnp.einsum('bgrtc,cj->bgtj', p_cmp, jnp.asarray(ov))
    t = jnp.arange(T)
    cur = t // SEL_LEN
    j = jnp.arange(n_sel)
    forced = (j[None, :] == 0) | (j[None, :] == cur[:, None]) | (j[None, :] == cur[:, None] - 1)
    future = j[None, :] > cur[:, None]
    imp = jnp.where(forced, jnp.inf, jnp.where(future, -jnp.inf, imp))
    _, idx = lax.top_k(imp, min(SEL_TOPK, n_sel))
    return idx


def sel_attention(q, ks, vs, idx):
    B, T, G, R, d = q.shape
    dv = vs.shape[-1]
    n_sel = T // SEL_LEN
    n = idx.shape[-1]
    kb = ks.reshape(B, n_sel, SEL_LEN, G, d).transpose(0, 3, 1, 2, 4).reshape(B * G, n_sel, SEL_LEN, d)
    vb = vs.reshape(B, n_sel, SEL_LEN, G, dv).transpose(0, 3, 1, 2, 4).reshape(B * G, n_sel, SEL_LEN, dv)
    nq = T // SEL_QB
    qx = q.reshape(B, nq, SEL_QB, G, R, d).transpose(1, 0, 2, 3, 4, 5)
    ix = idx.reshape(B * G, nq, SEL_QB, n).transpose(1, 0, 2, 3)
    gather = jax.vmap(lambda tab, ii: tab[ii])

    def one(args):
        qb, ib, start = args
        kg = gather(kb, ib).reshape(B, G, SEL_QB, n * SEL_LEN, d)
        vg = gather(vb, ib).reshape(B, G, SEL_QB, n * SEL_LEN, dv)
        s = jnp.einsum('bqgrd,bgqkd->bgrqk', qb, kg).astype(jnp.float32) * (NSA_DK ** -0.5)
        tq = start + jnp.arange(SEL_QB)
        tk = (ib.reshape(B, G, SEL_QB, n)[..., None] * SEL_LEN + jnp.arange(SEL_LEN)).reshape(B, G, SEL_QB, n * SEL_LEN)
        mask = (tk <= tq[None, None, :, None])[:, :, None]
        p = masked_softmax(s, mask)
        return jnp.einsum('bgrqk,bgqkd->bqgrd', p.astype(vg.dtype), vg)

    o = lax.map(one, (qx, ix, jnp.arange(nq) * SEL_QB))
    return o.transpose(1, 0, 2, 3, 4, 5).reshape(B, T, G, R, dv)


def win_attention(q, kw, vw):
    B, T, G, R, d = q.shape
    nb = T // WIN_QB
    P = WIN // WIN_QB

    def band(x):
        xp = jnp.pad(x, ((0, 0), (WIN, 0), (0, 0), (0, 0))).reshape(B, nb + P, WIN_QB, G, x.shape[-1])
        return jnp.concatenate([xp[:, j:j + nb] for j in range(P + 1)], axis=2)

    kb, vb = band(kw), band(vw)
    qb = q.reshape(B, nb, WIN_QB, G, R, d)
    s = jnp.einsum('bnqgrd,bnkgd->bgrnqk', qb, kb).astype(jnp.float32) * (NSA_DK ** -0.5)
    blk = jnp.arange(nb)[:, None] * WIN_QB
    tq = blk + jnp.arange(WIN_QB)[None, :]
    tk = blk - WIN + jnp.arange((P + 1) * WIN_QB)[None, :]
    diff = tq[:, :, None] - tk[:, None, :]
    mask = (diff >= 0) & (diff < WIN) & (tk[:, None, :] >= 0)
    p = masked_softmax(s, mask)
    o = jnp.einsum('bgrnqk,bnkgd->bnqgrd', p.astype(vb.dtype), vb)
    return o.reshape(B, T, G, R, vw.shape[-1])


def rotary(x):
    T, d = x.shape[1], x.shape[-1]
    inv = ROPE_BASE ** (-jnp.arange(0, d, 2, dtype=jnp.float32) / d)
    ang = jnp.arange(T, dtype=jnp.float32)[:, None] * inv[None, :]
    cos, sin = jnp.cos(ang)[None, :, None, :], jnp.sin(ang)[None, :, None, :]
    xf = x.astype(jnp.float32)
    x1, x2 = xf[..., 0::2], xf[..., 1::2]
    return jnp.stack([x1 * cos - x2 * sin, x1 * sin + x2 * cos], axis=-1).reshape(xf.shape)


def retention(q, k, v, gn_w):
    B, T, H, dk = q.shape
    dv = v.shape[-1]
    C = RET_CHUNK
    nc = T // C
    qf = rotary(q)
    kf = rotary(k) * (dk ** -0.5)
    log_g = jnp.log1p(-jnp.exp2(-5.0 - jnp.arange(H, dtype=jnp.float32)))
    i = jnp.arange(C, dtype=jnp.float32)
    rel = i[:, None] - i[None, :]
    decay = jnp.where(rel >= 0, jnp.exp(log_g[:, None, None] * jnp.maximum(rel, 0.0)), 0.0)
    qc = qf.reshape(B, nc, C, H, dk)
    kc = kf.reshape(B, nc, C, H, dk)
    vc = v.astype(jnp.float32).reshape(B, nc, C, H, dv)
    s = jnp.einsum('bnchd,bnmhd->bhncm', qc, kc) * decay[:, None]
    o_intra = jnp.einsum('bhncm,bnmhe->bnche', s, vc)
    w_k = jnp.exp(log_g[:, None] * (C - 1 - i)[None, :])
    kv = jnp.einsum('bnmhd,bnmhe,hm->nbhde', kc, vc, w_k)
    g_chunk = jnp.exp(log_g * C)

    def step(state, kv_n):
        return state * g_chunk[None, :, None, None] + kv_n, state

    _, prev = lax.scan(step, jnp.zeros((B, H, dk, dv), jnp.float32), kv)
    w_q = jnp.exp(log_g[:, None] * (i + 1.0)[None, :])
    o_inter = jnp.einsum('bnchd,nbhde->bnche', qc, prev) * w_q.T[None, None, :, :, None]
    o = (o_intra + o_inter).reshape(B, T, H, dv)
    mu = jnp.mean(o, axis=-1, keepdims=True)
    var = jnp.mean(jnp.square(o - mu), axis=-1, keepdims=True)
    o = ((o - mu) * lax.rsqrt(var + EPS)).reshape(B, T, H * dv) * gn_w.astype(jnp.float32)
    return o.astype(v.dtype)


def hybrid_layer(h, mem, attn_norm_w, w_in, cmp_pe_k, cmp_w1_k, cmp_w2_k, cmp_pe_v, cmp_w1_v, cmp_w2_v,
                 w_a, ret_gn_w, w_b, w_out, x_norm_w, mem_norm_w, wq_x, wk_x, wv_x, wo_x,
                 mlp_norm_w, w_up, w_down):
    B, T, _ = h.shape
    G, R = NSA_GROUPS, NSA_REP
    n = rmsnorm(h, attn_norm_w)
    (q_n, k_c, v_c, k_s, v_s, k_w, v_w, g_nsa,
     q_r, k_r, v_r, g_r, gate_a, gate_b) = split_in(n @ w_in)
    q_n = q_n.reshape(B, T, G, R, NSA_DK)
    kv = lambda a, d: a.reshape(B, T, G, d)
    kc = nsa_compress(kv(k_c, NSA_DK), cmp_pe_k, cmp_w1_k, cmp_w2_k)
    vc = nsa_compress(kv(v_c, NSA_DV), cmp_pe_v, cmp_w1_v, cmp_w2_v)
    o_cmp, p_cmp = cmp_attention(q_n, kc, vc)
    idx = select_blocks(p_cmp, T)
    o_sel = sel_attention(q_n, kv(k_s, NSA_DK), kv(v_s, NSA_DV), idx)
    o_win = win_attention(q_n, kv(k_w, NSA_DK), kv(v_w, NSA_DV))
    g3 = jax.nn.sigmoid(g_nsa).reshape(B, T, G, R, 3)
    o_nsa = (g3[..., 0:1] * o_cmp + g3[..., 1:2] * o_sel + g3[..., 2:3] * o_win).reshape(B, T, NSA_HEADS * NSA_DV)
    o_ret = retention(q_r.reshape(B, T, RET_HEADS, RET_DK), k_r.reshape(B, T, RET_HEADS, RET_DK),
                      v_r.reshape(B, T, RET_HEADS, RET_DV), ret_gn_w)
    o_ret = jax.nn.silu(g_r) * o_ret
    merged = jax.nn.sigmoid(gate_a) * (o_nsa @ w_a) + jax.nn.sigmoid(gate_b) * (o_ret @ w_b)
    h = h + merged @ w_out
    nx = rmsnorm(h, x_norm_w)
    m = rmsnorm(mem, mem_norm_w)
    qx = (nx @ wq_x).reshape(B, T, X_HEADS, X_DH)
    kx = (m @ wk_x).reshape(B, m.shape[1], X_HEADS, X_DH)
    vx = (m @ wv_x).reshape(B, m.shape[1], X_HEADS, X_DH)
    sx = jnp.einsum('bthd,bmhd->bhtm', qx, kx).astype(jnp.float32) * (X_DH ** -0.5)
    px = jax.nn.softmax(sx, axis=-1).astype(vx.dtype)
    ox = jnp.einsum('bhtm,bmhd->bthd', px, vx).reshape(B, T, X_HEADS * X_DH)
    h = h + ox @ wo_x
    nm = rmsnorm(h, mlp_norm_w)
    h = h + jnp.square(jax.nn.relu(nm @ w_up)) @ w_down
    return h


def setup_inputs(seed: int = 0) -> dict:
    key = jax.random.key(seed)
    ks = iter(jax.random.split(key, 32))
    L = DEPTH
    f32 = jnp.float32

    def w(shape, fan_in):
        return jax.random.normal(next(ks), shape, f32) * (fan_in ** -0.5)

    def gain(shape):
        return 1.0 + 0.1 * jax.random.normal(next(ks), shape, f32)

    return {
        "x": jax.random.normal(next(ks), (BATCH, SEQ, D_MODEL), f32),
        "mem": jax.random.normal(next(ks), (BATCH, MEM_LEN, D_MODEL), f32),
        "attn_norm_w": gain((L, D_MODEL)),
        "w_in": w((L, D_MODEL, IN_WIDTH), D_MODEL),
        "cmp_pe_k": 0.02 * jax.random.normal(next(ks), (L, CMP_LEN, NSA_DK), f32),
        "cmp_w1_k": w((L, CMP_LEN * NSA_DK, CMP_HIDDEN), CMP_LEN * NSA_DK),
        "cmp_w2_k": w((L, CMP_HIDDEN, NSA_DK), CMP_HIDDEN),
        "cmp_pe_v": 0.02 * jax.random.normal(next(ks), (L, CMP_LEN, NSA_DV), f32),
        "cmp_w1_v": w((L, CMP_LEN * NSA_DV, CMP_HIDDEN), CMP_LEN * NSA_DV),
        "cmp_w2_v": w((L, CMP_HIDDEN, NSA_DV), CMP_HIDDEN),
        "w_a": w((L, NSA_HEADS * NSA_DV, D_MODEL), NSA_HEADS * NSA_DV),
        "ret_gn_w": gain((L, RET_V)),
        "w_b": w((L, RET_V, D_MODEL), RET_V),
        "w_out": w((L, D_MODEL, D_MODEL), D_MODEL),
        "x_norm_w": gain((L, D_MODEL)),
        "mem_norm_w": gain((L, D_MODEL)),
        "wq_x": w((L, D_MODEL, X_HEADS * X_DH), D_MODEL),
        "wk_x": w((L, D_MODEL, X_HEADS * X_DH), D_MODEL),
        "wv_x": w((L, D_MODEL, X_HEADS * X_DH), D_MODEL),
        "wo_x": w((L, X_HEADS * X_DH, D_MODEL), X_HEADS * X_DH),
        "mlp_norm_w": gain((L, D_MODEL)),
        "w_up": w((L, D_MODEL, D_FF), D_MODEL),
        "w_down": w((L, D_FF, D_MODEL), D_FF),
        "final_norm_w": gain((D_MODEL,)),
    }


def reference(x, mem, attn_norm_w, w_in, cmp_pe_k, cmp_w1_k, cmp_w2_k, cmp_pe_v, cmp_w1_v, cmp_w2_v,
              w_a, ret_gn_w, w_b, w_out, x_norm_w, mem_norm_w, wq_x, wk_x, wv_x, wo_x,
              mlp_norm_w, w_up, w_down, final_norm_w):
    h = x
    for l in range(DEPTH):
        h = hybrid_layer(h, mem, attn_norm_w[l], w_in[l], cmp_pe_k[l], cmp_w1_k[l], cmp_w2_k[l],
                         cmp_pe_v[l], cmp_w1_v[l], cmp_w2_v[l], w_a[l], ret_gn_w[l], w_b[l], w_out[l],
                         x_norm_w[l], mem_norm_w[l], wq_x[l], wk_x[l], wv_x[l], wo_x[l],
                         mlp_norm_w[l], w_up[l], w_down[l])
    return rmsnorm(h, final_norm_w)
```

```python
import numpy as np
from concourse.bass_utils import run_bass_kernel_spmd
import concourse.bass as bass
import concourse.mybir as mybir

F32 = mybir.dt.float32
BF16 = mybir.dt.bfloat16
AF = mybir.ActivationFunctionType
ALU = mybir.AluOpType
AX = mybir.AxisListType

_DT_SIZE = {F32: 4, BF16: 2}


class Buf:
    def __init__(self, key, ap):
        self.key = key
        self.ap = ap

    def __getitem__(self, idx):
        return Buf(self.key, self.ap[idx])

    def v(self, ap):
        return Buf(self.key, ap)


class Op:
    __slots__ = ("eng", "fn", "reads", "writes", "is_dma", "semkey", "deps", "dma_deps",
                 "signal", "signum", "pos", "accum", "idx")


class Prog:
    ENGS = ("pe", "act", "dve", "pool", "sp")

    def __init__(self, nc):
        self.nc = nc
        self.ops = []
        self.eng_obj = {"pe": nc.tensor, "act": nc.scalar, "dve": nc.vector, "pool": nc.gpsimd, "sp": nc.sync}
        self.sync_same_engine_war = True

    def _add(self, eng, fn, reads, writes, is_dma=False, semkey=None, accum=False):
        o = Op()
        o.eng = eng
        o.fn = fn
        o.reads = [b.key for b in reads if b is not None]
        o.writes = [b.key for b in writes if b is not None]
        for kk in o.reads:
            if kk.startswith("ps") and kk not in o.writes:
                o.writes.append(kk)
        o.is_dma = is_dma
        o.semkey = semkey
        o.accum = accum
        o.idx = len(self.ops)
        self.ops.append(o)
        return o

    def op(self, eng, fn, reads=(), writes=()):
        return self._add(eng, fn, reads, writes)

    def barrier(self):
        o = Op()
        o.eng = None
        o.idx = len(self.ops)
        self.ops.append(o)

    def dma(self, eng, out, in_, semkey=None):
        reads, writes = [], []
        if isinstance(in_, Buf):
            reads.append(in_)
            in_ap = in_.ap
        else:
            in_ap = in_
        if isinstance(out, Buf):
            writes.append(out)
            out_ap = out.ap
            if semkey is None:
                semkey = "dma:" + out.key
        else:
            out_ap = out
            if semkey is None:
                semkey = "dma:store"

        def fn(e):
            return e.dma_start(out=out_ap, in_=in_ap)

        return self._add(eng, fn, reads, writes, is_dma=True, semkey=semkey)

    def mm(self, out, lhsT, rhs, start, stop, extra_reads=(), sgc=False):
        def fn(e):
            if sgc:
                return e.matmul(out.ap, lhsT.ap, rhs.ap, start=start, stop=stop, skip_group_check=True)
            return e.matmul(out.ap, lhsT.ap, rhs.ap, start=start, stop=stop)
        return self._add("pe", fn, [lhsT, rhs] + list(extra_reads), [out], accum=not start)

    def transpose(self, out, in_, ident):
        def fn(e):
            return e.transpose(out.ap, in_.ap, ident.ap)
        return self._add("pe", fn, [in_, ident], [out])

    def act(self, out, in_, func, bias=None, scale=1.0, accum_out=None, eng="act"):
        reads = [in_]
        kw = {}
        if isinstance(bias, Buf):
            reads.append(bias)
            kw["bias"] = bias.ap
        elif bias is not None:
            kw["bias"] = bias
        if isinstance(scale, Buf):
            reads.append(scale)
            kw["scale"] = scale.ap
        else:
            kw["scale"] = scale
        writes = [out]
        if accum_out is not None:
            writes.append(accum_out)
            kw["accum_out"] = accum_out.ap

        def fn(e):
            return e.activation(out=out.ap, in_=in_.ap, func=func, **kw)
        return self._add(eng, fn, reads, writes)

    def tt(self, eng, out, in0, in1, op):
        def fn(e):
            return e.tensor_tensor(out=out.ap, in0=in0.ap, in1=in1.ap, op=op)
        return self._add(eng, fn, [in0, in1], [out])

    def ts(self, eng, out, in0, s1, s2, op0, op1=None, accum_out=None):
        reads = [in0]
        a1 = s1
        a2 = s2
        if isinstance(s1, Buf):
            reads.append(s1)
            a1 = s1.ap
        if isinstance(s2, Buf):
            reads.append(s2)
            a2 = s2.ap
        writes = [out]
        kw = {}
        if op1 is not None:
            kw["op1"] = op1
        if accum_out is not None:
            writes.append(accum_out)
            kw["accum_out"] = accum_out.ap

        def fn(e):
            return e.tensor_scalar(out=out.ap, in0=in0.ap, scalar1=a1, scalar2=a2, op0=op0, **kw)
        return self._add(eng, fn, reads, writes)

    def stt(self, eng, out, in0, scalar, in1, op0, op1):
        reads = [in0, in1]
        a = scalar
        if isinstance(scalar, Buf):
            reads.append(scalar)
            a = scalar.ap

        def fn(e):
            return e.scalar_tensor_tensor(out=out.ap, in0=in0.ap, scalar=a, in1=in1.ap, op0=op0, op1=op1)
        return self._add(eng, fn, reads, [out])

    def copy(self, eng, out, in_):
        if eng == "act":
            def fn(e):
                return e.copy(out=out.ap, in_=in_.ap)
        else:
            def fn(e):
                return e.tensor_copy(out=out.ap, in_=in_.ap)
        return self._add(eng, fn, [in_], [out])

    def reduce(self, eng, out, in_, op, axis=AX.X):
        def fn(e):
            return e.tensor_reduce(out=out.ap, in_=in_.ap, axis=axis, op=op)
        return self._add(eng, fn, [in_], [out])

    def memset(self, eng, out, val):
        def fn(e):
            return e.memset(out.ap, val)
        return self._add(eng, fn, [], [out])

    def recip(self, out, in_):
        def fn(e):
            return e.reciprocal(out=out.ap, in_=in_.ap)
        return self._add("dve", fn, [in_], [out])

    def emit(self, final_wait_eng="sp"):
        nc = self.nc
        ops = self.ops
        last_writer = {}
        readers = {}
        pos_ctr = {e: 0 for e in self.ENGS}
        waited = {f: {e: -1 for e in self.ENGS} for f in self.ENGS}
        waited_dma = {f: {} for f in self.ENGS}
        dma_count = {}
        last_op_on = {e: None for e in self.ENGS}
        pending_barrier = {e: [] for e in self.ENGS}
        outstanding_dma = []

        for o in ops:
            if o.eng is None:
                for f in self.ENGS:
                    pending_barrier[f] = [last_op_on[e] for e in self.ENGS if e != f and last_op_on[e] is not None]
                continue
            f = o.eng
            deps = set()
            for k in o.reads:
                w = last_writer.get(k)
                if w is not None:
                    deps.add(w)
            for k in o.writes:
                w = last_writer.get(k)
                if w is not None:
                    deps.add(w)
                for r in readers.get(k, ()):
                    deps.add(r)
            for b in pending_barrier[f]:
                deps.add(b)
            pending_barrier[f] = []
            o.pos = pos_ctr[f]
            pos_ctr[f] += 1
            o.deps = []
            o.dma_deps = []
            o.signal = False
            for di in sorted(deps):
                d = ops[di]
                if d.idx == o.idx:
                    continue
                if d.is_dma:
                    cnt = d.signum
                    if waited_dma[f].get(d.semkey, 0) >= cnt:
                        continue
                    waited_dma[f][d.semkey] = cnt
                    o.dma_deps.append((d.semkey, cnt))
                else:
                    if d.eng == "pe" and f == "pe" and not o.is_dma:
                        continue
                    if d.eng == f and not self.sync_same_engine_war and not o.is_dma:
                        is_raw_waw = any(last_writer.get(k) == di for k in o.reads + o.writes)
                        if not is_raw_waw:
                            continue
                    if waited[f][d.eng] >= d.pos:
                        continue
                    waited[f][d.eng] = d.pos
                    d.signal = True
                    o.deps.append(di)
            if o.is_dma:
                dma_count[o.semkey] = dma_count.get(o.semkey, 0) + 16
                o.signum = dma_count[o.semkey]
                outstanding_dma.append(o)
            for k in o.reads:
                readers.setdefault(k, []).append(o.idx)
            for k in o.writes:
                last_writer[k] = o.idx
                readers[k] = []
            last_op_on[f] = o.idx

        tail_deps = []
        for e in self.ENGS:
            li = last_op_on[e]
            if li is not None and not ops[li].is_dma:
                ops[li].signal = True
                tail_deps.append(li)
        sig_ctr = {e: 0 for e in self.ENGS}
        for o in ops:
            if o.eng is None or o.is_dma:
                continue
            if o.signal:
                sig_ctr[o.eng] += 1
                o.signum = sig_ctr[o.eng]
        import contextlib
        es = contextlib.ExitStack()
        self._es = es
        sems = {e: es.enter_context(nc.semaphore("s_" + e)) for e in self.ENGS}
        dsems = {}
        for k in dma_count:
            dsems[k] = es.enter_context(nc.semaphore("d%d" % len(dsems)))
        n_wait = 0
        for o in ops:
            if o.eng is None:
                continue
            e = self.eng_obj[o.eng]
            for di in o.deps:
                d = ops[di]
                e.wait_ge(sems[d.eng], d.signum)
                n_wait += 1
            for (k, cnt) in o.dma_deps:
                e.wait_ge(dsems[k], cnt)
                n_wait += 1
            ins = o.fn(e)
            if o.is_dma:
                ins.then_inc(dsems[o.semkey], 16)
            elif o.signal:
                ins.then_inc(sems[o.eng], 1)
        fe = self.eng_obj[final_wait_eng]
        for di in tail_deps:
            d = ops[di]
            fe.wait_ge(sems[d.eng], d.signum)
        for k, cnt in dma_count.items():
            fe.wait_ge(dsems[k], cnt)
        self.stats = dict(n_ops=len(ops), n_wait=n_wait, sig=dict(sig_ctr), n_dsems=len(dsems))
        return self.stats


TOK = 1024
NB = 8
DM = 2048
KCH = 16
EPS = 1e-6
OQ, OKV, OC, OR_, OGR, OGA, OGB, OG = 0, 2048, 4096, 5120, 9216, 11264, 13312, 15360
IN_WIDTH = 15408
NEGB = -30000.0
QSCALE = 128 ** -0.5
POOL = "dve"


def _prod(xs):
    r = 1
    for x in xs:
        r *= int(x)
    return r


class Arena:
    uid = 0

    def __init__(self, full_ap, P, base, nel, name):
        self.ap = full_ap
        self.base = base
        self.top = 0
        self.nel = nel
        self.P = P
        self.peak = 0
        self.name = name

    def alloc(self, name, shape, dt):
        inner = _prod(shape[1:])
        n = inner * (2 if dt == F32 else 1)
        npad = (n + 15) // 16 * 16
        off = self.base + self.top
        self.top += npad
        self.peak = max(self.peak, self.top)
        assert self.top <= self.nel, ("SBUF arena overflow", self.name, name, self.top, self.nel)
        ap = self.ap[:shape[0], off:off + n]
        if dt == F32:
            ap = ap.bitcast(F32)
        if len(shape) == 3:
            ap = ap.rearrange("p (a b) -> p a b", b=shape[2])
        elif len(shape) == 4:
            ap = ap.rearrange("p (a b c) -> p a b c", b=shape[2], c=shape[3])
        Arena.uid += 1
        return Buf("%s#%d" % (name, Arena.uid), ap)

    def mark(self):
        return self.top

    def release(self, mark=0):
        self.top = mark
        self.P.barrier()


class WStream:
    def __init__(self, P, arena, nslots, slot_elems):
        self.P = P
        self.nslots = nslots
        self.slot_elems = slot_elems
        self.slots = [arena.alloc("wslot%d" % i, [128, slot_elems], BF16) for i in range(nslots)]
        self.ctr = 0
        self.loaded = {}

    def _load(self, item):
        key, src, shape = item
        if key in self.loaded:
            return
        s = self.slots[self.ctr % self.nslots]
        self.ctr += 1
        n = _prod(shape[1:])
        ap = s.ap[:, 0:n]
        if len(shape) == 3:
            ap = ap.rearrange("p (a b) -> p a b", b=shape[2])
        b = Buf(s.key, ap)
        self.P.dma("pool", b, src)
        self.loaded[key] = b

    def get(self, lst, i, depth=None):
        depth = self.nslots - 1 if depth is None else depth
        for j in range(i, min(len(lst), i + depth + 1)):
            self._load(lst[j])
        b = self.loaded.pop(lst[i][0])
        return b


def wtile_cols(w2d, c0, ncols):
    return w2d[:, c0:c0 + ncols].rearrange("(kc p) c -> p kc c", p=128), [128, 16, ncols]


def wtile_rows(w2d, r0, nrows):
    return w2d[r0:r0 + nrows, :].rearrange("(kc p) c -> p kc c", p=128), [128, nrows // 128, 2048]


class K:
    pass


def build_program(dbg=None, stop_after=None):
    nc = bass.Bass("TRN2", target_bir_lowering=False)
    P = Prog(nc)
    k = K()
    k.nc, k.P = nc, P

    def din(name, shape):
        return nc.dram_tensor(name, list(shape), F32, kind="ExternalInput").ap()

    D = {}
    D["xo"] = din("xo", [TOK, DM])
    D["xc"] = din("xc", [TOK, DM])
    D["mem"] = din("mem", [256, DM])
    D["w_in"] = din("w_in", [DM, IN_WIDTH])
    for nm in ("cmp_w1_k", "cmp_w1_v"):
        D[nm] = din(nm, [4096, 1024])
    for nm in ("cmp_w2_k", "cmp_w2_v"):
        D[nm] = din(nm, [1024, 128])
    for nm in ("peT_k", "peT_v"):
        D[nm] = din(nm, [128, 32])
    for nm in ("w_a", "w_b", "w_out"):
        D[nm] = din(nm, [DM, DM])
    for nm in ("wq_x", "wk_x", "wv_x"):
        D[nm] = din(nm, [DM, 512])
    D["wo_x"] = din("wo_x", [512, DM])
    D["w_up"] = din("w_up", [DM, 8192])
    D["w_down"] = din("w_down", [8192, DM])
    for nm in ("attn_norm_w", "ret_gn_w", "x_norm_w", "mem_norm_w", "mlp_norm_w", "final_norm_w"):
        D[nm] = din(nm, [1, DM])
    D["ident"] = din("ident", [128, 128])
    D["ropeq"] = din("ropeq", [128, 8, 128])
    D["ropek"] = din("ropek", [128, 16, 128])
    D["decayT"] = din("decayT", [128, 8, 128])
    D["wqB"] = din("wqB", [128, 8, 128])
    D["wk"] = din("wk", [128, 8])
    D["cmask"] = din("cmask", [128, 8, 128])
    D["addmask"] = din("addmask", [128, 8, 32])
    D["kvalid"] = din("kvalid", [128, 16])
    D["Emat"] = din("Emat", [32, 2048])
    D["causneg"] = din("causneg", [128, 512])
    D["winneg"] = din("winneg", [128, 512])
    out_d = nc.dram_tensor("out", [TOK, DM], F32, kind="ExternalOutput").ap()
    dbg_d = {}
    if dbg:
        for nm, shp in dbg.items():
            dbg_d[nm] = nc.dram_tensor("dbg_" + nm, list(shp), F32, kind="ExternalOutput").ap()
    k.D, k.out_d, k.dbg_d = D, out_d, dbg_d

    NEL = 106000
    full = nc.alloc_sbuf_tensor("arena", [128, NEL], BF16).ap()
    R0 = Arena(full, P, 0, 30000, "R0")
    R1 = Arena(full, P, 30000, 16384, "R1")
    R2 = Arena(full, P, 46384, 16384, "R2")
    R34 = Arena(full, P, 62768, NEL - 62768, "R34")
    k.R0, k.R1, k.R2, k.R34 = R0, R1, R2, R34
    A = R0
    k.psS = [Buf("psS%d" % i, nc.alloc_psum_tensor("psS%d" % i, [128, 512], F32).ap()) for i in range(3)]
    k.psC = Buf("psC", nc.alloc_psum_tensor("psC", [128, 512], F32).ap())
    k.psO = Buf("psO", nc.alloc_psum_tensor("psO", [128, 512], F32).ap())
    k.psV = [Buf("psV%d" % i, nc.alloc_psum_tensor("psV%d" % i, [128, 512], F32).ap()) for i in range(2)]
    k.psT = Buf("psT", nc.alloc_psum_tensor("psT", [128, 1024], BF16).ap())
    k.rot5 = [k.psS[0], k.psS[1], k.psS[2], k.psC, k.psO]
    k.rot_i = 0
    k.ev_i = 0

    k.ident = A.alloc("ident", [128, 128], BF16)
    P.dma("pool", k.ident, D["ident"][:, :])
    k.WS = WStream(P, A, 3, 16 * 512)

    def finish():
        st = P.emit()
        st["arena_peak_el"] = [R0.peak, R1.peak, R2.peak, R34.peak]
        return nc, st

    def dbg_store(name, buf, dram_view=None):
        if name in dbg_d:
            dv = dbg_d[name] if dram_view is None else dram_view
            P.dma("sp", dv, buf, semkey="dma:dbg")

    k.dbg_store = dbg_store

    def next_ps():
        b = k.rot5[k.rot_i % 5]
        k.rot_i += 1
        return b

    def evac(out, in_, scale=None):
        e = k.ev_i % 2
        k.ev_i += 1
        if e == 0:
            if scale is None:
                P.copy("act", out, in_)
            else:
                P.act(out, in_, AF.Copy, scale=scale)
        else:
            if scale is None:
                P.copy("dve", out, in_)
            else:
                P.ts("dve", out, in_, scale, None, ALU.mult)

    k.next_ps, k.evac = next_ps, evac

    def load_gain(name, reg):
        g = reg.alloc("gain_" + name, [128, DM], F32)
        P.dma("sp", g, D[name].to_broadcast([128, DM]))
        return g

    def norm_T(get_block, nblk, gain, nT, xn, stat):
        P.memset("dve", stat, 0.0)
        for tb in range(nblk):
            xt = get_block(tb)
            P.act(xn, xt, AF.Square, accum_out=stat[:, tb:tb + 1])
            P.ts("dve", stat[:, tb:tb + 1], stat[:, tb:tb + 1], 1.0 / DM, EPS, ALU.mult, ALU.add)
            P.act(stat[:, tb:tb + 1], stat[:, tb:tb + 1], AF.Sqrt)
            P.recip(stat[:, tb:tb + 1], stat[:, tb:tb + 1])
            P.stt("dve", xn, xt, stat[:, tb:tb + 1], gain, ALU.mult, ALU.mult)
            for half in range(2):
                for j in range(8):
                    kc = half * 8 + j
                    P.transpose(k.psT[:, j * 128:(j + 1) * 128], xn[:, kc * 128:(kc + 1) * 128], k.ident)
                src = k.psT.v(k.psT.ap.rearrange("p (a b) -> p a b", b=128))
                evac(nT[:, half * 8:half * 8 + 8, tb * 128:(tb + 1) * 128], src)

    def proj_T(wt, c0, nT, t0, ntok, out, scale=None, nk=KCH, out_view3=False):
        ps = next_ps()
        for kc in range(nk):
            P.mm(ps[:, 0:ntok], wt[:, kc, c0:c0 + 128], nT[:, kc, t0:t0 + ntok], kc == 0, kc == nk - 1)
        src = ps[:, 0:ntok]
        if out_view3:
            src = ps.v(ps.ap[:, 0:ntok].rearrange("p (a b) -> p a b", b=128))
        evac(out, src, scale)

    k.norm_T, k.proj_T, k.load_gain = norm_T, proj_T, load_gain

    nT_own = R1.alloc("nT_own", [128, KCH, TOK], BF16)
    nT_ctx = R2.alloc("nT_ctx", [128, KCH, TOK], BF16)
    k.state8 = R0.alloc("state8", [128, 8, 256], F32)
    k.kcmpT = R0.alloc("kcmpT", [128, 4, 128], BF16)
    k.vcmp = R0.alloc("vcmp", [128, 4, 128], BF16)
    gainA = load_gain("attn_norm_w", R34)
    xts = [R34.alloc("xt%d" % i, [128, DM], F32) for i in range(2)]
    xn = R34.alloc("xn", [128, DM], BF16)
    stat = R34.alloc("stat", [128, 16], F32)

    def mk_get(src, xts):
        def get(tb):
            xt = xts[tb % 2]
            P.dma("sp", xt, src[tb * 128:(tb + 1) * 128, :])
            return xt
        return get

    k.mk_get = mk_get
    norm_T(mk_get(D["xc"], xts), NB, gainA, nT_ctx, xn, stat)
    norm_T(mk_get(D["xo"], xts), NB, gainA, nT_own, xn, stat)
    if "nT_own" in dbg_d:
        tmpf = R34.alloc("dbgtmp", [128, KCH, TOK // 4], F32)
        P.copy("dve", tmpf, nT_own[:, :, 0:TOK // 4])
        dbg_store("nT_own", tmpf)
    R34.release()
    if stop_after == "norm":
        return finish()
    k.nT_own, k.nT_ctx = nT_own, nT_ctx
    k.finish = finish

    build_ret_ctx(k)
    R34.release()
    if stop_after == "ret_ctx":
        return finish()
    build_nsa(k, stop_after)
    if stop_after and stop_after.startswith("nsa"):
        return finish()
    R34.release(k.m_after_o)
    R2.release()
    k.mergedT = R2.alloc("mergedT", [128, KCH, TOK], BF16)
    build_merge(k, "a", k.o_nsaT)
    R34.release()
    if stop_after == "merge_a":
        return finish()
    build_ret_own(k)
    if stop_after == "ret":
        return finish()
    R34.release(k.m_after_o)
    build_merge(k, "b", k.o_retT)
    R34.release()
    R1.release()
    if stop_after == "merge_b":
        return finish()
    build_tail(k, stop_after)
    return finish()


def build_nsa(k, stop_after):
    P, A, D, WS = k.P, k.R34, k.D, k.WS
    nT_own, nT_ctx = k.nT_own, k.nT_ctx
    psS, psC, psO, psV, psT = k.psS, k.psC, k.psO, k.psV, k.psT
    evac, next_ps, proj_T = k.evac, k.next_ps, k.proj_T
    ident = k.ident
    w_in = D["w_in"]
    uchunks = [(nT_ctx, 0, 0), (nT_ctx, 512, 512), (nT_own, 0, 1024), (nT_own, 512, 1536)]

    o_nsaT = A.alloc("o_nsaT", [128, 16, TOK], BF16)
    k.o_nsaT = o_nsaT
    k.m_after_o = A.mark()
    kcmpT, vcmp = k.kcmpT, k.vcmp
    P.memset("dve", kcmpT, 0.0)
    P.memset("dve", vcmp, 0.0)
    m1 = A.mark()
    kvT = A.alloc("kvcT", [128, 4, 2048], BF16)
    hidT = A.alloc("hidT", [128, 8, 4, 128], BF16)
    peT = A.alloc("peT", [128, 32], BF16)
    w2 = A.alloc("w2", [128, 8, 128], BF16)
    cbias = A.alloc("cbias", [128, 8], F32)
    for kind in range(2):
        sfx = "_k" if kind == 0 else "_v"
        tiles = [("wc%d" % kind,) + wtile_cols(w_in, OC + 512 * kind, 512)]
        for hh in range(2):
            for lh in range(2):
                src = D["cmp_w1" + sfx][2048 * lh:2048 * (lh + 1), 512 * hh:512 * (hh + 1)].rearrange(
                    "(l p) c -> p l c", p=128)
                tiles.append(("w1%d_%d_%d" % (kind, hh, lh), src, [128, 16, 512]))
        P.dma("pool", peT, D["peT" + sfx][:, :])
        P.dma("pool", w2, D["cmp_w2" + sfx].rearrange("(hc p) c -> p hc c", p=128))
        wt = WS.get(tiles, 0)
        for g in range(4):
            for (nT, t0, u0) in uchunks:
                proj_T(wt, g * 128, nT, t0, 512, kvT[:, g, u0:u0 + 512])
        ti = 1
        for hh in range(2):
            accs = [psS[0], psS[1], psS[2], psC]
            for lh in range(2):
                wt = WS.get(tiles, ti)
                ti += 1
                for li in range(16):
                    l = lh * 16 + li
                    for hc in range(4):
                        lhsT = wt[:, li, hc * 128:(hc + 1) * 128]
                        for g in range(4):
                            rhs = kvT[:, g, l:l + 16 * 126 + 1:16]
                            P.mm(accs[hc][:, g * 127:(g + 1) * 127], lhsT, rhs, l == 0 and g == 0, l == 31, sgc=True)
                        P.mm(psO[:, hc:hc + 1], lhsT, peT[:, l:l + 1], l == 0 and hc == 0, l == 31, sgc=True)
            for hc in range(4):
                hcg = hh * 4 + hc
                P.copy("dve", cbias[:, hcg:hcg + 1], psO[:, hc:hc + 1])
                src = accs[hc].v(accs[hc].ap[:, 0:508].rearrange("p (g c) -> p g c", c=127))
                P.act(hidT[:, hcg, :, 0:127], src, AF.Silu, bias=cbias[:, hcg:hcg + 1])
        if kind == 0:
            ps = next_ps()
            for g in range(4):
                for hc in range(8):
                    P.mm(ps[:, g * 127:(g + 1) * 127], w2[:, hc, :], hidT[:, hc, g, 0:127], hc == 0, hc == 7)
            evac(kcmpT[:, :, 0:127], ps.v(ps.ap[:, 0:508].rearrange("p (g c) -> p g c", c=127)))
        else:
            ps = next_ps()
            for g in range(4):
                for hc in range(8):
                    P.mm(ps[0:127, g * 128:(g + 1) * 128], hidT[:, hc, g, 0:127], w2[:, hc, :], hc == 0, hc == 7)
            evac(vcmp[0:127, :, :], ps.v(ps.ap[0:127, :].rearrange("p (g c) -> p g c", c=128)))
    if "kcmpT" in k.dbg_d:
        tmpf = A.alloc("dbgtmp", [128, 4, 128], F32)
        P.copy("dve", tmpf, kcmpT)
        k.dbg_store("kcmpT", tmpf)
        tmpf2 = A.alloc("dbgtmp2", [128, 4, 128], F32)
        P.copy("dve", tmpf2, vcmp)
        k.dbg_store("vcmp", tmpf2)
    A.release(m1)
    if stop_after == "nsa_cmp":
        return

    cmask = A.alloc("cmask", [128, 8, 128], F32)
    addmask = A.alloc("addmask", [128, 8, 32], F32)
    kvalid = A.alloc("kvalid", [128, 16], BF16)
    Emat = A.alloc("Emat", [32, 2048], BF16)
    causneg = A.alloc("causneg", [128, 512], BF16)
    winneg = A.alloc("winneg", [128, 512], BF16)
    g3 = A.alloc("g3", [128, 8, 48], F32)
    P.dma("sp", cmask, D["cmask"][:, :, :])
    P.dma("sp", addmask, D["addmask"][:, :, :])
    P.dma("pool", kvalid, D["kvalid"][:, :])
    P.dma("pool", Emat, D["Emat"][:, :])
    P.dma("pool", causneg, D["causneg"][:, :])
    P.dma("pool", winneg, D["winneg"][:, :])
    tiles = [("wg3",) + wtile_cols(w_in, OG, 48)]
    for g in range(4):
        tiles.append(("wq%d" % g,) + wtile_cols(w_in, OQ + 512 * g, 512))
        tiles.append(("wkv%d" % g,) + wtile_cols(w_in, OKV + 512 * g, 512))
    wt = WS.get(tiles, 0)
    for tb in range(NB):
        ps = next_ps()
        for kc in range(KCH):
            P.mm(ps[:, 0:48], nT_own[:, kc, tb * 128:(tb + 1) * 128], wt[:, kc, 0:48], kc == 0, kc == KCH - 1)
        P.act(g3[:, tb, :], ps[:, 0:48], AF.Sigmoid)

    qT = A.alloc("qT", [128, NB, 4, 128], BF16)
    ksT = A.alloc("ksT", [128, 2048], BF16)
    kwT = A.alloc("kwT", [128, 1536], BF16)
    vs = A.alloc("vs", [128, 16, 130], BF16)
    vw = A.alloc("vw", [128, 12, 130], BF16)
    e32 = A.alloc("e32", [128, 4, 128], F32)
    p32 = A.alloc("p32", [128, 4, 128], F32)
    p16 = A.alloc("p16", [128, 4, 128], BF16)
    pT = A.alloc("pT", [128, 4, 128], BF16)
    Pg = A.alloc("Pg", [128, 128], F32)
    imp = A.alloc("imp", [128, 32], F32)
    imp2 = A.alloc("imp2", [128, 32], F32)
    m8 = A.alloc("m8", [128, 8], F32)
    sm4 = A.alloc("sm4", [128, 16], F32)
    selneg = A.alloc("selneg", [128, 32], BF16)
    negT = [A.alloc("negT%d" % i, [32, 4, 128], BF16) for i in range(2)]
    oacc = [A.alloc("oacc%d" % i, [128, 4, 128], F32) for i in range(2)]
    o16 = A.alloc("o16", [128, 4, 128], BF16)
    PT = [A.alloc("PT%d" % i, [128, 512], BF16) for i in range(2)]
    coef = A.alloc("coef", [128, 8], F32)
    P.memset("dve", vs, 0.0)
    P.memset("dve", vw, 0.0)

    def bc_heads(b):
        return b.v(b.ap.unsqueeze(1).to_broadcast([b.ap.shape[0], 4, 128]))

    for g in range(4):
        wq = WS.get(tiles, 1 + 2 * g)
        for hh in range(4):
            for tch in range(2):
                proj_T(wq, hh * 128, nT_own, tch * 512, 512, qT[:, 4 * tch:4 * tch + 4, hh, :], scale=QSCALE,
                       out_view3=True)
        wkv = WS.get(tiles, 2 + 2 * g)
        for (nT, t0, u0) in uchunks:
            proj_T(wkv, 0, nT, t0, 512, ksT[:, u0:u0 + 512])
        for (nT, t0, u0) in uchunks[1:]:
            proj_T(wkv, 256, nT, t0, 512, kwT[:, u0 - 512:u0])
        for (vbuf, c0, ub0) in ((vs, 128, 0), (vw, 384, 4)):
            for q4 in range(ub0 // 4, 4):
                ps = next_ps()
                for j in range(4):
                    ub = 4 * q4 + j
                    nT = nT_ctx if ub < 8 else nT_own
                    tb = ub % 8
                    for kc in range(KCH):
                        P.mm(ps[:, j * 128:(j + 1) * 128], nT[:, kc, tb * 128:(tb + 1) * 128],
                             wkv[:, kc, c0:c0 + 128], kc == 0, kc == KCH - 1)
                evac(vbuf[:, 4 * q4 - ub0:4 * q4 - ub0 + 4, 0:128],
                     ps.v(ps.ap.rearrange("p (a b) -> p a b", b=128)))
            P.copy("dve", vbuf[:, :, 128:129], kvalid.v(kvalid.ap[:, ub0:16].unsqueeze(2)))

        def cmp_stage(qb):
            par = qb % 2
            for hh in range(4):
                P.mm(psC[:, hh * 128:(hh + 1) * 128], qT[:, qb, hh, :], kcmpT[:, g, :], True, True)
            psC3 = psC.v(psC.ap.rearrange("p (a b) -> p a b", b=128))
            P.reduce("dve", sm4[:, 0:4], psC3, ALU.max)
            P.ts("dve", sm4[:, 4:8], sm4[:, 0:4], -1.0, None, ALU.mult)
            for hh in range(4):
                P.act(e32[:, hh, :], psC[:, hh * 128:(hh + 1) * 128], AF.Exp, bias=sm4[:, 4 + hh:5 + hh])
            P.tt("dve", e32, e32, bc_heads(cmask[:, qb, :]), ALU.mult)
            P.reduce("dve", sm4[:, 8:12], e32, ALU.add)
            P.ts("dve", sm4[:, 8:12], sm4[:, 8:12], 1e-30, None, ALU.max)
            P.recip(sm4[:, 12:16], sm4[:, 8:12])
            rb = sm4.v(sm4.ap[:, 12:16].unsqueeze(2).to_broadcast([128, 4, 128]))
            P.tt("dve", p32, e32, rb, ALU.mult)
            P.copy("act", p16, p32)
            P.reduce("dve", Pg, p32.v(p32.ap.rearrange("p h c -> p c h")), ALU.add)
            P.reduce("dve", imp, Pg.v(Pg.ap.rearrange("p (j f) -> p j f", f=4)), ALU.add)
            P.tt("dve", imp2[:, 1:32], imp[:, 1:32], Pg[:, 3:124:4], ALU.add)
            P.copy("dve", imp2[:, 0:1], imp[:, 0:1])
            P.tt("dve", imp, imp2, addmask[:, qb, :], ALU.add)
            P.op("dve", lambda e: e.max(out=m8.ap, in_=imp.ap), [imp], [m8])
            P.op("dve", lambda e: e.match_replace(out=imp2.ap, in_to_replace=m8.ap, in_values=imp.ap,
                                                   imm_value=-3e38), [m8, imp], [imp2])
            P.op("dve", lambda e: e.max(out=m8.ap, in_=imp2.ap), [imp2], [m8])
            P.ts("dve", selneg, imp, m8[:, 7:8], NEGB, ALU.is_lt, ALU.mult)
            P.transpose(psT[0:32, 0:128], selneg, ident)
            evac(negT[par], psT.v(psT.ap[0:32, 0:128].unsqueeze(1).to_broadcast([32, 4, 128])))
            for hh in range(4):
                P.transpose(psT[:, (1 + hh) * 128:(2 + hh) * 128], p16[:, hh, :], ident)
            evac(pT, psT.v(psT.ap[:, 128:640].rearrange("p (a b) -> p a b", b=128)))
            for hh in range(4):
                P.mm(psO[:, hh * 128:(hh + 1) * 128], pT[:, hh, :], vcmp[:, g, :], True, True)
            for hh in range(4):
                col = 3 * (4 * g + hh)
                P.act(oacc[par][:, hh, :], psO[:, hh * 128:(hh + 1) * 128], AF.Copy, scale=g3[:, qb, col:col + 1])
            if "imp" in k.dbg_d and g == 0:
                k.dbg_store("imp", imp, k.dbg_d["imp"][qb])
                k.dbg_store("Pg", Pg, k.dbg_d["Pg"][qb])

        def attn(qb, kT, kofs, vbuf, vofs, kbs, masks, gcol, final):
            par = qb % 2
            n = len(kbs)
            q3 = qT.v(qT.ap[:, qb, :, :].rearrange("p h t -> p (h t)"))
            pss = {}

            def stA(i):
                kb = kbs[i]
                ps = psS[i % 3]
                pss[i] = ps
                mk = masks.get(kb)
                P.mm(ps, kT[:, (kb - kofs) * 128:(kb - kofs + 1) * 128], q3, True, mk is None)
                if mk is not None:
                    P.mm(ps, mk[0], mk[1], False, True)
                P.act(PT[i % 2], ps, AF.Exp)

            def stB(i):
                kb = kbs[i]
                for hh in range(4):
                    acc = psV[hh // 2]
                    o = (hh % 2) * 130
                    P.mm(acc[:, o:o + 129], PT[i % 2][:, hh * 128:(hh + 1) * 128], vbuf[:, kb - vofs, 0:129],
                         i == 0 and hh % 2 == 0, i == n - 1, sgc=True)

            stA(0)
            for i in range(n):
                if i + 1 < n:
                    stA(i + 1)
                stB(i)
            for hh in range(4):
                acc = psV[hh // 2]
                o = (hh % 2) * 130
                P.copy("dve", coef[:, hh:hh + 1], acc[:, o + 128:o + 129])
            P.ts("dve", coef[:, 0:4], coef[:, 0:4], 1e-30, None, ALU.max)
            P.recip(coef[:, 4:8], coef[:, 0:4])
            base = 3 * 4 * g + gcol
            P.tt("dve", coef[:, 4:8], coef[:, 4:8], g3[:, qb, base:base + 10:3], ALU.mult)
            for hh in range(4):
                acc = psV[hh // 2]
                o = (hh % 2) * 130
                dst = o16[:, hh, :] if final else oacc[par][:, hh, :]
                P.stt("dve", dst, acc[:, o:o + 128], coef[:, 4 + hh:5 + hh], oacc[par][:, hh, :], ALU.mult, ALU.add)

        cmp_stage(0)
        for qb in range(NB):
            if qb + 1 < NB:
                cmp_stage(qb + 1)
            ub = 8 + qb
            par = qb % 2
            negbc = negT[par].v(negT[par].ap.rearrange("p h t -> p (h t)"))
            masks = {kb: (Emat[:, kb * 128:(kb + 1) * 128], negbc) for kb in range(0, ub)}
            masks[ub] = (ident, causneg)
            attn(qb, ksT, 0, vs, 0, list(range(0, ub + 1)), masks, 1, False)
            masks = {ub - 4: (ident, winneg), ub: (ident, causneg)}
            attn(qb, kwT, 4, vw, 4, list(range(ub - 4, ub + 1)), masks, 2, True)
            for hh in range(4):
                P.transpose(psT[:, (5 + hh % 2) * 128:(6 + hh % 2) * 128], o16[:, hh, :], ident)
                if hh % 2 == 1:
                    evac(o_nsaT[:, 4 * g + hh - 1:4 * g + hh + 1, qb * 128:(qb + 1) * 128],
                         psT.v(psT.ap[:, 640:896].rearrange("p (a b) -> p a b", b=128)))
        if stop_after == "nsa_g0":
            break
    if "o_nsaT" in k.dbg_d:
        tmpf = A.alloc("dbgtmp", [128, TOK], F32)
        for hh in range(4):
            P.copy("dve", tmpf, o_nsaT[:, hh, :])
            k.dbg_store("o_nsaT", tmpf, k.dbg_d["o_nsaT"][:, hh, :])


def _rotary(k, ps128, cos, sin, out_even_odd, tmp):
    P = k.P
    x1 = ps128.v(ps128.ap.rearrange("p (i two) -> p i two", two=2)[:, :, 0])
    x2 = ps128.v(ps128.ap.rearrange("p (i two) -> p i two", two=2)[:, :, 1])
    o1 = out_even_odd.v(out_even_odd.ap.rearrange("p (i two) -> p i two", two=2)[:, :, 0])
    o2 = out_even_odd.v(out_even_odd.ap.rearrange("p (i two) -> p i two", two=2)[:, :, 1])
    P.tt("dve", tmp[:, 0, :], x1, cos, ALU.mult)
    P.tt("dve", tmp[:, 1, :], x2, sin, ALU.mult)
    P.tt("dve", tmp[:, 2, :], x1, sin, ALU.mult)
    P.tt("dve", tmp[:, 3, :], x2, cos, ALU.mult)
    P.tt(POOL, o1, tmp[:, 0, :], tmp[:, 1, :], ALU.subtract)
    P.tt(POOL, o2, tmp[:, 2, :], tmp[:, 3, :], ALU.add)


def build_ret_ctx(k):
    P, A, D, WS = k.P, k.R34, k.D, k.WS
    nT_ctx = k.nT_ctx
    state8 = k.state8
    ropek = A.alloc("ropek", [128, 16, 128], F32)
    wk = A.alloc("wk", [128, 8], F32)
    P.dma("sp", ropek, D["ropek"][:, :, :])
    P.dma("sp", wk, D["wk"][:, :])
    P.memset("dve", state8, 0.0)
    Kr = [A.alloc("Kr%d" % i, [128, 128], F32) for i in range(2)]
    Ks = [A.alloc("Ks%d" % i, [128, 128], BF16) for i in range(2)]
    V = [A.alloc("V%d" % i, [128, 256], BF16) for i in range(2)]
    tmp = [A.alloc("rtmp%d" % i, [128, 4, 64], F32) for i in range(2)]
    tiles = [("wr_c%d" % h,) + wtile_cols(D["w_in"], OR_ + 512 * h, 512) for h in range(8)]
    for h in range(8):
        wt = WS.get(tiles, h)
        for cb in range(NB):
            par = cb % 2
            ps = k.next_ps()
            for kc in range(KCH):
                P.mm(ps[:, 0:384], nT_ctx[:, kc, cb * 128:(cb + 1) * 128], wt[:, kc, 128:512], kc == 0, kc == KCH - 1)
            _rotary(k, ps[:, 0:128], ropek[:, cb, 0:64], ropek[:, cb, 64:128], Kr[par], tmp[par])
            P.ts(POOL, Ks[par], Kr[par], wk[:, h:h + 1], None, ALU.mult)
            P.copy("act", V[par], ps[:, 128:384])
            ps2 = k.next_ps()
            P.mm(ps2[:, 0:256], Ks[par], V[par], True, True)
            P.stt("dve", state8[:, h, :], state8[:, h, :], _G_CHUNK[h], ps2[:, 0:256], ALU.mult, ALU.add)


def build_ret_own(k):
    P, A, D, WS = k.P, k.R34, k.D, k.WS
    nT_own = k.nT_own
    state8 = k.state8
    psT = k.psT
    ident = k.ident
    o_retT = A.alloc("o_retT", [128, 16, TOK], BF16)
    k.o_retT = o_retT
    k.m_after_o = A.mark()
    ropek = A.alloc("ropek", [128, 16, 128], F32)
    ropeq = A.alloc("ropeq", [128, 8, 128], F32)
    decayT = A.alloc("decayT", [128, 8, 128], F32)
    wqB = A.alloc("wqB", [128, 8, 128], F32)
    wk = A.alloc("wk", [128, 8], F32)
    gnB = k.load_gain("ret_gn_w", A)
    P.dma("sp", ropek, D["ropek"][:, :, :])
    P.dma("sp", ropeq, D["ropeq"][:, :, :])
    P.dma("sp", decayT, D["decayT"][:, :, :])
    P.dma("sp", wqB, D["wqB"][:, :, :])
    P.dma("sp", wk, D["wk"][:, :])
    Qr = [A.alloc("Qr%d" % i, [128, 128], BF16) for i in range(2)]
    Kr = [A.alloc("Kr%d" % i, [128, 128], F32) for i in range(2)]
    Kb = [A.alloc("Kb%d" % i, [128, 128], BF16) for i in range(2)]
    Ks = [A.alloc("Ks%d" % i, [128, 128], BF16) for i in range(2)]
    V = [A.alloc("V%d" % i, [128, 256], BF16) for i in range(2)]
    sg = [A.alloc("sg%d" % i, [128, 256], F32) for i in range(2)]
    tmp = [A.alloc("rtmp%d" % i, [128, 4, 64], F32) for i in range(2)]
    QT = A.alloc("QT", [128, 128], BF16)
    KT = A.alloc("KT", [128, 128], BF16)
    QsT = A.alloc("QsT", [128, 128], BF16)
    SdT = A.alloc("SdT", [128, 128], BF16)
    stbf = A.alloc("stbf", [128, 256], BF16)
    osb = A.alloc("osb", [128, 256], F32)
    junk = A.alloc("junk", [128, 256], BF16)
    y = A.alloc("y", [128, 256], F32)
    y16 = A.alloc("y16", [128, 256], BF16)
    gs = A.alloc("gs", [128, 8], F32)
    tiles = []
    for h in range(8):
        tiles.append(("wr_o%d" % h,) + wtile_cols(D["w_in"], OR_ + 512 * h, 512))
        if h % 2 == 0:
            tiles.append(("wgr%d" % (h // 2),) + wtile_cols(D["w_in"], OGR + 512 * (h // 2), 512))
    ti = 0
    wg = None
    for h in range(8):
        wt = WS.get(tiles, ti, depth=1)
        ti += 1
        if h % 2 == 0:
            wg = WS.get(tiles, ti, depth=1)
            ti += 1
        pss = {}

        def stage1(ob):
            par = ob % 2
            ub = 8 + ob
            ps = k.next_ps()
            for kc in range(KCH):
                P.mm(ps, nT_own[:, kc, ob * 128:(ob + 1) * 128], wt[:, kc, 0:512], kc == 0, kc == KCH - 1)
            psg = k.next_ps()
            for kc in range(KCH):
                P.mm(psg[:, 0:256], nT_own[:, kc, ob * 128:(ob + 1) * 128],
                     wg[:, kc, (h % 2) * 256:(h % 2) * 256 + 256], kc == 0, kc == KCH - 1)
            _rotary(k, ps[:, 0:128], ropeq[:, ob, 0:64], ropeq[:, ob, 64:128], Qr[par], tmp[par])
            _rotary(k, ps[:, 128:256], ropek[:, ub, 0:64], ropek[:, ub, 64:128], Kr[par], tmp[par])
            P.copy(POOL, Kb[par], Kr[par])
            P.ts(POOL, Ks[par], Kr[par], wk[:, h:h + 1], None, ALU.mult)
            P.copy("act", V[par], ps[:, 256:512])
            P.act(sg[par], psg[:, 0:256], AF.Silu)

        def stage2(ob):
            par = ob % 2
            P.transpose(psT[:, 0:128], Qr[par], ident)
            P.transpose(psT[:, 128:256], Kb[par], ident)
            P.copy("act", QT, psT[:, 0:128])
            P.copy("act", KT, psT[:, 128:256])
            P.tt("dve", QsT, psT[:, 0:128], wqB[:, h, :], ALU.mult)
            ps = k.next_ps()
            P.mm(ps[:, 0:128], KT, QT, True, True)
            P.tt("dve", SdT, ps[:, 0:128], decayT[:, h, :], ALU.mult)
            P.copy("act", stbf, state8[:, h, :])
            po = k.next_ps()
            P.mm(po[:, 0:256], SdT, V[par], True, False)
            P.mm(po[:, 0:256], QsT, stbf, False, True)
            P.memset("dve", gs, 0.0)
            P.act(osb, po[:, 0:256], AF.Copy, accum_out=gs[:, 0:1])
            P.act(junk, po[:, 0:256], AF.Square, accum_out=gs[:, 1:2])
            P.ts("dve", gs[:, 2:3], gs[:, 0:1], 1.0 / 256, None, ALU.mult)
            P.tt("dve", gs[:, 3:4], gs[:, 2:3], gs[:, 2:3], ALU.mult)
            P.stt("dve", gs[:, 4:5], gs[:, 1:2], 1.0 / 256, gs[:, 3:4], ALU.mult, ALU.subtract)
            P.ts("dve", gs[:, 4:5], gs[:, 4:5], EPS, None, ALU.add)
            P.act(gs[:, 5:6], gs[:, 4:5], AF.Sqrt)
            P.recip(gs[:, 6:7], gs[:, 5:6])
            P.ts("dve", y, osb, gs[:, 2:3], gs[:, 6:7], ALU.subtract, ALU.mult)
            P.tt(POOL, y, y, gnB[:, h * 256:(h + 1) * 256], ALU.mult)
            P.tt(POOL, y16, y, sg[par], ALU.mult)
            for j in range(2):
                P.transpose(psT[:, (2 + j) * 128:(3 + j) * 128], y16[:, j * 128:(j + 1) * 128], ident)
            k.evac(o_retT[:, 2 * h:2 * h + 2, ob * 128:(ob + 1) * 128],
                   psT.v(psT.ap[:, 256:512].rearrange("p (a b) -> p a b", b=128)))
            ps3 = k.next_ps()
            P.mm(ps3[:, 0:256], Ks[par], V[par], True, True)
            P.stt("dve", state8[:, h, :], state8[:, h, :], _G_CHUNK[h], ps3[:, 0:256], ALU.mult, ALU.add)

        stage1(0)
        for ob in range(NB):
            if ob + 1 < NB:
                stage1(ob + 1)
            stage2(ob)
    if "o_retT" in k.dbg_d:
        tmpf = A.alloc("dbgtmp", [128, 2, TOK], F32)
        P.copy("dve", tmpf, o_retT[:, 0:2, :])
        k.dbg_store("o_retT", tmpf)


def build_merge(k, which, srcT):
    P, A, D, WS = k.P, k.R34, k.D, k.WS
    nT_own, mergedT = k.nT_own, k.mergedT
    W = D["w_a"] if which == "a" else D["w_b"]
    og = OGA if which == "a" else OGB
    sig = [A.alloc("sig%d" % i, [128, 512], F32) for i in range(2)]
    tmp = [A.alloc("mtmp%d" % i, [128, 512], F32) for i in range(2)]
    tiles = []
    for i in range(4):
        tiles.append(("wm%s%d" % (which, i),) + wtile_cols(W, 512 * i, 512))
        tiles.append(("wgt%s%d" % (which, i),) + wtile_cols(D["w_in"], og + 512 * i, 512))
    n = 0
    for i in range(4):
        wm = WS.get(tiles, 2 * i, depth=1)
        wg = WS.get(tiles, 2 * i + 1, depth=1)
        for cc in range(4):
            for tch in range(2):
                psA = k.next_ps()
                for kc in range(KCH):
                    P.mm(psA, wm[:, kc, cc * 128:(cc + 1) * 128], srcT[:, kc, tch * 512:(tch + 1) * 512],
                         kc == 0, kc == KCH - 1)
                psG = k.next_ps()
                for kc in range(KCH):
                    P.mm(psG, wg[:, kc, cc * 128:(cc + 1) * 128], nT_own[:, kc, tch * 512:(tch + 1) * 512],
                         kc == 0, kc == KCH - 1)
                par = n % 2
                n += 1
                P.act(sig[par], psG, AF.Sigmoid)
                dst = mergedT[:, 4 * i + cc, tch * 512:(tch + 1) * 512]
                if which == "a":
                    P.tt("dve", dst, sig[par], psA, ALU.mult)
                else:
                    P.tt("dve", tmp[par], sig[par], psA, ALU.mult)
                    P.tt(POOL, dst, tmp[par], dst, ALU.add)
    if ("mergedT_" + which) in k.dbg_d:
        tmpf = A.alloc("dbgtmp", [128, 4, TOK], F32)
        P.copy("dve", tmpf, mergedT[:, 0:4, :])
        k.dbg_store("mergedT_" + which, tmpf)


def build_tail(k, stop_after):
    P, D, WS = k.P, k.D, k.WS
    R1, R2, R34 = k.R1, k.R2, k.R34
    psS, psV, psT, ident = k.psS, k.psV, k.psT, k.ident
    mergedT = k.mergedT
    hb = [R34.alloc("h%d" % tb, [128, DM], F32) for tb in range(NB)]
    for tb in range(NB):
        P.dma("sp", hb[tb], D["xo"][tb * 128:(tb + 1) * 128, :])
    tiles = [("wout%d" % i,) + wtile_cols(D["w_out"], 512 * i, 512) for i in range(4)]
    for i in range(4):
        wt = WS.get(tiles, i)
        for tb in range(NB):
            ps = k.next_ps()
            for kc in range(KCH):
                P.mm(ps, mergedT[:, kc, tb * 128:(tb + 1) * 128], wt[:, kc, :], kc == 0, kc == KCH - 1)
            hs = hb[tb][:, 512 * i:512 * (i + 1)]
            P.tt("dve", hs, hs, ps, ALU.add)
    if "h1" in k.dbg_d:
        for tb in range(NB):
            k.dbg_store("h1", hb[tb], k.dbg_d["h1"][tb * 128:(tb + 1) * 128, :])
    R2.release()
    if stop_after == "h1":
        return

    nxT = R1.alloc("nxT", [128, KCH, TOK], BF16)
    gainX = k.load_gain("x_norm_w", R2)
    xn = R2.alloc("xn", [128, DM], BF16)
    stat = R2.alloc("stat", [128, 16], F32)
    k.norm_T(lambda tb: hb[tb], NB, gainX, nxT, xn, stat)
    P.dma("sp", gainX, D["mem_norm_w"].to_broadcast([128, DM]))
    mh = R34.mark()
    mts = [R34.alloc("mt%d" % i, [128, DM], F32) for i in range(2)]
    mT = R2.alloc("mT", [128, KCH, 256], BF16)
    k.norm_T(k.mk_get(D["mem"], mts), 2, gainX, mT, xn, stat)
    R34.release(mh)
    qxT = R2.alloc("qxT", [128, 4, TOK], BF16)
    kxT = R34.alloc("kxT", [128, 4, 256], BF16)
    vx = R34.alloc("vx", [128, 2, 4, 130], BF16)
    PTx = [R34.alloc("PTx%d" % i, [128, 512], BF16) for i in range(2)]
    xc = R34.alloc("xcoef", [128, 4], F32)
    tiles = [("wqx",) + wtile_cols(D["wq_x"], 0, 512), ("wkx",) + wtile_cols(D["wk_x"], 0, 512),
             ("wvx",) + wtile_cols(D["wv_x"], 0, 512), ("wox",) + wtile_rows(D["wo_x"], 0, 512)]
    wq = WS.get(tiles, 0)
    for hh in range(4):
        for tch in range(2):
            k.proj_T(wq, hh * 128, nxT, tch * 512, 512, qxT[:, hh, tch * 512:(tch + 1) * 512], scale=QSCALE)
    wk_ = WS.get(tiles, 1)
    for hh in range(4):
        k.proj_T(wk_, hh * 128, mT, 0, 256, kxT[:, hh, :])
    wv = WS.get(tiles, 2)
    P.memset("dve", vx, 1.0)
    for mb in range(2):
        ps = k.next_ps()
        for kc in range(KCH):
            P.mm(ps, mT[:, kc, mb * 128:(mb + 1) * 128], wv[:, kc, :], kc == 0, kc == KCH - 1)
        k.evac(vx[:, mb, :, 0:128], ps.v(ps.ap.rearrange("p (a b) -> p a b", b=128)))
    R1.release()
    ox16 = R1.alloc("ox16", [128, NB, 512], BF16)
    oxT = R1.alloc("oxT", [128, 4, TOK], BF16)
    for hh in range(4):
        for tch in range(2):
            for mb in range(2):
                ps = psS[mb]
                P.mm(ps, kxT[:, hh, mb * 128:(mb + 1) * 128], qxT[:, hh, tch * 512:(tch + 1) * 512], True, True)
                P.act(PTx[mb], ps, AF.Exp)
            for tq in range(4):
                tb = tch * 4 + tq
                acc = psV[tq % 2]
                for mb in range(2):
                    P.mm(acc[:, 0:129], PTx[mb][:, tq * 128:(tq + 1) * 128], vx[:, mb, hh, 0:129], mb == 0, mb == 1)
                P.copy("dve", xc[:, 0:1], acc[:, 128:129])
                P.recip(xc[:, 1:2], xc[:, 0:1])
                P.ts("dve", ox16[:, tb, hh * 128:(hh + 1) * 128], acc[:, 0:128], xc[:, 1:2], None, ALU.mult)
    for tb in range(NB):
        for j in range(4):
            P.transpose(psT[:, j * 128:(j + 1) * 128], ox16[:, tb, j * 128:(j + 1) * 128], ident)
        k.evac(oxT[:, :, tb * 128:(tb + 1) * 128], psT.v(psT.ap[:, 0:512].rearrange("p (a b) -> p a b", b=128)))
    wo = WS.get(tiles, 3)
    for tb in range(NB):
        for cc in range(4):
            ps = k.next_ps()
            for kc in range(4):
                P.mm(ps, oxT[:, kc, tb * 128:(tb + 1) * 128], wo[:, kc, cc * 512:(cc + 1) * 512], kc == 0, kc == 3)
            hs = hb[tb][:, 512 * cc:512 * (cc + 1)]
            P.tt("dve", hs, hs, ps, ALU.add)
    if "h2" in k.dbg_d:
        for tb in range(NB):
            k.dbg_store("h2", hb[tb], k.dbg_d["h2"][tb * 128:(tb + 1) * 128, :])
    R1.release()
    R2.release()
    R34.release(mh)
    if stop_after == "h2":
        return

    nmT = R1.alloc("nmT", [128, KCH, TOK], BF16)
    gainM = k.load_gain("mlp_norm_w", R2)
    xn = R2.alloc("xn", [128, DM], BF16)
    stat = R2.alloc("stat", [128, 16], F32)
    k.norm_T(lambda tb: hb[tb], NB, gainM, nmT, xn, stat)
    aT = R2.alloc("aT", [128, 4, TOK], BF16)
    rl = [R2.alloc("rl%d" % i, [128, 512], F32) for i in range(2)]
    tiles = []
    for f in range(16):
        tiles.append(("wup%d" % f,) + wtile_cols(D["w_up"], 512 * f, 512))
        tiles.append(("wdn%d" % f,) + wtile_rows(D["w_down"], 512 * f, 512))
    n = 0
    for f in range(16):
        wu = WS.get(tiles, 2 * f)
        for cc in range(4):
            for tch in range(2):
                ps = k.next_ps()
                for kc in range(KCH):
                    P.mm(ps, wu[:, kc, cc * 128:(cc + 1) * 128], nmT[:, kc, tch * 512:(tch + 1) * 512],
                         kc == 0, kc == KCH - 1)
                par = n % 2
                n += 1
                P.act(rl[par], ps, AF.Relu)
                P.tt(POOL, aT[:, cc, tch * 512:(tch + 1) * 512], rl[par], rl[par], ALU.mult)
        wd = WS.get(tiles, 2 * f + 1)
        for tb in range(NB):
            for cc in range(4):
                ps = k.next_ps()
                for kc in range(4):
                    P.mm(ps, aT[:, kc, tb * 128:(tb + 1) * 128], wd[:, kc, cc * 512:(cc + 1) * 512], kc == 0, kc == 3)
                hs = hb[tb][:, 512 * cc:512 * (cc + 1)]
                P.tt("dve", hs, hs, ps, ALU.add)
    if "h3" in k.dbg_d:
        for tb in range(NB):
            k.dbg_store("h3", hb[tb], k.dbg_d["h3"][tb * 128:(tb + 1) * 128, :])
    R1.release()
    R2.release()

    gainF = k.load_gain("final_norm_w", R2)
    outt = [R2.alloc("outt%d" % i, [128, DM], F32) for i in range(2)]
    junk = R2.alloc("junkf", [128, DM], BF16)
    stat = R2.alloc("statf", [128, 16], F32)
    P.memset("dve", stat, 0.0)
    for tb in range(NB):
        P.act(junk, hb[tb], AF.Square, accum_out=stat[:, tb:tb + 1])
        P.ts("dve", stat[:, tb:tb + 1], stat[:, tb:tb + 1], 1.0 / DM, EPS, ALU.mult, ALU.add)
        P.act(stat[:, tb:tb + 1], stat[:, tb:tb + 1], AF.Sqrt)
        P.recip(stat[:, tb:tb + 1], stat[:, tb:tb + 1])
        P.stt("dve", outt[tb % 2], hb[tb], stat[:, tb:tb + 1], gainF, ALU.mult, ALU.mult)
        P.dma("sp", k.out_d[tb * 128:(tb + 1) * 128, :], outt[tb % 2])


def _w_in_perm():
    q = np.arange(0, 2048)
    parts = [q]
    for g in range(4):
        for base in (3072, 3584, 4096, 4608):
            parts.append(base + 128 * g + np.arange(128))
    parts.append(2048 + np.arange(512))
    parts.append(2560 + np.arange(512))
    for h in range(8):
        parts.append(5168 + 128 * h + np.arange(128))
        parts.append(6192 + 128 * h + np.arange(128))
        parts.append(7216 + 256 * h + np.arange(256))
    parts.append(np.arange(9264, 15408))
    parts.append(5120 + np.arange(48))
    perm = np.concatenate(parts)
    assert perm.shape[0] == IN_WIDTH and len(set(perm.tolist())) == IN_WIDTH
    return perm


def _tables(s):
    f32 = np.float32
    T = {}
    T["ident"] = np.eye(128, dtype=f32)
    u = np.arange(2048)
    t = u - 1024 + 1024 * s
    inv = (10000.0 ** (-np.arange(0, 128, 2, dtype=f32) / f32(128))).astype(f32)
    ang = t.astype(f32)[:, None] * inv[None, :]
    cos, sin = np.cos(ang).astype(f32), np.sin(ang).astype(f32)
    rk = np.concatenate([cos, sin], axis=1) * f32(128 ** -0.5)
    T["ropek"] = np.ascontiguousarray(rk.reshape(16, 128, 128).transpose(1, 0, 2)).astype(f32)
    rq = np.concatenate([cos, sin], axis=1)[1024:]
    T["ropeq"] = np.ascontiguousarray(rq.reshape(8, 128, 128).transpose(1, 0, 2)).astype(f32)
    H = 8
    log_g = np.log1p(-np.exp2(-5.0 - np.arange(H, dtype=f32))).astype(f32)
    i = np.arange(128, dtype=f32)
    rel = i[:, None] - i[None, :]
    decay = np.where(rel >= 0, np.exp(log_g[:, None, None] * np.maximum(rel, 0.0)), 0.0).astype(f32)
    T["decayT"] = np.ascontiguousarray(decay.transpose(2, 0, 1))
    w_q = np.exp(log_g[:, None] * (i + 1.0)[None, :]).astype(f32)
    T["wqB"] = np.ascontiguousarray(np.broadcast_to(w_q[None], (128, H, 128))).astype(f32)
    w_k = np.exp(log_g[:, None] * (127.0 - i)[None, :]).astype(f32)
    T["wk"] = np.ascontiguousarray(w_k.T)
    T["g_chunk"] = np.exp(log_g * f32(128.0)).astype(f32)
    uq = 1024 + np.arange(1024)
    lc = np.arange(128)
    vis = (16 * lc[None, :] + 31 <= uq[:, None]) & (lc[None, :] < 127)
    if s == 0:
        vis &= (lc[None, :] >= 64)
    T["cmask"] = np.ascontiguousarray(vis.astype(f32).reshape(8, 128, 128).transpose(1, 0, 2))
    lj = np.arange(32)
    lcur = (uq // 64)[:, None]
    first = 0 if s == 1 else 16
    am = np.zeros((1024, 32), f32)
    am[np.broadcast_to(lj[None, :] == first, am.shape)] = 1e9
    prev = (lj[None, :] == lcur - 1) & (lj[None, :] >= first)
    am[prev] = 2e9
    am[np.broadcast_to(lj[None, :], am.shape) == lcur] = 3e9
    am[(lj[None, :] > lcur) | (lj[None, :] < first)] = -1e9
    T["addmask"] = np.ascontiguousarray(am.reshape(8, 128, 32).transpose(1, 0, 2))
    kval = (t >= 0).astype(f32)
    T["kvalid"] = np.ascontiguousarray(kval.reshape(16, 128).T)
    kk = np.arange(2048)
    T["Emat"] = (kk[None, :] // 64 == np.arange(32)[:, None]).astype(f32)
    kq = np.arange(128)
    T["causneg"] = np.tile(np.where(kq[:, None] > kq[None, :], NEGB, 0.0).astype(f32), (1, 4))
    T["winneg"] = np.tile(np.where(kq[:, None] <= kq[None, :], NEGB, 0.0).astype(f32), (1, 4))
    return T


def make_in_maps(inputs):
    f32 = np.float32
    x = np.asarray(inputs["x"], f32)
    mem = np.asarray(inputs["mem"], f32)
    perm = _w_in_perm()
    shared = {}
    shared["w_in"] = np.ascontiguousarray(np.asarray(inputs["w_in"], f32)[0][:, perm])
    for nm in ("cmp_w1_k", "cmp_w1_v", "cmp_w2_k", "cmp_w2_v", "w_a", "w_b", "w_out", "wq_x", "wk_x", "wv_x",
               "wo_x", "w_up", "w_down"):
        shared[nm] = np.ascontiguousarray(np.asarray(inputs[nm], f32)[0])
    shared["peT_k"] = np.ascontiguousarray(np.asarray(inputs["cmp_pe_k"], f32)[0].T)
    shared["peT_v"] = np.ascontiguousarray(np.asarray(inputs["cmp_pe_v"], f32)[0].T)
    for nm in ("attn_norm_w", "ret_gn_w", "x_norm_w", "mem_norm_w", "mlp_norm_w"):
        shared[nm] = np.ascontiguousarray(np.asarray(inputs[nm], f32).reshape(1, DM))
    shared["final_norm_w"] = np.ascontiguousarray(np.asarray(inputs["final_norm_w"], f32).reshape(1, DM))
    tabs = [_tables(0), _tables(1)]
    zeros = np.zeros((TOK, DM), f32)
    in_maps = []
    for c in range(8):
        b, s = c // 2, c % 2
        m = dict(shared)
        m["xo"] = np.ascontiguousarray(x[b, 1024 * s:1024 * (s + 1)])
        m["xc"] = np.ascontiguousarray(x[b, 0:1024]) if s == 1 else zeros
        m["mem"] = np.ascontiguousarray(mem[b])
        for kk, v in tabs[s].items():
            if kk != "g_chunk":
                m[kk] = v
        in_maps.append(m)
    return in_maps


_G_CHUNK = [float(v) for v in np.exp(np.log1p(-np.exp2(-5.0 - np.arange(8, dtype=np.float32))).astype(np.float32)
                                     * np.float32(128.0)).astype(np.float32)]


def kernel(**inputs):
    in_maps = make_in_maps(inputs)
    nc, st = build_program()
    res = run_bass_kernel_spmd(nc, in_maps, core_ids=list(range(8)))
    out = np.zeros((4, 2048, DM), np.float32)
    for c in range(8):
        b, s = c // 2, c % 2
        out[b, 1024 * s:1024 * (s + 1)] = res.results[c]["out"]
    return out
```

```python
import numpy as np
from concourse.bass_utils import run_bass_kernel_spmd
import concourse.bass as bass
import concourse.mybir as mybir

F32 = mybir.dt.float32
BF16 = mybir.dt.bfloat16
AF = mybir.ActivationFunctionType
ALU = mybir.AluOpType
AX = mybir.AxisListType

_DT_SIZE = {F32: 4, BF16: 2}


class Buf:
    def __init__(self, key, ap):
        self.key = key
        self.ap = ap

    def __getitem__(self, idx):
        return Buf(self.key, self.ap[idx])

    def v(self, ap):
        return Buf(self.key, ap)


class Op:
    __slots__ = ("eng", "fn", "reads", "writes", "is_dma", "semkey", "deps", "dma_deps",
                 "signal", "signum", "pos", "accum", "idx")


class Prog:
    ENGS = ("pe", "act", "dve", "pool", "sp")

    def __init__(self, nc):
        self.nc = nc
        self.ops = []
        self.eng_obj = {"pe": nc.tensor, "act": nc.scalar, "dve": nc.vector, "pool": nc.gpsimd, "sp": nc.sync}
        self.sync_same_engine_war = False

    def _add(self, eng, fn, reads, writes, is_dma=False, semkey=None, accum=False):
        o = Op()
        o.eng = eng
        o.fn = fn
        o.reads = [b.key for b in reads if b is not None]
        o.writes = [b.key for b in writes if b is not None]
        for kk in o.reads:
            if kk.startswith("ps") and kk not in o.writes:
                o.writes.append(kk)
        o.is_dma = is_dma
        o.semkey = semkey
        o.accum = accum
        o.idx = len(self.ops)
        self.ops.append(o)
        return o

    def op(self, eng, fn, reads=(), writes=()):
        return self._add(eng, fn, reads, writes)

    def barrier(self):
        o = Op()
        o.eng = None
        o.idx = len(self.ops)
        self.ops.append(o)

    def dma(self, eng, out, in_, semkey=None):
        reads, writes = [], []
        if isinstance(in_, Buf):
            reads.append(in_)
            in_ap = in_.ap
        else:
            in_ap = in_
        if isinstance(out, Buf):
            writes.append(out)
            out_ap = out.ap
            if semkey is None:
                semkey = "dma:" + out.key
        else:
            out_ap = out
            if semkey is None:
                semkey = "dma:store"

        def fn(e):
            return e.dma_start(out=out_ap, in_=in_ap)

        return self._add(eng, fn, reads, writes, is_dma=True, semkey=semkey)

    def mm(self, out, lhsT, rhs, start, stop, extra_reads=(), sgc=False):
        def fn(e):
            if sgc:
                return e.matmul(out.ap, lhsT.ap, rhs.ap, start=start, stop=stop, skip_group_check=True)
            return e.matmul(out.ap, lhsT.ap, rhs.ap, start=start, stop=stop)
        return self._add("pe", fn, [lhsT, rhs] + list(extra_reads), [out], accum=not start)

    def transpose(self, out, in_, ident):
        def fn(e):
            return e.transpose(out.ap, in_.ap, ident.ap)
        return self._add("pe", fn, [in_, ident], [out])

    def act(self, out, in_, func, bias=None, scale=1.0, accum_out=None, eng="act"):
        reads = [in_]
        kw = {}
        if isinstance(bias, Buf):
            reads.append(bias)
            kw["bias"] = bias.ap
        elif bias is not None:
            kw["bias"] = bias
        if isinstance(scale, Buf):
            reads.append(scale)
            kw["scale"] = scale.ap
        else:
            kw["scale"] = scale
        writes = [out]
        if accum_out is not None:
            writes.append(accum_out)
            kw["accum_out"] = accum_out.ap

        def fn(e):
            return e.activation(out=out.ap, in_=in_.ap, func=func, **kw)
        return self._add(eng, fn, reads, writes)

    def tt(self, eng, out, in0, in1, op):
        def fn(e):
            return e.tensor_tensor(out=out.ap, in0=in0.ap, in1=in1.ap, op=op)
        return self._add(eng, fn, [in0, in1], [out])

    def ts(self, eng, out, in0, s1, s2, op0, op1=None, accum_out=None):
        reads = [in0]
        a1 = s1
        a2 = s2
        if isinstance(s1, Buf):
            reads.append(s1)
            a1 = s1.ap
        if isinstance(s2, Buf):
            reads.append(s2)
            a2 = s2.ap
        writes = [out]
        kw = {}
        if op1 is not None:
            kw["op1"] = op1
        if accum_out is not None:
            writes.append(accum_out)
            kw["accum_out"] = accum_out.ap

        def fn(e):
            return e.tensor_scalar(out=out.ap, in0=in0.ap, scalar1=a1, scalar2=a2, op0=op0, **kw)
        return self._add(eng, fn, reads, writes)

    def stt(self, eng, out, in0, scalar, in1, op0, op1):
        reads = [in0, in1]
        a = scalar
        if isinstance(scalar, Buf):
            reads.append(scalar)
            a = scalar.ap

        def fn(e):
            return e.scalar_tensor_tensor(out=out.ap, in0=in0.ap, scalar=a, in1=in1.ap, op0=op0, op1=op1)
        return self._add(eng, fn, reads, [out])

    def copy(self, eng, out, in_):
        if eng == "act":
            def fn(e):
                return e.copy(out=out.ap, in_=in_.ap)
        else:
            def fn(e):
                return e.tensor_copy(out=out.ap, in_=in_.ap)
        return self._add(eng, fn, [in_], [out])

    def reduce(self, eng, out, in_, op, axis=AX.X):
        def fn(e):
            return e.tensor_reduce(out=out.ap, in_=in_.ap, axis=axis, op=op)
        return self._add(eng, fn, [in_], [out])

    def memset(self, eng, out, val):
        def fn(e):
            return e.memset(out.ap, val)
        return self._add(eng, fn, [], [out])

    def recip(self, out, in_):
        def fn(e):
            return e.reciprocal(out=out.ap, in_=in_.ap)
        return self._add("dve", fn, [in_], [out])

    def emit(self, final_wait_eng="sp"):
        nc = self.nc
        ops = self.ops
        last_writer = {}
        readers = {}
        pos_ctr = {e: 0 for e in self.ENGS}
        waited = {f: {e: -1 for e in self.ENGS} for f in self.ENGS}
        waited_dma = {f: {} for f in self.ENGS}
        dma_count = {}
        last_op_on = {e: None for e in self.ENGS}
        pending_barrier = {e: [] for e in self.ENGS}
        outstanding_dma = []

        for o in ops:
            if o.eng is None:
                for f in self.ENGS:
                    pending_barrier[f] = [last_op_on[e] for e in self.ENGS if e != f and last_op_on[e] is not None]
                continue
            f = o.eng
            deps = set()
            for k in o.reads:
                w = last_writer.get(k)
                if w is not None:
                    deps.add(w)
            for k in o.writes:
                w = last_writer.get(k)
                if w is not None:
                    deps.add(w)
                for r in readers.get(k, ()):
                    deps.add(r)
            for b in pending_barrier[f]:
                deps.add(b)
            pending_barrier[f] = []
            o.pos = pos_ctr[f]
            pos_ctr[f] += 1
            o.deps = []
            o.dma_deps = []
            o.signal = False
            for di in sorted(deps):
                d = ops[di]
                if d.idx == o.idx:
                    continue
                if d.is_dma:
                    cnt = d.signum
                    if waited_dma[f].get(d.semkey, 0) >= cnt:
                        continue
                    waited_dma[f][d.semkey] = cnt
                    o.dma_deps.append((d.semkey, cnt))
                else:
                    if d.eng == "pe" and f == "pe" and not o.is_dma:
                        continue
                    if d.eng == f and not self.sync_same_engine_war and not o.is_dma:
                        is_raw_waw = any(last_writer.get(k) == di for k in o.reads + o.writes)
                        if not is_raw_waw:
                            continue
                    if waited[f][d.eng] >= d.pos:
                        continue
                    waited[f][d.eng] = d.pos
                    d.signal = True
                    o.deps.append(di)
            if o.is_dma:
                dma_count[o.semkey] = dma_count.get(o.semkey, 0) + 16
                o.signum = dma_count[o.semkey]
                outstanding_dma.append(o)
            for k in o.reads:
                readers.setdefault(k, []).append(o.idx)
            for k in o.writes:
                last_writer[k] = o.idx
                readers[k] = []
            last_op_on[f] = o.idx

        tail_deps = []
        for e in self.ENGS:
            li = last_op_on[e]
            if li is not None and not ops[li].is_dma:
                ops[li].signal = True
                tail_deps.append(li)
        sig_ctr = {e: 0 for e in self.ENGS}
        for o in ops:
            if o.eng is None or o.is_dma:
                continue
            if o.signal:
                sig_ctr[o.eng] += 1
                o.signum = sig_ctr[o.eng]
        import contextlib
        es = contextlib.ExitStack()
        self._es = es
        sems = {e: es.enter_context(nc.semaphore("s_" + e)) for e in self.ENGS}
        dsems = {}
        for k in dma_count:
            dsems[k] = es.enter_context(nc.semaphore("d%d" % len(dsems)))
        n_wait = 0
        for o in ops:
            if o.eng is None:
                continue
            e = self.eng_obj[o.eng]
            for di in o.deps:
                d = ops[di]
                e.wait_ge(sems[d.eng], d.signum)
                n_wait += 1
            for (k, cnt) in o.dma_deps:
                e.wait_ge(dsems[k], cnt)
                n_wait += 1
            ins = o.fn(e)
            if o.is_dma:
                ins.then_inc(dsems[o.semkey], 16)
            elif o.signal:
                ins.then_inc(sems[o.eng], 1)
        fe = self.eng_obj[final_wait_eng]
        for di in tail_deps:
            d = ops[di]
            fe.wait_ge(sems[d.eng], d.signum)
        for k, cnt in dma_count.items():
            fe.wait_ge(dsems[k], cnt)
        self.stats = dict(n_ops=len(ops), n_wait=n_wait, sig=dict(sig_ctr), n_dsems=len(dsems))
        return self.stats


TOK = 1024
NB = 8
DM = 2048
KCH = 16
EPS = 1e-6
OQ, OKV, OC, OR_, OGR, OGA, OGB, OG = 0, 2048, 4096, 5120, 9216, 11264, 13312, 15360
IN_WIDTH = 15408
NEGB = -30000.0
QSCALE = 128 ** -0.5
POOL = "dve"


def _prod(xs):
    r = 1
    for x in xs:
        r *= int(x)
    return r


class Arena:
    uid = 0

    def __init__(self, full_ap, P, base, nel, name):
        self.ap = full_ap
        self.base = base
        self.top = 0
        self.nel = nel
        self.P = P
        self.peak = 0
        self.name = name

    def alloc(self, name, shape, dt):
        inner = _prod(shape[1:])
        n = inner * (2 if dt == F32 else 1)
        npad = (n + 15) // 16 * 16
        off = self.base + self.top
        self.top += npad
        self.peak = max(self.peak, self.top)
        assert self.top <= self.nel, ("SBUF arena overflow", self.name, name, self.top, self.nel)
        ap = self.ap[:shape[0], off:off + n]
        if dt == F32:
            ap = ap.bitcast(F32)
        if len(shape) == 3:
            ap = ap.rearrange("p (a b) -> p a b", b=shape[2])
        elif len(shape) == 4:
            ap = ap.rearrange("p (a b c) -> p a b c", b=shape[2], c=shape[3])
        Arena.uid += 1
        return Buf("%s#%d" % (name, Arena.uid), ap)

    def mark(self):
        return self.top

    def release(self, mark=0):
        self.top = mark
        self.P.barrier()


class WStream:
    def __init__(self, P, arena, nslots, slot_elems):
        self.P = P
        self.nslots = nslots
        self.slot_elems = slot_elems
        self.slots = [arena.alloc("wslot%d" % i, [128, slot_elems], BF16) for i in range(nslots)]
        self.ctr = 0
        self.loaded = {}

    def _load(self, item):
        key, src, shape = item
        if key in self.loaded:
            return
        s = self.slots[self.ctr % self.nslots]
        self.ctr += 1
        n = _prod(shape[1:])
        ap = s.ap[:, 0:n]
        if len(shape) == 3:
            ap = ap.rearrange("p (a b) -> p a b", b=shape[2])
        b = Buf(s.key, ap)
        self.P.dma("pool", b, src)
        self.loaded[key] = b

    def get(self, lst, i, depth=None):
        depth = self.nslots - 1 if depth is None else depth
        for j in range(i, min(len(lst), i + depth + 1)):
            self._load(lst[j])
        b = self.loaded.pop(lst[i][0])
        return b


def wtile_cols(w2d, c0, ncols):
    return w2d[:, c0:c0 + ncols].rearrange("(kc p) c -> p kc c", p=128), [128, 16, ncols]


def wtile_rows(w2d, r0, nrows):
    return w2d[r0:r0 + nrows, :].rearrange("(kc p) c -> p kc c", p=128), [128, nrows // 128, 2048]


class K:
    pass


def build_program(dbg=None, stop_after=None):
    nc = bass.Bass("TRN2", target_bir_lowering=False)
    P = Prog(nc)
    k = K()
    k.nc, k.P = nc, P

    def din(name, shape):
        return nc.dram_tensor(name, list(shape), F32, kind="ExternalInput").ap()

    D = {}
    D["xo"] = din("xo", [TOK, DM])
    D["xc"] = din("xc", [TOK, DM])
    D["mem"] = din("mem", [256, DM])
    D["w_in"] = din("w_in", [DM, IN_WIDTH])
    for nm in ("cmp_w1_k", "cmp_w1_v"):
        D[nm] = din(nm, [4096, 1024])
    for nm in ("cmp_w2_k", "cmp_w2_v"):
        D[nm] = din(nm, [1024, 128])
    for nm in ("peT_k", "peT_v"):
        D[nm] = din(nm, [128, 32])
    for nm in ("w_a", "w_b", "w_out"):
        D[nm] = din(nm, [DM, DM])
    for nm in ("wq_x", "wk_x", "wv_x"):
        D[nm] = din(nm, [DM, 512])
    D["wo_x"] = din("wo_x", [512, DM])
    D["w_up"] = din("w_up", [DM, 8192])
    D["w_down"] = din("w_down", [8192, DM])
    for nm in ("attn_norm_w", "ret_gn_w", "x_norm_w", "mem_norm_w", "mlp_norm_w", "final_norm_w"):
        D[nm] = din(nm, [1, DM])
    D["ident"] = din("ident", [128, 128])
    D["ropeq"] = din("ropeq", [128, 8, 128])
    D["ropek"] = din("ropek", [128, 16, 128])
    D["decayT"] = din("decayT", [128, 8, 128])
    D["wqB"] = din("wqB", [128, 8, 128])
    D["wk"] = din("wk", [128, 8])
    D["wkc"] = din("wkc", [128, 8, 8])
    D["cmask"] = din("cmask", [128, 8, 128])
    D["addmask"] = din("addmask", [128, 8, 32])
    D["kvalid"] = din("kvalid", [128, 16])
    D["Emat"] = din("Emat", [32, 2048])
    D["causneg"] = din("causneg", [128, 512])
    D["winneg"] = din("winneg", [128, 512])
    out_d = nc.dram_tensor("out", [TOK, DM], F32, kind="ExternalOutput").ap()
    dbg_d = {}
    if dbg:
        for nm, shp in dbg.items():
            dbg_d[nm] = nc.dram_tensor("dbg_" + nm, list(shp), F32, kind="ExternalOutput").ap()
    k.D, k.out_d, k.dbg_d = D, out_d, dbg_d

    NEL = 106000
    full = nc.alloc_sbuf_tensor("arena", [128, NEL], BF16).ap()
    R0 = Arena(full, P, 0, 30000, "R0")
    R1 = Arena(full, P, 30000, 16384, "R1")
    R2 = Arena(full, P, 46384, 16384, "R2")
    R34 = Arena(full, P, 62768, NEL - 62768, "R34")
    k.R0, k.R1, k.R2, k.R34 = R0, R1, R2, R34
    A = R0
    k.psS = [Buf("psS%d" % i, nc.alloc_psum_tensor("psS%d" % i, [128, 512], F32).ap()) for i in range(3)]
    k.psC = Buf("psC", nc.alloc_psum_tensor("psC", [128, 512], F32).ap())
    k.psO = Buf("psO", nc.alloc_psum_tensor("psO", [128, 512], F32).ap())
    k.psV = [Buf("psV%d" % i, nc.alloc_psum_tensor("psV%d" % i, [128, 512], F32).ap()) for i in range(2)]
    k.psT = Buf("psT", nc.alloc_psum_tensor("psT", [128, 1024], BF16).ap())
    k.rot5 = [k.psS[0], k.psS[1], k.psS[2], k.psC, k.psO]
    k.rot_i = 0
    k.ev_i = 0

    k.ident = A.alloc("ident", [128, 128], BF16)
    P.dma("pool", k.ident, D["ident"][:, :])
    k.WS = WStream(P, A, 3, 16 * 512)

    def finish():
        st = P.emit()
        st["arena_peak_el"] = [R0.peak, R1.peak, R2.peak, R34.peak]
        return nc, st

    def dbg_store(name, buf, dram_view=None):
        if name in dbg_d:
            dv = dbg_d[name] if dram_view is None else dram_view
            P.dma("sp", dv, buf, semkey="dma:dbg")

    k.dbg_store = dbg_store

    def next_ps():
        b = k.rot5[k.rot_i % 5]
        k.rot_i += 1
        return b

    def evac(out, in_, scale=None):
        e = k.ev_i % 2
        k.ev_i += 1
        if e == 0:
            if scale is None:
                P.copy("act", out, in_)
            else:
                P.act(out, in_, AF.Copy, scale=scale)
        else:
            if scale is None:
                P.copy("dve", out, in_)
            else:
                P.ts("dve", out, in_, scale, None, ALU.mult)

    k.next_ps, k.evac = next_ps, evac

    def load_gain(name, reg):
        g = reg.alloc("gain_" + name, [128, DM], F32)
        P.dma("sp", g, D[name].to_broadcast([128, DM]))
        return g

    def norm_T(get_block, nblk, gain, nT, xn, stat):
        P.memset("dve", stat, 0.0)
        for tb in range(nblk):
            xt = get_block(tb)
            P.act(xn, xt, AF.Square, accum_out=stat[:, tb:tb + 1])
            P.ts("dve", stat[:, tb:tb + 1], stat[:, tb:tb + 1], 1.0 / DM, EPS, ALU.mult, ALU.add)
            P.act(stat[:, tb:tb + 1], stat[:, tb:tb + 1], AF.Sqrt)
            P.recip(stat[:, tb:tb + 1], stat[:, tb:tb + 1])
            P.stt("dve", xn, xt, stat[:, tb:tb + 1], gain, ALU.mult, ALU.mult)
            for half in range(2):
                for j in range(8):
                    kc = half * 8 + j
                    P.transpose(k.psT[:, j * 128:(j + 1) * 128], xn[:, kc * 128:(kc + 1) * 128], k.ident)
                src = k.psT.v(k.psT.ap.rearrange("p (a b) -> p a b", b=128))
                evac(nT[:, half * 8:half * 8 + 8, tb * 128:(tb + 1) * 128], src)

    def proj_T(wt, c0, nT, t0, ntok, out, scale=None, nk=KCH, out_view3=False):
        ps = next_ps()
        for kc in range(nk):
            P.mm(ps[:, 0:ntok], wt[:, kc, c0:c0 + 128], nT[:, kc, t0:t0 + ntok], kc == 0, kc == nk - 1)
        src = ps[:, 0:ntok]
        if out_view3:
            src = ps.v(ps.ap[:, 0:ntok].rearrange("p (a b) -> p a b", b=128))
        evac(out, src, scale)

    k.norm_T, k.proj_T, k.load_gain = norm_T, proj_T, load_gain

    nT_own = R1.alloc("nT_own", [128, KCH, TOK], BF16)
    nT_ctx = R2.alloc("nT_ctx", [128, KCH, TOK], BF16)
    k.state8 = R0.alloc("state8", [128, 8, 256], F32)
    k.kcmpT = R0.alloc("kcmpT", [128, 4, 128], BF16)
    k.vcmp = R0.alloc("vcmp", [128, 4, 128], BF16)
    gainA = load_gain("attn_norm_w", R34)
    xts = [R34.alloc("xt%d" % i, [128, DM], F32) for i in range(2)]
    xn = R34.alloc("xn", [128, DM], BF16)
    stat = R34.alloc("stat", [128, 16], F32)

    def mk_get(src, xts):
        def get(tb):
            xt = xts[tb % 2]
            P.dma("sp", xt, src[tb * 128:(tb + 1) * 128, :])
            return xt
        return get

    k.mk_get = mk_get
    norm_T(mk_get(D["xc"], xts), NB, gainA, nT_ctx, xn, stat)
    norm_T(mk_get(D["xo"], xts), NB, gainA, nT_own, xn, stat)
    if "nT_own" in dbg_d:
        tmpf = R34.alloc("dbgtmp", [128, KCH, TOK // 4], F32)
        P.copy("dve", tmpf, nT_own[:, :, 0:TOK // 4])
        dbg_store("nT_own", tmpf)
    R34.release()
    if stop_after == "norm":
        return finish()
    k.nT_own, k.nT_ctx = nT_own, nT_ctx
    k.finish = finish

    build_ret_ctx(k)
    R34.release()
    if stop_after == "ret_ctx":
        return finish()
    build_nsa(k, stop_after)
    if stop_after and stop_after.startswith("nsa"):
        return finish()
    R34.release(k.m_after_o)
    R2.release()
    k.mergedT = R2.alloc("mergedT", [128, KCH, TOK], BF16)
    build_merge(k, "a", k.o_nsaT)
    R34.release()
    if stop_after == "merge_a":
        return finish()
    build_ret_own(k)
    if stop_after == "ret":
        return finish()
    R34.release(k.m_after_o)
    build_merge(k, "b", k.o_retT)
    R34.release()
    R1.release()
    if stop_after == "merge_b":
        return finish()
    build_tail(k, stop_after)
    return finish()


def build_nsa(k, stop_after):
    P, A, D, WS = k.P, k.R34, k.D, k.WS
    nT_own, nT_ctx = k.nT_own, k.nT_ctx
    psS, psC, psO, psV, psT = k.psS, k.psC, k.psO, k.psV, k.psT
    evac, next_ps, proj_T = k.evac, k.next_ps, k.proj_T
    ident = k.ident
    w_in = D["w_in"]
    uchunks = [(nT_ctx, 0, 0), (nT_ctx, 512, 512), (nT_own, 0, 1024), (nT_own, 512, 1536)]

    o_nsaT = A.alloc("o_nsaT", [128, 16, TOK], BF16)
    k.o_nsaT = o_nsaT
    k.m_after_o = A.mark()
    kcmpT, vcmp = k.kcmpT, k.vcmp
    P.memset("dve", kcmpT, 0.0)
    P.memset("dve", vcmp, 0.0)
    m1 = A.mark()
    kvT = A.alloc("kvcT", [128, 4, 2048], BF16)
    hidT = A.alloc("hidT", [128, 8, 4, 128], BF16)
    peT = A.alloc("peT", [128, 32], BF16)
    w2 = A.alloc("w2", [128, 8, 128], BF16)
    cbias = A.alloc("cbias", [128, 8], F32)
    for kind in range(2):
        sfx = "_k" if kind == 0 else "_v"
        tiles = [("wc%d" % kind,) + wtile_cols(w_in, OC + 512 * kind, 512)]
        for hh in range(2):
            for lh in range(2):
                src = D["cmp_w1" + sfx][2048 * lh:2048 * (lh + 1), 512 * hh:512 * (hh + 1)].rearrange(
                    "(l p) c -> p l c", p=128)
                tiles.append(("w1%d_%d_%d" % (kind, hh, lh), src, [128, 16, 512]))
        P.dma("pool", peT, D["peT" + sfx][:, :])
        P.dma("pool", w2, D["cmp_w2" + sfx].rearrange("(hc p) c -> p hc c", p=128))
        wt = WS.get(tiles, 0)
        for g in range(4):
            for (nT, t0, u0) in uchunks:
                proj_T(wt, g * 128, nT, t0, 512, kvT[:, g, u0:u0 + 512])
        ti = 1
        for hh in range(2):
            accs = [psS[0], psS[1], psS[2], psC]
            for lh in range(2):
                wt = WS.get(tiles, ti)
                ti += 1
                for li in range(16):
                    l = lh * 16 + li
                    for hc in range(4):
                        lhsT = wt[:, li, hc * 128:(hc + 1) * 128]
                        for g in range(4):
                            rhs = kvT[:, g, l:l + 16 * 126 + 1:16]
                            P.mm(accs[hc][:, g * 127:(g + 1) * 127], lhsT, rhs, l == 0 and g == 0, l == 31, sgc=True)
                        P.mm(psO[:, hc:hc + 1], lhsT, peT[:, l:l + 1], l == 0 and hc == 0, l == 31, sgc=True)
            for hc in range(4):
                hcg = hh * 4 + hc
                P.copy("dve", cbias[:, hcg:hcg + 1], psO[:, hc:hc + 1])
                src = accs[hc].v(accs[hc].ap[:, 0:508].rearrange("p (g c) -> p g c", c=127))
                P.act(hidT[:, hcg, :, 0:127], src, AF.Silu, bias=cbias[:, hcg:hcg + 1])
        if kind == 0:
            ps = next_ps()
            for g in range(4):
                for hc in range(8):
                    P.mm(ps[:, g * 127:(g + 1) * 127], w2[:, hc, :], hidT[:, hc, g, 0:127], hc == 0, hc == 7)
            evac(kcmpT[:, :, 0:127], ps.v(ps.ap[:, 0:508].rearrange("p (g c) -> p g c", c=127)))
        else:
            ps = next_ps()
            for g in range(4):
                for hc in range(8):
                    P.mm(ps[0:127, g * 128:(g + 1) * 128], hidT[:, hc, g, 0:127], w2[:, hc, :], hc == 0, hc == 7)
            evac(vcmp[0:127, :, :], ps.v(ps.ap[0:127, :].rearrange("p (g c) -> p g c", c=128)))
    if "kcmpT" in k.dbg_d:
        tmpf = A.alloc("dbgtmp", [128, 4, 128], F32)
        P.copy("dve", tmpf, kcmpT)
        k.dbg_store("kcmpT", tmpf)
        tmpf2 = A.alloc("dbgtmp2", [128, 4, 128], F32)
        P.copy("dve", tmpf2, vcmp)
        k.dbg_store("vcmp", tmpf2)
    A.release(m1)
    if stop_after == "nsa_cmp":
        return

    cmask = A.alloc("cmask", [128, 8, 128], F32)
    addmask = A.alloc("addmask", [128, 8, 32], F32)
    kvalid = A.alloc("kvalid", [128, 16], BF16)
    Emat = A.alloc("Emat", [32, 2048], BF16)
    causneg = A.alloc("causneg", [128, 512], BF16)
    winneg = A.alloc("winneg", [128, 512], BF16)
    g3 = A.alloc("g3", [128, 8, 48], F32)
    P.dma("sp", cmask, D["cmask"][:, :, :])
    P.dma("sp", addmask, D["addmask"][:, :, :])
    P.dma("pool", kvalid, D["kvalid"][:, :])
    P.dma("pool", Emat, D["Emat"][:, :])
    P.dma("pool", causneg, D["causneg"][:, :])
    P.dma("pool", winneg, D["winneg"][:, :])
    tiles = [("wg3",) + wtile_cols(w_in, OG, 48)]
    for g in range(4):
        tiles.append(("wq%d" % g,) + wtile_cols(w_in, OQ + 512 * g, 512))
        tiles.append(("wkv%d" % g,) + wtile_cols(w_in, OKV + 512 * g, 512))
    wt = WS.get(tiles, 0)
    for tb in range(NB):
        ps = next_ps()
        for kc in range(KCH):
            P.mm(ps[:, 0:48], nT_own[:, kc, tb * 128:(tb + 1) * 128], wt[:, kc, 0:48], kc == 0, kc == KCH - 1)
        P.act(g3[:, tb, :], ps[:, 0:48], AF.Sigmoid)

    qT = A.alloc("qT", [128, NB, 4, 128], BF16)
    ksT = A.alloc("ksT", [128, 2048], BF16)
    kwT = A.alloc("kwT", [128, 1536], BF16)
    vs = A.alloc("vs", [128, 16, 130], BF16)
    vw = A.alloc("vw", [128, 12, 130], BF16)
    e32 = A.alloc("e32", [128, 4, 128], F32)
    p32 = A.alloc("p32", [128, 4, 128], F32)
    p16 = A.alloc("p16", [128, 4, 128], BF16)
    pT = A.alloc("pT", [128, 4, 128], BF16)
    Pg = A.alloc("Pg", [128, 128], F32)
    imp = A.alloc("imp", [128, 32], F32)
    imp2 = A.alloc("imp2", [128, 32], F32)
    m8 = A.alloc("m8", [128, 8], F32)
    sm4 = A.alloc("sm4", [128, 16], F32)
    selneg = A.alloc("selneg", [128, 32], BF16)
    negT = [A.alloc("negT%d" % i, [32, 4, 128], BF16) for i in range(2)]
    oacc = [A.alloc("oacc%d" % i, [128, 4, 128], F32) for i in range(2)]
    o16 = A.alloc("o16", [128, 4, 128], BF16)
    PT = [A.alloc("PT%d" % i, [128, 512], BF16) for i in range(2)]
    coef = A.alloc("coef", [128, 8], F32)
    P.memset("dve", vs, 0.0)
    P.memset("dve", vw, 0.0)

    def bc_heads(b):
        return b.v(b.ap.unsqueeze(1).to_broadcast([b.ap.shape[0], 4, 128]))

    for g in range(4):
        wq = WS.get(tiles, 1 + 2 * g)
        for hh in range(4):
            for tch in range(2):
                proj_T(wq, hh * 128, nT_own, tch * 512, 512, qT[:, 4 * tch:4 * tch + 4, hh, :], scale=QSCALE,
                       out_view3=True)
        wkv = WS.get(tiles, 2 + 2 * g)
        for (nT, t0, u0) in uchunks:
            proj_T(wkv, 0, nT, t0, 512, ksT[:, u0:u0 + 512])
        for (nT, t0, u0) in uchunks[1:]:
            proj_T(wkv, 256, nT, t0, 512, kwT[:, u0 - 512:u0])
        for (vbuf, c0, ub0) in ((vs, 128, 0), (vw, 384, 4)):
            for q4 in range(ub0 // 4, 4):
                ps = next_ps()
                for j in range(4):
                    ub = 4 * q4 + j
                    nT = nT_ctx if ub < 8 else nT_own
                    tb = ub % 8
                    for kc in range(KCH):
                        P.mm(ps[:, j * 128:(j + 1) * 128], nT[:, kc, tb * 128:(tb + 1) * 128],
                             wkv[:, kc, c0:c0 + 128], kc == 0, kc == KCH - 1)
                evac(vbuf[:, 4 * q4 - ub0:4 * q4 - ub0 + 4, 0:128],
                     ps.v(ps.ap.rearrange("p (a b) -> p a b", b=128)))
            P.copy("dve", vbuf[:, :, 128:129], kvalid.v(kvalid.ap[:, ub0:16].unsqueeze(2)))

        def cmp_stage(qb):
            par = qb % 2
            for hh in range(4):
                P.mm(psC[:, hh * 128:(hh + 1) * 128], qT[:, qb, hh, :], kcmpT[:, g, :], True, True)
            psC3 = psC.v(psC.ap.rearrange("p (a b) -> p a b", b=128))
            P.reduce("dve", sm4[:, 0:4], psC3, ALU.max)
            P.ts("dve", sm4[:, 4:8], sm4[:, 0:4], -1.0, None, ALU.mult)
            for hh in range(4):
                P.act(e32[:, hh, :], psC[:, hh * 128:(hh + 1) * 128], AF.Exp, bias=sm4[:, 4 + hh:5 + hh])
            P.tt("dve", e32, e32, bc_heads(cmask[:, qb, :]), ALU.mult)
            P.reduce("dve", sm4[:, 8:12], e32, ALU.add)
            P.ts("dve", sm4[:, 8:12], sm4[:, 8:12], 1e-30, None, ALU.max)
            P.recip(sm4[:, 12:16], sm4[:, 8:12])
            rb = sm4.v(sm4.ap[:, 12:16].unsqueeze(2).to_broadcast([128, 4, 128]))
            P.tt("dve", p32, e32, rb, ALU.mult)
            P.copy("act", p16, p32)
            P.reduce("dve", Pg, p32.v(p32.ap.rearrange("p h c -> p c h")), ALU.add)
            P.reduce("dve", imp, Pg.v(Pg.ap.rearrange("p (j f) -> p j f", f=4)), ALU.add)
            P.tt("dve", imp2[:, 1:32], imp[:, 1:32], Pg[:, 3:124:4], ALU.add)
            P.copy("dve", imp2[:, 0:1], imp[:, 0:1])
            P.tt("dve", imp, imp2, addmask[:, qb, :], ALU.add)
            P.op("dve", lambda e: e.max(out=m8.ap, in_=imp.ap), [imp], [m8])
            P.op("dve", lambda e: e.match_replace(out=imp2.ap, in_to_replace=m8.ap, in_values=imp.ap,
                                                   imm_value=-3e38), [m8, imp], [imp2])
            P.op("dve", lambda e: e.max(out=m8.ap, in_=imp2.ap), [imp2], [m8])
            P.ts("dve", selneg, imp, m8[:, 7:8], NEGB, ALU.is_lt, ALU.mult)
            if "imp" in k.dbg_d and g == 0:
                k.dbg_store("imp", imp, k.dbg_d["imp"][qb])
                k.dbg_store("Pg", Pg, k.dbg_d["Pg"][qb])

        def cmp_stage_b(qb):
            par = qb % 2
            P.transpose(psT[0:32, 0:128], selneg, ident)
            evac(negT[par], psT.v(psT.ap[0:32, 0:128].unsqueeze(1).to_broadcast([32, 4, 128])))
            for hh in range(4):
                P.transpose(psT[:, (1 + hh) * 128:(2 + hh) * 128], p16[:, hh, :], ident)
            evac(pT, psT.v(psT.ap[:, 128:640].rearrange("p (a b) -> p a b", b=128)))
            for hh in range(4):
                P.mm(psO[:, hh * 128:(hh + 1) * 128], pT[:, hh, :], vcmp[:, g, :], True, True)
            for hh in range(4):
                col = 3 * (4 * g + hh)
                P.act(oacc[par][:, hh, :], psO[:, hh * 128:(hh + 1) * 128], AF.Copy, scale=g3[:, qb, col:col + 1])

        def attn(qb, kT, kofs, vbuf, vofs, kbs, masks, gcol, final):
            par = qb % 2
            n = len(kbs)
            q3 = qT.v(qT.ap[:, qb, :, :].rearrange("p h t -> p (h t)"))
            pss = {}

            def stA(i):
                kb = kbs[i]
                ps = psS[i % 3]
                pss[i] = ps
                mk = masks.get(kb)
                P.mm(ps, kT[:, (kb - kofs) * 128:(kb - kofs + 1) * 128], q3, True, mk is None)
                if mk is not None:
                    P.mm(ps, mk[0], mk[1], False, True)
                P.act(PT[i % 2], ps, AF.Exp)

            def stB(i):
                kb = kbs[i]
                for hh in range(4):
                    acc = psV[hh // 2]
                    o = (hh % 2) * 130
                    P.mm(acc[:, o:o + 129], PT[i % 2][:, hh * 128:(hh + 1) * 128], vbuf[:, kb - vofs, 0:129],
                         i == 0 and hh % 2 == 0, i == n - 1, sgc=True)

            stA(0)
            for i in range(n):
                if i + 1 < n:
                    stA(i + 1)
                stB(i)
            for hh in range(4):
                acc = psV[hh // 2]
                o = (hh % 2) * 130
                P.copy("dve", coef[:, hh:hh + 1], acc[:, o + 128:o + 129])
            P.ts("dve", coef[:, 0:4], coef[:, 0:4], 1e-30, None, ALU.max)
            P.recip(coef[:, 4:8], coef[:, 0:4])
            base = 3 * 4 * g + gcol
            P.tt("dve", coef[:, 4:8], coef[:, 4:8], g3[:, qb, base:base + 10:3], ALU.mult)
            for hh in range(4):
                acc = psV[hh // 2]
                o = (hh % 2) * 130
                dst = o16[:, hh, :] if final else oacc[par][:, hh, :]
                P.stt("dve", dst, acc[:, o:o + 128], coef[:, 4 + hh:5 + hh], oacc[par][:, hh, :], ALU.mult, ALU.add)

        cmp_stage(0)
        cmp_stage_b(0)
        for qb in range(NB):
            if qb + 1 < NB:
                cmp_stage(qb + 1)
            ub = 8 + qb
            par = qb % 2
            negbc = negT[par].v(negT[par].ap.rearrange("p h t -> p (h t)"))
            masks = {kb: (Emat[:, kb * 128:(kb + 1) * 128], negbc) for kb in range(0, ub)}
            masks[ub] = (ident, causneg)
            attn(qb, ksT, 0, vs, 0, list(range(0, ub + 1)), masks, 1, False)
            if qb + 1 < NB:
                cmp_stage_b(qb + 1)
            masks = {ub - 4: (ident, winneg), ub: (ident, causneg)}
            attn(qb, kwT, 4, vw, 4, list(range(ub - 4, ub + 1)), masks, 2, True)
            for hh in range(4):
                P.transpose(psT[:, (5 + hh % 2) * 128:(6 + hh % 2) * 128], o16[:, hh, :], ident)
                if hh % 2 == 1:
                    evac(o_nsaT[:, 4 * g + hh - 1:4 * g + hh + 1, qb * 128:(qb + 1) * 128],
                         psT.v(psT.ap[:, 640:896].rearrange("p (a b) -> p a b", b=128)))
        if stop_after == "nsa_g0":
            break
    if "o_nsaT" in k.dbg_d:
        tmpf = A.alloc("dbgtmp", [128, TOK], F32)
        for hh in range(4):
            P.copy("dve", tmpf, o_nsaT[:, hh, :])
            k.dbg_store("o_nsaT", tmpf, k.dbg_d["o_nsaT"][:, hh, :])


def _rotary(k, ps128, cos, sin, out_even_odd, tmp):
    P = k.P
    x1 = ps128.v(ps128.ap.rearrange("p (i two) -> p i two", two=2)[:, :, 0])
    x2 = ps128.v(ps128.ap.rearrange("p (i two) -> p i two", two=2)[:, :, 1])
    o1 = out_even_odd.v(out_even_odd.ap.rearrange("p (i two) -> p i two", two=2)[:, :, 0])
    o2 = out_even_odd.v(out_even_odd.ap.rearrange("p (i two) -> p i two", two=2)[:, :, 1])
    P.tt("dve", tmp[:, 0, :], x1, cos, ALU.mult)
    P.tt("dve", tmp[:, 1, :], x2, sin, ALU.mult)
    P.tt("dve", tmp[:, 2, :], x1, sin, ALU.mult)
    P.tt("dve", tmp[:, 3, :], x2, cos, ALU.mult)
    P.tt(POOL, o1, tmp[:, 0, :], tmp[:, 1, :], ALU.subtract)
    P.tt(POOL, o2, tmp[:, 2, :], tmp[:, 3, :], ALU.add)


def build_ret_ctx(k):
    P, A, D, WS = k.P, k.R34, k.D, k.WS
    nT_ctx = k.nT_ctx
    state8 = k.state8
    ropek = A.alloc("ropek", [128, 16, 128], F32)
    wkc = A.alloc("wkc", [128, 8, 8], F32)
    P.dma("sp", ropek, D["ropek"][:, :, :])
    P.dma("sp", wkc, D["wkc"][:, :, :])
    Kr = [A.alloc("Kr%d" % i, [128, 128], F32) for i in range(2)]
    Ks = [A.alloc("Ks%d" % i, [128, 128], BF16) for i in range(2)]
    V = [A.alloc("V%d" % i, [128, 256], BF16) for i in range(2)]
    tmp = [A.alloc("rtmp%d" % i, [128, 4, 64], F32) for i in range(2)]
    tiles = [("wr_c%d" % h,) + wtile_cols(D["w_in"], OR_ + 512 * h, 512) for h in range(8)]
    for h in range(8):
        wt = WS.get(tiles, h)
        for cb in range(NB):
            par = cb % 2
            ps = k.next_ps()
            for kc in range(KCH):
                P.mm(ps[:, 0:384], nT_ctx[:, kc, cb * 128:(cb + 1) * 128], wt[:, kc, 128:512], kc == 0, kc == KCH - 1)
            _rotary(k, ps[:, 0:128], ropek[:, cb, 0:64], ropek[:, cb, 64:128], Kr[par], tmp[par])
            P.ts(POOL, Ks[par], Kr[par], wkc[:, cb, h:h + 1], None, ALU.mult)
            P.copy("act", V[par], ps[:, 128:384])
            P.mm(k.psV[h % 2][:, 0:256], Ks[par], V[par], cb == 0, cb == NB - 1)
        P.copy("act", state8[:, h, :], k.psV[h % 2][:, 0:256])


def build_ret_own(k):
    P, A, D, WS = k.P, k.R34, k.D, k.WS
    nT_own = k.nT_own
    state8 = k.state8
    psT = k.psT
    ident = k.ident
    o_retT = A.alloc("o_retT", [128, 16, TOK], BF16)
    k.o_retT = o_retT
    k.m_after_o = A.mark()
    ropek = A.alloc("ropek", [128, 16, 128], F32)
    ropeq = A.alloc("ropeq", [128, 8, 128], F32)
    decayT = A.alloc("decayT", [128, 8, 128], F32)
    wqB = A.alloc("wqB", [128, 8, 128], F32)
    wk = A.alloc("wk", [128, 8], F32)
    gnB = k.load_gain("ret_gn_w", A)
    P.dma("sp", ropek, D["ropek"][:, :, :])
    P.dma("sp", ropeq, D["ropeq"][:, :, :])
    P.dma("sp", decayT, D["decayT"][:, :, :])
    P.dma("sp", wqB, D["wqB"][:, :, :])
    P.dma("sp", wk, D["wk"][:, :])
    Qr = [A.alloc("Qr%d" % i, [128, 128], BF16) for i in range(2)]
    Kr = [A.alloc("Kr%d" % i, [128, 128], F32) for i in range(2)]
    Kb = [A.alloc("Kb%d" % i, [128, 128], BF16) for i in range(2)]
    Ks = [A.alloc("Ks%d" % i, [128, 128], BF16) for i in range(2)]
    V = [A.alloc("V%d" % i, [128, 256], BF16) for i in range(2)]
    sg = [A.alloc("sg%d" % i, [128, 256], F32) for i in range(2)]
    tmp = [A.alloc("rtmp%d" % i, [128, 4, 64], F32) for i in range(2)]
    QT = A.alloc("QT", [128, 128], BF16)
    KT = A.alloc("KT", [128, 128], BF16)
    QsT = A.alloc("QsT", [128, 128], BF16)
    SdT = A.alloc("SdT", [128, 128], BF16)
    stbf = A.alloc("stbf", [128, 256], BF16)
    osb = A.alloc("osb", [128, 256], F32)
    junk = A.alloc("junk", [128, 256], BF16)
    y = A.alloc("y", [128, 256], F32)
    y16 = A.alloc("y16", [128, 256], BF16)
    gs = A.alloc("gs", [128, 8], F32)
    tiles = []
    for h in range(8):
        tiles.append(("wr_o%d" % h,) + wtile_cols(D["w_in"], OR_ + 512 * h, 512))
        if h % 2 == 0:
            tiles.append(("wgr%d" % (h // 2),) + wtile_cols(D["w_in"], OGR + 512 * (h // 2), 512))
    ti = 0
    wg = None
    for h in range(8):
        wt = WS.get(tiles, ti, depth=1)
        ti += 1
        if h % 2 == 0:
            wg = WS.get(tiles, ti, depth=1)
            ti += 1
        pss = {}

        def stage1(ob):
            par = ob % 2
            ub = 8 + ob
            ps = k.next_ps()
            for kc in range(KCH):
                P.mm(ps, nT_own[:, kc, ob * 128:(ob + 1) * 128], wt[:, kc, 0:512], kc == 0, kc == KCH - 1)
            psg = k.next_ps()
            for kc in range(KCH):
                P.mm(psg[:, 0:256], nT_own[:, kc, ob * 128:(ob + 1) * 128],
                     wg[:, kc, (h % 2) * 256:(h % 2) * 256 + 256], kc == 0, kc == KCH - 1)
            _rotary(k, ps[:, 0:128], ropeq[:, ob, 0:64], ropeq[:, ob, 64:128], Qr[par], tmp[par])
            _rotary(k, ps[:, 128:256], ropek[:, ub, 0:64], ropek[:, ub, 64:128], Kr[par], tmp[par])
            P.copy(POOL, Kb[par], Kr[par])
            P.ts(POOL, Ks[par], Kr[par], wk[:, h:h + 1], None, ALU.mult)
            P.copy("act", V[par], ps[:, 256:512])
            P.act(sg[par], psg[:, 0:256], AF.Silu)

        def stage2(ob):
            par = ob % 2
            P.transpose(psT[:, 0:128], Qr[par], ident)
            P.transpose(psT[:, 128:256], Kb[par], ident)
            P.copy("act", QT, psT[:, 0:128])
            P.copy("act", KT, psT[:, 128:256])
            P.tt("dve", QsT, psT[:, 0:128], wqB[:, h, :], ALU.mult)
            ps = k.next_ps()
            P.mm(ps[:, 0:128], KT, QT, True, True)
            P.tt("dve", SdT, ps[:, 0:128], decayT[:, h, :], ALU.mult)
            P.copy("act", stbf, state8[:, h, :])
            po = k.next_ps()
            P.mm(po[:, 0:256], SdT, V[par], True, False)
            P.mm(po[:, 0:256], QsT, stbf, False, True)
            P.memset("dve", gs, 0.0)
            P.act(osb, po[:, 0:256], AF.Copy, accum_out=gs[:, 0:1])
            P.act(junk, po[:, 0:256], AF.Square, accum_out=gs[:, 1:2])
            P.ts("dve", gs[:, 2:3], gs[:, 0:1], 1.0 / 256, None, ALU.mult)
            P.tt("dve", gs[:, 3:4], gs[:, 2:3], gs[:, 2:3], ALU.mult)
            P.stt("dve", gs[:, 4:5], gs[:, 1:2], 1.0 / 256, gs[:, 3:4], ALU.mult, ALU.subtract)
            P.ts("dve", gs[:, 4:5], gs[:, 4:5], EPS, None, ALU.add)
            P.act(gs[:, 5:6], gs[:, 4:5], AF.Sqrt)
            P.recip(gs[:, 6:7], gs[:, 5:6])
            P.ts("dve", y, osb, gs[:, 2:3], gs[:, 6:7], ALU.subtract, ALU.mult)
            P.tt(POOL, y, y, gnB[:, h * 256:(h + 1) * 256], ALU.mult)
            P.tt(POOL, y16, y, sg[par], ALU.mult)
            for j in range(2):
                P.transpose(psT[:, (2 + j) * 128:(3 + j) * 128], y16[:, j * 128:(j + 1) * 128], ident)
            k.evac(o_retT[:, 2 * h:2 * h + 2, ob * 128:(ob + 1) * 128],
                   psT.v(psT.ap[:, 256:512].rearrange("p (a b) -> p a b", b=128)))
            ps3 = k.next_ps()
            P.mm(ps3[:, 0:256], Ks[par], V[par], True, True)
            P.stt("dve", state8[:, h, :], state8[:, h, :], _G_CHUNK[h], ps3[:, 0:256], ALU.mult, ALU.add)

        stage1(0)
        for ob in range(NB):
            if ob + 1 < NB:
                stage1(ob + 1)
            stage2(ob)
    if "o_retT" in k.dbg_d:
        tmpf = A.alloc("dbgtmp", [128, 2, TOK], F32)
        P.copy("dve", tmpf, o_retT[:, 0:2, :])
        k.dbg_store("o_retT", tmpf)


def build_merge(k, which, srcT):
    P, A, D, WS = k.P, k.R34, k.D, k.WS
    nT_own, mergedT = k.nT_own, k.mergedT
    W = D["w_a"] if which == "a" else D["w_b"]
    og = OGA if which == "a" else OGB
    sig = [A.alloc("sig%d" % i, [128, 512], F32) for i in range(2)]
    tmp = [A.alloc("mtmp%d" % i, [128, 512], F32) for i in range(2)]
    tiles = []
    for i in range(4):
        tiles.append(("wm%s%d" % (which, i),) + wtile_cols(W, 512 * i, 512))
        tiles.append(("wgt%s%d" % (which, i),) + wtile_cols(D["w_in"], og + 512 * i, 512))
    n = 0
    for i in range(4):
        wm = WS.get(tiles, 2 * i, depth=1)
        wg = WS.get(tiles, 2 * i + 1, depth=1)
        for cc in range(4):
            for tch in range(2):
                psA = k.next_ps()
                for kc in range(KCH):
                    P.mm(psA, wm[:, kc, cc * 128:(cc + 1) * 128], srcT[:, kc, tch * 512:(tch + 1) * 512],
                         kc == 0, kc == KCH - 1)
                psG = k.next_ps()
                for kc in range(KCH):
                    P.mm(psG, wg[:, kc, cc * 128:(cc + 1) * 128], nT_own[:, kc, tch * 512:(tch + 1) * 512],
                         kc == 0, kc == KCH - 1)
                par = n % 2
                n += 1
                P.act(sig[par], psG, AF.Sigmoid)
                dst = mergedT[:, 4 * i + cc, tch * 512:(tch + 1) * 512]
                if which == "a":
                    P.tt("dve", dst, sig[par], psA, ALU.mult)
                else:
                    P.tt("dve", tmp[par], sig[par], psA, ALU.mult)
                    P.tt(POOL, dst, tmp[par], dst, ALU.add)
    if ("mergedT_" + which) in k.dbg_d:
        tmpf = A.alloc("dbgtmp", [128, 4, TOK], F32)
        P.copy("dve", tmpf, mergedT[:, 0:4, :])
        k.dbg_store("mergedT_" + which, tmpf)


def build_tail(k, stop_after):
    P, D, WS = k.P, k.D, k.WS
    R1, R2, R34 = k.R1, k.R2, k.R34
    psS, psV, psT, ident = k.psS, k.psV, k.psT, k.ident
    mergedT = k.mergedT
    hb = [R34.alloc("h%d" % tb, [128, DM], F32) for tb in range(NB)]
    for tb in range(NB):
        P.dma("sp", hb[tb], D["xo"][tb * 128:(tb + 1) * 128, :])
    tiles = [("wout%d" % i,) + wtile_cols(D["w_out"], 512 * i, 512) for i in range(4)]
    for i in range(4):
        wt = WS.get(tiles, i)
        for tb in range(NB):
            ps = k.next_ps()
            for kc in range(KCH):
                P.mm(ps, mergedT[:, kc, tb * 128:(tb + 1) * 128], wt[:, kc, :], kc == 0, kc == KCH - 1)
            hs = hb[tb][:, 512 * i:512 * (i + 1)]
            P.tt("dve", hs, hs, ps, ALU.add)
    if "h1" in k.dbg_d:
        for tb in range(NB):
            k.dbg_store("h1", hb[tb], k.dbg_d["h1"][tb * 128:(tb + 1) * 128, :])
    R2.release()
    if stop_after == "h1":
        return

    nxT = R1.alloc("nxT", [128, KCH, TOK], BF16)
    gainX = k.load_gain("x_norm_w", R2)
    xn = R2.alloc("xn", [128, DM], BF16)
    stat = R2.alloc("stat", [128, 16], F32)
    k.norm_T(lambda tb: hb[tb], NB, gainX, nxT, xn, stat)
    P.dma("sp", gainX, D["mem_norm_w"].to_broadcast([128, DM]))
    mh = R34.mark()
    mts = [R34.alloc("mt%d" % i, [128, DM], F32) for i in range(2)]
    mT = R2.alloc("mT", [128, KCH, 256], BF16)
    k.norm_T(k.mk_get(D["mem"], mts), 2, gainX, mT, xn, stat)
    R34.release(mh)
    qxT = R2.alloc("qxT", [128, 4, TOK], BF16)
    kxT = R34.alloc("kxT", [128, 4, 256], BF16)
    vx = R34.alloc("vx", [128, 2, 4, 130], BF16)
    PTx = [R34.alloc("PTx%d" % i, [128, 512], BF16) for i in range(2)]
    xc = R34.alloc("xcoef", [128, 4], F32)
    tiles = [("wqx",) + wtile_cols(D["wq_x"], 0, 512), ("wkx",) + wtile_cols(D["wk_x"], 0, 512),
             ("wvx",) + wtile_cols(D["wv_x"], 0, 512), ("wox",) + wtile_rows(D["wo_x"], 0, 512)]
    wq = WS.get(tiles, 0)
    for hh in range(4):
        for tch in range(2):
            k.proj_T(wq, hh * 128, nxT, tch * 512, 512, qxT[:, hh, tch * 512:(tch + 1) * 512], scale=QSCALE)
    wk_ = WS.get(tiles, 1)
    for hh in range(4):
        k.proj_T(wk_, hh * 128, mT, 0, 256, kxT[:, hh, :])
    wv = WS.get(tiles, 2)
    P.memset("dve", vx, 1.0)
    for mb in range(2):
        ps = k.next_ps()
        for kc in range(KCH):
            P.mm(ps, mT[:, kc, mb * 128:(mb + 1) * 128], wv[:, kc, :], kc == 0, kc == KCH - 1)
        k.evac(vx[:, mb, :, 0:128], ps.v(ps.ap.rearrange("p (a b) -> p a b", b=128)))
    R1.release()
    ox16 = R1.alloc("ox16", [128, NB, 512], BF16)
    oxT = R1.alloc("oxT", [128, 4, TOK], BF16)
    for hh in range(4):
        for tch in range(2):
            for mb in range(2):
                ps = psS[mb]
                P.mm(ps, kxT[:, hh, mb * 128:(mb + 1) * 128], qxT[:, hh, tch * 512:(tch + 1) * 512], True, True)
                P.act(PTx[mb], ps, AF.Exp)
            for tq in range(4):
                tb = tch * 4 + tq
                acc = psV[tq % 2]
                for mb in range(2):
                    P.mm(acc[:, 0:129], PTx[mb][:, tq * 128:(tq + 1) * 128], vx[:, mb, hh, 0:129], mb == 0, mb == 1)
                P.copy("dve", xc[:, 0:1], acc[:, 128:129])
                P.recip(xc[:, 1:2], xc[:, 0:1])
                P.ts("dve", ox16[:, tb, hh * 128:(hh + 1) * 128], acc[:, 0:128], xc[:, 1:2], None, ALU.mult)
    for tb in range(NB):
        for j in range(4):
            P.transpose(psT[:, j * 128:(j + 1) * 128], ox16[:, tb, j * 128:(j + 1) * 128], ident)
        k.evac(oxT[:, :, tb * 128:(tb + 1) * 128], psT.v(psT.ap[:, 0:512].rearrange("p (a b) -> p a b", b=128)))
    wo = WS.get(tiles, 3)
    for tb in range(NB):
        for cc in range(4):
            ps = k.next_ps()
            for kc in range(4):
                P.mm(ps, oxT[:, kc, tb * 128:(tb + 1) * 128], wo[:, kc, cc * 512:(cc + 1) * 512], kc == 0, kc == 3)
            hs = hb[tb][:, 512 * cc:512 * (cc + 1)]
            P.tt("dve", hs, hs, ps, ALU.add)
    if "h2" in k.dbg_d:
        for tb in range(NB):
            k.dbg_store("h2", hb[tb], k.dbg_d["h2"][tb * 128:(tb + 1) * 128, :])
    R1.release()
    R2.release()
    R34.release(mh)
    if stop_after == "h2":
        return

    nmT = R1.alloc("nmT", [128, KCH, TOK], BF16)
    gainM = k.load_gain("mlp_norm_w", R2)
    xn = R2.alloc("xn", [128, DM], BF16)
    stat = R2.alloc("stat", [128, 16], F32)
    k.norm_T(lambda tb: hb[tb], NB, gainM, nmT, xn, stat)
    aT = R2.alloc("aT", [128, 4, TOK], BF16)
    rl = [R2.alloc("rl%d" % i, [128, 512], F32) for i in range(2)]
    tiles = []
    for f in range(16):
        tiles.append(("wup%d" % f,) + wtile_cols(D["w_up"], 512 * f, 512))
        tiles.append(("wdn%d" % f,) + wtile_rows(D["w_down"], 512 * f, 512))
    n = 0
    for f in range(16):
        wu = WS.get(tiles, 2 * f)
        for cc in range(4):
            for tch in range(2):
                ps = k.next_ps()
                for kc in range(KCH):
                    P.mm(ps, wu[:, kc, cc * 128:(cc + 1) * 128], nmT[:, kc, tch * 512:(tch + 1) * 512],
                         kc == 0, kc == KCH - 1)
                par = n % 2
                n += 1
                P.act(rl[par], ps, AF.Relu)
                P.tt(POOL, aT[:, cc, tch * 512:(tch + 1) * 512], rl[par], rl[par], ALU.mult)
        wd = WS.get(tiles, 2 * f + 1)
        for tb in range(NB):
            for cc in range(4):
                ps = k.next_ps()
                for kc in range(4):
                    P.mm(ps, aT[:, kc, tb * 128:(tb + 1) * 128], wd[:, kc, cc * 512:(cc + 1) * 512], kc == 0, kc == 3)
                hs = hb[tb][:, 512 * cc:512 * (cc + 1)]
                P.tt("dve", hs, hs, ps, ALU.add)
    if "h3" in k.dbg_d:
        for tb in range(NB):
            k.dbg_store("h3", hb[tb], k.dbg_d["h3"][tb * 128:(tb + 1) * 128, :])
    R1.release()
    R2.release()

    gainF = k.load_gain("final_norm_w", R2)
    outt = [R2.alloc("outt%d" % i, [128, DM], F32) for i in range(2)]
    junk = R2.alloc("junkf", [128, DM], BF16)
    stat = R2.alloc("statf", [128, 16], F32)
    P.memset("dve", stat, 0.0)
    for tb in range(NB):
        P.act(junk, hb[tb], AF.Square, accum_out=stat[:, tb:tb + 1])
        P.ts("dve", stat[:, tb:tb + 1], stat[:, tb:tb + 1], 1.0 / DM, EPS, ALU.mult, ALU.add)
        P.act(stat[:, tb:tb + 1], stat[:, tb:tb + 1], AF.Sqrt)
        P.recip(stat[:, tb:tb + 1], stat[:, tb:tb + 1])
        P.stt("dve", outt[tb % 2], hb[tb], stat[:, tb:tb + 1], gainF, ALU.mult, ALU.mult)
        P.dma("sp", k.out_d[tb * 128:(tb + 1) * 128, :], outt[tb % 2])


def _w_in_perm():
    q = np.arange(0, 2048)
    parts = [q]
    for g in range(4):
        for base in (3072, 3584, 4096, 4608):
            parts.append(base + 128 * g + np.arange(128))
    parts.append(2048 + np.arange(512))
    parts.append(2560 + np.arange(512))
    for h in range(8):
        parts.append(5168 + 128 * h + np.arange(128))
        parts.append(6192 + 128 * h + np.arange(128))
        parts.append(7216 + 256 * h + np.arange(256))
    parts.append(np.arange(9264, 15408))
    parts.append(5120 + np.arange(48))
    perm = np.concatenate(parts)
    assert perm.shape[0] == IN_WIDTH and len(set(perm.tolist())) == IN_WIDTH
    return perm


def _tables(s):
    f32 = np.float32
    T = {}
    T["ident"] = np.eye(128, dtype=f32)
    u = np.arange(2048)
    t = u - 1024 + 1024 * s
    inv = (10000.0 ** (-np.arange(0, 128, 2, dtype=f32) / f32(128))).astype(f32)
    ang = t.astype(f32)[:, None] * inv[None, :]
    cos, sin = np.cos(ang).astype(f32), np.sin(ang).astype(f32)
    rk = np.concatenate([cos, sin], axis=1) * f32(128 ** -0.5)
    T["ropek"] = np.ascontiguousarray(rk.reshape(16, 128, 128).transpose(1, 0, 2)).astype(f32)
    rq = np.concatenate([cos, sin], axis=1)[1024:]
    T["ropeq"] = np.ascontiguousarray(rq.reshape(8, 128, 128).transpose(1, 0, 2)).astype(f32)
    H = 8
    log_g = np.log1p(-np.exp2(-5.0 - np.arange(H, dtype=f32))).astype(f32)
    i = np.arange(128, dtype=f32)
    rel = i[:, None] - i[None, :]
    decay = np.where(rel >= 0, np.exp(log_g[:, None, None] * np.maximum(rel, 0.0)), 0.0).astype(f32)
    T["decayT"] = np.ascontiguousarray(decay.transpose(2, 0, 1))
    w_q = np.exp(log_g[:, None] * (i + 1.0)[None, :]).astype(f32)
    T["wqB"] = np.ascontiguousarray(np.broadcast_to(w_q[None], (128, H, 128))).astype(f32)
    w_k = np.exp(log_g[:, None] * (127.0 - i)[None, :]).astype(f32)
    T["wk"] = np.ascontiguousarray(w_k.T)
    T["g_chunk"] = np.exp(log_g * f32(128.0)).astype(f32)
    gpow = np.stack([T["g_chunk"] ** f32(7 - blk) for blk in range(8)], 0).astype(f32)
    T["wkc"] = np.ascontiguousarray((w_k.T[:, None, :] * gpow[None, :, :]).astype(f32))
    uq = 1024 + np.arange(1024)
    lc = np.arange(128)
    vis = (16 * lc[None, :] + 31 <= uq[:, None]) & (lc[None, :] < 127)
    if s == 0:
        vis &= (lc[None, :] >= 64)
    T["cmask"] = np.ascontiguousarray(vis.astype(f32).reshape(8, 128, 128).transpose(1, 0, 2))
    lj = np.arange(32)
    lcur = (uq // 64)[:, None]
    first = 0 if s == 1 else 16
    am = np.zeros((1024, 32), f32)
    am[np.broadcast_to(lj[None, :] == first, am.shape)] = 1e9
    prev = (lj[None, :] == lcur - 1) & (lj[None, :] >= first)
    am[prev] = 2e9
    am[np.broadcast_to(lj[None, :], am.shape) == lcur] = 3e9
    am[(lj[None, :] > lcur) | (lj[None, :] < first)] = -1e9
    T["addmask"] = np.ascontiguousarray(am.reshape(8, 128, 32).transpose(1, 0, 2))
    kval = (t >= 0).astype(f32)
    T["kvalid"] = np.ascontiguousarray(kval.reshape(16, 128).T)
    kk = np.arange(2048)
    T["Emat"] = (kk[None, :] // 64 == np.arange(32)[:, None]).astype(f32)
    kq = np.arange(128)
    T["causneg"] = np.tile(np.where(kq[:, None] > kq[None, :], NEGB, 0.0).astype(f32), (1, 4))
    T["winneg"] = np.tile(np.where(kq[:, None] <= kq[None, :], NEGB, 0.0).astype(f32), (1, 4))
    return T


def make_in_maps(inputs):
    f32 = np.float32
    x = np.asarray(inputs["x"], f32)
    mem = np.asarray(inputs["mem"], f32)
    perm = _w_in_perm()
    shared = {}
    shared["w_in"] = np.ascontiguousarray(np.asarray(inputs["w_in"], f32)[0][:, perm])
    for nm in ("cmp_w1_k", "cmp_w1_v", "cmp_w2_k", "cmp_w2_v", "w_a", "w_b", "w_out", "wq_x", "wk_x", "wv_x",
               "wo_x", "w_up", "w_down"):
        shared[nm] = np.ascontiguousarray(np.asarray(inputs[nm], f32)[0])
    shared["peT_k"] = np.ascontiguousarray(np.asarray(inputs["cmp_pe_k"], f32)[0].T)
    shared["peT_v"] = np.ascontiguousarray(np.asarray(inputs["cmp_pe_v"], f32)[0].T)
    for nm in ("attn_norm_w", "ret_gn_w", "x_norm_w", "mem_norm_w", "mlp_norm_w"):
        shared[nm] = np.ascontiguousarray(np.asarray(inputs[nm], f32).reshape(1, DM))
    shared["final_norm_w"] = np.ascontiguousarray(np.asarray(inputs["final_norm_w"], f32).reshape(1, DM))
    tabs = [_tables(0), _tables(1)]
    zeros = np.zeros((TOK, DM), f32)
    in_maps = []
    for c in range(8):
        b, s = c // 2, c % 2
        m = dict(shared)
        m["xo"] = np.ascontiguousarray(x[b, 1024 * s:1024 * (s + 1)])
        m["xc"] = np.ascontiguousarray(x[b, 0:1024]) if s == 1 else zeros
        m["mem"] = np.ascontiguousarray(mem[b])
        for kk, v in tabs[s].items():
            if kk != "g_chunk":
                m[kk] = v
        in_maps.append(m)
    return in_maps


_G_CHUNK = [float(v) for v in np.exp(np.log1p(-np.exp2(-5.0 - np.arange(8, dtype=np.float32))).astype(np.float32)
                                     * np.float32(128.0)).astype(np.float32)]


def kernel(**inputs):
    in_maps = make_in_maps(inputs)
    nc, st = build_program()
    res = run_bass_kernel_spmd(nc, in_maps, core_ids=list(range(8)))
    out = np.zeros((4, 2048, DM), np.float32)
    for c in range(8):
        b, s = c // 2, c % 2
        out[b, 1024 * s:1024 * (s + 1)] = res.results[c]["out"]
    return out
```

```python
import numpy as np
from concourse.bass_utils import run_bass_kernel_spmd
import concourse.bass as bass
import concourse.mybir as mybir

F32 = mybir.dt.float32
BF16 = mybir.dt.bfloat16
AF = mybir.ActivationFunctionType
ALU = mybir.AluOpType
AX = mybir.AxisListType

_DT_SIZE = {F32: 4, BF16: 2}


class Buf:
    def __init__(self, key, ap):
        self.key = key
        self.ap = ap

    def __getitem__(self, idx):
        return Buf(self.key, self.ap[idx])

    def v(self, ap):
        return Buf(self.key, ap)


class Op:
    __slots__ = ("eng", "fn", "reads", "writes", "is_dma", "semkey", "deps", "dma_deps",
                 "signal", "signum", "pos", "accum", "idx")


class Prog:
    ENGS = ("pe", "act", "dve", "pool", "sp")

    def __init__(self, nc):
        self.nc = nc
        self.ops = []
        self.eng_obj = {"pe": nc.tensor, "act": nc.scalar, "dve": nc.vector, "pool": nc.gpsimd, "sp": nc.sync}
        self.sync_same_engine_war = False

    def _add(self, eng, fn, reads, writes, is_dma=False, semkey=None, accum=False):
        o = Op()
        o.eng = eng
        o.fn = fn
        o.reads = [b.key for b in reads if b is not None]
        o.writes = [b.key for b in writes if b is not None]
        for kk in o.reads:
            if kk.startswith("ps") and kk not in o.writes:
                o.writes.append(kk)
        o.is_dma = is_dma
        o.semkey = semkey
        o.accum = accum
        o.idx = len(self.ops)
        self.ops.append(o)
        return o

    def op(self, eng, fn, reads=(), writes=()):
        return self._add(eng, fn, reads, writes)

    def barrier(self):
        o = Op()
        o.eng = None
        o.idx = len(self.ops)
        self.ops.append(o)

    def dma(self, eng, out, in_, semkey=None):
        reads, writes = [], []
        if isinstance(in_, Buf):
            reads.append(in_)
            in_ap = in_.ap
        else:
            in_ap = in_
        if isinstance(out, Buf):
            writes.append(out)
            out_ap = out.ap
            if semkey is None:
                semkey = "dma:" + out.key
        else:
            out_ap = out
            if semkey is None:
                semkey = "dma:store"

        def fn(e):
            return e.dma_start(out=out_ap, in_=in_ap)

        return self._add(eng, fn, reads, writes, is_dma=True, semkey=semkey)

    def mm(self, out, lhsT, rhs, start, stop, extra_reads=(), sgc=False):
        def fn(e):
            if sgc:
                return e.matmul(out.ap, lhsT.ap, rhs.ap, start=start, stop=stop, skip_group_check=True)
            return e.matmul(out.ap, lhsT.ap, rhs.ap, start=start, stop=stop)
        return self._add("pe", fn, [lhsT, rhs] + list(extra_reads), [out], accum=not start)

    def transpose(self, out, in_, ident):
        def fn(e):
            return e.transpose(out.ap, in_.ap, ident.ap)
        return self._add("pe", fn, [in_, ident], [out])

    def act(self, out, in_, func, bias=None, scale=1.0, accum_out=None, eng="act"):
        reads = [in_]
        kw = {}
        if isinstance(bias, Buf):
            reads.append(bias)
            kw["bias"] = bias.ap
        elif bias is not None:
            kw["bias"] = bias
        if isinstance(scale, Buf):
            reads.append(scale)
            kw["scale"] = scale.ap
        else:
            kw["scale"] = scale
        writes = [out]
        if accum_out is not None:
            writes.append(accum_out)
            kw["accum_out"] = accum_out.ap

        def fn(e):
            return e.activation(out=out.ap, in_=in_.ap, func=func, **kw)
        return self._add(eng, fn, reads, writes)

    def tt(self, eng, out, in0, in1, op):
        def fn(e):
            return e.tensor_tensor(out=out.ap, in0=in0.ap, in1=in1.ap, op=op)
        return self._add(eng, fn, [in0, in1], [out])

    def ts(self, eng, out, in0, s1, s2, op0, op1=None, accum_out=None):
        reads = [in0]
        a1 = s1
        a2 = s2
        if isinstance(s1, Buf):
            reads.append(s1)
            a1 = s1.ap
        if isinstance(s2, Buf):
            reads.append(s2)
            a2 = s2.ap
        writes = [out]
        kw = {}
        if op1 is not None:
            kw["op1"] = op1
        if accum_out is not None:
            writes.append(accum_out)
            kw["accum_out"] = accum_out.ap

        def fn(e):
            return e.tensor_scalar(out=out.ap, in0=in0.ap, scalar1=a1, scalar2=a2, op0=op0, **kw)
        return self._add(eng, fn, reads, writes)

    def stt(self, eng, out, in0, scalar, in1, op0, op1):
        reads = [in0, in1]
        a = scalar
        if isinstance(scalar, Buf):
            reads.append(scalar)
            a = scalar.ap

        def fn(e):
            return e.scalar_tensor_tensor(out=out.ap, in0=in0.ap, scalar=a, in1=in1.ap, op0=op0, op1=op1)
        return self._add(eng, fn, reads, [out])

    def copy(self, eng, out, in_):
        if eng == "act":
            def fn(e):
                return e.copy(out=out.ap, in_=in_.ap)
        else:
            def fn(e):
                return e.tensor_copy(out=out.ap, in_=in_.ap)
        return self._add(eng, fn, [in_], [out])

    def reduce(self, eng, out, in_, op, axis=AX.X):
        def fn(e):
            return e.tensor_reduce(out=out.ap, in_=in_.ap, axis=axis, op=op)
        return self._add(eng, fn, [in_], [out])

    def memset(self, eng, out, val):
        def fn(e):
            return e.memset(out.ap, val)
        return self._add(eng, fn, [], [out])

    def recip(self, out, in_):
        def fn(e):
            return e.reciprocal(out=out.ap, in_=in_.ap)
        return self._add("dve", fn, [in_], [out])

    def emit(self, final_wait_eng="sp"):
        nc = self.nc
        ops = self.ops
        last_writer = {}
        readers = {}
        pos_ctr = {e: 0 for e in self.ENGS}
        waited = {f: {e: -1 for e in self.ENGS} for f in self.ENGS}
        waited_dma = {f: {} for f in self.ENGS}
        dma_count = {}
        last_op_on = {e: None for e in self.ENGS}
        pending_barrier = {e: [] for e in self.ENGS}
        outstanding_dma = []

        for o in ops:
            if o.eng is None:
                for f in self.ENGS:
                    pending_barrier[f] = [last_op_on[e] for e in self.ENGS if e != f and last_op_on[e] is not None]
                continue
            f = o.eng
            deps = set()
            for k in o.reads:
                w = last_writer.get(k)
                if w is not None:
                    deps.add(w)
            for k in o.writes:
                w = last_writer.get(k)
                if w is not None:
                    deps.add(w)
                for r in readers.get(k, ()):
                    deps.add(r)
            for b in pending_barrier[f]:
                deps.add(b)
            pending_barrier[f] = []
            o.pos = pos_ctr[f]
            pos_ctr[f] += 1
            o.deps = []
            o.dma_deps = []
            o.signal = False
            for di in sorted(deps):
                d = ops[di]
                if d.idx == o.idx:
                    continue
                if d.is_dma:
                    cnt = d.signum
                    if waited_dma[f].get(d.semkey, 0) >= cnt:
                        continue
                    waited_dma[f][d.semkey] = cnt
                    o.dma_deps.append((d.semkey, cnt))
                else:
                    if d.eng == "pe" and f == "pe" and not o.is_dma:
                        continue
                    if d.eng == f and not self.sync_same_engine_war and not o.is_dma:
                        is_raw_waw = any(last_writer.get(k) == di for k in o.reads + o.writes)
                        if not is_raw_waw:
                            continue
                    if waited[f][d.eng] >= d.pos:
                        continue
                    waited[f][d.eng] = d.pos
                    d.signal = True
                    o.deps.append(di)
            if o.is_dma:
                dma_count[o.semkey] = dma_count.get(o.semkey, 0) + 16
                o.signum = dma_count[o.semkey]
                outstanding_dma.append(o)
            for k in o.reads:
                readers.setdefault(k, []).append(o.idx)
            for k in o.writes:
                last_writer[k] = o.idx
                readers[k] = []
            last_op_on[f] = o.idx

        tail_deps = []
        for e in self.ENGS:
            li = last_op_on[e]
            if li is not None and not ops[li].is_dma:
                ops[li].signal = True
                tail_deps.append(li)
        sig_ctr = {e: 0 for e in self.ENGS}
        for o in ops:
            if o.eng is None or o.is_dma:
                continue
            if o.signal:
                sig_ctr[o.eng] += 1
                o.signum = sig_ctr[o.eng]
        import contextlib
        es = contextlib.ExitStack()
        self._es = es
        sems = {e: es.enter_context(nc.semaphore("s_" + e)) for e in self.ENGS}
        dsems = {}
        for k in dma_count:
            dsems[k] = es.enter_context(nc.semaphore("d%d" % len(dsems)))
        n_wait = 0
        for o in ops:
            if o.eng is None:
                continue
            e = self.eng_obj[o.eng]
            for di in o.deps:
                d = ops[di]
                e.wait_ge(sems[d.eng], d.signum)
                n_wait += 1
            for (k, cnt) in o.dma_deps:
                e.wait_ge(dsems[k], cnt)
                n_wait += 1
            ins = o.fn(e)
            if o.is_dma:
                ins.then_inc(dsems[o.semkey], 16)
            elif o.signal:
                ins.then_inc(sems[o.eng], 1)
        fe = self.eng_obj[final_wait_eng]
        for di in tail_deps:
            d = ops[di]
            fe.wait_ge(sems[d.eng], d.signum)
        for k, cnt in dma_count.items():
            fe.wait_ge(dsems[k], cnt)
        self.stats = dict(n_ops=len(ops), n_wait=n_wait, sig=dict(sig_ctr), n_dsems=len(dsems))
        return self.stats


TOK = 1024
NB = 8
DM = 2048
KCH = 16
EPS = 1e-6
OQ, OKV, OC, OR_, OGR, OGA, OGB, OG = 0, 2048, 4096, 5120, 9216, 11264, 13312, 15360
IN_WIDTH = 15408
NEGB = -30000.0
QSCALE = 128 ** -0.5
POOL = "dve"


def _prod(xs):
    r = 1
    for x in xs:
        r *= int(x)
    return r


class Arena:
    uid = 0

    def __init__(self, full_ap, P, base, nel, name):
        self.ap = full_ap
        self.base = base
        self.top = 0
        self.nel = nel
        self.P = P
        self.peak = 0
        self.name = name

    def alloc(self, name, shape, dt):
        inner = _prod(shape[1:])
        n = inner * (2 if dt == F32 else 1)
        npad = (n + 15) // 16 * 16
        off = self.base + self.top
        self.top += npad
        self.peak = max(self.peak, self.top)
        assert self.top <= self.nel, ("SBUF arena overflow", self.name, name, self.top, self.nel)
        ap = self.ap[:shape[0], off:off + n]
        if dt == F32:
            ap = ap.bitcast(F32)
        if len(shape) == 3:
            ap = ap.rearrange("p (a b) -> p a b", b=shape[2])
        elif len(shape) == 4:
            ap = ap.rearrange("p (a b c) -> p a b c", b=shape[2], c=shape[3])
        Arena.uid += 1
        return Buf("%s#%d" % (name, Arena.uid), ap)

    def mark(self):
        return self.top

    def release(self, mark=0):
        self.top = mark
        self.P.barrier()


class WStream:
    def __init__(self, P, arena, nslots, slot_elems):
        self.P = P
        self.nslots = nslots
        self.slot_elems = slot_elems
        self.slots = [arena.alloc("wslot%d" % i, [128, slot_elems], BF16) for i in range(nslots)]
        self.ctr = 0
        self.loaded = {}

    def _load(self, item):
        key, src, shape = item
        if key in self.loaded:
            return
        s = self.slots[self.ctr % self.nslots]
        self.ctr += 1
        n = _prod(shape[1:])
        ap = s.ap[:, 0:n]
        if len(shape) == 3:
            ap = ap.rearrange("p (a b) -> p a b", b=shape[2])
        b = Buf(s.key, ap)
        self.P.dma("pool", b, src)
        self.loaded[key] = b

    def get(self, lst, i, depth=None):
        depth = self.nslots - 1 if depth is None else depth
        for j in range(i, min(len(lst), i + depth + 1)):
            self._load(lst[j])
        b = self.loaded.pop(lst[i][0])
        return b


def wtile_cols(w2d, c0, ncols):
    return w2d[:, c0:c0 + ncols].rearrange("(kc p) c -> p kc c", p=128), [128, 16, ncols]


def wtile_rows(w2d, r0, nrows):
    return w2d[r0:r0 + nrows, :].rearrange("(kc p) c -> p kc c", p=128), [128, nrows // 128, 2048]


class K:
    pass


def build_program(dbg=None, stop_after=None):
    nc = bass.Bass("TRN2", target_bir_lowering=False)
    P = Prog(nc)
    k = K()
    k.nc, k.P = nc, P

    def din(name, shape):
        return nc.dram_tensor(name, list(shape), F32, kind="ExternalInput").ap()

    D = {}
    D["xo"] = din("xo", [TOK, DM])
    D["xc"] = din("xc", [TOK, DM])
    D["mem"] = din("mem", [256, DM])
    D["w_in"] = din("w_in", [DM, IN_WIDTH])
    for nm in ("cmp_w1_k", "cmp_w1_v"):
        D[nm] = din(nm, [4096, 1024])
    for nm in ("cmp_w2_k", "cmp_w2_v"):
        D[nm] = din(nm, [1024, 128])
    for nm in ("peT_k", "peT_v"):
        D[nm] = din(nm, [128, 32])
    for nm in ("w_a", "w_b", "w_out"):
        D[nm] = din(nm, [DM, DM])
    for nm in ("wq_x", "wk_x", "wv_x"):
        D[nm] = din(nm, [DM, 512])
    D["wo_x"] = din("wo_x", [512, DM])
    D["w_up"] = din("w_up", [DM, 8192])
    D["w_down"] = din("w_down", [8192, DM])
    for nm in ("attn_norm_w", "ret_gn_w", "x_norm_w", "mem_norm_w", "mlp_norm_w", "final_norm_w"):
        D[nm] = din(nm, [1, DM])
    D["ident"] = din("ident", [128, 128])
    D["ropeq"] = din("ropeq", [128, 8, 128])
    D["ropek"] = din("ropek", [128, 16, 128])
    D["decayT"] = din("decayT", [128, 8, 128])
    D["wqB"] = din("wqB", [128, 8, 128])
    D["wk"] = din("wk", [128, 8])
    D["wkc"] = din("wkc", [128, 8, 8])
    D["cmask"] = din("cmask", [128, 8, 128])
    D["addmask"] = din("addmask", [128, 8, 32])
    D["kvalid"] = din("kvalid", [128, 16])
    D["Emat"] = din("Emat", [32, 2048])
    D["causneg"] = din("causneg", [128, 512])
    D["winneg"] = din("winneg", [128, 512])
    out_d = nc.dram_tensor("out", [TOK, DM], F32, kind="ExternalOutput").ap()
    dbg_d = {}
    if dbg:
        for nm, shp in dbg.items():
            dbg_d[nm] = nc.dram_tensor("dbg_" + nm, list(shp), F32, kind="ExternalOutput").ap()
    k.D, k.out_d, k.dbg_d = D, out_d, dbg_d

    NEL = 106000
    full = nc.alloc_sbuf_tensor("arena", [128, NEL], BF16).ap()
    R0 = Arena(full, P, 0, 30000, "R0")
    R1 = Arena(full, P, 30000, 16384, "R1")
    R2 = Arena(full, P, 46384, 16384, "R2")
    R34 = Arena(full, P, 62768, NEL - 62768, "R34")
    k.R0, k.R1, k.R2, k.R34 = R0, R1, R2, R34
    A = R0
    k.psS = [Buf("psS%d" % i, nc.alloc_psum_tensor("psS%d" % i, [128, 512], F32).ap()) for i in range(3)]
    k.psC = Buf("psC", nc.alloc_psum_tensor("psC", [128, 512], F32).ap())
    k.psO = Buf("psO", nc.alloc_psum_tensor("psO", [128, 512], F32).ap())
    k.psV = [Buf("psV%d" % i, nc.alloc_psum_tensor("psV%d" % i, [128, 512], F32).ap()) for i in range(2)]
    k.psT = Buf("psT", nc.alloc_psum_tensor("psT", [128, 1024], BF16).ap())
    k.rot5 = [k.psS[0], k.psS[1], k.psS[2], k.psC, k.psO]
    k.rot_i = 0
    k.ev_i = 0

    k.ident = A.alloc("ident", [128, 128], BF16)
    P.dma("pool", k.ident, D["ident"][:, :])
    k.WS = WStream(P, A, 3, 16 * 512)

    def finish():
        st = P.emit()
        st["arena_peak_el"] = [R0.peak, R1.peak, R2.peak, R34.peak]
        return nc, st

    def dbg_store(name, buf, dram_view=None):
        if name in dbg_d:
            dv = dbg_d[name] if dram_view is None else dram_view
            P.dma("sp", dv, buf, semkey="dma:dbg")

    k.dbg_store = dbg_store

    def next_ps():
        b = k.rot5[k.rot_i % 5]
        k.rot_i += 1
        return b

    def evac(out, in_, scale=None):
        e = k.ev_i % 2
        k.ev_i += 1
        if e == 0:
            if scale is None:
                P.copy("act", out, in_)
            else:
                P.act(out, in_, AF.Copy, scale=scale)
        else:
            if scale is None:
                P.copy("dve", out, in_)
            else:
                P.ts("dve", out, in_, scale, None, ALU.mult)

    k.next_ps, k.evac = next_ps, evac

    def load_gain(name, reg):
        g = reg.alloc("gain_" + name, [128, DM], F32)
        P.dma("sp", g, D[name].to_broadcast([128, DM]))
        return g

    def norm_T(get_block, nblk, gain, nT, xn, stat):
        P.memset("dve", stat, 0.0)
        for tb in range(nblk):
            xt = get_block(tb)
            P.act(xn, xt, AF.Square, accum_out=stat[:, tb:tb + 1])
            P.ts("dve", stat[:, tb:tb + 1], stat[:, tb:tb + 1], 1.0 / DM, EPS, ALU.mult, ALU.add)
            P.act(stat[:, tb:tb + 1], stat[:, tb:tb + 1], AF.Sqrt)
            P.recip(stat[:, tb:tb + 1], stat[:, tb:tb + 1])
            P.stt("dve", xn, xt, stat[:, tb:tb + 1], gain, ALU.mult, ALU.mult)
            for half in range(2):
                for j in range(8):
                    kc = half * 8 + j
                    P.transpose(k.psT[:, j * 128:(j + 1) * 128], xn[:, kc * 128:(kc + 1) * 128], k.ident)
                src = k.psT.v(k.psT.ap.rearrange("p (a b) -> p a b", b=128))
                evac(nT[:, half * 8:half * 8 + 8, tb * 128:(tb + 1) * 128], src)

    def proj_T(wt, c0, nT, t0, ntok, out, scale=None, nk=KCH, out_view3=False):
        ps = next_ps()
        for kc in range(nk):
            P.mm(ps[:, 0:ntok], wt[:, kc, c0:c0 + 128], nT[:, kc, t0:t0 + ntok], kc == 0, kc == nk - 1)
        src = ps[:, 0:ntok]
        if out_view3:
            src = ps.v(ps.ap[:, 0:ntok].rearrange("p (a b) -> p a b", b=128))
        evac(out, src, scale)

    k.norm_T, k.proj_T, k.load_gain = norm_T, proj_T, load_gain

    nT_own = R1.alloc("nT_own", [128, KCH, TOK], BF16)
    nT_ctx = R2.alloc("nT_ctx", [128, KCH, TOK], BF16)
    k.state8 = R0.alloc("state8", [128, 8, 256], F32)
    k.kcmpT = R0.alloc("kcmpT", [128, 4, 128], BF16)
    k.vcmp = R0.alloc("vcmp", [128, 4, 128], BF16)
    gainA = load_gain("attn_norm_w", R34)
    xts = [R34.alloc("xt%d" % i, [128, DM], F32) for i in range(2)]
    xn = R34.alloc("xn", [128, DM], BF16)
    stat = R34.alloc("stat", [128, 16], F32)

    def mk_get(src, xts):
        def get(tb):
            xt = xts[tb % 2]
            P.dma("sp", xt, src[tb * 128:(tb + 1) * 128, :])
            return xt
        return get

    k.mk_get = mk_get
    norm_T(mk_get(D["xc"], xts), NB, gainA, nT_ctx, xn, stat)
    norm_T(mk_get(D["xo"], xts), NB, gainA, nT_own, xn, stat)
    if "nT_own" in dbg_d:
        tmpf = R34.alloc("dbgtmp", [128, KCH, TOK // 4], F32)
        P.copy("dve", tmpf, nT_own[:, :, 0:TOK // 4])
        dbg_store("nT_own", tmpf)
    R34.release()
    if stop_after == "norm":
        return finish()
    k.nT_own, k.nT_ctx = nT_own, nT_ctx
    k.finish = finish

    build_ret_ctx(k)
    R34.release()
    if stop_after == "ret_ctx":
        return finish()
    build_nsa(k, stop_after)
    if stop_after and stop_after.startswith("nsa"):
        return finish()
    R34.release(k.m_after_o)
    R2.release()
    k.mergedT = R2.alloc("mergedT", [128, KCH, TOK], BF16)
    build_merge(k, "a", k.o_nsaT)
    R34.release()
    if stop_after == "merge_a":
        return finish()
    build_ret_own(k)
    if stop_after == "ret":
        return finish()
    R34.release(k.m_after_o)
    build_merge(k, "b", k.o_retT)
    R34.release()
    R1.release()
    if stop_after == "merge_b":
        return finish()
    build_tail(k, stop_after)
    return finish()


def build_nsa(k, stop_after):
    P, A, D, WS = k.P, k.R34, k.D, k.WS
    nT_own, nT_ctx = k.nT_own, k.nT_ctx
    psS, psC, psO, psV, psT = k.psS, k.psC, k.psO, k.psV, k.psT
    evac, next_ps, proj_T = k.evac, k.next_ps, k.proj_T
    ident = k.ident
    w_in = D["w_in"]
    uchunks = [(nT_ctx, 0, 0), (nT_ctx, 512, 512), (nT_own, 0, 1024), (nT_own, 512, 1536)]

    o_nsaT = A.alloc("o_nsaT", [128, 16, TOK], BF16)
    k.o_nsaT = o_nsaT
    k.m_after_o = A.mark()
    kcmpT, vcmp = k.kcmpT, k.vcmp
    P.memset("dve", kcmpT, 0.0)
    P.memset("dve", vcmp, 0.0)
    m1 = A.mark()
    kvT = A.alloc("kvcT", [128, 4, 2048], BF16)
    hidT = A.alloc("hidT", [128, 8, 4, 128], BF16)
    peT = A.alloc("peT", [128, 32], BF16)
    w2 = A.alloc("w2", [128, 8, 128], BF16)
    cbias = A.alloc("cbias", [128, 8], F32)
    for kind in range(2):
        sfx = "_k" if kind == 0 else "_v"
        tiles = [("wc%d" % kind,) + wtile_cols(w_in, OC + 512 * kind, 512)]
        for hh in range(2):
            for lh in range(2):
                src = D["cmp_w1" + sfx][2048 * lh:2048 * (lh + 1), 512 * hh:512 * (hh + 1)].rearrange(
                    "(l p) c -> p l c", p=128)
                tiles.append(("w1%d_%d_%d" % (kind, hh, lh), src, [128, 16, 512]))
        P.dma("pool", peT, D["peT" + sfx][:, :])
        P.dma("pool", w2, D["cmp_w2" + sfx].rearrange("(hc p) c -> p hc c", p=128))
        wt = WS.get(tiles, 0)
        for g in range(4):
            for (nT, t0, u0) in uchunks:
                proj_T(wt, g * 128, nT, t0, 512, kvT[:, g, u0:u0 + 512])
        ti = 1
        for hh in range(2):
            accs = [psS[0], psS[1], psS[2], psC]
            for lh in range(2):
                wt = WS.get(tiles, ti)
                ti += 1
                for li in range(16):
                    l = lh * 16 + li
                    for hc in range(4):
                        lhsT = wt[:, li, hc * 128:(hc + 1) * 128]
                        for g in range(4):
                            rhs = kvT[:, g, l:l + 16 * 126 + 1:16]
                            P.mm(accs[hc][:, g * 127:(g + 1) * 127], lhsT, rhs, l == 0 and g == 0, l == 31, sgc=True)
                        P.mm(psO[:, hc:hc + 1], lhsT, peT[:, l:l + 1], l == 0 and hc == 0, l == 31, sgc=True)
            for hc in range(4):
                hcg = hh * 4 + hc
                P.copy("dve", cbias[:, hcg:hcg + 1], psO[:, hc:hc + 1])
                src = accs[hc].v(accs[hc].ap[:, 0:508].rearrange("p (g c) -> p g c", c=127))
                P.act(hidT[:, hcg, :, 0:127], src, AF.Silu, bias=cbias[:, hcg:hcg + 1])
        if kind == 0:
            ps = next_ps()
            for g in range(4):
                for hc in range(8):
                    P.mm(ps[:, g * 127:(g + 1) * 127], w2[:, hc, :], hidT[:, hc, g, 0:127], hc == 0, hc == 7)
            evac(kcmpT[:, :, 0:127], ps.v(ps.ap[:, 0:508].rearrange("p (g c) -> p g c", c=127)))
        else:
            ps = next_ps()
            for g in range(4):
                for hc in range(8):
                    P.mm(ps[0:127, g * 128:(g + 1) * 128], hidT[:, hc, g, 0:127], w2[:, hc, :], hc == 0, hc == 7)
            evac(vcmp[0:127, :, :], ps.v(ps.ap[0:127, :].rearrange("p (g c) -> p g c", c=128)))
    if "kcmpT" in k.dbg_d:
        tmpf = A.alloc("dbgtmp", [128, 4, 128], F32)
        P.copy("dve", tmpf, kcmpT)
        k.dbg_store("kcmpT", tmpf)
        tmpf2 = A.alloc("dbgtmp2", [128, 4, 128], F32)
        P.copy("dve", tmpf2, vcmp)
        k.dbg_store("vcmp", tmpf2)
    A.release(m1)
    if stop_after == "nsa_cmp":
        return

    cmask = A.alloc("cmask", [128, 8, 128], F32)
    addmask = A.alloc("addmask", [128, 8, 32], F32)
    kvalid = A.alloc("kvalid", [128, 16], BF16)
    Emat = A.alloc("Emat", [32, 2048], BF16)
    causneg = A.alloc("causneg", [128, 512], BF16)
    winneg = A.alloc("winneg", [128, 512], BF16)
    g3 = A.alloc("g3", [128, 8, 48], F32)
    P.dma("sp", cmask, D["cmask"][:, :, :])
    P.dma("sp", addmask, D["addmask"][:, :, :])
    P.dma("pool", kvalid, D["kvalid"][:, :])
    P.dma("pool", Emat, D["Emat"][:, :])
    P.dma("pool", causneg, D["causneg"][:, :])
    P.dma("pool", winneg, D["winneg"][:, :])
    tiles = [("wg3",) + wtile_cols(w_in, OG, 48)]
    for g in range(4):
        tiles.append(("wq%d" % g,) + wtile_cols(w_in, OQ + 512 * g, 512))
        tiles.append(("wkv%d" % g,) + wtile_cols(w_in, OKV + 512 * g, 512))
    wt = WS.get(tiles, 0)
    for tb in range(NB):
        ps = next_ps()
        for kc in range(KCH):
            P.mm(ps[:, 0:48], nT_own[:, kc, tb * 128:(tb + 1) * 128], wt[:, kc, 0:48], kc == 0, kc == KCH - 1)
        P.act(g3[:, tb, :], ps[:, 0:48], AF.Sigmoid)

    qT = A.alloc("qT", [128, NB, 4, 128], BF16)
    ksT = A.alloc("ksT", [128, 2048], BF16)
    kwT = A.alloc("kwT", [128, 1536], BF16)
    vs = A.alloc("vs", [128, 16, 130], BF16)
    vw = A.alloc("vw", [128, 12, 130], BF16)
    e32 = A.alloc("e32", [128, 4, 128], F32)
    p32 = A.alloc("p32", [128, 4, 128], F32)
    p16 = A.alloc("p16", [128, 4, 128], BF16)
    pT = A.alloc("pT", [128, 4, 128], BF16)
    Pg = A.alloc("Pg", [128, 128], F32)
    imp = A.alloc("imp", [128, 32], F32)
    imp2 = A.alloc("imp2", [128, 32], F32)
    m8 = A.alloc("m8", [128, 8], F32)
    sm4 = A.alloc("sm4", [128, 16], F32)
    selneg = A.alloc("selneg", [128, 32], BF16)
    negT = [A.alloc("negT%d" % i, [32, 4, 128], BF16) for i in range(2)]
    oacc = [A.alloc("oacc%d" % i, [128, 4, 128], F32) for i in range(2)]
    o16 = A.alloc("o16", [128, 4, 128], BF16)
    PT = [A.alloc("PT%d" % i, [128, 512], BF16) for i in range(2)]
    coef = A.alloc("coef", [128, 8], F32)
    P.memset("dve", vs, 0.0)
    P.memset("dve", vw, 0.0)

    def bc_heads(b):
        return b.v(b.ap.unsqueeze(1).to_broadcast([b.ap.shape[0], 4, 128]))

    for g in range(4):
        wq = WS.get(tiles, 1 + 2 * g)
        for hh in range(4):
            for tch in range(2):
                proj_T(wq, hh * 128, nT_own, tch * 512, 512, qT[:, 4 * tch:4 * tch + 4, hh, :], scale=QSCALE,
                       out_view3=True)
        wkv = WS.get(tiles, 2 + 2 * g)
        for (nT, t0, u0) in uchunks:
            proj_T(wkv, 0, nT, t0, 512, ksT[:, u0:u0 + 512])
        for (nT, t0, u0) in uchunks[1:]:
            proj_T(wkv, 256, nT, t0, 512, kwT[:, u0 - 512:u0])
        for (vbuf, c0, ub0) in ((vs, 128, 0), (vw, 384, 4)):
            for q4 in range(ub0 // 4, 4):
                ps = next_ps()
                for j in range(4):
                    ub = 4 * q4 + j
                    nT = nT_ctx if ub < 8 else nT_own
                    tb = ub % 8
                    for kc in range(KCH):
                        P.mm(ps[:, j * 128:(j + 1) * 128], nT[:, kc, tb * 128:(tb + 1) * 128],
                             wkv[:, kc, c0:c0 + 128], kc == 0, kc == KCH - 1)
                evac(vbuf[:, 4 * q4 - ub0:4 * q4 - ub0 + 4, 0:128],
                     ps.v(ps.ap.rearrange("p (a b) -> p a b", b=128)))
            P.copy("dve", vbuf[:, :, 128:129], kvalid.v(kvalid.ap[:, ub0:16].unsqueeze(2)))

        def cmp_stage(qb):
            par = qb % 2
            for hh in range(4):
                P.mm(psC[:, hh * 128:(hh + 1) * 128], qT[:, qb, hh, :], kcmpT[:, g, :], True, True)
            psC3 = psC.v(psC.ap.rearrange("p (a b) -> p a b", b=128))
            P.reduce("dve", sm4[:, 0:4], psC3, ALU.max)
            P.ts("dve", sm4[:, 4:8], sm4[:, 0:4], -1.0, None, ALU.mult)
            for hh in range(4):
                P.act(e32[:, hh, :], psC[:, hh * 128:(hh + 1) * 128], AF.Exp, bias=sm4[:, 4 + hh:5 + hh])
            P.tt("dve", e32, e32, bc_heads(cmask[:, qb, :]), ALU.mult)
            P.reduce("dve", sm4[:, 8:12], e32, ALU.add)
            P.ts("dve", sm4[:, 8:12], sm4[:, 8:12], 1e-30, None, ALU.max)
            P.recip(sm4[:, 12:16], sm4[:, 8:12])
            rb = sm4.v(sm4.ap[:, 12:16].unsqueeze(2).to_broadcast([128, 4, 128]))
            P.tt("dve", p32, e32, rb, ALU.mult)
            P.copy("act", p16, p32)
            P.reduce("dve", Pg, p32.v(p32.ap.rearrange("p h c -> p c h")), ALU.add)
            P.reduce("dve", imp, Pg.v(Pg.ap.rearrange("p (j f) -> p j f", f=4)), ALU.add)
            P.tt("dve", imp2[:, 1:32], imp[:, 1:32], Pg[:, 3:124:4], ALU.add)
            P.copy("dve", imp2[:, 0:1], imp[:, 0:1])
            P.tt("dve", imp, imp2, addmask[:, qb, :], ALU.add)
            P.op("dve", lambda e: e.max(out=m8.ap, in_=imp.ap), [imp], [m8])
            P.op("dve", lambda e: e.match_replace(out=imp2.ap, in_to_replace=m8.ap, in_values=imp.ap,
                                                   imm_value=-3e38), [m8, imp], [imp2])
            P.op("dve", lambda e: e.max(out=m8.ap, in_=imp2.ap), [imp2], [m8])
            P.ts("dve", selneg, imp, m8[:, 7:8], NEGB, ALU.is_lt, ALU.mult)
            if "imp" in k.dbg_d and g == 0:
                k.dbg_store("imp", imp, k.dbg_d["imp"][qb])
                k.dbg_store("Pg", Pg, k.dbg_d["Pg"][qb])

        def cmp_stage_b(qb):
            par = qb % 2
            P.transpose(psT[0:32, 0:128], selneg, ident)
            evac(negT[par], psT.v(psT.ap[0:32, 0:128].unsqueeze(1).to_broadcast([32, 4, 128])))
            for hh in range(4):
                P.transpose(psT[:, (1 + hh) * 128:(2 + hh) * 128], p16[:, hh, :], ident)
            evac(pT, psT.v(psT.ap[:, 128:640].rearrange("p (a b) -> p a b", b=128)))
            for hh in range(4):
                P.mm(psO[:, hh * 128:(hh + 1) * 128], pT[:, hh, :], vcmp[:, g, :], True, True)
            for hh in range(4):
                col = 3 * (4 * g + hh)
                P.act(oacc[par][:, hh, :], psO[:, hh * 128:(hh + 1) * 128], AF.Copy, scale=g3[:, qb, col:col + 1])

        def attn(qb, kT, kofs, vbuf, vofs, kbs, masks, gcol, final):
            par = qb % 2
            n = len(kbs)
            q3 = qT.v(qT.ap[:, qb, :, :].rearrange("p h t -> p (h t)"))
            pss = {}

            def stA(i):
                kb = kbs[i]
                ps = psS[i % 3]
                pss[i] = ps
                mk = masks.get(kb)
                P.mm(ps, kT[:, (kb - kofs) * 128:(kb - kofs + 1) * 128], q3, True, mk is None)
                if mk is not None:
                    P.mm(ps, mk[0], mk[1], False, True)
                P.act(PT[i % 2], ps, AF.Exp)

            def stB(i):
                kb = kbs[i]
                for hh in range(4):
                    acc = psV[hh // 2]
                    o = (hh % 2) * 130
                    P.mm(acc[:, o:o + 129], PT[i % 2][:, hh * 128:(hh + 1) * 128], vbuf[:, kb - vofs, 0:129],
                         i == 0 and hh % 2 == 0, i == n - 1, sgc=True)

            stA(0)
            for i in range(n):
                if i + 1 < n:
                    stA(i + 1)
                stB(i)
            for hh in range(4):
                acc = psV[hh // 2]
                o = (hh % 2) * 130
                P.copy("dve", coef[:, hh:hh + 1], acc[:, o + 128:o + 129])
            P.ts("dve", coef[:, 0:4], coef[:, 0:4], 1e-30, None, ALU.max)
            P.recip(coef[:, 4:8], coef[:, 0:4])
            base = 3 * 4 * g + gcol
            P.tt("dve", coef[:, 4:8], coef[:, 4:8], g3[:, qb, base:base + 10:3], ALU.mult)
            for hh in range(4):
                acc = psV[hh // 2]
                o = (hh % 2) * 130
                dst = o16[:, hh, :] if final else oacc[par][:, hh, :]
                P.stt("dve", dst, acc[:, o:o + 128], coef[:, 4 + hh:5 + hh], oacc[par][:, hh, :], ALU.mult, ALU.add)

        cmp_stage(0)
        cmp_stage_b(0)
        for qb in range(NB):
            if qb + 1 < NB:
                cmp_stage(qb + 1)
            ub = 8 + qb
            par = qb % 2
            negbc = negT[par].v(negT[par].ap.rearrange("p h t -> p (h t)"))
            masks = {kb: (Emat[:, kb * 128:(kb + 1) * 128], negbc) for kb in range(0, ub)}
            masks[ub] = (ident, causneg)
            attn(qb, ksT, 0, vs, 0, list(range(0, ub + 1)), masks, 1, False)
            if qb + 1 < NB:
                cmp_stage_b(qb + 1)
            masks = {ub - 4: (ident, winneg), ub: (ident, causneg)}
            attn(qb, kwT, 4, vw, 4, list(range(ub - 4, ub + 1)), masks, 2, True)
            for hh in range(4):
                P.transpose(psT[:, (5 + hh % 2) * 128:(6 + hh % 2) * 128], o16[:, hh, :], ident)
                if hh % 2 == 1:
                    evac(o_nsaT[:, 4 * g + hh - 1:4 * g + hh + 1, qb * 128:(qb + 1) * 128],
                         psT.v(psT.ap[:, 640:896].rearrange("p (a b) -> p a b", b=128)))
        if stop_after == "nsa_g0":
            break
    if "o_nsaT" in k.dbg_d:
        tmpf = A.alloc("dbgtmp", [128, TOK], F32)
        for hh in range(4):
            P.copy("dve", tmpf, o_nsaT[:, hh, :])
            k.dbg_store("o_nsaT", tmpf, k.dbg_d["o_nsaT"][:, hh, :])


def _rotary(k, ps128, cos, sin, out_even_odd, tmp):
    P = k.P
    x1 = ps128.v(ps128.ap.rearrange("p (i two) -> p i two", two=2)[:, :, 0])
    x2 = ps128.v(ps128.ap.rearrange("p (i two) -> p i two", two=2)[:, :, 1])
    o1 = out_even_odd.v(out_even_odd.ap.rearrange("p (i two) -> p i two", two=2)[:, :, 0])
    o2 = out_even_odd.v(out_even_odd.ap.rearrange("p (i two) -> p i two", two=2)[:, :, 1])
    P.tt("dve", tmp[:, 0, :], x1, cos, ALU.mult)
    P.tt("dve", tmp[:, 1, :], x2, sin, ALU.mult)
    P.tt("dve", tmp[:, 2, :], x1, sin, ALU.mult)
    P.tt("dve", tmp[:, 3, :], x2, cos, ALU.mult)
    P.tt(POOL, o1, tmp[:, 0, :], tmp[:, 1, :], ALU.subtract)
    P.tt(POOL, o2, tmp[:, 2, :], tmp[:, 3, :], ALU.add)


def build_ret_ctx(k):
    P, A, D, WS = k.P, k.R34, k.D, k.WS
    nT_ctx = k.nT_ctx
    state8 = k.state8
    ropek = A.alloc("ropek", [128, 16, 128], F32)
    wkc = A.alloc("wkc", [128, 8, 8], F32)
    P.dma("sp", ropek, D["ropek"][:, :, :])
    P.dma("sp", wkc, D["wkc"][:, :, :])
    Kr = [A.alloc("Kr%d" % i, [128, 128], F32) for i in range(2)]
    Ks = [A.alloc("Ks%d" % i, [128, 128], BF16) for i in range(2)]
    V = [A.alloc("V%d" % i, [128, 256], BF16) for i in range(2)]
    tmp = [A.alloc("rtmp%d" % i, [128, 4, 64], F32) for i in range(2)]
    tiles = [("wr_c%d" % h,) + wtile_cols(D["w_in"], OR_ + 512 * h, 512) for h in range(8)]
    for h in range(8):
        wt = WS.get(tiles, h)
        for cb in range(NB):
            par = cb % 2
            ps = k.next_ps()
            for kc in range(KCH):
                P.mm(ps[:, 0:384], nT_ctx[:, kc, cb * 128:(cb + 1) * 128], wt[:, kc, 128:512], kc == 0, kc == KCH - 1)
            _rotary(k, ps[:, 0:128], ropek[:, cb, 0:64], ropek[:, cb, 64:128], Kr[par], tmp[par])
            P.ts(POOL, Ks[par], Kr[par], wkc[:, cb, h:h + 1], None, ALU.mult)
            P.copy("act", V[par], ps[:, 128:384])
            P.mm(k.psV[h % 2][:, 0:256], Ks[par], V[par], cb == 0, cb == NB - 1)
        P.copy("act", state8[:, h, :], k.psV[h % 2][:, 0:256])


def build_ret_own(k):
    P, A, D, WS = k.P, k.R34, k.D, k.WS
    nT_own = k.nT_own
    state8 = k.state8
    psT = k.psT
    ident = k.ident
    o_retT = A.alloc("o_retT", [128, 16, TOK], BF16)
    k.o_retT = o_retT
    k.m_after_o = A.mark()
    ropek = A.alloc("ropek", [128, 16, 128], F32)
    ropeq = A.alloc("ropeq", [128, 8, 128], F32)
    decayT = A.alloc("decayT", [128, 8, 128], F32)
    wqB = A.alloc("wqB", [128, 8, 128], F32)
    wk = A.alloc("wk", [128, 8], F32)
    gnB = k.load_gain("ret_gn_w", A)
    P.dma("sp", ropek, D["ropek"][:, :, :])
    P.dma("sp", ropeq, D["ropeq"][:, :, :])
    P.dma("sp", decayT, D["decayT"][:, :, :])
    P.dma("sp", wqB, D["wqB"][:, :, :])
    P.dma("sp", wk, D["wk"][:, :])
    def mk_bufs(j):
        B = {}
        for nm, shp, dt in (("Qr", [128, 128], BF16), ("Kr", [128, 128], F32), ("Kb", [128, 128], BF16),
                            ("Ks", [128, 128], BF16), ("V", [128, 256], BF16), ("sg", [128, 256], F32),
                            ("tmp", [128, 4, 64], F32)):
            B[nm] = [A.alloc("%s%d_%d" % (nm, j, i), shp, dt) for i in range(2)]
        for nm, shp, dt in (("QT", [128, 128], BF16), ("KT", [128, 128], BF16), ("QsT", [128, 128], BF16),
                            ("SdT", [128, 128], BF16), ("stbf", [128, 256], BF16), ("osb", [128, 256], F32),
                            ("junk", [128, 256], BF16), ("y", [128, 256], F32), ("y16", [128, 256], BF16),
                            ("gs", [128, 8], F32)):
            B[nm] = A.alloc("%s%d" % (nm, j), shp, dt)
        return B

    bufs = [mk_bufs(0), mk_bufs(1)]
    tiles = []
    for p in range(4):
        tiles.append(("wr_o%d" % (2 * p),) + wtile_cols(D["w_in"], OR_ + 512 * (2 * p), 512))
        tiles.append(("wr_o%d" % (2 * p + 1),) + wtile_cols(D["w_in"], OR_ + 512 * (2 * p + 1), 512))
        tiles.append(("wgr%d" % p,) + wtile_cols(D["w_in"], OGR + 512 * p, 512))

    def run2(sa, sb):
        n = max(len(sa), len(sb))
        for i in range(n):
            if i < len(sa):
                sa[i]()
            if i < len(sb):
                sb[i]()

    for p in range(4):
        wts = [WS.get(tiles, 3 * p + j, depth=0) for j in range(2)]
        wg = WS.get(tiles, 3 * p + 2, depth=0)

        def stage1_steps(j, ob):
            B = bufs[j]
            h = 2 * p + j
            wt = wts[j]
            par = ob % 2
            ub = 8 + ob
            st = {}
            steps = []

            def s_proj():
                st["ps"] = k.next_ps()
                for kc in range(KCH):
                    P.mm(st["ps"], nT_own[:, kc, ob * 128:(ob + 1) * 128], wt[:, kc, 0:512], kc == 0, kc == KCH - 1)
            steps.append(s_proj)

            def s_projg():
                st["psg"] = k.next_ps()
                for kc in range(KCH):
                    P.mm(st["psg"][:, 0:256], nT_own[:, kc, ob * 128:(ob + 1) * 128],
                         wg[:, kc, j * 256:j * 256 + 256], kc == 0, kc == KCH - 1)
            steps.append(s_projg)
            steps.append(lambda: _rotary(k, st["ps"][:, 0:128], ropeq[:, ob, 0:64], ropeq[:, ob, 64:128],
                                         B["Qr"][par], B["tmp"][par]))
            steps.append(lambda: P.copy("act", B["V"][par], st["ps"][:, 256:512]))
            steps.append(lambda: _rotary(k, st["ps"][:, 128:256], ropek[:, ub, 0:64], ropek[:, ub, 64:128],
                                         B["Kr"][par], B["tmp"][par]))
            steps.append(lambda: P.act(B["sg"][par], st["psg"][:, 0:256], AF.Silu))
            steps.append(lambda: P.copy(POOL, B["Kb"][par], B["Kr"][par]))
            steps.append(lambda: P.ts(POOL, B["Ks"][par], B["Kr"][par], wk[:, h:h + 1], None, ALU.mult))
            return steps

        def stage2_steps(j, ob):
            B = bufs[j]
            h = 2 * p + j
            par = ob % 2
            c0 = j * 512
            st = {}
            gs = B["gs"]
            steps = []

            def s_tr():
                P.transpose(psT[:, c0:c0 + 128], B["Qr"][par], ident)
                P.transpose(psT[:, c0 + 128:c0 + 256], B["Kb"][par], ident)
            steps.append(s_tr)
            steps.append(lambda: P.copy("act", B["QT"], psT[:, c0:c0 + 128]))
            steps.append(lambda: P.copy("act", B["KT"], psT[:, c0 + 128:c0 + 256]))
            steps.append(lambda: P.tt("dve", B["QsT"], psT[:, c0:c0 + 128], wqB[:, h, :], ALU.mult))

            def s_S():
                st["ps"] = k.next_ps()
                P.mm(st["ps"][:, 0:128], B["KT"], B["QT"], True, True)
            steps.append(s_S)
            steps.append(lambda: P.tt("dve", B["SdT"], st["ps"][:, 0:128], decayT[:, h, :], ALU.mult))
            steps.append(lambda: P.copy("act", B["stbf"], state8[:, h, :]))

            def s_o():
                st["po"] = k.next_ps()
                P.mm(st["po"][:, 0:256], B["SdT"], B["V"][par], True, False)
                P.mm(st["po"][:, 0:256], B["QsT"], B["stbf"], False, True)
            steps.append(s_o)
            steps.append(lambda: P.memset("dve", gs, 0.0))
            steps.append(lambda: P.act(B["osb"], st["po"][:, 0:256], AF.Copy, accum_out=gs[:, 0:1]))
            steps.append(lambda: P.act(B["junk"], st["po"][:, 0:256], AF.Square, accum_out=gs[:, 1:2]))

            def s_state():
                st["ps3"] = k.next_ps()
                P.mm(st["ps3"][:, 0:256], B["Ks"][par], B["V"][par], True, True)
            steps.append(s_state)
            steps.append(lambda: P.stt("dve", state8[:, h, :], state8[:, h, :], _G_CHUNK[h], st["ps3"][:, 0:256],
                                       ALU.mult, ALU.add))
            steps.append(lambda: P.ts("dve", gs[:, 2:3], gs[:, 0:1], 1.0 / 256, None, ALU.mult))
            steps.append(lambda: P.tt("dve", gs[:, 3:4], gs[:, 2:3], gs[:, 2:3], ALU.mult))
            steps.append(lambda: P.stt("dve", gs[:, 4:5], gs[:, 1:2], 1.0 / 256, gs[:, 3:4], ALU.mult, ALU.subtract))
            steps.append(lambda: P.ts("dve", gs[:, 4:5], gs[:, 4:5], EPS, None, ALU.add))
            steps.append(lambda: P.act(gs[:, 5:6], gs[:, 4:5], AF.Sqrt))
            steps.append(lambda: P.recip(gs[:, 6:7], gs[:, 5:6]))
            steps.append(lambda: P.ts("dve", B["y"], B["osb"], gs[:, 2:3], gs[:, 6:7], ALU.subtract, ALU.mult))
            steps.append(lambda: P.tt(POOL, B["y"], B["y"], gnB[:, h * 256:(h + 1) * 256], ALU.mult))
            steps.append(lambda: P.tt(POOL, B["y16"], B["y"], B["sg"][par], ALU.mult))

            def s_tr2():
                for jj in range(2):
                    P.transpose(psT[:, c0 + (2 + jj) * 128:c0 + (3 + jj) * 128], B["y16"][:, jj * 128:(jj + 1) * 128], ident)
            steps.append(s_tr2)
            steps.append(lambda: k.evac(o_retT[:, 2 * h:2 * h + 2, ob * 128:(ob + 1) * 128],
                                        psT.v(psT.ap[:, c0 + 256:c0 + 512].rearrange("p (a b) -> p a b", b=128))))
            return steps

        run2(stage1_steps(0, 0), stage1_steps(1, 0))
        for ob in range(NB):
            if ob + 1 < NB:
                run2(stage1_steps(0, ob + 1), stage1_steps(1, ob + 1))
            run2(stage2_steps(0, ob), stage2_steps(1, ob))
    if "o_retT" in k.dbg_d:
        tmpf = A.alloc("dbgtmp", [128, 2, TOK], F32)
        P.copy("dve", tmpf, o_retT[:, 0:2, :])
        k.dbg_store("o_retT", tmpf)


def build_merge(k, which, srcT):
    P, A, D, WS = k.P, k.R34, k.D, k.WS
    nT_own, mergedT = k.nT_own, k.mergedT
    W = D["w_a"] if which == "a" else D["w_b"]
    og = OGA if which == "a" else OGB
    sig = [A.alloc("sig%d" % i, [128, 512], F32) for i in range(2)]
    tmp = [A.alloc("mtmp%d" % i, [128, 512], F32) for i in range(2)]
    tiles = []
    for i in range(4):
        tiles.append(("wm%s%d" % (which, i),) + wtile_cols(W, 512 * i, 512))
        tiles.append(("wgt%s%d" % (which, i),) + wtile_cols(D["w_in"], og + 512 * i, 512))
    n = 0
    for i in range(4):
        wm = WS.get(tiles, 2 * i, depth=1)
        wg = WS.get(tiles, 2 * i + 1, depth=1)
        for cc in range(4):
            for tch in range(2):
                psA = k.next_ps()
                for kc in range(KCH):
                    P.mm(psA, wm[:, kc, cc * 128:(cc + 1) * 128], srcT[:, kc, tch * 512:(tch + 1) * 512],
                         kc == 0, kc == KCH - 1)
                psG = k.next_ps()
                for kc in range(KCH):
                    P.mm(psG, wg[:, kc, cc * 128:(cc + 1) * 128], nT_own[:, kc, tch * 512:(tch + 1) * 512],
                         kc == 0, kc == KCH - 1)
                par = n % 2
                n += 1
                P.act(sig[par], psG, AF.Sigmoid)
                dst = mergedT[:, 4 * i + cc, tch * 512:(tch + 1) * 512]
                if which == "a":
                    P.tt("dve", dst, sig[par], psA, ALU.mult)
                else:
                    P.tt("dve", tmp[par], sig[par], psA, ALU.mult)
                    P.tt(POOL, dst, tmp[par], dst, ALU.add)
    if ("mergedT_" + which) in k.dbg_d:
        tmpf = A.alloc("dbgtmp", [128, 4, TOK], F32)
        P.copy("dve", tmpf, mergedT[:, 0:4, :])
        k.dbg_store("mergedT_" + which, tmpf)


def build_tail(k, stop_after):
    P, D, WS = k.P, k.D, k.WS
    R1, R2, R34 = k.R1, k.R2, k.R34
    psS, psV, psT, ident = k.psS, k.psV, k.psT, k.ident
    mergedT = k.mergedT
    hb = [R34.alloc("h%d" % tb, [128, DM], F32) for tb in range(NB)]
    for tb in range(NB):
        P.dma("sp", hb[tb], D["xo"][tb * 128:(tb + 1) * 128, :])
    tiles = [("wout%d" % i,) + wtile_cols(D["w_out"], 512 * i, 512) for i in range(4)]
    for i in range(4):
        wt = WS.get(tiles, i)
        for tb in range(NB):
            ps = k.next_ps()
            for kc in range(KCH):
                P.mm(ps, mergedT[:, kc, tb * 128:(tb + 1) * 128], wt[:, kc, :], kc == 0, kc == KCH - 1)
            hs = hb[tb][:, 512 * i:512 * (i + 1)]
            P.tt("dve", hs, hs, ps, ALU.add)
    if "h1" in k.dbg_d:
        for tb in range(NB):
            k.dbg_store("h1", hb[tb], k.dbg_d["h1"][tb * 128:(tb + 1) * 128, :])
    R2.release()
    if stop_after == "h1":
        return

    nxT = R1.alloc("nxT", [128, KCH, TOK], BF16)
    gainX = k.load_gain("x_norm_w", R2)
    xn = R2.alloc("xn", [128, DM], BF16)
    stat = R2.alloc("stat", [128, 16], F32)
    k.norm_T(lambda tb: hb[tb], NB, gainX, nxT, xn, stat)
    P.dma("sp", gainX, D["mem_norm_w"].to_broadcast([128, DM]))
    mh = R34.mark()
    mts = [R34.alloc("mt%d" % i, [128, DM], F32) for i in range(2)]
    mT = R2.alloc("mT", [128, KCH, 256], BF16)
    k.norm_T(k.mk_get(D["mem"], mts), 2, gainX, mT, xn, stat)
    R34.release(mh)
    qxT = R2.alloc("qxT", [128, 4, TOK], BF16)
    kxT = R34.alloc("kxT", [128, 4, 256], BF16)
    vx = R34.alloc("vx", [128, 2, 4, 130], BF16)
    PTx = [R34.alloc("PTx%d" % i, [128, 512], BF16) for i in range(2)]
    xc = R34.alloc("xcoef", [128, 4], F32)
    tiles = [("wqx",) + wtile_cols(D["wq_x"], 0, 512), ("wkx",) + wtile_cols(D["wk_x"], 0, 512),
             ("wvx",) + wtile_cols(D["wv_x"], 0, 512), ("wox",) + wtile_rows(D["wo_x"], 0, 512)]
    wq = WS.get(tiles, 0)
    for hh in range(4):
        for tch in range(2):
            k.proj_T(wq, hh * 128, nxT, tch * 512, 512, qxT[:, hh, tch * 512:(tch + 1) * 512], scale=QSCALE)
    wk_ = WS.get(tiles, 1)
    for hh in range(4):
        k.proj_T(wk_, hh * 128, mT, 0, 256, kxT[:, hh, :])
    wv = WS.get(tiles, 2)
    P.memset("dve", vx, 1.0)
    for mb in range(2):
        ps = k.next_ps()
        for kc in range(KCH):
            P.mm(ps, mT[:, kc, mb * 128:(mb + 1) * 128], wv[:, kc, :], kc == 0, kc == KCH - 1)
        k.evac(vx[:, mb, :, 0:128], ps.v(ps.ap.rearrange("p (a b) -> p a b", b=128)))
    R1.release()
    ox16 = R1.alloc("ox16", [128, NB, 512], BF16)
    oxT = R1.alloc("oxT", [128, 4, TOK], BF16)
    for hh in range(4):
        for tch in range(2):
            for mb in range(2):
                ps = psS[mb]
                P.mm(ps, kxT[:, hh, mb * 128:(mb + 1) * 128], qxT[:, hh, tch * 512:(tch + 1) * 512], True, True)
                P.act(PTx[mb], ps, AF.Exp)
            for tq in range(4):
                tb = tch * 4 + tq
                acc = psV[tq % 2]
                for mb in range(2):
                    P.mm(acc[:, 0:129], PTx[mb][:, tq * 128:(tq + 1) * 128], vx[:, mb, hh, 0:129], mb == 0, mb == 1)
                P.copy("dve", xc[:, 0:1], acc[:, 128:129])
                P.recip(xc[:, 1:2], xc[:, 0:1])
                P.ts("dve", ox16[:, tb, hh * 128:(hh + 1) * 128], acc[:, 0:128], xc[:, 1:2], None, ALU.mult)
    for tb in range(NB):
        for j in range(4):
            P.transpose(psT[:, j * 128:(j + 1) * 128], ox16[:, tb, j * 128:(j + 1) * 128], ident)
        k.evac(oxT[:, :, tb * 128:(tb + 1) * 128], psT.v(psT.ap[:, 0:512].rearrange("p (a b) -> p a b", b=128)))
    wo = WS.get(tiles, 3)
    for tb in range(NB):
        for cc in range(4):
            ps = k.next_ps()
            for kc in range(4):
                P.mm(ps, oxT[:, kc, tb * 128:(tb + 1) * 128], wo[:, kc, cc * 512:(cc + 1) * 512], kc == 0, kc == 3)
            hs = hb[tb][:, 512 * cc:512 * (cc + 1)]
            P.tt("dve", hs, hs, ps, ALU.add)
    if "h2" in k.dbg_d:
        for tb in range(NB):
            k.dbg_store("h2", hb[tb], k.dbg_d["h2"][tb * 128:(tb + 1) * 128, :])
    R1.release()
    R2.release()
    R34.release(mh)
    if stop_after == "h2":
        return

    nmT = R1.alloc("nmT", [128, KCH, TOK], BF16)
    gainM = k.load_gain("mlp_norm_w", R2)
    xn = R2.alloc("xn", [128, DM], BF16)
    stat = R2.alloc("stat", [128, 16], F32)
    k.norm_T(lambda tb: hb[tb], NB, gainM, nmT, xn, stat)
    aT = R2.alloc("aT", [128, 4, TOK], BF16)
    rl = [R2.alloc("rl%d" % i, [128, 512], F32) for i in range(2)]
    tiles = []
    for f in range(16):
        tiles.append(("wup%d" % f,) + wtile_cols(D["w_up"], 512 * f, 512))
        tiles.append(("wdn%d" % f,) + wtile_rows(D["w_down"], 512 * f, 512))
    n = 0
    for f in range(16):
        wu = WS.get(tiles, 2 * f)
        for cc in range(4):
            for tch in range(2):
                ps = k.next_ps()
                for kc in range(KCH):
                    P.mm(ps, wu[:, kc, cc * 128:(cc + 1) * 128], nmT[:, kc, tch * 512:(tch + 1) * 512],
                         kc == 0, kc == KCH - 1)
                par = n % 2
                n += 1
                P.act(rl[par], ps, AF.Relu)
                P.tt(POOL, aT[:, cc, tch * 512:(tch + 1) * 512], rl[par], rl[par], ALU.mult)
        wd = WS.get(tiles, 2 * f + 1)
        for tb in range(NB):
            for cc in range(4):
                ps = k.next_ps()
                for kc in range(4):
                    P.mm(ps, aT[:, kc, tb * 128:(tb + 1) * 128], wd[:, kc, cc * 512:(cc + 1) * 512], kc == 0, kc == 3)
                hs = hb[tb][:, 512 * cc:512 * (cc + 1)]
                P.tt("dve", hs, hs, ps, ALU.add)
    if "h3" in k.dbg_d:
        for tb in range(NB):
            k.dbg_store("h3", hb[tb], k.dbg_d["h3"][tb * 128:(tb + 1) * 128, :])
    R1.release()
    R2.release()

    gainF = k.load_gain("final_norm_w", R2)
    outt = [R2.alloc("outt%d" % i, [128, DM], F32) for i in range(2)]
    junk = R2.alloc("junkf", [128, DM], BF16)
    stat = R2.alloc("statf", [128, 16], F32)
    P.memset("dve", stat, 0.0)
    for tb in range(NB):
        P.act(junk, hb[tb], AF.Square, accum_out=stat[:, tb:tb + 1])
        P.ts("dve", stat[:, tb:tb + 1], stat[:, tb:tb + 1], 1.0 / DM, EPS, ALU.mult, ALU.add)
        P.act(stat[:, tb:tb + 1], stat[:, tb:tb + 1], AF.Sqrt)
        P.recip(stat[:, tb:tb + 1], stat[:, tb:tb + 1])
        P.stt("dve", outt[tb % 2], hb[tb], stat[:, tb:tb + 1], gainF, ALU.mult, ALU.mult)
        P.dma("sp", k.out_d[tb * 128:(tb + 1) * 128, :], outt[tb % 2])


def _w_in_perm():
    q = np.arange(0, 2048)
    parts = [q]
    for g in range(4):
        for base in (3072, 3584, 4096, 4608):
            parts.append(base + 128 * g + np.arange(128))
    parts.append(2048 + np.arange(512))
    parts.append(2560 + np.arange(512))
    for h in range(8):
        parts.append(5168 + 128 * h + np.arange(128))
        parts.append(6192 + 128 * h + np.arange(128))
        parts.append(7216 + 256 * h + np.arange(256))
    parts.append(np.arange(9264, 15408))
    parts.append(5120 + np.arange(48))
    perm = np.concatenate(parts)
    assert perm.shape[0] == IN_WIDTH and len(set(perm.tolist())) == IN_WIDTH
    return perm


def _tables(s):
    f32 = np.float32
    T = {}
    T["ident"] = np.eye(128, dtype=f32)
    u = np.arange(2048)
    t = u - 1024 + 1024 * s
    inv = (10000.0 ** (-np.arange(0, 128, 2, dtype=f32) / f32(128))).astype(f32)
    ang = t.astype(f32)[:, None] * inv[None, :]
    cos, sin = np.cos(ang).astype(f32), np.sin(ang).astype(f32)
    rk = np.concatenate([cos, sin], axis=1) * f32(128 ** -0.5)
    T["ropek"] = np.ascontiguousarray(rk.reshape(16, 128, 128).transpose(1, 0, 2)).astype(f32)
    rq = np.concatenate([cos, sin], axis=1)[1024:]
    T["ropeq"] = np.ascontiguousarray(rq.reshape(8, 128, 128).transpose(1, 0, 2)).astype(f32)
    H = 8
    log_g = np.log1p(-np.exp2(-5.0 - np.arange(H, dtype=f32))).astype(f32)
    i = np.arange(128, dtype=f32)
    rel = i[:, None] - i[None, :]
    decay = np.where(rel >= 0, np.exp(log_g[:, None, None] * np.maximum(rel, 0.0)), 0.0).astype(f32)
    T["decayT"] = np.ascontiguousarray(decay.transpose(2, 0, 1))
    w_q = np.exp(log_g[:, None] * (i + 1.0)[None, :]).astype(f32)
    T["wqB"] = np.ascontiguousarray(np.broadcast_to(w_q[None], (128, H, 128))).astype(f32)
    w_k = np.exp(log_g[:, None] * (127.0 - i)[None, :]).astype(f32)
    T["wk"] = np.ascontiguousarray(w_k.T)
    T["g_chunk"] = np.exp(log_g * f32(128.0)).astype(f32)
    gpow = np.stack([T["g_chunk"] ** f32(7 - blk) for blk in range(8)], 0).astype(f32)
    T["wkc"] = np.ascontiguousarray((w_k.T[:, None, :] * gpow[None, :, :]).astype(f32))
    uq = 1024 + np.arange(1024)
    lc = np.arange(128)
    vis = (16 * lc[None, :] + 31 <= uq[:, None]) & (lc[None, :] < 127)
    if s == 0:
        vis &= (lc[None, :] >= 64)
    T["cmask"] = np.ascontiguousarray(vis.astype(f32).reshape(8, 128, 128).transpose(1, 0, 2))
    lj = np.arange(32)
    lcur = (uq // 64)[:, None]
    first = 0 if s == 1 else 16
    am = np.zeros((1024, 32), f32)
    am[np.broadcast_to(lj[None, :] == first, am.shape)] = 1e9
    prev = (lj[None, :] == lcur - 1) & (lj[None, :] >= first)
    am[prev] = 2e9
    am[np.broadcast_to(lj[None, :], am.shape) == lcur] = 3e9
    am[(lj[None, :] > lcur) | (lj[None, :] < first)] = -1e9
    T["addmask"] = np.ascontiguousarray(am.reshape(8, 128, 32).transpose(1, 0, 2))
    kval = (t >= 0).astype(f32)
    T["kvalid"] = np.ascontiguousarray(kval.reshape(16, 128).T)
    kk = np.arange(2048)
    T["Emat"] = (kk[None, :] // 64 == np.arange(32)[:, None]).astype(f32)
    kq = np.arange(128)
    T["causneg"] = np.tile(np.where(kq[:, None] > kq[None, :], NEGB, 0.0).astype(f32), (1, 4))
    T["winneg"] = np.tile(np.where(kq[:, None] <= kq[None, :], NEGB, 0.0).astype(f32), (1, 4))
    return T


def make_in_maps(inputs):
    f32 = np.float32
    x = np.asarray(inputs["x"], f32)
    mem = np.asarray(inputs["mem"], f32)
    perm = _w_in_perm()
    shared = {}
    shared["w_in"] = np.ascontiguousarray(np.asarray(inputs["w_in"], f32)[0][:, perm])
    for nm in ("cmp_w1_k", "cmp_w1_v", "cmp_w2_k", "cmp_w2_v", "w_a", "w_b", "w_out", "wq_x", "wk_x", "wv_x",
               "wo_x", "w_up", "w_down"):
        shared[nm] = np.ascontiguousarray(np.asarray(inputs[nm], f32)[0])
    shared["peT_k"] = np.ascontiguousarray(np.asarray(inputs["cmp_pe_k"], f32)[0].T)
    shared["peT_v"] = np.ascontiguousarray(np.asarray(inputs["cmp_pe_v"], f32)[0].T)
    for nm in ("attn_norm_w", "ret_gn_w", "x_norm_w", "mem_norm_w", "mlp_norm_w"):
        shared[nm] = np.ascontiguousarray(np.asarray(inputs[nm], f32).reshape(1, DM))
    shared["final_norm_w"] = np.ascontiguousarray(np.asarray(inputs["final_norm_w"], f32).reshape(1, DM))
    tabs = [_tables(0), _tables(1)]
    zeros = np.zeros((TOK, DM), f32)
    in_maps = []
    for c in range(8):
        b, s = c // 2, c % 2
        m = dict(shared)
        m["xo"] = np.ascontiguousarray(x[b, 1024 * s:1024 * (s + 1)])
        m["xc"] = np.ascontiguousarray(x[b, 0:1024]) if s == 1 else zeros
        m["mem"] = np.ascontiguousarray(mem[b])
        for kk, v in tabs[s].items():
            if kk != "g_chunk":
                m[kk] = v
        in_maps.append(m)
    return in_maps


_G_CHUNK = [float(v) for v in np.exp(np.log1p(-np.exp2(-5.0 - np.arange(8, dtype=np.float32))).astype(np.float32)
                                     * np.float32(128.0)).astype(np.float32)]


def kernel(**inputs):
    in_maps = make_in_maps(inputs)
    nc, st = build_program()
    res = run_bass_kernel_spmd(nc, in_maps, core_ids=list(range(8)))
    out = np.zeros((4, 2048, DM), np.float32)
    for c in range(8):
        b, s = c // 2, c % 2
        out[b, 1024 * s:1024 * (s + 1)] = res.results[c]["out"]
    return out
```

```python
import numpy as np
from concourse.bass_utils import run_bass_kernel_spmd
import concourse.bass as bass
import concourse.mybir as mybir

F32 = mybir.dt.float32
BF16 = mybir.dt.bfloat16
AF = mybir.ActivationFunctionType
ALU = mybir.AluOpType
AX = mybir.AxisListType

_DT_SIZE = {F32: 4, BF16: 2}


class Buf:
    def __init__(self, key, ap):
        self.key = key
        self.ap = ap

    def __getitem__(self, idx):
        return Buf(self.key, self.ap[idx])

    def v(self, ap):
        return Buf(self.key, ap)


class Op:
    __slots__ = ("eng", "fn", "reads", "writes", "is_dma", "semkey", "deps", "dma_deps",
                 "signal", "signum", "pos", "accum", "idx")


class Prog:
    ENGS = ("pe", "act", "dve", "pool", "sp")

    def __init__(self, nc):
        self.nc = nc
        self.ops = []
        self.eng_obj = {"pe": nc.tensor, "act": nc.scalar, "dve": nc.vector, "pool": nc.gpsimd, "sp": nc.sync}
        self.sync_same_engine_war = False

    def _add(self, eng, fn, reads, writes, is_dma=False, semkey=None, accum=False):
        o = Op()
        o.eng = eng
        o.fn = fn
        o.reads = [b.key for b in reads if b is not None]
        o.writes = [b.key for b in writes if b is not None]
        for kk in o.reads:
            if kk.startswith("ps") and kk not in o.writes:
                o.writes.append(kk)
        o.is_dma = is_dma
        o.semkey = semkey
        o.accum = accum
        o.idx = len(self.ops)
        self.ops.append(o)
        return o

    def op(self, eng, fn, reads=(), writes=()):
        return self._add(eng, fn, reads, writes)

    def barrier(self):
        o = Op()
        o.eng = None
        o.idx = len(self.ops)
        self.ops.append(o)

    def dma(self, eng, out, in_, semkey=None):
        reads, writes = [], []
        if isinstance(in_, Buf):
            reads.append(in_)
            in_ap = in_.ap
        else:
            in_ap = in_
        if isinstance(out, Buf):
            writes.append(out)
            out_ap = out.ap
            if semkey is None:
                semkey = "dma:" + out.key
        else:
            out_ap = out
            if semkey is None:
                semkey = "dma:store"

        def fn(e):
            return e.dma_start(out=out_ap, in_=in_ap)

        return self._add(eng, fn, reads, writes, is_dma=True, semkey=semkey)

    def mm(self, out, lhsT, rhs, start, stop, extra_reads=(), sgc=False):
        def fn(e):
            if sgc:
                return e.matmul(out.ap, lhsT.ap, rhs.ap, start=start, stop=stop, skip_group_check=True)
            return e.matmul(out.ap, lhsT.ap, rhs.ap, start=start, stop=stop)
        return self._add("pe", fn, [lhsT, rhs] + list(extra_reads), [out], accum=not start)

    def transpose(self, out, in_, ident):
        def fn(e):
            return e.transpose(out.ap, in_.ap, ident.ap)
        return self._add("pe", fn, [in_, ident], [out])

    def act(self, out, in_, func, bias=None, scale=1.0, accum_out=None, eng="act"):
        reads = [in_]
        kw = {}
        if isinstance(bias, Buf):
            reads.append(bias)
            kw["bias"] = bias.ap
        elif bias is not None:
            kw["bias"] = bias
        if isinstance(scale, Buf):
            reads.append(scale)
            kw["scale"] = scale.ap
        else:
            kw["scale"] = scale
        writes = [out]
        if accum_out is not None:
            writes.append(accum_out)
            kw["accum_out"] = accum_out.ap

        def fn(e):
            return e.activation(out=out.ap, in_=in_.ap, func=func, **kw)
        return self._add(eng, fn, reads, writes)

    def tt(self, eng, out, in0, in1, op):
        def fn(e):
            return e.tensor_tensor(out=out.ap, in0=in0.ap, in1=in1.ap, op=op)
        return self._add(eng, fn, [in0, in1], [out])

    def ts(self, eng, out, in0, s1, s2, op0, op1=None, accum_out=None):
        reads = [in0]
        a1 = s1
        a2 = s2
        if isinstance(s1, Buf):
            reads.append(s1)
            a1 = s1.ap
        if isinstance(s2, Buf):
            reads.append(s2)
            a2 = s2.ap
        writes = [out]
        kw = {}
        if op1 is not None:
            kw["op1"] = op1
        if accum_out is not None:
            writes.append(accum_out)
            kw["accum_out"] = accum_out.ap

        def fn(e):
            return e.tensor_scalar(out=out.ap, in0=in0.ap, scalar1=a1, scalar2=a2, op0=op0, **kw)
        return self._add(eng, fn, reads, writes)

    def stt(self, eng, out, in0, scalar, in1, op0, op1):
        reads = [in0, in1]
        a = scalar
        if isinstance(scalar, Buf):
            reads.append(scalar)
            a = scalar.ap

        def fn(e):
            return e.scalar_tensor_tensor(out=out.ap, in0=in0.ap, scalar=a, in1=in1.ap, op0=op0, op1=op1)
        return self._add(eng, fn, reads, [out])

    def copy(self, eng, out, in_):
        if eng == "act":
            def fn(e):
                return e.copy(out=out.ap, in_=in_.ap)
        else:
            def fn(e):
                return e.tensor_copy(out=out.ap, in_=in_.ap)
        return self._add(eng, fn, [in_], [out])

    def reduce(self, eng, out, in_, op, axis=AX.X):
        def fn(e):
            return e.tensor_reduce(out=out.ap, in_=in_.ap, axis=axis, op=op)
        return self._add(eng, fn, [in_], [out])

    def memset(self, eng, out, val):
        def fn(e):
            return e.memset(out.ap, val)
        return self._add(eng, fn, [], [out])

    def recip(self, out, in_):
        def fn(e):
            return e.reciprocal(out=out.ap, in_=in_.ap)
        return self._add("dve", fn, [in_], [out])

    def emit(self, final_wait_eng="sp"):
        nc = self.nc
        ops = self.ops
        last_writer = {}
        readers = {}
        pos_ctr = {e: 0 for e in self.ENGS}
        waited = {f: {e: -1 for e in self.ENGS} for f in self.ENGS}
        waited_dma = {f: {} for f in self.ENGS}
        dma_count = {}
        last_op_on = {e: None for e in self.ENGS}
        pending_barrier = {e: [] for e in self.ENGS}
        outstanding_dma = []

        for o in ops:
            if o.eng is None:
                for f in self.ENGS:
                    pending_barrier[f] = [last_op_on[e] for e in self.ENGS if e != f and last_op_on[e] is not None]
                continue
            f = o.eng
            deps = set()
            for k in o.reads:
                w = last_writer.get(k)
                if w is not None:
                    deps.add(w)
            for k in o.writes:
                w = last_writer.get(k)
                if w is not None:
                    deps.add(w)
                for r in readers.get(k, ()):
                    deps.add(r)
            for b in pending_barrier[f]:
                deps.add(b)
            pending_barrier[f] = []
            o.pos = pos_ctr[f]
            pos_ctr[f] += 1
            o.deps = []
            o.dma_deps = []
            o.signal = False
            for di in sorted(deps):
                d = ops[di]
                if d.idx == o.idx:
                    continue
                if d.is_dma:
                    cnt = d.signum
                    if waited_dma[f].get(d.semkey, 0) >= cnt:
                        continue
                    waited_dma[f][d.semkey] = cnt
                    o.dma_deps.append((d.semkey, cnt))
                else:
                    if d.eng == "pe" and f == "pe" and not o.is_dma:
                        continue
                    if d.eng == f and not self.sync_same_engine_war and not o.is_dma:
                        is_raw_waw = any(last_writer.get(k) == di for k in o.reads + o.writes)
                        if not is_raw_waw:
                            continue
                    if waited[f][d.eng] >= d.pos:
                        continue
                    waited[f][d.eng] = d.pos
                    d.signal = True
                    o.deps.append(di)
            if o.is_dma:
                dma_count[o.semkey] = dma_count.get(o.semkey, 0) + 16
                o.signum = dma_count[o.semkey]
                outstanding_dma.append(o)
            for k in o.reads:
                readers.setdefault(k, []).append(o.idx)
            for k in o.writes:
                last_writer[k] = o.idx
                readers[k] = []
            last_op_on[f] = o.idx

        tail_deps = []
        for e in self.ENGS:
            li = last_op_on[e]
            if li is not None and not ops[li].is_dma:
                ops[li].signal = True
                tail_deps.append(li)
        sig_ctr = {e: 0 for e in self.ENGS}
        for o in ops:
            if o.eng is None or o.is_dma:
                continue
            if o.signal:
                sig_ctr[o.eng] += 1
                o.signum = sig_ctr[o.eng]
        import contextlib
        es = contextlib.ExitStack()
        self._es = es
        sems = {e: es.enter_context(nc.semaphore("s_" + e)) for e in self.ENGS}
        dsems = {}
        for k in dma_count:
            dsems[k] = es.enter_context(nc.semaphore("d%d" % len(dsems)))
        n_wait = 0
        for o in ops:
            if o.eng is None:
                continue
            e = self.eng_obj[o.eng]
            for di in o.deps:
                d = ops[di]
                e.wait_ge(sems[d.eng], d.signum)
                n_wait += 1
            for (k, cnt) in o.dma_deps:
                e.wait_ge(dsems[k], cnt)
                n_wait += 1
            ins = o.fn(e)
            if o.is_dma:
                ins.then_inc(dsems[o.semkey], 16)
            elif o.signal:
                ins.then_inc(sems[o.eng], 1)
        fe = self.eng_obj[final_wait_eng]
        for di in tail_deps:
            d = ops[di]
            fe.wait_ge(sems[d.eng], d.signum)
        for k, cnt in dma_count.items():
            fe.wait_ge(dsems[k], cnt)
        self.stats = dict(n_ops=len(ops), n_wait=n_wait, sig=dict(sig_ctr), n_dsems=len(dsems))
        return self.stats


TOK = 1024
NB = 8
DM = 2048
KCH = 16
EPS = 1e-6
OQ, OKV, OC, OR_, OGR, OGA, OGB, OG = 0, 2048, 4096, 5120, 9216, 11264, 13312, 15360
IN_WIDTH = 15408
NEGB = -30000.0
QSCALE = 128 ** -0.5
POOL = "dve"


def _prod(xs):
    r = 1
    for x in xs:
        r *= int(x)
    return r


class Arena:
    uid = 0

    def __init__(self, full_ap, P, base, nel, name):
        self.ap = full_ap
        self.base = base
        self.top = 0
        self.nel = nel
        self.P = P
        self.peak = 0
        self.name = name

    def alloc(self, name, shape, dt):
        inner = _prod(shape[1:])
        n = inner * (2 if dt == F32 else 1)
        npad = (n + 15) // 16 * 16
        off = self.base + self.top
        self.top += npad
        self.peak = max(self.peak, self.top)
        assert self.top <= self.nel, ("SBUF arena overflow", self.name, name, self.top, self.nel)
        ap = self.ap[:shape[0], off:off + n]
        if dt == F32:
            ap = ap.bitcast(F32)
        if len(shape) == 3:
            ap = ap.rearrange("p (a b) -> p a b", b=shape[2])
        elif len(shape) == 4:
            ap = ap.rearrange("p (a b c) -> p a b c", b=shape[2], c=shape[3])
        Arena.uid += 1
        return Buf("%s#%d" % (name, Arena.uid), ap)

    def mark(self):
        return self.top

    def release(self, mark=0):
        self.top = mark
        self.P.barrier()


class WStream:
    def __init__(self, P, arena, nslots, slot_elems):
        self.P = P
        self.nslots = nslots
        self.slot_elems = slot_elems
        self.slots = [arena.alloc("wslot%d" % i, [128, slot_elems], BF16) for i in range(nslots)]
        self.ctr = 0
        self.loaded = {}

    def _load(self, item):
        key, src, shape = item
        if key in self.loaded:
            return
        s = self.slots[self.ctr % self.nslots]
        self.ctr += 1
        n = _prod(shape[1:])
        ap = s.ap[:, 0:n]
        if len(shape) == 3:
            ap = ap.rearrange("p (a b) -> p a b", b=shape[2])
        b = Buf(s.key, ap)
        self.P.dma("pool", b, src)
        self.loaded[key] = b

    def get(self, lst, i, depth=None):
        depth = self.nslots - 1 if depth is None else depth
        for j in range(i, min(len(lst), i + depth + 1)):
            self._load(lst[j])
        b = self.loaded.pop(lst[i][0])
        return b


def wtile_cols(w2d, c0, ncols):
    return w2d[:, c0:c0 + ncols].rearrange("(kc p) c -> p kc c", p=128), [128, 16, ncols]


def wtile_rows(w2d, r0, nrows):
    return w2d[r0:r0 + nrows, :].rearrange("(kc p) c -> p kc c", p=128), [128, nrows // 128, 2048]


class K:
    pass


def build_program(dbg=None, stop_after=None):
    nc = bass.Bass("TRN2", target_bir_lowering=False)
    P = Prog(nc)
    k = K()
    k.nc, k.P = nc, P

    def din(name, shape):
        return nc.dram_tensor(name, list(shape), F32, kind="ExternalInput").ap()

    D = {}
    D["xo"] = din("xo", [TOK, DM])
    D["xc"] = din("xc", [TOK, DM])
    D["mem"] = din("mem", [256, DM])
    D["w_in"] = din("w_in", [DM, IN_WIDTH])
    for nm in ("cmp_w1_k", "cmp_w1_v"):
        D[nm] = din(nm, [4096, 1024])
    for nm in ("cmp_w2_k", "cmp_w2_v"):
        D[nm] = din(nm, [1024, 128])
    for nm in ("peT_k", "peT_v"):
        D[nm] = din(nm, [128, 32])
    for nm in ("w_a", "w_b", "w_out"):
        D[nm] = din(nm, [DM, DM])
    for nm in ("wq_x", "wk_x", "wv_x"):
        D[nm] = din(nm, [DM, 512])
    D["wo_x"] = din("wo_x", [512, DM])
    D["w_up"] = din("w_up", [DM, 8192])
    D["w_down"] = din("w_down", [8192, DM])
    for nm in ("attn_norm_w", "ret_gn_w", "x_norm_w", "mem_norm_w", "mlp_norm_w", "final_norm_w"):
        D[nm] = din(nm, [1, DM])
    D["ident"] = din("ident", [128, 128])
    D["ropeq"] = din("ropeq", [128, 8, 128])
    D["ropek"] = din("ropek", [128, 16, 128])
    D["decayT"] = din("decayT", [128, 8, 128])
    D["wqB"] = din("wqB", [128, 8, 128])
    D["wk"] = din("wk", [128, 8])
    D["wkc"] = din("wkc", [128, 8, 8])
    D["cmask"] = din("cmask", [128, 8, 128])
    D["addmask"] = din("addmask", [128, 8, 32])
    D["kvalid"] = din("kvalid", [128, 16])
    D["Emat"] = din("Emat", [32, 2048])
    D["causneg"] = din("causneg", [128, 512])
    D["winneg"] = din("winneg", [128, 512])
    out_d = nc.dram_tensor("out", [TOK, DM], F32, kind="ExternalOutput").ap()
    dbg_d = {}
    if dbg:
        for nm, shp in dbg.items():
            dbg_d[nm] = nc.dram_tensor("dbg_" + nm, list(shp), F32, kind="ExternalOutput").ap()
    k.D, k.out_d, k.dbg_d = D, out_d, dbg_d

    NEL = 106000
    full = nc.alloc_sbuf_tensor("arena", [128, NEL], BF16).ap()
    R0 = Arena(full, P, 0, 30000, "R0")
    R1 = Arena(full, P, 30000, 16384, "R1")
    R2 = Arena(full, P, 46384, 16384, "R2")
    R34 = Arena(full, P, 62768, NEL - 62768, "R34")
    k.R0, k.R1, k.R2, k.R34 = R0, R1, R2, R34
    A = R0
    k.psS = [Buf("psS%d" % i, nc.alloc_psum_tensor("psS%d" % i, [128, 512], F32).ap()) for i in range(3)]
    k.psC = Buf("psC", nc.alloc_psum_tensor("psC", [128, 512], F32).ap())
    k.psO = Buf("psO", nc.alloc_psum_tensor("psO", [128, 512], F32).ap())
    k.psV = [Buf("psV%d" % i, nc.alloc_psum_tensor("psV%d" % i, [128, 512], F32).ap()) for i in range(2)]
    k.psT = Buf("psT", nc.alloc_psum_tensor("psT", [128, 1024], BF16).ap())
    k.rot5 = [k.psS[0], k.psS[1], k.psS[2], k.psC, k.psO]
    k.rot_i = 0
    k.ev_i = 0

    k.ident = A.alloc("ident", [128, 128], BF16)
    P.dma("pool", k.ident, D["ident"][:, :])
    k.WS = WStream(P, A, 3, 16 * 512)

    def finish():
        st = P.emit()
        st["arena_peak_el"] = [R0.peak, R1.peak, R2.peak, R34.peak]
        return nc, st

    def dbg_store(name, buf, dram_view=None):
        if name in dbg_d:
            dv = dbg_d[name] if dram_view is None else dram_view
            P.dma("sp", dv, buf, semkey="dma:dbg")

    k.dbg_store = dbg_store

    def next_ps():
        b = k.rot5[k.rot_i % 5]
        k.rot_i += 1
        return b

    def evac(out, in_, scale=None):
        e = k.ev_i % 2
        k.ev_i += 1
        if e == 0:
            if scale is None:
                P.copy("act", out, in_)
            else:
                P.act(out, in_, AF.Copy, scale=scale)
        else:
            if scale is None:
                P.copy("dve", out, in_)
            else:
                P.ts("dve", out, in_, scale, None, ALU.mult)

    k.next_ps, k.evac = next_ps, evac

    def load_gain(name, reg):
        g = reg.alloc("gain_" + name, [128, DM], F32)
        P.dma("sp", g, D[name].to_broadcast([128, DM]))
        return g

    def norm_T(get_block, nblk, gain, nT, xn, stat):
        P.memset("dve", stat, 0.0)
        for tb in range(nblk):
            xt = get_block(tb)
            P.act(xn, xt, AF.Square, accum_out=stat[:, tb:tb + 1])
            P.ts("dve", stat[:, tb:tb + 1], stat[:, tb:tb + 1], 1.0 / DM, EPS, ALU.mult, ALU.add)
            P.act(stat[:, tb:tb + 1], stat[:, tb:tb + 1], AF.Sqrt)
            P.recip(stat[:, tb:tb + 1], stat[:, tb:tb + 1])
            P.stt("dve", xn, xt, stat[:, tb:tb + 1], gain, ALU.mult, ALU.mult)
            for half in range(2):
                for j in range(8):
                    kc = half * 8 + j
                    P.transpose(k.psT[:, j * 128:(j + 1) * 128], xn[:, kc * 128:(kc + 1) * 128], k.ident)
                src = k.psT.v(k.psT.ap.rearrange("p (a b) -> p a b", b=128))
                evac(nT[:, half * 8:half * 8 + 8, tb * 128:(tb + 1) * 128], src)

    def proj_T(wt, c0, nT, t0, ntok, out, scale=None, nk=KCH, out_view3=False):
        ps = next_ps()
        for kc in range(nk):
            P.mm(ps[:, 0:ntok], wt[:, kc, c0:c0 + 128], nT[:, kc, t0:t0 + ntok], kc == 0, kc == nk - 1)
        src = ps[:, 0:ntok]
        if out_view3:
            src = ps.v(ps.ap[:, 0:ntok].rearrange("p (a b) -> p a b", b=128))
        evac(out, src, scale)

    k.norm_T, k.proj_T, k.load_gain = norm_T, proj_T, load_gain

    nT_own = R1.alloc("nT_own", [128, KCH, TOK], BF16)
    nT_ctx = R2.alloc("nT_ctx", [128, KCH, TOK], BF16)
    k.state8 = R0.alloc("state8", [128, 8, 256], F32)
    k.kcmpT = R0.alloc("kcmpT", [128, 4, 128], BF16)
    k.vcmp = R0.alloc("vcmp", [128, 4, 128], BF16)
    gainA = load_gain("attn_norm_w", R34)
    xts = [R34.alloc("xt%d" % i, [128, DM], F32) for i in range(2)]
    xn = R34.alloc("xn", [128, DM], BF16)
    stat = R34.alloc("stat", [128, 16], F32)

    def mk_get(src, xts):
        def get(tb):
            xt = xts[tb % 2]
            P.dma("sp", xt, src[tb * 128:(tb + 1) * 128, :])
            return xt
        return get

    k.mk_get = mk_get
    norm_T(mk_get(D["xc"], xts), NB, gainA, nT_ctx, xn, stat)
    norm_T(mk_get(D["xo"], xts), NB, gainA, nT_own, xn, stat)
    if "nT_own" in dbg_d:
        tmpf = R34.alloc("dbgtmp", [128, KCH, TOK // 4], F32)
        P.copy("dve", tmpf, nT_own[:, :, 0:TOK // 4])
        dbg_store("nT_own", tmpf)
    R34.release()
    if stop_after == "norm":
        return finish()
    k.nT_own, k.nT_ctx = nT_own, nT_ctx
    k.finish = finish

    build_ret_ctx(k)
    R34.release()
    if stop_after == "ret_ctx":
        return finish()
    build_nsa(k, stop_after)
    if stop_after and stop_after.startswith("nsa"):
        return finish()
    R34.release(k.m_after_o)
    R2.release()
    k.mergedT = R2.alloc("mergedT", [128, KCH, TOK], BF16)
    build_merge(k, "a", k.o_nsaT)
    R34.release()
    if stop_after == "merge_a":
        return finish()
    build_ret_own(k)
    if stop_after == "ret":
        return finish()
    R34.release(k.m_after_o)
    build_merge(k, "b", k.o_retT)
    R34.release()
    R1.release()
    if stop_after == "merge_b":
        return finish()
    build_tail(k, stop_after)
    return finish()


def build_nsa(k, stop_after):
    P, A, D, WS = k.P, k.R34, k.D, k.WS
    nT_own, nT_ctx = k.nT_own, k.nT_ctx
    psS, psC, psO, psV, psT = k.psS, k.psC, k.psO, k.psV, k.psT
    evac, next_ps, proj_T = k.evac, k.next_ps, k.proj_T
    ident = k.ident
    w_in = D["w_in"]
    uchunks = [(nT_ctx, 0, 0), (nT_ctx, 512, 512), (nT_own, 0, 1024), (nT_own, 512, 1536)]

    o_nsaT = A.alloc("o_nsaT", [128, 16, TOK], BF16)
    k.o_nsaT = o_nsaT
    k.m_after_o = A.mark()
    kcmpT, vcmp = k.kcmpT, k.vcmp
    P.memset("dve", kcmpT, 0.0)
    P.memset("dve", vcmp, 0.0)
    m1 = A.mark()
    kvT = A.alloc("kvcT", [128, 4, 2048], BF16)
    hidT = A.alloc("hidT", [128, 8, 4, 128], BF16)
    peT = A.alloc("peT", [128, 32], BF16)
    w2 = A.alloc("w2", [128, 8, 128], BF16)
    cbias = A.alloc("cbias", [128, 8], F32)
    for kind in range(2):
        sfx = "_k" if kind == 0 else "_v"
        tiles = [("wc%d" % kind,) + wtile_cols(w_in, OC + 512 * kind, 512)]
        for hh in range(2):
            for lh in range(2):
                src = D["cmp_w1" + sfx][2048 * lh:2048 * (lh + 1), 512 * hh:512 * (hh + 1)].rearrange(
                    "(l p) c -> p l c", p=128)
                tiles.append(("w1%d_%d_%d" % (kind, hh, lh), src, [128, 16, 512]))
        P.dma("pool", peT, D["peT" + sfx][:, :])
        P.dma("pool", w2, D["cmp_w2" + sfx].rearrange("(hc p) c -> p hc c", p=128))
        wt = WS.get(tiles, 0)
        for g in range(4):
            for (nT, t0, u0) in uchunks:
                proj_T(wt, g * 128, nT, t0, 512, kvT[:, g, u0:u0 + 512])
        ti = 1
        for hh in range(2):
            accs = [psS[0], psS[1], psS[2], psC]
            for lh in range(2):
                wt = WS.get(tiles, ti)
                ti += 1
                for li in range(16):
                    l = lh * 16 + li
                    for hc in range(4):
                        lhsT = wt[:, li, hc * 128:(hc + 1) * 128]
                        for g in range(4):
                            rhs = kvT[:, g, l:l + 16 * 126 + 1:16]
                            P.mm(accs[hc][:, g * 127:(g + 1) * 127], lhsT, rhs, l == 0 and g == 0, l == 31, sgc=True)
                        P.mm(psO[:, hc:hc + 1], lhsT, peT[:, l:l + 1], l == 0 and hc == 0, l == 31, sgc=True)
            for hc in range(4):
                hcg = hh * 4 + hc
                P.copy("dve", cbias[:, hcg:hcg + 1], psO[:, hc:hc + 1])
                src = accs[hc].v(accs[hc].ap[:, 0:508].rearrange("p (g c) -> p g c", c=127))
                P.act(hidT[:, hcg, :, 0:127], src, AF.Silu, bias=cbias[:, hcg:hcg + 1])
        if kind == 0:
            ps = next_ps()
            for g in range(4):
                for hc in range(8):
                    P.mm(ps[:, g * 127:(g + 1) * 127], w2[:, hc, :], hidT[:, hc, g, 0:127], hc == 0, hc == 7)
            evac(kcmpT[:, :, 0:127], ps.v(ps.ap[:, 0:508].rearrange("p (g c) -> p g c", c=127)))
        else:
            ps = next_ps()
            for g in range(4):
                for hc in range(8):
                    P.mm(ps[0:127, g * 128:(g + 1) * 128], hidT[:, hc, g, 0:127], w2[:, hc, :], hc == 0, hc == 7)
            evac(vcmp[0:127, :, :], ps.v(ps.ap[0:127, :].rearrange("p (g c) -> p g c", c=128)))
    if "kcmpT" in k.dbg_d:
        tmpf = A.alloc("dbgtmp", [128, 4, 128], F32)
        P.copy("dve", tmpf, kcmpT)
        k.dbg_store("kcmpT", tmpf)
        tmpf2 = A.alloc("dbgtmp2", [128, 4, 128], F32)
        P.copy("dve", tmpf2, vcmp)
        k.dbg_store("vcmp", tmpf2)
    A.release(m1)
    if stop_after == "nsa_cmp":
        return

    cmask = A.alloc("cmask", [128, 8, 128], F32)
    addmask = A.alloc("addmask", [128, 8, 32], F32)
    kvalid = A.alloc("kvalid", [128, 16], BF16)
    Emat = A.alloc("Emat", [32, 2048], BF16)
    causneg = A.alloc("causneg", [128, 512], BF16)
    winneg = A.alloc("winneg", [128, 512], BF16)
    g3 = A.alloc("g3", [128, 8, 48], F32)
    P.dma("sp", cmask, D["cmask"][:, :, :])
    P.dma("sp", addmask, D["addmask"][:, :, :])
    P.dma("pool", kvalid, D["kvalid"][:, :])
    P.dma("pool", Emat, D["Emat"][:, :])
    P.dma("pool", causneg, D["causneg"][:, :])
    P.dma("pool", winneg, D["winneg"][:, :])
    tiles = [("wg3",) + wtile_cols(w_in, OG, 48)]
    for g in range(4):
        tiles.append(("wq%d" % g,) + wtile_cols(w_in, OQ + 512 * g, 512))
        tiles.append(("wkv%d" % g,) + wtile_cols(w_in, OKV + 512 * g, 512))
    wt = WS.get(tiles, 0)
    for tb in range(NB):
        ps = next_ps()
        for kc in range(KCH):
            P.mm(ps[:, 0:48], nT_own[:, kc, tb * 128:(tb + 1) * 128], wt[:, kc, 0:48], kc == 0, kc == KCH - 1)
        P.act(g3[:, tb, :], ps[:, 0:48], AF.Sigmoid)

    qT = A.alloc("qT", [128, NB, 4, 128], BF16)
    ksT = A.alloc("ksT", [128, 2048], BF16)
    kwT = A.alloc("kwT", [128, 1536], BF16)
    vs = A.alloc("vs", [128, 16, 130], BF16)
    vw = A.alloc("vw", [128, 12, 130], BF16)
    e32 = A.alloc("e32", [128, 4, 128], F32)
    p32 = A.alloc("p32", [128, 4, 128], F32)
    p16 = A.alloc("p16", [128, 4, 128], BF16)
    pT = A.alloc("pT", [128, 4, 128], BF16)
    Pg = A.alloc("Pg", [128, 128], F32)
    imp = A.alloc("imp", [128, 32], F32)
    imp2 = A.alloc("imp2", [128, 32], F32)
    m8 = A.alloc("m8", [128, 8], F32)
    sm4 = A.alloc("sm4", [128, 16], F32)
    selneg = A.alloc("selneg", [128, 32], BF16)
    negT = [A.alloc("negT%d" % i, [32, 4, 128], BF16) for i in range(2)]
    oacc = [A.alloc("oacc%d" % i, [128, 4, 128], F32) for i in range(2)]
    o16 = A.alloc("o16", [128, 4, 128], BF16)
    PT = [A.alloc("PT%d" % i, [128, 512], BF16) for i in range(2)]
    coef = A.alloc("coef", [128, 8], F32)
    P.memset("dve", vs, 0.0)
    P.memset("dve", vw, 0.0)

    def bc_heads(b):
        return b.v(b.ap.unsqueeze(1).to_broadcast([b.ap.shape[0], 4, 128]))

    for g in range(4):
        wq = WS.get(tiles, 1 + 2 * g)
        for hh in range(4):
            for tch in range(2):
                proj_T(wq, hh * 128, nT_own, tch * 512, 512, qT[:, 4 * tch:4 * tch + 4, hh, :], scale=QSCALE,
                       out_view3=True)
        wkv = WS.get(tiles, 2 + 2 * g)
        for (nT, t0, u0) in uchunks:
            proj_T(wkv, 0, nT, t0, 512, ksT[:, u0:u0 + 512])
        for (nT, t0, u0) in uchunks[1:]:
            proj_T(wkv, 256, nT, t0, 512, kwT[:, u0 - 512:u0])
        for (vbuf, c0, ub0) in ((vs, 128, 0), (vw, 384, 4)):
            for q4 in range(ub0 // 4, 4):
                ps = next_ps()
                for j in range(4):
                    ub = 4 * q4 + j
                    nT = nT_ctx if ub < 8 else nT_own
                    tb = ub % 8
                    for kc in range(KCH):
                        P.mm(ps[:, j * 128:(j + 1) * 128], nT[:, kc, tb * 128:(tb + 1) * 128],
                             wkv[:, kc, c0:c0 + 128], kc == 0, kc == KCH - 1)
                evac(vbuf[:, 4 * q4 - ub0:4 * q4 - ub0 + 4, 0:128],
                     ps.v(ps.ap.rearrange("p (a b) -> p a b", b=128)))
            P.copy("dve", vbuf[:, :, 128:129], kvalid.v(kvalid.ap[:, ub0:16].unsqueeze(2)))

        def cmp_stage(qb):
            par = qb % 2
            for hh in range(4):
                P.mm(psC[:, hh * 128:(hh + 1) * 128], qT[:, qb, hh, :], kcmpT[:, g, :], True, True)
            psC3 = psC.v(psC.ap.rearrange("p (a b) -> p a b", b=128))
            P.reduce("dve", sm4[:, 0:4], psC3, ALU.max)
            P.ts("dve", sm4[:, 4:8], sm4[:, 0:4], -1.0, None, ALU.mult)
            for hh in range(4):
                P.act(e32[:, hh, :], psC[:, hh * 128:(hh + 1) * 128], AF.Exp, bias=sm4[:, 4 + hh:5 + hh])
            P.tt("dve", e32, e32, bc_heads(cmask[:, qb, :]), ALU.mult)
            P.reduce("dve", sm4[:, 8:12], e32, ALU.add)
            P.ts("dve", sm4[:, 8:12], sm4[:, 8:12], 1e-30, None, ALU.max)
            P.recip(sm4[:, 12:16], sm4[:, 8:12])
            rb = sm4.v(sm4.ap[:, 12:16].unsqueeze(2).to_broadcast([128, 4, 128]))
            P.tt("dve", p32, e32, rb, ALU.mult)
            P.copy("act", p16, p32)
            P.reduce("dve", Pg, p32.v(p32.ap.rearrange("p h c -> p c h")), ALU.add)
            P.reduce("dve", imp, Pg.v(Pg.ap.rearrange("p (j f) -> p j f", f=4)), ALU.add)
            P.tt("dve", imp2[:, 1:32], imp[:, 1:32], Pg[:, 3:124:4], ALU.add)
            P.copy("dve", imp2[:, 0:1], imp[:, 0:1])
            P.tt("dve", imp, imp2, addmask[:, qb, :], ALU.add)
            P.op("dve", lambda e: e.max(out=m8.ap, in_=imp.ap), [imp], [m8])
            P.op("dve", lambda e: e.match_replace(out=imp2.ap, in_to_replace=m8.ap, in_values=imp.ap,
                                                   imm_value=-3e38), [m8, imp], [imp2])
            P.op("dve", lambda e: e.max(out=m8.ap, in_=imp2.ap), [imp2], [m8])
            P.ts("dve", selneg, imp, m8[:, 7:8], NEGB, ALU.is_lt, ALU.mult)
            if "imp" in k.dbg_d and g == 0:
                k.dbg_store("imp", imp, k.dbg_d["imp"][qb])
                k.dbg_store("Pg", Pg, k.dbg_d["Pg"][qb])

        def cmp_stage_b(qb):
            par = qb % 2
            P.transpose(psT[0:32, 0:128], selneg, ident)
            evac(negT[par], psT.v(psT.ap[0:32, 0:128].unsqueeze(1).to_broadcast([32, 4, 128])))
            for hh in range(4):
                P.transpose(psT[:, (1 + hh) * 128:(2 + hh) * 128], p16[:, hh, :], ident)
            evac(pT, psT.v(psT.ap[:, 128:640].rearrange("p (a b) -> p a b", b=128)))
            for hh in range(4):
                P.mm(psO[:, hh * 128:(hh + 1) * 128], pT[:, hh, :], vcmp[:, g, :], True, True)
            for hh in range(4):
                col = 3 * (4 * g + hh)
                P.act(oacc[par][:, hh, :], psO[:, hh * 128:(hh + 1) * 128], AF.Copy, scale=g3[:, qb, col:col + 1])

        def attn(qb, kT, kofs, vbuf, vofs, kbs, masks, gcol, final):
            par = qb % 2
            n = len(kbs)
            q3 = qT.v(qT.ap[:, qb, :, :].rearrange("p h t -> p (h t)"))
            pss = {}

            def stA(i):
                kb = kbs[i]
                ps = psS[i % 3]
                pss[i] = ps
                mk = masks.get(kb)
                P.mm(ps, kT[:, (kb - kofs) * 128:(kb - kofs + 1) * 128], q3, True, mk is None)
                if mk is not None:
                    P.mm(ps, mk[0], mk[1], False, True)
                P.act(PT[i % 2], ps, AF.Exp)

            def stB(i):
                kb = kbs[i]
                for hh in range(4):
                    acc = psV[hh // 2]
                    o = (hh % 2) * 130
                    P.mm(acc[:, o:o + 129], PT[i % 2][:, hh * 128:(hh + 1) * 128], vbuf[:, kb - vofs, 0:129],
                         i == 0 and hh % 2 == 0, i == n - 1, sgc=True)

            stA(0)
            for i in range(n):
                if i + 1 < n:
                    stA(i + 1)
                stB(i)
            for hh in range(4):
                acc = psV[hh // 2]
                o = (hh % 2) * 130
                P.copy("dve", coef[:, hh:hh + 1], acc[:, o + 128:o + 129])
            P.ts("dve", coef[:, 0:4], coef[:, 0:4], 1e-30, None, ALU.max)
            P.recip(coef[:, 4:8], coef[:, 0:4])
            base = 3 * 4 * g + gcol
            P.tt("dve", coef[:, 4:8], coef[:, 4:8], g3[:, qb, base:base + 10:3], ALU.mult)
            for hh in range(4):
                acc = psV[hh // 2]
                o = (hh % 2) * 130
                dst = o16[:, hh, :] if final else oacc[par][:, hh, :]
                P.stt("dve", dst, acc[:, o:o + 128], coef[:, 4 + hh:5 + hh], oacc[par][:, hh, :], ALU.mult, ALU.add)

        cmp_stage(0)
        cmp_stage_b(0)
        for qb in range(NB):
            if qb + 1 < NB:
                cmp_stage(qb + 1)
            ub = 8 + qb
            par = qb % 2
            negbc = negT[par].v(negT[par].ap.rearrange("p h t -> p (h t)"))
            masks = {kb: (Emat[:, kb * 128:(kb + 1) * 128], negbc) for kb in range(0, ub)}
            masks[ub] = (ident, causneg)
            attn(qb, ksT, 0, vs, 0, list(range(0, ub + 1)), masks, 1, False)
            if qb + 1 < NB:
                cmp_stage_b(qb + 1)
            masks = {ub - 4: (ident, winneg), ub: (ident, causneg)}
            attn(qb, kwT, 4, vw, 4, list(range(ub - 4, ub + 1)), masks, 2, True)
            for hh in range(4):
                P.transpose(psT[:, (5 + hh % 2) * 128:(6 + hh % 2) * 128], o16[:, hh, :], ident)
                if hh % 2 == 1:
                    evac(o_nsaT[:, 4 * g + hh - 1:4 * g + hh + 1, qb * 128:(qb + 1) * 128],
                         psT.v(psT.ap[:, 640:896].rearrange("p (a b) -> p a b", b=128)))
        if stop_after == "nsa_g0":
            break
    if "o_nsaT" in k.dbg_d:
        tmpf = A.alloc("dbgtmp", [128, TOK], F32)
        for hh in range(4):
            P.copy("dve", tmpf, o_nsaT[:, hh, :])
            k.dbg_store("o_nsaT", tmpf, k.dbg_d["o_nsaT"][:, hh, :])


def _rotary(k, ps128, cos, sin, out_even_odd, tmp):
    P = k.P
    x1 = ps128.v(ps128.ap.rearrange("p (i two) -> p i two", two=2)[:, :, 0])
    x2 = ps128.v(ps128.ap.rearrange("p (i two) -> p i two", two=2)[:, :, 1])
    o1 = out_even_odd.v(out_even_odd.ap.rearrange("p (i two) -> p i two", two=2)[:, :, 0])
    o2 = out_even_odd.v(out_even_odd.ap.rearrange("p (i two) -> p i two", two=2)[:, :, 1])
    P.tt("dve", tmp[:, 0, :], x1, cos, ALU.mult)
    P.tt("dve", tmp[:, 1, :], x2, sin, ALU.mult)
    P.tt("dve", tmp[:, 2, :], x1, sin, ALU.mult)
    P.tt("dve", tmp[:, 3, :], x2, cos, ALU.mult)
    P.tt(POOL, o1, tmp[:, 0, :], tmp[:, 1, :], ALU.subtract)
    P.tt(POOL, o2, tmp[:, 2, :], tmp[:, 3, :], ALU.add)


def build_ret_ctx(k):
    P, A, D, WS = k.P, k.R34, k.D, k.WS
    nT_ctx = k.nT_ctx
    state8 = k.state8
    ropek = A.alloc("ropek", [128, 16, 128], F32)
    wkc = A.alloc("wkc", [128, 8, 8], F32)
    P.dma("sp", ropek, D["ropek"][:, :, :])
    P.dma("sp", wkc, D["wkc"][:, :, :])
    Kr = [A.alloc("Kr%d" % i, [128, 128], F32) for i in range(2)]
    Ks = [A.alloc("Ks%d" % i, [128, 128], BF16) for i in range(2)]
    V = [A.alloc("V%d" % i, [128, 256], BF16) for i in range(2)]
    tmp = [A.alloc("rtmp%d" % i, [128, 4, 64], F32) for i in range(2)]
    tiles = [("wr_c%d" % h,) + wtile_cols(D["w_in"], OR_ + 512 * h, 512) for h in range(8)]
    for h in range(8):
        wt = WS.get(tiles, h)
        for cb in range(NB):
            par = cb % 2
            ps = k.next_ps()
            for kc in range(KCH):
                P.mm(ps[:, 0:384], nT_ctx[:, kc, cb * 128:(cb + 1) * 128], wt[:, kc, 128:512], kc == 0, kc == KCH - 1)
            _rotary(k, ps[:, 0:128], ropek[:, cb, 0:64], ropek[:, cb, 64:128], Kr[par], tmp[par])
            P.ts(POOL, Ks[par], Kr[par], wkc[:, cb, h:h + 1], None, ALU.mult)
            P.copy("act", V[par], ps[:, 128:384])
            P.mm(k.psV[h % 2][:, 0:256], Ks[par], V[par], cb == 0, cb == NB - 1)
        P.copy("act", state8[:, h, :], k.psV[h % 2][:, 0:256])


def build_ret_own(k):
    P, A, D, WS = k.P, k.R34, k.D, k.WS
    nT_own = k.nT_own
    state8 = k.state8
    psT = k.psT
    ident = k.ident
    o_retT = A.alloc("o_retT", [128, 16, TOK], BF16)
    k.o_retT = o_retT
    k.m_after_o = A.mark()
    ropek = A.alloc("ropek", [128, 16, 128], F32)
    ropeq = A.alloc("ropeq", [128, 8, 128], F32)
    decayT = A.alloc("decayT", [128, 8, 128], F32)
    wqB = A.alloc("wqB", [128, 8, 128], F32)
    wk = A.alloc("wk", [128, 8], F32)
    gnB = k.load_gain("ret_gn_w", A)
    P.dma("sp", ropek, D["ropek"][:, :, :])
    P.dma("sp", ropeq, D["ropeq"][:, :, :])
    P.dma("sp", decayT, D["decayT"][:, :, :])
    P.dma("sp", wqB, D["wqB"][:, :, :])
    P.dma("sp", wk, D["wk"][:, :])
    def mk_bufs(j):
        B = {}
        for nm, shp, dt in (("Qr", [128, 128], BF16), ("Kr", [128, 128], F32), ("Kb", [128, 128], BF16),
                            ("Ks", [128, 128], BF16), ("V", [128, 256], BF16), ("sg", [128, 256], F32),
                            ("tmp", [128, 4, 64], F32)):
            B[nm] = [A.alloc("%s%d_%d" % (nm, j, i), shp, dt) for i in range(2)]
        for nm, shp, dt in (("QT", [128, 128], BF16), ("KT", [128, 128], BF16), ("QsT", [128, 128], BF16),
                            ("SdT", [128, 128], BF16), ("stbf", [128, 256], BF16), ("osb", [128, 256], F32),
                            ("junk", [128, 256], BF16), ("y", [128, 256], F32), ("y16", [128, 256], BF16),
                            ("gs", [128, 8], F32)):
            B[nm] = A.alloc("%s%d" % (nm, j), shp, dt)
        return B

    bufs = [mk_bufs(0), mk_bufs(1)]
    tiles = []
    for p in range(4):
        tiles.append(("wr_o%d" % (2 * p),) + wtile_cols(D["w_in"], OR_ + 512 * (2 * p), 512))
        tiles.append(("wr_o%d" % (2 * p + 1),) + wtile_cols(D["w_in"], OR_ + 512 * (2 * p + 1), 512))
        tiles.append(("wgr%d" % p,) + wtile_cols(D["w_in"], OGR + 512 * p, 512))

    def run2(sa, sb):
        n = max(len(sa), len(sb))
        for i in range(n):
            if i < len(sa):
                sa[i]()
            if i < len(sb):
                sb[i]()

    for p in range(4):
        wts = [WS.get(tiles, 3 * p + j, depth=0) for j in range(2)]
        wg = WS.get(tiles, 3 * p + 2, depth=0)

        def stage1_steps(j, ob):
            B = bufs[j]
            h = 2 * p + j
            wt = wts[j]
            par = ob % 2
            ub = 8 + ob
            st = {}
            steps = []

            def s_proj():
                st["ps"] = k.next_ps()
                for kc in range(KCH):
                    P.mm(st["ps"], nT_own[:, kc, ob * 128:(ob + 1) * 128], wt[:, kc, 0:512], kc == 0, kc == KCH - 1)
            steps.append(s_proj)

            def s_projg():
                st["psg"] = k.next_ps()
                for kc in range(KCH):
                    P.mm(st["psg"][:, 0:256], nT_own[:, kc, ob * 128:(ob + 1) * 128],
                         wg[:, kc, j * 256:j * 256 + 256], kc == 0, kc == KCH - 1)
            steps.append(s_projg)
            steps.append(lambda: _rotary(k, st["ps"][:, 0:128], ropeq[:, ob, 0:64], ropeq[:, ob, 64:128],
                                         B["Qr"][par], B["tmp"][par]))
            steps.append(lambda: P.copy("act", B["V"][par], st["ps"][:, 256:512]))
            steps.append(lambda: _rotary(k, st["ps"][:, 128:256], ropek[:, ub, 0:64], ropek[:, ub, 64:128],
                                         B["Kr"][par], B["tmp"][par]))
            steps.append(lambda: P.act(B["sg"][par], st["psg"][:, 0:256], AF.Silu))
            steps.append(lambda: P.copy(POOL, B["Kb"][par], B["Kr"][par]))
            steps.append(lambda: P.ts(POOL, B["Ks"][par], B["Kr"][par], wk[:, h:h + 1], None, ALU.mult))
            return steps

        def stage2_steps(j, ob):
            B = bufs[j]
            h = 2 * p + j
            par = ob % 2
            c0 = j * 512
            st = {}
            gs = B["gs"]
            steps = []

            def s_tr():
                P.transpose(psT[:, c0:c0 + 128], B["Qr"][par], ident)
                P.transpose(psT[:, c0 + 128:c0 + 256], B["Kb"][par], ident)
            steps.append(s_tr)
            steps.append(lambda: P.copy("act", B["QT"], psT[:, c0:c0 + 128]))
            steps.append(lambda: P.copy("act", B["KT"], psT[:, c0 + 128:c0 + 256]))
            steps.append(lambda: P.tt("dve", B["QsT"], psT[:, c0:c0 + 128], wqB[:, h, :], ALU.mult))

            def s_S():
                st["ps"] = k.next_ps()
                P.mm(st["ps"][:, 0:128], B["KT"], B["QT"], True, True)
            steps.append(s_S)
            steps.append(lambda: P.tt("dve", B["SdT"], st["ps"][:, 0:128], decayT[:, h, :], ALU.mult))
            steps.append(lambda: P.copy("act", B["stbf"], state8[:, h, :]))

            def s_o():
                st["po"] = k.next_ps()
                P.mm(st["po"][:, 0:256], B["SdT"], B["V"][par], True, False)
                P.mm(st["po"][:, 0:256], B["QsT"], B["stbf"], False, True)
            steps.append(s_o)
            steps.append(lambda: P.memset("dve", gs, 0.0))
            steps.append(lambda: P.act(B["osb"], st["po"][:, 0:256], AF.Copy, accum_out=gs[:, 0:1]))
            steps.append(lambda: P.act(B["junk"], st["po"][:, 0:256], AF.Square, accum_out=gs[:, 1:2]))

            def s_state():
                st["ps3"] = k.next_ps()
                P.mm(st["ps3"][:, 0:256], B["Ks"][par], B["V"][par], True, True)
            steps.append(s_state)
            steps.append(lambda: P.stt("dve", state8[:, h, :], state8[:, h, :], _G_CHUNK[h], st["ps3"][:, 0:256],
                                       ALU.mult, ALU.add))
            steps.append(lambda: P.ts("dve", gs[:, 2:3], gs[:, 0:1], 1.0 / 256, None, ALU.mult))
            steps.append(lambda: P.tt("dve", gs[:, 3:4], gs[:, 2:3], gs[:, 2:3], ALU.mult))
            steps.append(lambda: P.stt("dve", gs[:, 4:5], gs[:, 1:2], 1.0 / 256, gs[:, 3:4], ALU.mult, ALU.subtract))
            steps.append(lambda: P.ts("dve", gs[:, 4:5], gs[:, 4:5], EPS, None, ALU.add))
            steps.append(lambda: P.act(gs[:, 5:6], gs[:, 4:5], AF.Sqrt))
            steps.append(lambda: P.recip(gs[:, 6:7], gs[:, 5:6]))
            steps.append(lambda: P.ts("dve", B["y"], B["osb"], gs[:, 2:3], gs[:, 6:7], ALU.subtract, ALU.mult))
            steps.append(lambda: P.tt(POOL, B["y"], B["y"], gnB[:, h * 256:(h + 1) * 256], ALU.mult))
            steps.append(lambda: P.tt(POOL, B["y16"], B["y"], B["sg"][par], ALU.mult))

            def s_tr2():
                for jj in range(2):
                    P.transpose(psT[:, c0 + (2 + jj) * 128:c0 + (3 + jj) * 128], B["y16"][:, jj * 128:(jj + 1) * 128], ident)
            steps.append(s_tr2)
            steps.append(lambda: k.evac(o_retT[:, 2 * h:2 * h + 2, ob * 128:(ob + 1) * 128],
                                        psT.v(psT.ap[:, c0 + 256:c0 + 512].rearrange("p (a b) -> p a b", b=128))))
            return steps

        run2(stage1_steps(0, 0), stage1_steps(1, 0))
        for ob in range(NB):
            if ob + 1 < NB:
                run2(stage1_steps(0, ob + 1), stage1_steps(1, ob + 1))
            run2(stage2_steps(0, ob), stage2_steps(1, ob))
    if "o_retT" in k.dbg_d:
        tmpf = A.alloc("dbgtmp", [128, 2, TOK], F32)
        P.copy("dve", tmpf, o_retT[:, 0:2, :])
        k.dbg_store("o_retT", tmpf)


def build_merge(k, which, srcT):
    P, A, D, WS = k.P, k.R34, k.D, k.WS
    nT_own, mergedT = k.nT_own, k.mergedT
    W = D["w_a"] if which == "a" else D["w_b"]
    og = OGA if which == "a" else OGB
    sig = [A.alloc("sig%d" % i, [128, 512], F32) for i in range(2)]
    tmp = [A.alloc("mtmp%d" % i, [128, 512], F32) for i in range(2)]
    tiles = []
    for i in range(4):
        tiles.append(("wm%s%d" % (which, i),) + wtile_cols(W, 512 * i, 512))
        tiles.append(("wgt%s%d" % (which, i),) + wtile_cols(D["w_in"], og + 512 * i, 512))
    n = 0
    for i in range(4):
        wm = WS.get(tiles, 2 * i, depth=1)
        wg = WS.get(tiles, 2 * i + 1, depth=1)
        for cc in range(4):
            for tch in range(2):
                psA = k.next_ps()
                for kc in range(KCH):
                    P.mm(psA, wm[:, kc, cc * 128:(cc + 1) * 128], srcT[:, kc, tch * 512:(tch + 1) * 512],
                         kc == 0, kc == KCH - 1)
                psG = k.next_ps()
                for kc in range(KCH):
                    P.mm(psG, wg[:, kc, cc * 128:(cc + 1) * 128], nT_own[:, kc, tch * 512:(tch + 1) * 512],
                         kc == 0, kc == KCH - 1)
                par = n % 2
                n += 1
                P.act(sig[par], psG, AF.Sigmoid)
                dst = mergedT[:, 4 * i + cc, tch * 512:(tch + 1) * 512]
                if which == "a":
                    P.tt("dve", dst, sig[par], psA, ALU.mult)
                else:
                    P.tt("dve", tmp[par], sig[par], psA, ALU.mult)
                    P.tt(POOL, dst, tmp[par], dst, ALU.add)
    if ("mergedT_" + which) in k.dbg_d:
        tmpf = A.alloc("dbgtmp", [128, 4, TOK], F32)
        P.copy("dve", tmpf, mergedT[:, 0:4, :])
        k.dbg_store("mergedT_" + which, tmpf)


def build_tail(k, stop_after):
    P, D, WS = k.P, k.D, k.WS
    R1, R2, R34 = k.R1, k.R2, k.R34
    psS, psV, psT, ident = k.psS, k.psV, k.psT, k.ident
    mergedT = k.mergedT
    hb = [R34.alloc("h%d" % tb, [128, DM], F32) for tb in range(NB)]
    for tb in range(NB):
        P.dma("sp", hb[tb], D["xo"][tb * 128:(tb + 1) * 128, :])
    tiles = [("wout%d" % i,) + wtile_cols(D["w_out"], 512 * i, 512) for i in range(4)]
    for i in range(4):
        wt = WS.get(tiles, i)
        for tb in range(NB):
            ps = k.next_ps()
            for kc in range(KCH):
                P.mm(ps, mergedT[:, kc, tb * 128:(tb + 1) * 128], wt[:, kc, :], kc == 0, kc == KCH - 1)
            hs = hb[tb][:, 512 * i:512 * (i + 1)]
            P.tt("dve", hs, hs, ps, ALU.add)
    if "h1" in k.dbg_d:
        for tb in range(NB):
            k.dbg_store("h1", hb[tb], k.dbg_d["h1"][tb * 128:(tb + 1) * 128, :])
    R2.release()
    if stop_after == "h1":
        return

    nxT = R1.alloc("nxT", [128, KCH, TOK], BF16)
    gainX = k.load_gain("x_norm_w", R2)
    xn = R2.alloc("xn", [128, DM], BF16)
    stat = R2.alloc("stat", [128, 16], F32)
    k.norm_T(lambda tb: hb[tb], NB, gainX, nxT, xn, stat)
    P.dma("sp", gainX, D["mem_norm_w"].to_broadcast([128, DM]))
    mh = R34.mark()
    mts = [R34.alloc("mt%d" % i, [128, DM], F32) for i in range(2)]
    mT = R2.alloc("mT", [128, KCH, 256], BF16)
    k.norm_T(k.mk_get(D["mem"], mts), 2, gainX, mT, xn, stat)
    R34.release(mh)
    qxT = R2.alloc("qxT", [128, 4, TOK], BF16)
    kxT = R34.alloc("kxT", [128, 4, 256], BF16)
    vx = R34.alloc("vx", [128, 2, 4, 130], BF16)
    PTx = [R34.alloc("PTx%d" % i, [128, 512], BF16) for i in range(2)]
    xc = R34.alloc("xcoef", [128, 4], F32)
    tiles = [("wqx",) + wtile_cols(D["wq_x"], 0, 512), ("wkx",) + wtile_cols(D["wk_x"], 0, 512),
             ("wvx",) + wtile_cols(D["wv_x"], 0, 512), ("wox",) + wtile_rows(D["wo_x"], 0, 512)]
    wq = WS.get(tiles, 0)
    for hh in range(4):
        for tch in range(2):
            k.proj_T(wq, hh * 128, nxT, tch * 512, 512, qxT[:, hh, tch * 512:(tch + 1) * 512], scale=QSCALE)
    wk_ = WS.get(tiles, 1)
    for hh in range(4):
        k.proj_T(wk_, hh * 128, mT, 0, 256, kxT[:, hh, :])
    wv = WS.get(tiles, 2)
    P.memset("dve", vx, 1.0)
    for mb in range(2):
        ps = k.next_ps()
        for kc in range(KCH):
            P.mm(ps, mT[:, kc, mb * 128:(mb + 1) * 128], wv[:, kc, :], kc == 0, kc == KCH - 1)
        k.evac(vx[:, mb, :, 0:128], ps.v(ps.ap.rearrange("p (a b) -> p a b", b=128)))
    R1.release()
    ox16 = R1.alloc("ox16", [128, NB, 512], BF16)
    oxT = R1.alloc("oxT", [128, 4, TOK], BF16)
    for hh in range(4):
        for tch in range(2):
            for mb in range(2):
                ps = psS[mb]
                P.mm(ps, kxT[:, hh, mb * 128:(mb + 1) * 128], qxT[:, hh, tch * 512:(tch + 1) * 512], True, True)
                P.act(PTx[mb], ps, AF.Exp)
            for tq in range(4):
                tb = tch * 4 + tq
                acc = psV[tq % 2]
                for mb in range(2):
                    P.mm(acc[:, 0:129], PTx[mb][:, tq * 128:(tq + 1) * 128], vx[:, mb, hh, 0:129], mb == 0, mb == 1)
                P.copy("dve", xc[:, 0:1], acc[:, 128:129])
                P.recip(xc[:, 1:2], xc[:, 0:1])
                P.ts("dve", ox16[:, tb, hh * 128:(hh + 1) * 128], acc[:, 0:128], xc[:, 1:2], None, ALU.mult)
    for tb in range(NB):
        for j in range(4):
            P.transpose(psT[:, j * 128:(j + 1) * 128], ox16[:, tb, j * 128:(j + 1) * 128], ident)
        k.evac(oxT[:, :, tb * 128:(tb + 1) * 128], psT.v(psT.ap[:, 0:512].rearrange("p (a b) -> p a b", b=128)))
    wo = WS.get(tiles, 3)
    for tb in range(NB):
        for cc in range(4):
            ps = k.next_ps()
            for kc in range(4):
                P.mm(ps, oxT[:, kc, tb * 128:(tb + 1) * 128], wo[:, kc, cc * 512:(cc + 1) * 512], kc == 0, kc == 3)
            hs = hb[tb][:, 512 * cc:512 * (cc + 1)]
            P.tt("dve", hs, hs, ps, ALU.add)
    if "h2" in k.dbg_d:
        for tb in range(NB):
            k.dbg_store("h2", hb[tb], k.dbg_d["h2"][tb * 128:(tb + 1) * 128, :])
    R1.release()
    R2.release()
    R34.release(mh)
    if stop_after == "h2":
        return

    nmT = R1.alloc("nmT", [128, KCH, TOK], BF16)
    gainM = k.load_gain("mlp_norm_w", R2)
    xn = R2.alloc("xn", [128, DM], BF16)
    stat = R2.alloc("stat", [128, 16], F32)
    k.norm_T(lambda tb: hb[tb], NB, gainM, nmT, xn, stat)
    aT = R2.alloc("aT", [128, 4, TOK], BF16)
    rl = [R2.alloc("rl%d" % i, [128, 512], F32) for i in range(2)]
    tiles = []
    for f in range(16):
        tiles.append(("wup%d" % f,) + wtile_cols(D["w_up"], 512 * f, 512))
        tiles.append(("wdn%d" % f,) + wtile_rows(D["w_down"], 512 * f, 512))
    n = 0
    for f in range(16):
        wu = WS.get(tiles, 2 * f)
        for cc in range(4):
            for tch in range(2):
                ps = k.next_ps()
                for kc in range(KCH):
                    P.mm(ps, wu[:, kc, cc * 128:(cc + 1) * 128], nmT[:, kc, tch * 512:(tch + 1) * 512],
                         kc == 0, kc == KCH - 1)
                par = n % 2
                n += 1
                P.act(rl[par], ps, AF.Relu)
                P.tt(POOL, aT[:, cc, tch * 512:(tch + 1) * 512], rl[par], rl[par], ALU.mult)
        wd = WS.get(tiles, 2 * f + 1)
        for tb in range(NB):
            for cc in range(4):
                ps = k.next_ps()
                for kc in range(4):
                    P.mm(ps, aT[:, kc, tb * 128:(tb + 1) * 128], wd[:, kc, cc * 512:(cc + 1) * 512], kc == 0, kc == 3)
                hs = hb[tb][:, 512 * cc:512 * (cc + 1)]
                P.tt("dve", hs, hs, ps, ALU.add)
    if "h3" in k.dbg_d:
        for tb in range(NB):
            k.dbg_store("h3", hb[tb], k.dbg_d["h3"][tb * 128:(tb + 1) * 128, :])
    R1.release()
    R2.release()

    gainF = k.load_gain("final_norm_w", R2)
    outt = [R2.alloc("outt%d" % i, [128, DM], F32) for i in range(2)]
    junk = R2.alloc("junkf", [128, DM], BF16)
    stat = R2.alloc("statf", [128, 16], F32)
    P.memset("dve", stat, 0.0)
    for tb in range(NB):
        P.act(junk, hb[tb], AF.Square, accum_out=stat[:, tb:tb + 1])
        P.ts("dve", stat[:, tb:tb + 1], stat[:, tb:tb + 1], 1.0 / DM, EPS, ALU.mult, ALU.add)
        P.act(stat[:, tb:tb + 1], stat[:, tb:tb + 1], AF.Sqrt)
        P.recip(stat[:, tb:tb + 1], stat[:, tb:tb + 1])
        P.stt("dve", outt[tb % 2], hb[tb], stat[:, tb:tb + 1], gainF, ALU.mult, ALU.mult)
        P.dma("sp", k.out_d[tb * 128:(tb + 1) * 128, :], outt[tb % 2], semkey="dma:store%d" % (tb % 2))


def _w_in_perm():
    q = np.arange(0, 2048)
    parts = [q]
    for g in range(4):
        for base in (3072, 3584, 4096, 4608):
            parts.append(base + 128 * g + np.arange(128))
    parts.append(2048 + np.arange(512))
    parts.append(2560 + np.arange(512))
    for h in range(8):
        parts.append(5168 + 128 * h + np.arange(128))
        parts.append(6192 + 128 * h + np.arange(128))
        parts.append(7216 + 256 * h + np.arange(256))
    parts.append(np.arange(9264, 15408))
    parts.append(5120 + np.arange(48))
    perm = np.concatenate(parts)
    assert perm.shape[0] == IN_WIDTH and len(set(perm.tolist())) == IN_WIDTH
    return perm


def _tables(s):
    f32 = np.float32
    T = {}
    T["ident"] = np.eye(128, dtype=f32)
    u = np.arange(2048)
    t = u - 1024 + 1024 * s
    inv = (10000.0 ** (-np.arange(0, 128, 2, dtype=f32) / f32(128))).astype(f32)
    ang = t.astype(f32)[:, None] * inv[None, :]
    cos, sin = np.cos(ang).astype(f32), np.sin(ang).astype(f32)
    rk = np.concatenate([cos, sin], axis=1) * f32(128 ** -0.5)
    T["ropek"] = np.ascontiguousarray(rk.reshape(16, 128, 128).transpose(1, 0, 2)).astype(f32)
    rq = np.concatenate([cos, sin], axis=1)[1024:]
    T["ropeq"] = np.ascontiguousarray(rq.reshape(8, 128, 128).transpose(1, 0, 2)).astype(f32)
    H = 8
    log_g = np.log1p(-np.exp2(-5.0 - np.arange(H, dtype=f32))).astype(f32)
    i = np.arange(128, dtype=f32)
    rel = i[:, None] - i[None, :]
    decay = np.where(rel >= 0, np.exp(log_g[:, None, None] * np.maximum(rel, 0.0)), 0.0).astype(f32)
    T["decayT"] = np.ascontiguousarray(decay.transpose(2, 0, 1))
    w_q = np.exp(log_g[:, None] * (i + 1.0)[None, :]).astype(f32)
    T["wqB"] = np.ascontiguousarray(np.broadcast_to(w_q[None], (128, H, 128))).astype(f32)
    w_k = np.exp(log_g[:, None] * (127.0 - i)[None, :]).astype(f32)
    T["wk"] = np.ascontiguousarray(w_k.T)
    T["g_chunk"] = np.exp(log_g * f32(128.0)).astype(f32)
    gpow = np.stack([T["g_chunk"] ** f32(7 - blk) for blk in range(8)], 0).astype(f32)
    T["wkc"] = np.ascontiguousarray((w_k.T[:, None, :] * gpow[None, :, :]).astype(f32))
    uq = 1024 + np.arange(1024)
    lc = np.arange(128)
    vis = (16 * lc[None, :] + 31 <= uq[:, None]) & (lc[None, :] < 127)
    if s == 0:
        vis &= (lc[None, :] >= 64)
    T["cmask"] = np.ascontiguousarray(vis.astype(f32).reshape(8, 128, 128).transpose(1, 0, 2))
    lj = np.arange(32)
    lcur = (uq // 64)[:, None]
    first = 0 if s == 1 else 16
    am = np.zeros((1024, 32), f32)
    am[np.broadcast_to(lj[None, :] == first, am.shape)] = 1e9
    prev = (lj[None, :] == lcur - 1) & (lj[None, :] >= first)
    am[prev] = 2e9
    am[np.broadcast_to(lj[None, :], am.shape) == lcur] = 3e9
    am[(lj[None, :] > lcur) | (lj[None, :] < first)] = -1e9
    T["addmask"] = np.ascontiguousarray(am.reshape(8, 128, 32).transpose(1, 0, 2))
    kval = (t >= 0).astype(f32)
    T["kvalid"] = np.ascontiguousarray(kval.reshape(16, 128).T)
    kk = np.arange(2048)
    T["Emat"] = (kk[None, :] // 64 == np.arange(32)[:, None]).astype(f32)
    kq = np.arange(128)
    T["causneg"] = np.tile(np.where(kq[:, None] > kq[None, :], NEGB, 0.0).astype(f32), (1, 4))
    T["winneg"] = np.tile(np.where(kq[:, None] <= kq[None, :], NEGB, 0.0).astype(f32), (1, 4))
    return T


def make_in_maps(inputs):
    f32 = np.float32
    x = np.asarray(inputs["x"], f32)
    mem = np.asarray(inputs["mem"], f32)
    perm = _w_in_perm()
    shared = {}
    shared["w_in"] = np.ascontiguousarray(np.asarray(inputs["w_in"], f32)[0][:, perm])
    for nm in ("cmp_w1_k", "cmp_w1_v", "cmp_w2_k", "cmp_w2_v", "w_a", "w_b", "w_out", "wq_x", "wk_x", "wv_x",
               "wo_x", "w_up", "w_down"):
        shared[nm] = np.ascontiguousarray(np.asarray(inputs[nm], f32)[0])
    shared["peT_k"] = np.ascontiguousarray(np.asarray(inputs["cmp_pe_k"], f32)[0].T)
    shared["peT_v"] = np.ascontiguousarray(np.asarray(inputs["cmp_pe_v"], f32)[0].T)
    for nm in ("attn_norm_w", "ret_gn_w", "x_norm_w", "mem_norm_w", "mlp_norm_w"):
        shared[nm] = np.ascontiguousarray(np.asarray(inputs[nm], f32).reshape(1, DM))
    shared["final_norm_w"] = np.ascontiguousarray(np.asarray(inputs["final_norm_w"], f32).reshape(1, DM))
    tabs = [_tables(0), _tables(1)]
    zeros = np.zeros((TOK, DM), f32)
    in_maps = []
    for c in range(8):
        b, s = c // 2, c % 2
        m = dict(shared)
        m["xo"] = np.ascontiguousarray(x[b, 1024 * s:1024 * (s + 1)])
        m["xc"] = np.ascontiguousarray(x[b, 0:1024]) if s == 1 else zeros
        m["mem"] = np.ascontiguousarray(mem[b])
        for kk, v in tabs[s].items():
            if kk != "g_chunk":
                m[kk] = v
        in_maps.append(m)
    return in_maps


_G_CHUNK = [float(v) for v in np.exp(np.log1p(-np.exp2(-5.0 - np.arange(8, dtype=np.float32))).astype(np.float32)
                                     * np.float32(128.0)).astype(np.float32)]


def kernel(**inputs):
    in_maps = make_in_maps(inputs)
    nc, st = build_program()
    res = run_bass_kernel_spmd(nc, in_maps, core_ids=list(range(8)))
    out = np.zeros((4, 2048, DM), np.float32)
    for c in range(8):
        b, s = c // 2, c % 2
        out[b, 1024 * s:1024 * (s + 1)] = res.results[c]["out"]
    return out
```

```python
import numpy as np
from concourse.bass_utils import run_bass_kernel_spmd
import concourse.bass as bass
import concourse.mybir as mybir

F32 = mybir.dt.float32
BF16 = mybir.dt.bfloat16
AF = mybir.ActivationFunctionType
ALU = mybir.AluOpType
AX = mybir.AxisListType

_DT_SIZE = {F32: 4, BF16: 2}


class Buf:
    def __init__(self, key, ap):
        self.key = key
        self.ap = ap

    def __getitem__(self, idx):
        return Buf(self.key, self.ap[idx])

    def v(self, ap):
        return Buf(self.key, ap)


class Op:
    __slots__ = ("eng", "fn", "reads", "writes", "is_dma", "semkey", "deps", "dma_deps",
                 "signal", "signum", "pos", "accum", "idx")


class Prog:
    ENGS = ("pe", "act", "dve", "pool", "sp")

    def __init__(self, nc):
        self.nc = nc
        self.ops = []
        self.eng_obj = {"pe": nc.tensor, "act": nc.scalar, "dve": nc.vector, "pool": nc.gpsimd, "sp": nc.sync}
        self.sync_same_engine_war = False

    def _add(self, eng, fn, reads, writes, is_dma=False, semkey=None, accum=False):
        o = Op()
        o.eng = eng
        o.fn = fn
        o.reads = [b.key for b in reads if b is not None]
        o.writes = [b.key for b in writes if b is not None]
        for kk in o.reads:
            if kk.startswith("ps") and kk not in o.writes:
                o.writes.append(kk)
        o.is_dma = is_dma
        o.semkey = semkey
        o.accum = accum
        o.idx = len(self.ops)
        self.ops.append(o)
        return o

    def op(self, eng, fn, reads=(), writes=()):
        return self._add(eng, fn, reads, writes)

    def barrier(self):
        o = Op()
        o.eng = None
        o.idx = len(self.ops)
        self.ops.append(o)

    def dma(self, eng, out, in_, semkey=None):
        reads, writes = [], []
        if isinstance(in_, Buf):
            reads.append(in_)
            in_ap = in_.ap
        else:
            in_ap = in_
        if isinstance(out, Buf):
            writes.append(out)
            out_ap = out.ap
            if semkey is None:
                semkey = "dma:" + out.key
        else:
            out_ap = out
            if semkey is None:
                semkey = "dma:store"

        def fn(e):
            return e.dma_start(out=out_ap, in_=in_ap)

        return self._add(eng, fn, reads, writes, is_dma=True, semkey=semkey)

    def mm(self, out, lhsT, rhs, start, stop, extra_reads=(), sgc=False):
        def fn(e):
            if sgc:
                return e.matmul(out.ap, lhsT.ap, rhs.ap, start=start, stop=stop, skip_group_check=True)
            return e.matmul(out.ap, lhsT.ap, rhs.ap, start=start, stop=stop)
        return self._add("pe", fn, [lhsT, rhs] + list(extra_reads), [out], accum=not start)

    def transpose(self, out, in_, ident):
        def fn(e):
            return e.transpose(out.ap, in_.ap, ident.ap)
        return self._add("pe", fn, [in_, ident], [out])

    def act(self, out, in_, func, bias=None, scale=1.0, accum_out=None, eng="act"):
        reads = [in_]
        kw = {}
        if isinstance(bias, Buf):
            reads.append(bias)
            kw["bias"] = bias.ap
        elif bias is not None:
            kw["bias"] = bias
        if isinstance(scale, Buf):
            reads.append(scale)
            kw["scale"] = scale.ap
        else:
            kw["scale"] = scale
        writes = [out]
        if accum_out is not None:
            writes.append(accum_out)
            kw["accum_out"] = accum_out.ap

        def fn(e):
            return e.activation(out=out.ap, in_=in_.ap, func=func, **kw)
        return self._add(eng, fn, reads, writes)

    def tt(self, eng, out, in0, in1, op):
        def fn(e):
            return e.tensor_tensor(out=out.ap, in0=in0.ap, in1=in1.ap, op=op)
        return self._add(eng, fn, [in0, in1], [out])

    def ts(self, eng, out, in0, s1, s2, op0, op1=None, accum_out=None):
        reads = [in0]
        a1 = s1
        a2 = s2
        if isinstance(s1, Buf):
            reads.append(s1)
            a1 = s1.ap
        if isinstance(s2, Buf):
            reads.append(s2)
            a2 = s2.ap
        writes = [out]
        kw = {}
        if op1 is not None:
            kw["op1"] = op1
        if accum_out is not None:
            writes.append(accum_out)
            kw["accum_out"] = accum_out.ap

        def fn(e):
            return e.tensor_scalar(out=out.ap, in0=in0.ap, scalar1=a1, scalar2=a2, op0=op0, **kw)
        return self._add(eng, fn, reads, writes)

    def stt(self, eng, out, in0, scalar, in1, op0, op1):
        reads = [in0, in1]
        a = scalar
        if isinstance(scalar, Buf):
            reads.append(scalar)
            a = scalar.ap

        def fn(e):
            return e.scalar_tensor_tensor(out=out.ap, in0=in0.ap, scalar=a, in1=in1.ap, op0=op0, op1=op1)
        return self._add(eng, fn, reads, [out])

    def copy(self, eng, out, in_):
        if eng == "act":
            def fn(e):
                return e.copy(out=out.ap, in_=in_.ap)
        else:
            def fn(e):
                return e.tensor_copy(out=out.ap, in_=in_.ap)
        return self._add(eng, fn, [in_], [out])

    def reduce(self, eng, out, in_, op, axis=AX.X):
        def fn(e):
            return e.tensor_reduce(out=out.ap, in_=in_.ap, axis=axis, op=op)
        return self._add(eng, fn, [in_], [out])

    def memset(self, eng, out, val):
        def fn(e):
            return e.memset(out.ap, val)
        return self._add(eng, fn, [], [out])

    def recip(self, out, in_):
        def fn(e):
            return e.reciprocal(out=out.ap, in_=in_.ap)
        return self._add("dve", fn, [in_], [out])

    def emit(self, final_wait_eng="sp"):
        nc = self.nc
        ops = self.ops
        last_writer = {}
        readers = {}
        pos_ctr = {e: 0 for e in self.ENGS}
        waited = {f: {e: -1 for e in self.ENGS} for f in self.ENGS}
        waited_dma = {f: {} for f in self.ENGS}
        dma_count = {}
        last_op_on = {e: None for e in self.ENGS}
        pending_barrier = {e: [] for e in self.ENGS}
        outstanding_dma = []

        for o in ops:
            if o.eng is None:
                for f in self.ENGS:
                    pending_barrier[f] = [last_op_on[e] for e in self.ENGS if e != f and last_op_on[e] is not None]
                continue
            f = o.eng
            deps = set()
            for k in o.reads:
                w = last_writer.get(k)
                if w is not None:
                    deps.add(w)
            for k in o.writes:
                w = last_writer.get(k)
                if w is not None:
                    deps.add(w)
                for r in readers.get(k, ()):
                    deps.add(r)
            for b in pending_barrier[f]:
                deps.add(b)
            pending_barrier[f] = []
            o.pos = pos_ctr[f]
            pos_ctr[f] += 1
            o.deps = []
            o.dma_deps = []
            o.signal = False
            for di in sorted(deps):
                d = ops[di]
                if d.idx == o.idx:
                    continue
                if d.is_dma:
                    cnt = d.signum
                    if waited_dma[f].get(d.semkey, 0) >= cnt:
                        continue
                    waited_dma[f][d.semkey] = cnt
                    o.dma_deps.append((d.semkey, cnt))
                else:
                    if d.eng == "pe" and f == "pe" and not o.is_dma:
                        continue
                    if d.eng == f and not self.sync_same_engine_war and not o.is_dma:
                        is_raw_waw = any(last_writer.get(k) == di for k in o.reads + o.writes)
                        if not is_raw_waw:
                            continue
                    if waited[f][d.eng] >= d.pos:
                        continue
                    waited[f][d.eng] = d.pos
                    d.signal = True
                    o.deps.append(di)
            if o.is_dma:
                dma_count[o.semkey] = dma_count.get(o.semkey, 0) + 16
                o.signum = dma_count[o.semkey]
                outstanding_dma.append(o)
            for k in o.reads:
                readers.setdefault(k, []).append(o.idx)
            for k in o.writes:
                last_writer[k] = o.idx
                readers[k] = []
            last_op_on[f] = o.idx

        tail_deps = []
        for e in self.ENGS:
            li = last_op_on[e]
            if li is not None and not ops[li].is_dma:
                ops[li].signal = True
                tail_deps.append(li)
        sig_ctr = {e: 0 for e in self.ENGS}
        for o in ops:
            if o.eng is None or o.is_dma:
                continue
            if o.signal:
                sig_ctr[o.eng] += 1
                o.signum = sig_ctr[o.eng]
        import contextlib
        es = contextlib.ExitStack()
        self._es = es
        sems = {e: es.enter_context(nc.semaphore("s_" + e)) for e in self.ENGS}
        dsems = {}
        for k in dma_count:
            dsems[k] = es.enter_context(nc.semaphore("d%d" % len(dsems)))
        n_wait = 0
        for o in ops:
            if o.eng is None:
                continue
            e = self.eng_obj[o.eng]
            for di in o.deps:
                d = ops[di]
                e.wait_ge(sems[d.eng], d.signum)
                n_wait += 1
            for (k, cnt) in o.dma_deps:
                e.wait_ge(dsems[k], cnt)
                n_wait += 1
            ins = o.fn(e)
            if o.is_dma:
                ins.then_inc(dsems[o.semkey], 16)
            elif o.signal:
                ins.then_inc(sems[o.eng], 1)
        fe = self.eng_obj[final_wait_eng]
        for di in tail_deps:
            d = ops[di]
            fe.wait_ge(sems[d.eng], d.signum)
        for k, cnt in dma_count.items():
            fe.wait_ge(dsems[k], cnt)
        self.stats = dict(n_ops=len(ops), n_wait=n_wait, sig=dict(sig_ctr), n_dsems=len(dsems))
        return self.stats


TOK = 1024
NB = 8
DM = 2048
KCH = 16
EPS = 1e-6
OQ, OKV, OC, OR_, OGR, OGA, OGB, OG = 0, 2048, 4096, 5120, 9216, 11264, 13312, 15360
IN_WIDTH = 15408
NEGB = -30000.0
QSCALE = 128 ** -0.5
POOL = "dve"


def _prod(xs):
    r = 1
    for x in xs:
        r *= int(x)
    return r


class Arena:
    uid = 0

    def __init__(self, full_ap, P, base, nel, name):
        self.ap = full_ap
        self.base = base
        self.top = 0
        self.nel = nel
        self.P = P
        self.peak = 0
        self.name = name

    def alloc(self, name, shape, dt):
        inner = _prod(shape[1:])
        n = inner * (2 if dt == F32 else 1)
        npad = (n + 15) // 16 * 16
        off = self.base + self.top
        self.top += npad
        self.peak = max(self.peak, self.top)
        assert self.top <= self.nel, ("SBUF arena overflow", self.name, name, self.top, self.nel)
        ap = self.ap[:shape[0], off:off + n]
        if dt == F32:
            ap = ap.bitcast(F32)
        if len(shape) == 3:
            ap = ap.rearrange("p (a b) -> p a b", b=shape[2])
        elif len(shape) == 4:
            ap = ap.rearrange("p (a b c) -> p a b c", b=shape[2], c=shape[3])
        Arena.uid += 1
        return Buf("%s#%d" % (name, Arena.uid), ap)

    def mark(self):
        return self.top

    def release(self, mark=0):
        self.top = mark
        self.P.barrier()


class WStream:
    def __init__(self, P, arena, nslots, slot_elems):
        self.P = P
        self.nslots = nslots
        self.slot_elems = slot_elems
        self.slots = [arena.alloc("wslot%d" % i, [128, slot_elems], BF16) for i in range(nslots)]
        self.ctr = 0
        self.loaded = {}

    def _load(self, item):
        key, src, shape = item
        if key in self.loaded:
            return
        s = self.slots[self.ctr % self.nslots]
        self.ctr += 1
        n = _prod(shape[1:])
        ap = s.ap[:, 0:n]
        if len(shape) == 3:
            ap = ap.rearrange("p (a b) -> p a b", b=shape[2])
        b = Buf(s.key, ap)
        self.P.dma("pool", b, src)
        self.loaded[key] = b

    def get(self, lst, i, depth=None):
        depth = self.nslots - 1 if depth is None else depth
        for j in range(i, min(len(lst), i + depth + 1)):
            self._load(lst[j])
        b = self.loaded.pop(lst[i][0])
        return b


def wtile_cols(w2d, c0, ncols):
    return w2d[:, c0:c0 + ncols].rearrange("(kc p) c -> p kc c", p=128), [128, 16, ncols]


def wtile_rows(w2d, r0, nrows):
    return w2d[r0:r0 + nrows, :].rearrange("(kc p) c -> p kc c", p=128), [128, nrows // 128, 2048]


class K:
    pass


def build_program(dbg=None, stop_after=None):
    nc = bass.Bass("TRN2", target_bir_lowering=False)
    P = Prog(nc)
    k = K()
    k.nc, k.P = nc, P

    def din(name, shape):
        return nc.dram_tensor(name, list(shape), F32, kind="ExternalInput").ap()

    D = {}
    D["xo"] = din("xo", [TOK, DM])
    D["xc"] = din("xc", [TOK, DM])
    D["mem"] = din("mem", [256, DM])
    D["w_in"] = din("w_in", [DM, IN_WIDTH])
    for nm in ("cmp_w1_k", "cmp_w1_v"):
        D[nm] = din(nm, [4096, 1024])
    for nm in ("cmp_w2_k", "cmp_w2_v"):
        D[nm] = din(nm, [1024, 128])
    for nm in ("peT_k", "peT_v"):
        D[nm] = din(nm, [128, 32])
    for nm in ("w_a", "w_b", "w_out"):
        D[nm] = din(nm, [DM, DM])
    for nm in ("wq_x", "wk_x", "wv_x"):
        D[nm] = din(nm, [DM, 512])
    D["wo_x"] = din("wo_x", [512, DM])
    D["w_up"] = din("w_up", [DM, 8192])
    D["w_down"] = din("w_down", [8192, DM])
    for nm in ("attn_norm_w", "ret_gn_w", "x_norm_w", "mem_norm_w", "mlp_norm_w", "final_norm_w"):
        D[nm] = din(nm, [1, DM])
    D["ident"] = din("ident", [128, 128])
    D["ropeq"] = din("ropeq", [128, 8, 128])
    D["ropek"] = din("ropek", [128, 16, 128])
    D["decayT"] = din("decayT", [128, 8, 128])
    D["wqB"] = din("wqB", [128, 8, 128])
    D["wk"] = din("wk", [128, 8])
    D["wkc"] = din("wkc", [128, 8, 8])
    D["cmask"] = din("cmask", [128, 8, 128])
    D["addmask"] = din("addmask", [128, 8, 32])
    D["kvalid"] = din("kvalid", [128, 16])
    D["Emat"] = din("Emat", [128, 2048])
    D["causneg"] = din("causneg", [128, 512])
    D["winneg"] = din("winneg", [128, 512])
    out_d = nc.dram_tensor("out", [TOK, DM], F32, kind="ExternalOutput").ap()
    dbg_d = {}
    if dbg:
        for nm, shp in dbg.items():
            dbg_d[nm] = nc.dram_tensor("dbg_" + nm, list(shp), F32, kind="ExternalOutput").ap()
    k.D, k.out_d, k.dbg_d = D, out_d, dbg_d

    NEL = 106000
    full = nc.alloc_sbuf_tensor("arena", [128, NEL], BF16).ap()
    R0 = Arena(full, P, 0, 30000, "R0")
    R1 = Arena(full, P, 30000, 16384, "R1")
    R2 = Arena(full, P, 46384, 16384, "R2")
    R34 = Arena(full, P, 62768, NEL - 62768, "R34")
    k.R0, k.R1, k.R2, k.R34 = R0, R1, R2, R34
    A = R0
    k.psS = [Buf("psS%d" % i, nc.alloc_psum_tensor("psS%d" % i, [128, 512], F32).ap()) for i in range(3)]
    k.psC = Buf("psC", nc.alloc_psum_tensor("psC", [128, 512], F32).ap())
    k.psO = Buf("psO", nc.alloc_psum_tensor("psO", [128, 512], F32).ap())
    k.psV = [Buf("psV%d" % i, nc.alloc_psum_tensor("psV%d" % i, [128, 512], F32).ap()) for i in range(2)]
    k.psT = Buf("psT", nc.alloc_psum_tensor("psT", [128, 1024], BF16).ap())
    k.rot5 = [k.psS[0], k.psS[1], k.psS[2], k.psC, k.psO]
    k.rot_i = 0
    k.ev_i = 0

    k.ident = A.alloc("ident", [128, 128], BF16)
    P.dma("pool", k.ident, D["ident"][:, :])
    k.WS = WStream(P, A, 3, 16 * 512)

    def finish():
        st = P.emit()
        st["arena_peak_el"] = [R0.peak, R1.peak, R2.peak, R34.peak]
        return nc, st

    def dbg_store(name, buf, dram_view=None):
        if name in dbg_d:
            dv = dbg_d[name] if dram_view is None else dram_view
            P.dma("sp", dv, buf, semkey="dma:dbg")

    k.dbg_store = dbg_store

    def next_ps():
        b = k.rot5[k.rot_i % 5]
        k.rot_i += 1
        return b

    def evac(out, in_, scale=None):
        e = k.ev_i % 2
        k.ev_i += 1
        if e == 0:
            if scale is None:
                P.copy("act", out, in_)
            else:
                P.act(out, in_, AF.Copy, scale=scale)
        else:
            if scale is None:
                P.copy("dve", out, in_)
            else:
                P.ts("dve", out, in_, scale, None, ALU.mult)

    k.next_ps, k.evac = next_ps, evac

    def load_gain(name, reg):
        g = reg.alloc("gain_" + name, [128, DM], F32)
        P.dma("sp", g, D[name].to_broadcast([128, DM]))
        return g

    def norm_T(get_block, nblk, gain, nT, xn, stat):
        P.memset("dve", stat, 0.0)
        for tb in range(nblk):
            xt = get_block(tb)
            P.act(xn, xt, AF.Square, accum_out=stat[:, tb:tb + 1])
            P.ts("dve", stat[:, tb:tb + 1], stat[:, tb:tb + 1], 1.0 / DM, EPS, ALU.mult, ALU.add)
            P.act(stat[:, tb:tb + 1], stat[:, tb:tb + 1], AF.Sqrt)
            P.recip(stat[:, tb:tb + 1], stat[:, tb:tb + 1])
            P.stt("dve", xn, xt, stat[:, tb:tb + 1], gain, ALU.mult, ALU.mult)
            for half in range(2):
                for j in range(8):
                    kc = half * 8 + j
                    P.transpose(k.psT[:, j * 128:(j + 1) * 128], xn[:, kc * 128:(kc + 1) * 128], k.ident)
                src = k.psT.v(k.psT.ap.rearrange("p (a b) -> p a b", b=128))
                evac(nT[:, half * 8:half * 8 + 8, tb * 128:(tb + 1) * 128], src)

    def proj_T(wt, c0, nT, t0, ntok, out, scale=None, nk=KCH, out_view3=False):
        ps = next_ps()
        for kc in range(nk):
            P.mm(ps[:, 0:ntok], wt[:, kc, c0:c0 + 128], nT[:, kc, t0:t0 + ntok], kc == 0, kc == nk - 1)
        src = ps[:, 0:ntok]
        if out_view3:
            src = ps.v(ps.ap[:, 0:ntok].rearrange("p (a b) -> p a b", b=128))
        evac(out, src, scale)

    k.norm_T, k.proj_T, k.load_gain = norm_T, proj_T, load_gain

    nT_own = R1.alloc("nT_own", [128, KCH, TOK], BF16)
    nT_ctx = R2.alloc("nT_ctx", [128, KCH, TOK], BF16)
    k.state8 = R0.alloc("state8", [128, 8, 256], F32)
    k.kcmpT = R0.alloc("kcmpT", [128, 4, 128], BF16)
    k.vcmp = R0.alloc("vcmp", [128, 4, 128], BF16)
    gainA = load_gain("attn_norm_w", R34)
    xts = [R34.alloc("xt%d" % i, [128, DM], F32) for i in range(2)]
    xn = R34.alloc("xn", [128, DM], BF16)
    stat = R34.alloc("stat", [128, 16], F32)

    def mk_get(src, xts):
        def get(tb):
            xt = xts[tb % 2]
            P.dma("sp", xt, src[tb * 128:(tb + 1) * 128, :])
            return xt
        return get

    k.mk_get = mk_get
    norm_T(mk_get(D["xc"], xts), NB, gainA, nT_ctx, xn, stat)
    norm_T(mk_get(D["xo"], xts), NB, gainA, nT_own, xn, stat)
    if "nT_own" in dbg_d:
        tmpf = R34.alloc("dbgtmp", [128, KCH, TOK // 4], F32)
        P.copy("dve", tmpf, nT_own[:, :, 0:TOK // 4])
        dbg_store("nT_own", tmpf)
    R34.release()
    if stop_after == "norm":
        return finish()
    k.nT_own, k.nT_ctx = nT_own, nT_ctx
    k.finish = finish

    build_ret_ctx(k)
    R34.release()
    if stop_after == "ret_ctx":
        return finish()
    build_nsa(k, stop_after)
    if stop_after and stop_after.startswith("nsa"):
        return finish()
    R34.release(k.m_after_o)
    R2.release()
    k.mergedT = R2.alloc("mergedT", [128, KCH, TOK], BF16)
    build_merge(k, "a", k.o_nsaT)
    R34.release()
    if stop_after == "merge_a":
        return finish()
    build_ret_own(k)
    if stop_after == "ret":
        return finish()
    R34.release(k.m_after_o)
    build_merge(k, "b", k.o_retT)
    R34.release()
    R1.release()
    if stop_after == "merge_b":
        return finish()
    build_tail(k, stop_after)
    return finish()


def build_nsa(k, stop_after):
    P, A, D, WS = k.P, k.R34, k.D, k.WS
    nT_own, nT_ctx = k.nT_own, k.nT_ctx
    psS, psC, psO, psV, psT = k.psS, k.psC, k.psO, k.psV, k.psT
    evac, next_ps, proj_T = k.evac, k.next_ps, k.proj_T
    ident = k.ident
    w_in = D["w_in"]
    uchunks = [(nT_ctx, 0, 0), (nT_ctx, 512, 512), (nT_own, 0, 1024), (nT_own, 512, 1536)]

    o_nsaT = A.alloc("o_nsaT", [128, 16, TOK], BF16)
    k.o_nsaT = o_nsaT
    k.m_after_o = A.mark()
    kcmpT, vcmp = k.kcmpT, k.vcmp
    P.memset("dve", kcmpT, 0.0)
    P.memset("dve", vcmp, 0.0)
    m1 = A.mark()
    kvT = A.alloc("kvcT", [128, 4, 2048], BF16)
    hidT = A.alloc("hidT", [128, 8, 4, 128], BF16)
    peT = A.alloc("peT", [128, 32], BF16)
    w2 = A.alloc("w2", [128, 8, 128], BF16)
    cbias = A.alloc("cbias", [128, 8], F32)
    for kind in range(2):
        sfx = "_k" if kind == 0 else "_v"
        tiles = [("wc%d" % kind,) + wtile_cols(w_in, OC + 512 * kind, 512)]
        for hh in range(2):
            for lh in range(2):
                src = D["cmp_w1" + sfx][2048 * lh:2048 * (lh + 1), 512 * hh:512 * (hh + 1)].rearrange(
                    "(l p) c -> p l c", p=128)
                tiles.append(("w1%d_%d_%d" % (kind, hh, lh), src, [128, 16, 512]))
        P.dma("pool", peT, D["peT" + sfx][:, :])
        P.dma("pool", w2, D["cmp_w2" + sfx].rearrange("(hc p) c -> p hc c", p=128))
        wt = WS.get(tiles, 0)
        for g in range(4):
            for (nT, t0, u0) in uchunks:
                proj_T(wt, g * 128, nT, t0, 512, kvT[:, g, u0:u0 + 512])
        ti = 1
        for hh in range(2):
            accs = [psS[0], psS[1], psS[2], psC]
            for lh in range(2):
                wt = WS.get(tiles, ti)
                ti += 1
                for li in range(16):
                    l = lh * 16 + li
                    for hc in range(4):
                        lhsT = wt[:, li, hc * 128:(hc + 1) * 128]
                        for g in range(4):
                            rhs = kvT[:, g, l:l + 16 * 126 + 1:16]
                            P.mm(accs[hc][:, g * 127:(g + 1) * 127], lhsT, rhs, l == 0 and g == 0, l == 31, sgc=True)
                        P.mm(psO[:, hc:hc + 1], lhsT, peT[:, l:l + 1], l == 0 and hc == 0, l == 31, sgc=True)
            for hc in range(4):
                hcg = hh * 4 + hc
                P.copy("dve", cbias[:, hcg:hcg + 1], psO[:, hc:hc + 1])
                src = accs[hc].v(accs[hc].ap[:, 0:508].rearrange("p (g c) -> p g c", c=127))
                P.act(hidT[:, hcg, :, 0:127], src, AF.Silu, bias=cbias[:, hcg:hcg + 1])
        if kind == 0:
            ps = next_ps()
            for g in range(4):
                for hc in range(8):
                    P.mm(ps[:, g * 127:(g + 1) * 127], w2[:, hc, :], hidT[:, hc, g, 0:127], hc == 0, hc == 7)
            evac(kcmpT[:, :, 0:127], ps.v(ps.ap[:, 0:508].rearrange("p (g c) -> p g c", c=127)))
        else:
            ps = next_ps()
            for g in range(4):
                for hc in range(8):
                    P.mm(ps[0:127, g * 128:(g + 1) * 128], hidT[:, hc, g, 0:127], w2[:, hc, :], hc == 0, hc == 7)
            evac(vcmp[0:127, :, :], ps.v(ps.ap[0:127, :].rearrange("p (g c) -> p g c", c=128)))
    if "kcmpT" in k.dbg_d:
        tmpf = A.alloc("dbgtmp", [128, 4, 128], F32)
        P.copy("dve", tmpf, kcmpT)
        k.dbg_store("kcmpT", tmpf)
        tmpf2 = A.alloc("dbgtmp2", [128, 4, 128], F32)
        P.copy("dve", tmpf2, vcmp)
        k.dbg_store("vcmp", tmpf2)
    A.release(m1)
    if stop_after == "nsa_cmp":
        return

    cmask = A.alloc("cmask", [128, 8, 128], F32)
    addmask = A.alloc("addmask", [128, 8, 32], F32)
    kvalid = A.alloc("kvalid", [128, 16], BF16)
    Emat = A.alloc("Emat", [128, 2048], BF16)
    causneg = A.alloc("causneg", [128, 512], BF16)
    winneg = A.alloc("winneg", [128, 512], BF16)
    g3 = A.alloc("g3", [128, 8, 48], F32)
    P.dma("sp", cmask, D["cmask"][:, :, :])
    P.dma("sp", addmask, D["addmask"][:, :, :])
    P.dma("pool", kvalid, D["kvalid"][:, :])
    P.dma("pool", Emat, D["Emat"][:, :])
    P.dma("pool", causneg, D["causneg"][:, :])
    P.dma("pool", winneg, D["winneg"][:, :])
    tiles = [("wg3",) + wtile_cols(w_in, OG, 48)]
    for g in range(4):
        tiles.append(("wq%d" % g,) + wtile_cols(w_in, OQ + 512 * g, 512))
        tiles.append(("wkv%d" % g,) + wtile_cols(w_in, OKV + 512 * g, 512))
    wt = WS.get(tiles, 0)
    for tb in range(NB):
        ps = next_ps()
        for kc in range(KCH):
            P.mm(ps[:, 0:48], nT_own[:, kc, tb * 128:(tb + 1) * 128], wt[:, kc, 0:48], kc == 0, kc == KCH - 1)
        P.act(g3[:, tb, :], ps[:, 0:48], AF.Sigmoid)

    qT = A.alloc("qT", [128, NB, 4, 128], BF16)
    ksT = A.alloc("ksT", [128, 2048], BF16)
    kwT = A.alloc("kwT", [128, 1536], BF16)
    vs = A.alloc("vs", [128, 16, 130], BF16)
    vw = A.alloc("vw", [128, 12, 130], BF16)
    e32 = A.alloc("e32", [128, 4, 128], F32)
    p32 = A.alloc("p32", [128, 4, 128], F32)
    p16 = A.alloc("p16", [128, 4, 128], BF16)
    pT = A.alloc("pT", [128, 4, 128], BF16)
    Pg = A.alloc("Pg", [128, 128], F32)
    imp = A.alloc("imp", [128, 32], F32)
    imp2 = A.alloc("imp2", [128, 32], F32)
    m8 = A.alloc("m8", [128, 8], F32)
    sm4 = A.alloc("sm4", [128, 16], F32)
    selneg = A.alloc("selneg", [128, 32], BF16)
    negT = [A.alloc("negT%d" % i, [128, 4, 128], BF16) for i in range(2)]
    for nb_ in negT:
        P.memset("dve", nb_, 0.0)
    oacc = [A.alloc("oacc%d" % i, [128, 4, 128], F32) for i in range(2)]
    o16 = A.alloc("o16", [128, 4, 128], BF16)
    PT = [A.alloc("PT%d" % i, [128, 512], BF16) for i in range(2)]
    coef = A.alloc("coef", [128, 8], F32)
    P.memset("dve", vs, 0.0)
    P.memset("dve", vw, 0.0)

    def bc_heads(b):
        return b.v(b.ap.unsqueeze(1).to_broadcast([b.ap.shape[0], 4, 128]))

    for g in range(4):
        wq = WS.get(tiles, 1 + 2 * g)
        for hh in range(4):
            for tch in range(2):
                proj_T(wq, hh * 128, nT_own, tch * 512, 512, qT[:, 4 * tch:4 * tch + 4, hh, :], scale=QSCALE,
                       out_view3=True)
        wkv = WS.get(tiles, 2 + 2 * g)
        for (nT, t0, u0) in uchunks:
            proj_T(wkv, 0, nT, t0, 512, ksT[:, u0:u0 + 512])
        for (nT, t0, u0) in uchunks[1:]:
            proj_T(wkv, 256, nT, t0, 512, kwT[:, u0 - 512:u0])
        for (vbuf, c0, ub0) in ((vs, 128, 0), (vw, 384, 4)):
            for q4 in range(ub0 // 4, 4):
                ps = next_ps()
                for j in range(4):
                    ub = 4 * q4 + j
                    nT = nT_ctx if ub < 8 else nT_own
                    tb = ub % 8
                    for kc in range(KCH):
                        P.mm(ps[:, j * 128:(j + 1) * 128], nT[:, kc, tb * 128:(tb + 1) * 128],
                             wkv[:, kc, c0:c0 + 128], kc == 0, kc == KCH - 1)
                evac(vbuf[:, 4 * q4 - ub0:4 * q4 - ub0 + 4, 0:128],
                     ps.v(ps.ap.rearrange("p (a b) -> p a b", b=128)))
            P.copy("dve", vbuf[:, :, 128:129], kvalid.v(kvalid.ap[:, ub0:16].unsqueeze(2)))

        def cmp_stage(qb):
            par = qb % 2
            for hh in range(4):
                P.mm(psC[:, hh * 128:(hh + 1) * 128], qT[:, qb, hh, :], kcmpT[:, g, :], True, True)
            psC3 = psC.v(psC.ap.rearrange("p (a b) -> p a b", b=128))
            P.reduce("dve", sm4[:, 0:4], psC3, ALU.max)
            P.ts("dve", sm4[:, 4:8], sm4[:, 0:4], -1.0, None, ALU.mult)
            for hh in range(4):
                P.act(e32[:, hh, :], psC[:, hh * 128:(hh + 1) * 128], AF.Exp, bias=sm4[:, 4 + hh:5 + hh])
            P.tt("dve", e32, e32, bc_heads(cmask[:, qb, :]), ALU.mult)
            P.reduce("dve", sm4[:, 8:12], e32, ALU.add)
            P.ts("dve", sm4[:, 8:12], sm4[:, 8:12], 1e-30, None, ALU.max)
            P.recip(sm4[:, 12:16], sm4[:, 8:12])
            rb = sm4.v(sm4.ap[:, 12:16].unsqueeze(2).to_broadcast([128, 4, 128]))
            P.tt("dve", p32, e32, rb, ALU.mult)
            P.copy("act", p16, p32)
            P.reduce("dve", Pg, p32.v(p32.ap.rearrange("p h c -> p c h")), ALU.add)
            P.reduce("dve", imp, Pg.v(Pg.ap.rearrange("p (j f) -> p j f", f=4)), ALU.add)
            P.tt("dve", imp2[:, 1:32], imp[:, 1:32], Pg[:, 3:124:4], ALU.add)
            P.copy("dve", imp2[:, 0:1], imp[:, 0:1])
            P.tt("dve", imp, imp2, addmask[:, qb, :], ALU.add)
            P.op("dve", lambda e: e.max(out=m8.ap, in_=imp.ap), [imp], [m8])
            P.op("dve", lambda e: e.match_replace(out=imp2.ap, in_to_replace=m8.ap, in_values=imp.ap,
                                                   imm_value=-3e38), [m8, imp], [imp2])
            P.op("dve", lambda e: e.max(out=m8.ap, in_=imp2.ap), [imp2], [m8])
            P.ts("dve", selneg, imp, m8[:, 7:8], NEGB, ALU.is_lt, ALU.mult)
            if "imp" in k.dbg_d and g == 0:
                k.dbg_store("imp", imp, k.dbg_d["imp"][qb])
                k.dbg_store("Pg", Pg, k.dbg_d["Pg"][qb])

        def cmp_stage_b(qb):
            par = qb % 2
            P.transpose(psT[0:32, 0:128], selneg, ident)
            evac(negT[par][0:32, :, :], psT.v(psT.ap[0:32, 0:128].unsqueeze(1).to_broadcast([32, 4, 128])))
            for hh in range(4):
                P.transpose(psT[:, (1 + hh) * 128:(2 + hh) * 128], p16[:, hh, :], ident)
            evac(pT, psT.v(psT.ap[:, 128:640].rearrange("p (a b) -> p a b", b=128)))
            for hh in range(4):
                P.mm(psO[:, hh * 128:(hh + 1) * 128], pT[:, hh, :], vcmp[:, g, :], True, True)
            for hh in range(4):
                col = 3 * (4 * g + hh)
                P.act(oacc[par][:, hh, :], psO[:, hh * 128:(hh + 1) * 128], AF.Copy, scale=g3[:, qb, col:col + 1])

        def attn(qb, kT, kofs, vbuf, vofs, kbs, masks, gcol, final):
            par = qb % 2
            n = len(kbs)
            q3 = qT.v(qT.ap[:, qb, :, :].rearrange("p h t -> p (h t)"))
            pss = {}

            def stA(i):
                kb = kbs[i]
                ps = psS[i % 3]
                pss[i] = ps
                mk = masks.get(kb)
                P.mm(ps, kT[:, (kb - kofs) * 128:(kb - kofs + 1) * 128], q3, True, mk is None)
                if mk is not None:
                    P.mm(ps, mk[0], mk[1], False, True)
                P.act(PT[i % 2], ps, AF.Exp)

            def stB(i):
                kb = kbs[i]
                for hh in range(4):
                    acc = psV[hh // 2]
                    o = (hh % 2) * 130
                    P.mm(acc[:, o:o + 129], PT[i % 2][:, hh * 128:(hh + 1) * 128], vbuf[:, kb - vofs, 0:129],
                         i == 0 and hh % 2 == 0, i == n - 1, sgc=True)

            stA(0)
            for i in range(n):
                if i + 1 < n:
                    stA(i + 1)
                stB(i)
            for hh in range(4):
                acc = psV[hh // 2]
                o = (hh % 2) * 130
                P.copy("dve", coef[:, hh:hh + 1], acc[:, o + 128:o + 129])
            P.ts("dve", coef[:, 0:4], coef[:, 0:4], 1e-30, None, ALU.max)
            P.recip(coef[:, 4:8], coef[:, 0:4])
            base = 3 * 4 * g + gcol
            P.tt("dve", coef[:, 4:8], coef[:, 4:8], g3[:, qb, base:base + 10:3], ALU.mult)
            for hh in range(4):
                acc = psV[hh // 2]
                o = (hh % 2) * 130
                dst = o16[:, hh, :] if final else oacc[par][:, hh, :]
                P.stt("dve", dst, acc[:, o:o + 128], coef[:, 4 + hh:5 + hh], oacc[par][:, hh, :], ALU.mult, ALU.add)

        cmp_stage(0)
        cmp_stage_b(0)
        for qb in range(NB):
            if qb + 1 < NB:
                cmp_stage(qb + 1)
            ub = 8 + qb
            par = qb % 2
            negbc = negT[par].v(negT[par].ap.rearrange("p h t -> p (h t)"))
            masks = {kb: (Emat[:, kb * 128:(kb + 1) * 128], negbc) for kb in range(0, ub)}
            masks[ub] = (ident, causneg)
            attn(qb, ksT, 0, vs, 0, list(range(0, ub + 1)), masks, 1, False)
            if qb + 1 < NB:
                cmp_stage_b(qb + 1)
            masks = {ub - 4: (ident, winneg), ub: (ident, causneg)}
            attn(qb, kwT, 4, vw, 4, list(range(ub - 4, ub + 1)), masks, 2, True)
            for hh in range(4):
                P.transpose(psT[:, (5 + hh % 2) * 128:(6 + hh % 2) * 128], o16[:, hh, :], ident)
                if hh % 2 == 1:
                    evac(o_nsaT[:, 4 * g + hh - 1:4 * g + hh + 1, qb * 128:(qb + 1) * 128],
                         psT.v(psT.ap[:, 640:896].rearrange("p (a b) -> p a b", b=128)))
        if stop_after == "nsa_g0":
            break
    if "o_nsaT" in k.dbg_d:
        tmpf = A.alloc("dbgtmp", [128, TOK], F32)
        for hh in range(4):
            P.copy("dve", tmpf, o_nsaT[:, hh, :])
            k.dbg_store("o_nsaT", tmpf, k.dbg_d["o_nsaT"][:, hh, :])


def _rotary(k, ps128, cos, sin, out_even_odd, tmp):
    P = k.P
    x1 = ps128.v(ps128.ap.rearrange("p (i two) -> p i two", two=2)[:, :, 0])
    x2 = ps128.v(ps128.ap.rearrange("p (i two) -> p i two", two=2)[:, :, 1])
    o1 = out_even_odd.v(out_even_odd.ap.rearrange("p (i two) -> p i two", two=2)[:, :, 0])
    o2 = out_even_odd.v(out_even_odd.ap.rearrange("p (i two) -> p i two", two=2)[:, :, 1])
    P.tt("dve", tmp[:, 0, :], x1, cos, ALU.mult)
    P.tt("dve", tmp[:, 1, :], x2, sin, ALU.mult)
    P.tt("dve", tmp[:, 2, :], x1, sin, ALU.mult)
    P.tt("dve", tmp[:, 3, :], x2, cos, ALU.mult)
    P.tt(POOL, o1, tmp[:, 0, :], tmp[:, 1, :], ALU.subtract)
    P.tt(POOL, o2, tmp[:, 2, :], tmp[:, 3, :], ALU.add)


def build_ret_ctx(k):
    P, A, D, WS = k.P, k.R34, k.D, k.WS
    nT_ctx = k.nT_ctx
    state8 = k.state8
    ropek = A.alloc("ropek", [128, 16, 128], F32)
    wkc = A.alloc("wkc", [128, 8, 8], F32)
    P.dma("sp", ropek, D["ropek"][:, :, :])
    P.dma("sp", wkc, D["wkc"][:, :, :])
    Kr = [A.alloc("Kr%d" % i, [128, 128], F32) for i in range(2)]
    Ks = [A.alloc("Ks%d" % i, [128, 128], BF16) for i in range(2)]
    V = [A.alloc("V%d" % i, [128, 256], BF16) for i in range(2)]
    tmp = [A.alloc("rtmp%d" % i, [128, 4, 64], F32) for i in range(2)]
    tiles = [("wr_c%d" % h,) + wtile_cols(D["w_in"], OR_ + 512 * h, 512) for h in range(8)]
    for h in range(8):
        wt = WS.get(tiles, h)
        for cb in range(NB):
            par = cb % 2
            ps = k.next_ps()
            for kc in range(KCH):
                P.mm(ps[:, 0:384], nT_ctx[:, kc, cb * 128:(cb + 1) * 128], wt[:, kc, 128:512], kc == 0, kc == KCH - 1)
            _rotary(k, ps[:, 0:128], ropek[:, cb, 0:64], ropek[:, cb, 64:128], Kr[par], tmp[par])
            P.ts(POOL, Ks[par], Kr[par], wkc[:, cb, h:h + 1], None, ALU.mult)
            P.copy("act", V[par], ps[:, 128:384])
            P.mm(k.psV[h % 2][:, 0:256], Ks[par], V[par], cb == 0, cb == NB - 1)
        P.copy("act", state8[:, h, :], k.psV[h % 2][:, 0:256])


def build_ret_own(k):
    P, A, D, WS = k.P, k.R34, k.D, k.WS
    nT_own = k.nT_own
    state8 = k.state8
    psT = k.psT
    ident = k.ident
    o_retT = A.alloc("o_retT", [128, 16, TOK], BF16)
    k.o_retT = o_retT
    k.m_after_o = A.mark()
    ropek = A.alloc("ropek", [128, 16, 128], F32)
    ropeq = A.alloc("ropeq", [128, 8, 128], F32)
    decayT = A.alloc("decayT", [128, 8, 128], F32)
    wqB = A.alloc("wqB", [128, 8, 128], F32)
    wk = A.alloc("wk", [128, 8], F32)
    gnB = k.load_gain("ret_gn_w", A)
    P.dma("sp", ropek, D["ropek"][:, :, :])
    P.dma("sp", ropeq, D["ropeq"][:, :, :])
    P.dma("sp", decayT, D["decayT"][:, :, :])
    P.dma("sp", wqB, D["wqB"][:, :, :])
    P.dma("sp", wk, D["wk"][:, :])
    def mk_bufs(j):
        B = {}
        for nm, shp, dt in (("Qr", [128, 128], BF16), ("Kr", [128, 128], F32), ("Kb", [128, 128], BF16),
                            ("Ks", [128, 128], BF16), ("V", [128, 256], BF16), ("sg", [128, 256], F32),
                            ("tmp", [128, 4, 64], F32)):
            B[nm] = [A.alloc("%s%d_%d" % (nm, j, i), shp, dt) for i in range(2)]
        for nm, shp, dt in (("QT", [128, 128], BF16), ("KT", [128, 128], BF16), ("QsT", [128, 128], BF16),
                            ("SdT", [128, 128], BF16), ("stbf", [128, 256], BF16), ("osb", [128, 256], F32),
                            ("junk", [128, 256], BF16), ("y", [128, 256], F32), ("y16", [128, 256], BF16),
                            ("gs", [128, 8], F32)):
            B[nm] = A.alloc("%s%d" % (nm, j), shp, dt)
        return B

    bufs = [mk_bufs(0), mk_bufs(1)]
    tiles = []
    for p in range(4):
        tiles.append(("wr_o%d" % (2 * p),) + wtile_cols(D["w_in"], OR_ + 512 * (2 * p), 512))
        tiles.append(("wr_o%d" % (2 * p + 1),) + wtile_cols(D["w_in"], OR_ + 512 * (2 * p + 1), 512))
        tiles.append(("wgr%d" % p,) + wtile_cols(D["w_in"], OGR + 512 * p, 512))

    def run2(sa, sb):
        n = max(len(sa), len(sb))
        for i in range(n):
            if i < len(sa):
                sa[i]()
            if i < len(sb):
                sb[i]()

    for p in range(4):
        wts = [WS.get(tiles, 3 * p + j, depth=0) for j in range(2)]
        wg = WS.get(tiles, 3 * p + 2, depth=0)

        def stage1_steps(j, ob):
            B = bufs[j]
            h = 2 * p + j
            wt = wts[j]
            par = ob % 2
            ub = 8 + ob
            st = {}
            steps = []

            def s_proj():
                st["ps"] = k.next_ps()
                for kc in range(KCH):
                    P.mm(st["ps"], nT_own[:, kc, ob * 128:(ob + 1) * 128], wt[:, kc, 0:512], kc == 0, kc == KCH - 1)
            steps.append(s_proj)

            def s_projg():
                st["psg"] = k.next_ps()
                for kc in range(KCH):
                    P.mm(st["psg"][:, 0:256], nT_own[:, kc, ob * 128:(ob + 1) * 128],
                         wg[:, kc, j * 256:j * 256 + 256], kc == 0, kc == KCH - 1)
            steps.append(s_projg)
            steps.append(lambda: _rotary(k, st["ps"][:, 0:128], ropeq[:, ob, 0:64], ropeq[:, ob, 64:128],
                                         B["Qr"][par], B["tmp"][par]))
            steps.append(lambda: P.copy("act", B["V"][par], st["ps"][:, 256:512]))
            steps.append(lambda: _rotary(k, st["ps"][:, 128:256], ropek[:, ub, 0:64], ropek[:, ub, 64:128],
                                         B["Kr"][par], B["tmp"][par]))
            steps.append(lambda: P.act(B["sg"][par], st["psg"][:, 0:256], AF.Silu))
            steps.append(lambda: P.copy(POOL, B["Kb"][par], B["Kr"][par]))
            steps.append(lambda: P.ts(POOL, B["Ks"][par], B["Kr"][par], wk[:, h:h + 1], None, ALU.mult))
            return steps

        def stage2_steps(j, ob):
            B = bufs[j]
            h = 2 * p + j
            par = ob % 2
            c0 = j * 512
            st = {}
            gs = B["gs"]
            steps = []

            def s_tr():
                P.transpose(psT[:, c0:c0 + 128], B["Qr"][par], ident)
                P.transpose(psT[:, c0 + 128:c0 + 256], B["Kb"][par], ident)
            steps.append(s_tr)
            steps.append(lambda: P.copy("act", B["QT"], psT[:, c0:c0 + 128]))
            steps.append(lambda: P.copy("act", B["KT"], psT[:, c0 + 128:c0 + 256]))
            steps.append(lambda: P.tt("dve", B["QsT"], psT[:, c0:c0 + 128], wqB[:, h, :], ALU.mult))

            def s_S():
                st["ps"] = k.next_ps()
                P.mm(st["ps"][:, 0:128], B["KT"], B["QT"], True, True)
            steps.append(s_S)
            steps.append(lambda: P.tt("dve", B["SdT"], st["ps"][:, 0:128], decayT[:, h, :], ALU.mult))
            steps.append(lambda: P.copy("act", B["stbf"], state8[:, h, :]))

            def s_o():
                st["po"] = k.next_ps()
                P.mm(st["po"][:, 0:256], B["SdT"], B["V"][par], True, False)
                P.mm(st["po"][:, 0:256], B["QsT"], B["stbf"], False, True)
            steps.append(s_o)
            steps.append(lambda: P.memset("dve", gs, 0.0))
            steps.append(lambda: P.act(B["osb"], st["po"][:, 0:256], AF.Copy, accum_out=gs[:, 0:1]))
            steps.append(lambda: P.act(B["junk"], st["po"][:, 0:256], AF.Square, accum_out=gs[:, 1:2]))

            def s_state():
                st["ps3"] = k.next_ps()
                P.mm(st["ps3"][:, 0:256], B["Ks"][par], B["V"][par], True, True)
            steps.append(s_state)
            steps.append(lambda: P.stt("dve", state8[:, h, :], state8[:, h, :], _G_CHUNK[h], st["ps3"][:, 0:256],
                                       ALU.mult, ALU.add))
            steps.append(lambda: P.ts("dve", gs[:, 2:3], gs[:, 0:1], 1.0 / 256, None, ALU.mult))
            steps.append(lambda: P.tt("dve", gs[:, 3:4], gs[:, 2:3], gs[:, 2:3], ALU.mult))
            steps.append(lambda: P.stt("dve", gs[:, 4:5], gs[:, 1:2], 1.0 / 256, gs[:, 3:4], ALU.mult, ALU.subtract))
            steps.append(lambda: P.ts("dve", gs[:, 4:5], gs[:, 4:5], EPS, None, ALU.add))
            steps.append(lambda: P.act(gs[:, 5:6], gs[:, 4:5], AF.Sqrt))
            steps.append(lambda: P.recip(gs[:, 6:7], gs[:, 5:6]))
            steps.append(lambda: P.ts("dve", B["y"], B["osb"], gs[:, 2:3], gs[:, 6:7], ALU.subtract, ALU.mult))
            steps.append(lambda: P.tt(POOL, B["y"], B["y"], gnB[:, h * 256:(h + 1) * 256], ALU.mult))
            steps.append(lambda: P.tt(POOL, B["y16"], B["y"], B["sg"][par], ALU.mult))

            def s_tr2():
                for jj in range(2):
                    P.transpose(psT[:, c0 + (2 + jj) * 128:c0 + (3 + jj) * 128], B["y16"][:, jj * 128:(jj + 1) * 128], ident)
            steps.append(s_tr2)
            steps.append(lambda: k.evac(o_retT[:, 2 * h:2 * h + 2, ob * 128:(ob + 1) * 128],
                                        psT.v(psT.ap[:, c0 + 256:c0 + 512].rearrange("p (a b) -> p a b", b=128))))
            return steps

        run2(stage1_steps(0, 0), stage1_steps(1, 0))
        for ob in range(NB):
            if ob + 1 < NB:
                run2(stage1_steps(0, ob + 1), stage1_steps(1, ob + 1))
            run2(stage2_steps(0, ob), stage2_steps(1, ob))
    if "o_retT" in k.dbg_d:
        tmpf = A.alloc("dbgtmp", [128, 2, TOK], F32)
        P.copy("dve", tmpf, o_retT[:, 0:2, :])
        k.dbg_store("o_retT", tmpf)


def build_merge(k, which, srcT):
    P, A, D, WS = k.P, k.R34, k.D, k.WS
    nT_own, mergedT = k.nT_own, k.mergedT
    W = D["w_a"] if which == "a" else D["w_b"]
    og = OGA if which == "a" else OGB
    sig = [A.alloc("sig%d" % i, [128, 512], F32) for i in range(2)]
    tmp = [A.alloc("mtmp%d" % i, [128, 512], F32) for i in range(2)]
    tiles = []
    for i in range(4):
        tiles.append(("wm%s%d" % (which, i),) + wtile_cols(W, 512 * i, 512))
        tiles.append(("wgt%s%d" % (which, i),) + wtile_cols(D["w_in"], og + 512 * i, 512))
    n = 0
    for i in range(4):
        wm = WS.get(tiles, 2 * i, depth=1)
        wg = WS.get(tiles, 2 * i + 1, depth=1)
        for cc in range(4):
            for tch in range(2):
                psA = k.next_ps()
                for kc in range(KCH):
                    P.mm(psA, wm[:, kc, cc * 128:(cc + 1) * 128], srcT[:, kc, tch * 512:(tch + 1) * 512],
                         kc == 0, kc == KCH - 1)
                psG = k.next_ps()
                for kc in range(KCH):
                    P.mm(psG, wg[:, kc, cc * 128:(cc + 1) * 128], nT_own[:, kc, tch * 512:(tch + 1) * 512],
                         kc == 0, kc == KCH - 1)
                par = n % 2
                n += 1
                P.act(sig[par], psG, AF.Sigmoid)
                dst = mergedT[:, 4 * i + cc, tch * 512:(tch + 1) * 512]
                if which == "a":
                    P.tt("dve", dst, sig[par], psA, ALU.mult)
                else:
                    P.tt("dve", tmp[par], sig[par], psA, ALU.mult)
                    P.tt(POOL, dst, tmp[par], dst, ALU.add)
    if ("mergedT_" + which) in k.dbg_d:
        tmpf = A.alloc("dbgtmp", [128, 4, TOK], F32)
        P.copy("dve", tmpf, mergedT[:, 0:4, :])
        k.dbg_store("mergedT_" + which, tmpf)


def build_tail(k, stop_after):
    P, D, WS = k.P, k.D, k.WS
    R1, R2, R34 = k.R1, k.R2, k.R34
    psS, psV, psT, ident = k.psS, k.psV, k.psT, k.ident
    mergedT = k.mergedT
    hb = [R34.alloc("h%d" % tb, [128, DM], F32) for tb in range(NB)]
    for tb in range(NB):
        P.dma("sp", hb[tb], D["xo"][tb * 128:(tb + 1) * 128, :])
    tiles = [("wout%d" % i,) + wtile_cols(D["w_out"], 512 * i, 512) for i in range(4)]
    for i in range(4):
        wt = WS.get(tiles, i)
        for tb in range(NB):
            ps = k.next_ps()
            for kc in range(KCH):
                P.mm(ps, mergedT[:, kc, tb * 128:(tb + 1) * 128], wt[:, kc, :], kc == 0, kc == KCH - 1)
            hs = hb[tb][:, 512 * i:512 * (i + 1)]
            P.tt("dve", hs, hs, ps, ALU.add)
    if "h1" in k.dbg_d:
        for tb in range(NB):
            k.dbg_store("h1", hb[tb], k.dbg_d["h1"][tb * 128:(tb + 1) * 128, :])
    R2.release()
    if stop_after == "h1":
        return

    nxT = R1.alloc("nxT", [128, KCH, TOK], BF16)
    gainX = k.load_gain("x_norm_w", R2)
    xn = R2.alloc("xn", [128, DM], BF16)
    stat = R2.alloc("stat", [128, 16], F32)
    k.norm_T(lambda tb: hb[tb], NB, gainX, nxT, xn, stat)
    P.dma("sp", gainX, D["mem_norm_w"].to_broadcast([128, DM]))
    mh = R34.mark()
    mts = [R34.alloc("mt%d" % i, [128, DM], F32) for i in range(2)]
    mT = R2.alloc("mT", [128, KCH, 256], BF16)
    k.norm_T(k.mk_get(D["mem"], mts), 2, gainX, mT, xn, stat)
    R34.release(mh)
    qxT = R2.alloc("qxT", [128, 4, TOK], BF16)
    kxT = R34.alloc("kxT", [128, 4, 256], BF16)
    vx = R34.alloc("vx", [128, 2, 4, 130], BF16)
    PTx = [R34.alloc("PTx%d" % i, [128, 512], BF16) for i in range(2)]
    xc = R34.alloc("xcoef", [128, 4], F32)
    tiles = [("wqx",) + wtile_cols(D["wq_x"], 0, 512), ("wkx",) + wtile_cols(D["wk_x"], 0, 512),
             ("wvx",) + wtile_cols(D["wv_x"], 0, 512), ("wox",) + wtile_rows(D["wo_x"], 0, 512)]
    wq = WS.get(tiles, 0)
    for hh in range(4):
        for tch in range(2):
            k.proj_T(wq, hh * 128, nxT, tch * 512, 512, qxT[:, hh, tch * 512:(tch + 1) * 512], scale=QSCALE)
    wk_ = WS.get(tiles, 1)
    for hh in range(4):
        k.proj_T(wk_, hh * 128, mT, 0, 256, kxT[:, hh, :])
    wv = WS.get(tiles, 2)
    P.memset("dve", vx, 1.0)
    for mb in range(2):
        ps = k.next_ps()
        for kc in range(KCH):
            P.mm(ps, mT[:, kc, mb * 128:(mb + 1) * 128], wv[:, kc, :], kc == 0, kc == KCH - 1)
        k.evac(vx[:, mb, :, 0:128], ps.v(ps.ap.rearrange("p (a b) -> p a b", b=128)))
    R1.release()
    ox16 = R1.alloc("ox16", [128, NB, 512], BF16)
    oxT = R1.alloc("oxT", [128, 4, TOK], BF16)
    for hh in range(4):
        for tch in range(2):
            for mb in range(2):
                ps = psS[mb]
                P.mm(ps, kxT[:, hh, mb * 128:(mb + 1) * 128], qxT[:, hh, tch * 512:(tch + 1) * 512], True, True)
                P.act(PTx[mb], ps, AF.Exp)
            for tq in range(4):
                tb = tch * 4 + tq
                acc = psV[tq % 2]
                for mb in range(2):
                    P.mm(acc[:, 0:129], PTx[mb][:, tq * 128:(tq + 1) * 128], vx[:, mb, hh, 0:129], mb == 0, mb == 1)
                P.copy("dve", xc[:, 0:1], acc[:, 128:129])
                P.recip(xc[:, 1:2], xc[:, 0:1])
                P.ts("dve", ox16[:, tb, hh * 128:(hh + 1) * 128], acc[:, 0:128], xc[:, 1:2], None, ALU.mult)
    for tb in range(NB):
        for j in range(4):
            P.transpose(psT[:, j * 128:(j + 1) * 128], ox16[:, tb, j * 128:(j + 1) * 128], ident)
        k.evac(oxT[:, :, tb * 128:(tb + 1) * 128], psT.v(psT.ap[:, 0:512].rearrange("p (a b) -> p a b", b=128)))
    wo = WS.get(tiles, 3)
    for tb in range(NB):
        for cc in range(4):
            ps = k.next_ps()
            for kc in range(4):
                P.mm(ps, oxT[:, kc, tb * 128:(tb + 1) * 128], wo[:, kc, cc * 512:(cc + 1) * 512], kc == 0, kc == 3)
            hs = hb[tb][:, 512 * cc:512 * (cc + 1)]
            P.tt("dve", hs, hs, ps, ALU.add)
    if "h2" in k.dbg_d:
        for tb in range(NB):
            k.dbg_store("h2", hb[tb], k.dbg_d["h2"][tb * 128:(tb + 1) * 128, :])
    R1.release()
    R2.release()
    R34.release(mh)
    if stop_after == "h2":
        return

    nmT = R1.alloc("nmT", [128, KCH, TOK], BF16)
    gainM = k.load_gain("mlp_norm_w", R2)
    xn = R2.alloc("xn", [128, DM], BF16)
    stat = R2.alloc("stat", [128, 16], F32)
    k.norm_T(lambda tb: hb[tb], NB, gainM, nmT, xn, stat)
    aT = R2.alloc("aT", [128, 4, TOK], BF16)
    rl = [R2.alloc("rl%d" % i, [128, 512], F32) for i in range(2)]
    tiles = []
    for f in range(16):
        tiles.append(("wup%d" % f,) + wtile_cols(D["w_up"], 512 * f, 512))
        tiles.append(("wdn%d" % f,) + wtile_rows(D["w_down"], 512 * f, 512))
    n = 0
    for f in range(16):
        wu = WS.get(tiles, 2 * f)
        for cc in range(4):
            for tch in range(2):
                ps = k.next_ps()
                for kc in range(KCH):
                    P.mm(ps, wu[:, kc, cc * 128:(cc + 1) * 128], nmT[:, kc, tch * 512:(tch + 1) * 512],
                         kc == 0, kc == KCH - 1)
                par = n % 2
                n += 1
                P.act(rl[par], ps, AF.Relu)
                P.tt(POOL, aT[:, cc, tch * 512:(tch + 1) * 512], rl[par], rl[par], ALU.mult)
        wd = WS.get(tiles, 2 * f + 1)
        for tb in range(NB):
            for cc in range(4):
                ps = k.next_ps()
                for kc in range(4):
                    P.mm(ps, aT[:, kc, tb * 128:(tb + 1) * 128], wd[:, kc, cc * 512:(cc + 1) * 512], kc == 0, kc == 3)
                hs = hb[tb][:, 512 * cc:512 * (cc + 1)]
                P.tt("dve", hs, hs, ps, ALU.add)
    if "h3" in k.dbg_d:
        for tb in range(NB):
            k.dbg_store("h3", hb[tb], k.dbg_d["h3"][tb * 128:(tb + 1) * 128, :])
    R1.release()
    R2.release()

    gainF = k.load_gain("final_norm_w", R2)
    outt = [R2.alloc("outt%d" % i, [128, DM], F32) for i in range(2)]
    junk = R2.alloc("junkf", [128, DM], BF16)
    stat = R2.alloc("statf", [128, 16], F32)
    P.memset("dve", stat, 0.0)
    for tb in range(NB):
        P.act(junk, hb[tb], AF.Square, accum_out=stat[:, tb:tb + 1])
        P.ts("dve", stat[:, tb:tb + 1], stat[:, tb:tb + 1], 1.0 / DM, EPS, ALU.mult, ALU.add)
        P.act(stat[:, tb:tb + 1], stat[:, tb:tb + 1], AF.Sqrt)
        P.recip(stat[:, tb:tb + 1], stat[:, tb:tb + 1])
        P.stt("dve", outt[tb % 2], hb[tb], stat[:, tb:tb + 1], gainF, ALU.mult, ALU.mult)
        P.dma("sp", k.out_d[tb * 128:(tb + 1) * 128, :], outt[tb % 2], semkey="dma:store%d" % (tb % 2))


def _w_in_perm():
    q = np.arange(0, 2048)
    parts = [q]
    for g in range(4):
        for base in (3072, 3584, 4096, 4608):
            parts.append(base + 128 * g + np.arange(128))
    parts.append(2048 + np.arange(512))
    parts.append(2560 + np.arange(512))
    for h in range(8):
        parts.append(5168 + 128 * h + np.arange(128))
        parts.append(6192 + 128 * h + np.arange(128))
        parts.append(7216 + 256 * h + np.arange(256))
    parts.append(np.arange(9264, 15408))
    parts.append(5120 + np.arange(48))
    perm = np.concatenate(parts)
    assert perm.shape[0] == IN_WIDTH and len(set(perm.tolist())) == IN_WIDTH
    return perm


def _tables(s):
    f32 = np.float32
    T = {}
    T["ident"] = np.eye(128, dtype=f32)
    u = np.arange(2048)
    t = u - 1024 + 1024 * s
    inv = (10000.0 ** (-np.arange(0, 128, 2, dtype=f32) / f32(128))).astype(f32)
    ang = t.astype(f32)[:, None] * inv[None, :]
    cos, sin = np.cos(ang).astype(f32), np.sin(ang).astype(f32)
    rk = np.concatenate([cos, sin], axis=1) * f32(128 ** -0.5)
    T["ropek"] = np.ascontiguousarray(rk.reshape(16, 128, 128).transpose(1, 0, 2)).astype(f32)
    rq = np.concatenate([cos, sin], axis=1)[1024:]
    T["ropeq"] = np.ascontiguousarray(rq.reshape(8, 128, 128).transpose(1, 0, 2)).astype(f32)
    H = 8
    log_g = np.log1p(-np.exp2(-5.0 - np.arange(H, dtype=f32))).astype(f32)
    i = np.arange(128, dtype=f32)
    rel = i[:, None] - i[None, :]
    decay = np.where(rel >= 0, np.exp(log_g[:, None, None] * np.maximum(rel, 0.0)), 0.0).astype(f32)
    T["decayT"] = np.ascontiguousarray(decay.transpose(2, 0, 1))
    w_q = np.exp(log_g[:, None] * (i + 1.0)[None, :]).astype(f32)
    T["wqB"] = np.ascontiguousarray(np.broadcast_to(w_q[None], (128, H, 128))).astype(f32)
    w_k = np.exp(log_g[:, None] * (127.0 - i)[None, :]).astype(f32)
    T["wk"] = np.ascontiguousarray(w_k.T)
    T["g_chunk"] = np.exp(log_g * f32(128.0)).astype(f32)
    gpow = np.stack([T["g_chunk"] ** f32(7 - blk) for blk in range(8)], 0).astype(f32)
    T["wkc"] = np.ascontiguousarray((w_k.T[:, None, :] * gpow[None, :, :]).astype(f32))
    uq = 1024 + np.arange(1024)
    lc = np.arange(128)
    vis = (16 * lc[None, :] + 31 <= uq[:, None]) & (lc[None, :] < 127)
    if s == 0:
        vis &= (lc[None, :] >= 64)
    T["cmask"] = np.ascontiguousarray(vis.astype(f32).reshape(8, 128, 128).transpose(1, 0, 2))
    lj = np.arange(32)
    lcur = (uq // 64)[:, None]
    first = 0 if s == 1 else 16
    am = np.zeros((1024, 32), f32)
    am[np.broadcast_to(lj[None, :] == first, am.shape)] = 1e9
    prev = (lj[None, :] == lcur - 1) & (lj[None, :] >= first)
    am[prev] = 2e9
    am[np.broadcast_to(lj[None, :], am.shape) == lcur] = 3e9
    am[(lj[None, :] > lcur) | (lj[None, :] < first)] = -1e9
    T["addmask"] = np.ascontiguousarray(am.reshape(8, 128, 32).transpose(1, 0, 2))
    kval = (t >= 0).astype(f32)
    T["kvalid"] = np.ascontiguousarray(kval.reshape(16, 128).T)
    kk = np.arange(2048)
    T["Emat"] = (kk[None, :] // 64 == np.arange(128)[:, None]).astype(f32)
    kq = np.arange(128)
    T["causneg"] = np.tile(np.where(kq[:, None] > kq[None, :], NEGB, 0.0).astype(f32), (1, 4))
    T["winneg"] = np.tile(np.where(kq[:, None] <= kq[None, :], NEGB, 0.0).astype(f32), (1, 4))
    return T


def make_in_maps(inputs):
    f32 = np.float32
    x = np.asarray(inputs["x"], f32)
    mem = np.asarray(inputs["mem"], f32)
    perm = _w_in_perm()
    shared = {}
    shared["w_in"] = np.ascontiguousarray(np.asarray(inputs["w_in"], f32)[0][:, perm])
    for nm in ("cmp_w1_k", "cmp_w1_v", "cmp_w2_k", "cmp_w2_v", "w_a", "w_b", "w_out", "wq_x", "wk_x", "wv_x",
               "wo_x", "w_up", "w_down"):
        shared[nm] = np.ascontiguousarray(np.asarray(inputs[nm], f32)[0])
    shared["peT_k"] = np.ascontiguousarray(np.asarray(inputs["cmp_pe_k"], f32)[0].T)
    shared["peT_v"] = np.ascontiguousarray(np.asarray(inputs["cmp_pe_v"], f32)[0].T)
    for nm in ("attn_norm_w", "ret_gn_w", "x_norm_w", "mem_norm_w", "mlp_norm_w"):
        shared[nm] = np.ascontiguousarray(np.asarray(inputs[nm], f32).reshape(1, DM))
    shared["final_norm_w"] = np.ascontiguousarray(np.asarray(inputs["final_norm_w"], f32).reshape(1, DM))
    tabs = [_tables(0), _tables(1)]
    zeros = np.zeros((TOK, DM), f32)
    in_maps = []
    for c in range(8):
        b, s = c // 2, c % 2
        m = dict(shared)
        m["xo"] = np.ascontiguousarray(x[b, 1024 * s:1024 * (s + 1)])
        m["xc"] = np.ascontiguousarray(x[b, 0:1024]) if s == 1 else zeros
        m["mem"] = np.ascontiguousarray(mem[b])
        for kk, v in tabs[s].items():
            if kk != "g_chunk":
                m[kk] = v
        in_maps.append(m)
    return in_maps


_G_CHUNK = [float(v) for v in np.exp(np.log1p(-np.exp2(-5.0 - np.arange(8, dtype=np.float32))).astype(np.float32)
                                     * np.float32(128.0)).astype(np.float32)]


def kernel(**inputs):
    in_maps = make_in_maps(inputs)
    nc, st = build_program()
    res = run_bass_kernel_spmd(nc, in_maps, core_ids=list(range(8)))
    out = np.zeros((4, 2048, DM), np.float32)
    for c in range(8):
        b, s = c // 2, c % 2
        out[b, 1024 * s:1024 * (s + 1)] = res.results[c]["out"]
    return out
```

```python
import numpy as np
from concourse.bass_utils import run_bass_kernel_spmd
import concourse.bass as bass
import concourse.mybir as mybir

F32 = mybir.dt.float32
BF16 = mybir.dt.bfloat16
AF = mybir.ActivationFunctionType
ALU = mybir.AluOpType
AX = mybir.AxisListType

_DT_SIZE = {F32: 4, BF16: 2}


class Buf:
    def __init__(self, key, ap):
        self.key = key
        self.ap = ap

    def __getitem__(self, idx):
        return Buf(self.key, self.ap[idx])

    def v(self, ap):
        return Buf(self.key, ap)


class Op:
    __slots__ = ("eng", "fn", "reads", "writes", "is_dma", "semkey", "deps", "dma_deps",
                 "signal", "signum", "pos", "accum", "idx")


class Prog:
    ENGS = ("pe", "act", "dve", "pool", "sp")

    def __init__(self, nc):
        self.nc = nc
        self.ops = []
        self.eng_obj = {"pe": nc.tensor, "act": nc.scalar, "dve": nc.vector, "pool": nc.gpsimd, "sp": nc.sync}
        self.sync_same_engine_war = False

    def _add(self, eng, fn, reads, writes, is_dma=False, semkey=None, accum=False):
        o = Op()
        o.eng = eng
        o.fn = fn
        o.reads = [b.key for b in reads if b is not None]
        o.writes = [b.key for b in writes if b is not None]
        for kk in o.reads:
            if kk.startswith("ps") and kk not in o.writes:
                o.writes.append(kk)
        o.is_dma = is_dma
        o.semkey = semkey
        o.accum = accum
        o.idx = len(self.ops)
        self.ops.append(o)
        return o

    def op(self, eng, fn, reads=(), writes=()):
        return self._add(eng, fn, reads, writes)

    def barrier(self):
        o = Op()
        o.eng = None
        o.idx = len(self.ops)
        self.ops.append(o)

    def dma(self, eng, out, in_, semkey=None):
        reads, writes = [], []
        if isinstance(in_, Buf):
            reads.append(in_)
            in_ap = in_.ap
        else:
            in_ap = in_
        if isinstance(out, Buf):
            writes.append(out)
            out_ap = out.ap
            if semkey is None:
                semkey = "dma:" + out.key
        else:
            out_ap = out
            if semkey is None:
                semkey = "dma:store"

        def fn(e):
            return e.dma_start(out=out_ap, in_=in_ap)

        return self._add(eng, fn, reads, writes, is_dma=True, semkey=semkey)

    def mm(self, out, lhsT, rhs, start, stop, extra_reads=(), sgc=False):
        def fn(e):
            if sgc:
                return e.matmul(out.ap, lhsT.ap, rhs.ap, start=start, stop=stop, skip_group_check=True)
            return e.matmul(out.ap, lhsT.ap, rhs.ap, start=start, stop=stop)
        return self._add("pe", fn, [lhsT, rhs] + list(extra_reads), [out], accum=not start)

    def transpose(self, out, in_, ident):
        def fn(e):
            return e.transpose(out.ap, in_.ap, ident.ap)
        return self._add("pe", fn, [in_, ident], [out])

    def act(self, out, in_, func, bias=None, scale=1.0, accum_out=None, eng="act"):
        reads = [in_]
        kw = {}
        if isinstance(bias, Buf):
            reads.append(bias)
            kw["bias"] = bias.ap
        elif bias is not None:
            kw["bias"] = bias
        if isinstance(scale, Buf):
            reads.append(scale)
            kw["scale"] = scale.ap
        else:
            kw["scale"] = scale
        writes = [out]
        if accum_out is not None:
            writes.append(accum_out)
            kw["accum_out"] = accum_out.ap

        def fn(e):
            return e.activation(out=out.ap, in_=in_.ap, func=func, **kw)
        return self._add(eng, fn, reads, writes)

    def tt(self, eng, out, in0, in1, op):
        def fn(e):
            return e.tensor_tensor(out=out.ap, in0=in0.ap, in1=in1.ap, op=op)
        return self._add(eng, fn, [in0, in1], [out])

    def ts(self, eng, out, in0, s1, s2, op0, op1=None, accum_out=None):
        reads = [in0]
        a1 = s1
        a2 = s2
        if isinstance(s1, Buf):
            reads.append(s1)
            a1 = s1.ap
        if isinstance(s2, Buf):
            reads.append(s2)
            a2 = s2.ap
        writes = [out]
        kw = {}
        if op1 is not None:
            kw["op1"] = op1
        if accum_out is not None:
            writes.append(accum_out)
            kw["accum_out"] = accum_out.ap

        def fn(e):
            return e.tensor_scalar(out=out.ap, in0=in0.ap, scalar1=a1, scalar2=a2, op0=op0, **kw)
        return self._add(eng, fn, reads, writes)

    def stt(self, eng, out, in0, scalar, in1, op0, op1):
        reads = [in0, in1]
        a = scalar
        if isinstance(scalar, Buf):
            reads.append(scalar)
            a = scalar.ap

        def fn(e):
            return e.scalar_tensor_tensor(out=out.ap, in0=in0.ap, scalar=a, in1=in1.ap, op0=op0, op1=op1)
        return self._add(eng, fn, reads, [out])

    def copy(self, eng, out, in_):
        if eng == "act":
            def fn(e):
                return e.copy(out=out.ap, in_=in_.ap)
        else:
            def fn(e):
                return e.tensor_copy(out=out.ap, in_=in_.ap)
        return self._add(eng, fn, [in_], [out])

    def reduce(self, eng, out, in_, op, axis=AX.X):
        def fn(e):
            return e.tensor_reduce(out=out.ap, in_=in_.ap, axis=axis, op=op)
        return self._add(eng, fn, [in_], [out])

    def memset(self, eng, out, val):
        def fn(e):
            return e.memset(out.ap, val)
        return self._add(eng, fn, [], [out])

    def recip(self, out, in_):
        def fn(e):
            return e.reciprocal(out=out.ap, in_=in_.ap)
        return self._add("dve", fn, [in_], [out])

    def emit(self, final_wait_eng="sp"):
        nc = self.nc
        ops = self.ops
        last_writer = {}
        readers = {}
        pos_ctr = {e: 0 for e in self.ENGS}
        waited = {f: {e: -1 for e in self.ENGS} for f in self.ENGS}
        waited_dma = {f: {} for f in self.ENGS}
        dma_count = {}
        last_op_on = {e: None for e in self.ENGS}
        pending_barrier = {e: [] for e in self.ENGS}
        outstanding_dma = []

        for o in ops:
            if o.eng is None:
                for f in self.ENGS:
                    pending_barrier[f] = [last_op_on[e] for e in self.ENGS if e != f and last_op_on[e] is not None]
                continue
            f = o.eng
            deps = set()
            for k in o.reads:
                w = last_writer.get(k)
                if w is not None:
                    deps.add(w)
            for k in o.writes:
                w = last_writer.get(k)
                if w is not None:
                    deps.add(w)
                for r in readers.get(k, ()):
                    deps.add(r)
            for b in pending_barrier[f]:
                deps.add(b)
            pending_barrier[f] = []
            o.pos = pos_ctr[f]
            pos_ctr[f] += 1
            o.deps = []
            o.dma_deps = []
            o.signal = False
            for di in sorted(deps):
                d = ops[di]
                if d.idx == o.idx:
                    continue
                if d.is_dma:
                    cnt = d.signum
                    if waited_dma[f].get(d.semkey, 0) >= cnt:
                        continue
                    waited_dma[f][d.semkey] = cnt
                    o.dma_deps.append((d.semkey, cnt))
                else:
                    if d.eng == "pe" and f == "pe" and not o.is_dma:
                        continue
                    if d.eng == f and not self.sync_same_engine_war and not o.is_dma:
                        is_raw_waw = any(last_writer.get(k) == di for k in o.reads + o.writes)
                        if not is_raw_waw:
                            continue
                    if waited[f][d.eng] >= d.pos:
                        continue
                    waited[f][d.eng] = d.pos
                    d.signal = True
                    o.deps.append(di)
            if o.is_dma:
                dma_count[o.semkey] = dma_count.get(o.semkey, 0) + 16
                o.signum = dma_count[o.semkey]
                outstanding_dma.append(o)
            for k in o.reads:
                readers.setdefault(k, []).append(o.idx)
            for k in o.writes:
                last_writer[k] = o.idx
                readers[k] = []
            last_op_on[f] = o.idx

        tail_deps = []
        for e in self.ENGS:
            li = last_op_on[e]
            if li is not None and not ops[li].is_dma:
                ops[li].signal = True
                tail_deps.append(li)
        sig_ctr = {e: 0 for e in self.ENGS}
        for o in ops:
            if o.eng is None or o.is_dma:
                continue
            if o.signal:
                sig_ctr[o.eng] += 1
                o.signum = sig_ctr[o.eng]
        import contextlib
        es = contextlib.ExitStack()
        self._es = es
        sems = {e: es.enter_context(nc.semaphore("s_" + e)) for e in self.ENGS}
        dsems = {}
        for k in dma_count:
            dsems[k] = es.enter_context(nc.semaphore("d%d" % len(dsems)))
        n_wait = 0
        for o in ops:
            if o.eng is None:
                continue
            e = self.eng_obj[o.eng]
            for di in o.deps:
                d = ops[di]
                e.wait_ge(sems[d.eng], d.signum)
                n_wait += 1
            for (k, cnt) in o.dma_deps:
                e.wait_ge(dsems[k], cnt)
                n_wait += 1
            ins = o.fn(e)
            if o.is_dma:
                ins.then_inc(dsems[o.semkey], 16)
            elif o.signal:
                ins.then_inc(sems[o.eng], 1)
        fe = self.eng_obj[final_wait_eng]
        for di in tail_deps:
            d = ops[di]
            fe.wait_ge(sems[d.eng], d.signum)
        for k, cnt in dma_count.items():
            fe.wait_ge(dsems[k], cnt)
        self.stats = dict(n_ops=len(ops), n_wait=n_wait, sig=dict(sig_ctr), n_dsems=len(dsems))
        return self.stats


TOK = 1024
NB = 8
DM = 2048
KCH = 16
EPS = 1e-6
OQ, OKV, OC, OR_, OGR, OGA, OGB, OG = 0, 2048, 4096, 5120, 9216, 11264, 13312, 15360
IN_WIDTH = 15408
NEGB = -30000.0
QSCALE = 128 ** -0.5
POOL = "dve"


def _prod(xs):
    r = 1
    for x in xs:
        r *= int(x)
    return r


class Arena:
    uid = 0

    def __init__(self, full_ap, P, base, nel, name):
        self.ap = full_ap
        self.base = base
        self.top = 0
        self.nel = nel
        self.P = P
        self.peak = 0
        self.name = name

    def alloc(self, name, shape, dt):
        inner = _prod(shape[1:])
        n = inner * (2 if dt == F32 else 1)
        npad = (n + 15) // 16 * 16
        off = self.base + self.top
        self.top += npad
        self.peak = max(self.peak, self.top)
        assert self.top <= self.nel, ("SBUF arena overflow", self.name, name, self.top, self.nel)
        ap = self.ap[:shape[0], off:off + n]
        if dt == F32:
            ap = ap.bitcast(F32)
        if len(shape) == 3:
            ap = ap.rearrange("p (a b) -> p a b", b=shape[2])
        elif len(shape) == 4:
            ap = ap.rearrange("p (a b c) -> p a b c", b=shape[2], c=shape[3])
        Arena.uid += 1
        return Buf("%s#%d" % (name, Arena.uid), ap)

    def mark(self):
        return self.top

    def release(self, mark=0):
        self.top = mark
        self.P.barrier()


class WStream:
    def __init__(self, P, arena, nslots, slot_elems):
        self.P = P
        self.nslots = nslots
        self.slot_elems = slot_elems
        self.slots = [arena.alloc("wslot%d" % i, [128, slot_elems], BF16) for i in range(nslots)]
        self.ctr = 0
        self.loaded = {}

    def _load(self, item):
        key, src, shape = item
        if key in self.loaded:
            return
        s = self.slots[self.ctr % self.nslots]
        self.ctr += 1
        n = _prod(shape[1:])
        ap = s.ap[:, 0:n]
        if len(shape) == 3:
            ap = ap.rearrange("p (a b) -> p a b", b=shape[2])
        b = Buf(s.key, ap)
        self.P.dma("pool", b, src)
        self.loaded[key] = b

    def get(self, lst, i, depth=None):
        depth = self.nslots - 1 if depth is None else depth
        for j in range(i, min(len(lst), i + depth + 1)):
            self._load(lst[j])
        b = self.loaded.pop(lst[i][0])
        return b


def wtile_cols(w2d, c0, ncols):
    return w2d[:, c0:c0 + ncols].rearrange("(kc p) c -> p kc c", p=128), [128, 16, ncols]


def wtile_rows(w2d, r0, nrows):
    return w2d[r0:r0 + nrows, :].rearrange("(kc p) c -> p kc c", p=128), [128, nrows // 128, 2048]


class K:
    pass


def build_program(dbg=None, stop_after=None):
    nc = bass.Bass("TRN2", target_bir_lowering=False)
    P = Prog(nc)
    k = K()
    k.nc, k.P = nc, P

    def din(name, shape):
        return nc.dram_tensor(name, list(shape), F32, kind="ExternalInput").ap()

    D = {}
    D["xo"] = din("xo", [TOK, DM])
    D["xc"] = din("xc", [TOK, DM])
    D["mem"] = din("mem", [256, DM])
    D["w_in"] = din("w_in", [DM, IN_WIDTH])
    for nm in ("cmp_w1_k", "cmp_w1_v"):
        D[nm] = din(nm, [4096, 1024])
    for nm in ("cmp_w2_k", "cmp_w2_v"):
        D[nm] = din(nm, [1024, 128])
    for nm in ("peT_k", "peT_v"):
        D[nm] = din(nm, [128, 32])
    for nm in ("w_a", "w_b", "w_out"):
        D[nm] = din(nm, [DM, DM])
    for nm in ("wq_x", "wk_x", "wv_x"):
        D[nm] = din(nm, [DM, 512])
    D["wo_x"] = din("wo_x", [512, DM])
    D["w_up"] = din("w_up", [DM, 8192])
    D["w_down"] = din("w_down", [8192, DM])
    for nm in ("attn_norm_w", "ret_gn_w", "x_norm_w", "mem_norm_w", "mlp_norm_w", "final_norm_w"):
        D[nm] = din(nm, [1, DM])
    D["ident"] = din("ident", [128, 128])
    D["ropeq"] = din("ropeq", [128, 8, 128])
    D["ropek"] = din("ropek", [128, 16, 128])
    D["decayT"] = din("decayT", [128, 8, 128])
    D["wqB"] = din("wqB", [128, 8, 128])
    D["wk"] = din("wk", [128, 8])
    D["wkc"] = din("wkc", [128, 8, 8])
    D["cmask"] = din("cmask", [128, 8, 128])
    D["addmask"] = din("addmask", [128, 8, 32])
    D["kvalid"] = din("kvalid", [128, 16])
    D["Emat"] = din("Emat", [128, 2048])
    D["causneg"] = din("causneg", [128, 512])
    D["winneg"] = din("winneg", [128, 512])
    out_d = nc.dram_tensor("out", [TOK, DM], F32, kind="ExternalOutput").ap()
    dbg_d = {}
    if dbg:
        for nm, shp in dbg.items():
            dbg_d[nm] = nc.dram_tensor("dbg_" + nm, list(shp), F32, kind="ExternalOutput").ap()
    k.D, k.out_d, k.dbg_d = D, out_d, dbg_d

    NEL = 106000
    full = nc.alloc_sbuf_tensor("arena", [128, NEL], BF16).ap()
    R0 = Arena(full, P, 0, 30000, "R0")
    R1 = Arena(full, P, 30000, 16384, "R1")
    R2 = Arena(full, P, 46384, 16384, "R2")
    R34 = Arena(full, P, 62768, NEL - 62768, "R34")
    k.R0, k.R1, k.R2, k.R34 = R0, R1, R2, R34
    A = R0
    k.psS = [Buf("psS%d" % i, nc.alloc_psum_tensor("psS%d" % i, [128, 512], F32).ap()) for i in range(3)]
    k.psC = Buf("psC", nc.alloc_psum_tensor("psC", [128, 512], F32).ap())
    k.psO = Buf("psO", nc.alloc_psum_tensor("psO", [128, 512], F32).ap())
    k.psV = [Buf("psV%d" % i, nc.alloc_psum_tensor("psV%d" % i, [128, 512], F32).ap()) for i in range(2)]
    k.psT = Buf("psT", nc.alloc_psum_tensor("psT", [128, 1024], BF16).ap())
    k.rot5 = [k.psS[0], k.psS[1], k.psS[2], k.psC, k.psO]
    k.rot_i = 0
    k.ev_i = 0

    k.ident = A.alloc("ident", [128, 128], BF16)
    P.dma("pool", k.ident, D["ident"][:, :])
    k.WS = WStream(P, A, 3, 16 * 512)

    def finish():
        st = P.emit()
        st["arena_peak_el"] = [R0.peak, R1.peak, R2.peak, R34.peak]
        return nc, st

    def dbg_store(name, buf, dram_view=None):
        if name in dbg_d:
            dv = dbg_d[name] if dram_view is None else dram_view
            P.dma("sp", dv, buf, semkey="dma:dbg")

    k.dbg_store = dbg_store

    def next_ps():
        b = k.rot5[k.rot_i % 5]
        k.rot_i += 1
        return b

    def evac(out, in_, scale=None):
        e = k.ev_i % 2
        k.ev_i += 1
        if e == 0:
            if scale is None:
                P.copy("act", out, in_)
            else:
                P.act(out, in_, AF.Copy, scale=scale)
        else:
            if scale is None:
                P.copy("dve", out, in_)
            else:
                P.ts("dve", out, in_, scale, None, ALU.mult)

    k.next_ps, k.evac = next_ps, evac

    def load_gain(name, reg):
        g = reg.alloc("gain_" + name, [128, DM], F32)
        P.dma("sp", g, D[name].to_broadcast([128, DM]))
        return g

    def norm_T(get_block, nblk, gain, nT, xn, stat):
        P.memset("dve", stat, 0.0)
        for tb in range(nblk):
            xt = get_block(tb)
            P.act(xn, xt, AF.Square, accum_out=stat[:, tb:tb + 1])
            P.ts("dve", stat[:, tb:tb + 1], stat[:, tb:tb + 1], 1.0 / DM, EPS, ALU.mult, ALU.add)
            P.act(stat[:, tb:tb + 1], stat[:, tb:tb + 1], AF.Sqrt)
            P.recip(stat[:, tb:tb + 1], stat[:, tb:tb + 1])
            P.stt("dve", xn, xt, stat[:, tb:tb + 1], gain, ALU.mult, ALU.mult)
            for half in range(2):
                for j in range(8):
                    kc = half * 8 + j
                    P.transpose(k.psT[:, j * 128:(j + 1) * 128], xn[:, kc * 128:(kc + 1) * 128], k.ident)
                src = k.psT.v(k.psT.ap.rearrange("p (a b) -> p a b", b=128))
                evac(nT[:, half * 8:half * 8 + 8, tb * 128:(tb + 1) * 128], src)

    def proj_T(wt, c0, nT, t0, ntok, out, scale=None, nk=KCH, out_view3=False):
        ps = next_ps()
        for kc in range(nk):
            P.mm(ps[:, 0:ntok], wt[:, kc, c0:c0 + 128], nT[:, kc, t0:t0 + ntok], kc == 0, kc == nk - 1)
        src = ps[:, 0:ntok]
        if out_view3:
            src = ps.v(ps.ap[:, 0:ntok].rearrange("p (a b) -> p a b", b=128))
        evac(out, src, scale)

    k.norm_T, k.proj_T, k.load_gain = norm_T, proj_T, load_gain

    nT_own = R1.alloc("nT_own", [128, KCH, TOK], BF16)
    nT_ctx = R2.alloc("nT_ctx", [128, KCH, TOK], BF16)
    k.state8 = R0.alloc("state8", [128, 8, 256], F32)
    k.kcmpT = R0.alloc("kcmpT", [128, 4, 128], BF16)
    k.vcmp = R0.alloc("vcmp", [128, 4, 128], BF16)
    gainA = load_gain("attn_norm_w", R34)
    xts = [R34.alloc("xt%d" % i, [128, DM], F32) for i in range(2)]
    xn = R34.alloc("xn", [128, DM], BF16)
    stat = R34.alloc("stat", [128, 16], F32)

    def mk_get(src, xts):
        def get(tb):
            xt = xts[tb % 2]
            P.dma("sp", xt, src[tb * 128:(tb + 1) * 128, :])
            return xt
        return get

    k.mk_get = mk_get
    norm_T(mk_get(D["xc"], xts), NB, gainA, nT_ctx, xn, stat)
    norm_T(mk_get(D["xo"], xts), NB, gainA, nT_own, xn, stat)
    if "nT_own" in dbg_d:
        tmpf = R34.alloc("dbgtmp", [128, KCH, TOK // 4], F32)
        P.copy("dve", tmpf, nT_own[:, :, 0:TOK // 4])
        dbg_store("nT_own", tmpf)
    R34.release()
    if stop_after == "norm":
        return finish()
    k.nT_own, k.nT_ctx = nT_own, nT_ctx
    k.finish = finish

    build_ret_ctx(k)
    R34.release()
    if stop_after == "ret_ctx":
        return finish()
    build_nsa(k, stop_after)
    if stop_after and stop_after.startswith("nsa"):
        return finish()
    R34.release(k.m_after_o)
    R2.release()
    k.mergedT = R2.alloc("mergedT", [128, KCH, TOK], BF16)
    build_merge(k, "a", k.o_nsaT)
    R34.release()
    if stop_after == "merge_a":
        return finish()
    build_ret_own(k)
    if stop_after == "ret":
        return finish()
    R34.release(k.m_after_o)
    build_merge(k, "b", k.o_retT)
    R34.release()
    R1.release()
    if stop_after == "merge_b":
        return finish()
    build_tail(k, stop_after)
    return finish()


def build_nsa(k, stop_after):
    P, A, D, WS = k.P, k.R34, k.D, k.WS
    nT_own, nT_ctx = k.nT_own, k.nT_ctx
    psS, psC, psO, psV, psT = k.psS, k.psC, k.psO, k.psV, k.psT
    evac, next_ps, proj_T = k.evac, k.next_ps, k.proj_T
    ident = k.ident
    w_in = D["w_in"]
    uchunks = [(nT_ctx, 0, 0), (nT_ctx, 512, 512), (nT_own, 0, 1024), (nT_own, 512, 1536)]

    o_nsaT = A.alloc("o_nsaT", [128, 16, TOK], BF16)
    k.o_nsaT = o_nsaT
    k.m_after_o = A.mark()
    kcmpT, vcmp = k.kcmpT, k.vcmp
    P.memset("dve", kcmpT, 0.0)
    P.memset("dve", vcmp, 0.0)
    m1 = A.mark()
    kvT = A.alloc("kvcT", [128, 4, 2048], BF16)
    kvD = A.alloc("kvcD", [128, 4, 16, 128], BF16)
    hidT = A.alloc("hidT", [128, 8, 4, 128], BF16)
    peT = A.alloc("peT", [128, 32], BF16)
    w2 = A.alloc("w2", [128, 8, 128], BF16)
    cbias = A.alloc("cbias", [128, 8], F32)
    for kind in range(2):
        sfx = "_k" if kind == 0 else "_v"
        tiles = [("wc%d" % kind,) + wtile_cols(w_in, OC + 512 * kind, 512)]
        for hh in range(2):
            for lh in range(2):
                src = D["cmp_w1" + sfx][2048 * lh:2048 * (lh + 1), 512 * hh:512 * (hh + 1)].rearrange(
                    "(l p) c -> p l c", p=128)
                tiles.append(("w1%d_%d_%d" % (kind, hh, lh), src, [128, 16, 512]))
        P.dma("pool", peT, D["peT" + sfx][:, :])
        P.dma("pool", w2, D["cmp_w2" + sfx].rearrange("(hc p) c -> p hc c", p=128))
        wt = WS.get(tiles, 0)
        for g in range(4):
            for (nT, t0, u0) in uchunks:
                proj_T(wt, g * 128, nT, t0, 512, kvT[:, g, u0:u0 + 512])
        for g in range(4):
            src = kvT.v(kvT.ap[:, g, :].rearrange("p (j r) -> p r j", r=16))
            if g % 2 == 0:
                P.copy("act", kvD[:, g, :, :], src)
            else:
                P.copy("dve", kvD[:, g, :, :], src)
        ti = 1
        for hh in range(2):
            accs = [psS[0], psS[1], psS[2], psC]
            for lh in range(2):
                wt = WS.get(tiles, ti)
                ti += 1
                for li in range(16):
                    l = lh * 16 + li
                    for hc in range(4):
                        lhsT = wt[:, li, hc * 128:(hc + 1) * 128]
                        for g in range(4):
                            rhs = kvD[:, g, l, 0:127] if l < 16 else kvD[:, g, l - 16, 1:128]
                            P.mm(accs[hc][:, g * 127:(g + 1) * 127], lhsT, rhs, l == 0 and g == 0, l == 31, sgc=True)
                        P.mm(psO[:, hc:hc + 1], lhsT, peT[:, l:l + 1], l == 0 and hc == 0, l == 31, sgc=True)
            for hc in range(4):
                hcg = hh * 4 + hc
                P.copy("dve", cbias[:, hcg:hcg + 1], psO[:, hc:hc + 1])
                src = accs[hc].v(accs[hc].ap[:, 0:508].rearrange("p (g c) -> p g c", c=127))
                P.act(hidT[:, hcg, :, 0:127], src, AF.Silu, bias=cbias[:, hcg:hcg + 1])
        if kind == 0:
            ps = next_ps()
            for g in range(4):
                for hc in range(8):
                    P.mm(ps[:, g * 127:(g + 1) * 127], w2[:, hc, :], hidT[:, hc, g, 0:127], hc == 0, hc == 7)
            evac(kcmpT[:, :, 0:127], ps.v(ps.ap[:, 0:508].rearrange("p (g c) -> p g c", c=127)))
        else:
            ps = next_ps()
            for g in range(4):
                for hc in range(8):
                    P.mm(ps[0:127, g * 128:(g + 1) * 128], hidT[:, hc, g, 0:127], w2[:, hc, :], hc == 0, hc == 7)
            evac(vcmp[0:127, :, :], ps.v(ps.ap[0:127, :].rearrange("p (g c) -> p g c", c=128)))
    if "kcmpT" in k.dbg_d:
        tmpf = A.alloc("dbgtmp", [128, 4, 128], F32)
        P.copy("dve", tmpf, kcmpT)
        k.dbg_store("kcmpT", tmpf)
        tmpf2 = A.alloc("dbgtmp2", [128, 4, 128], F32)
        P.copy("dve", tmpf2, vcmp)
        k.dbg_store("vcmp", tmpf2)
    A.release(m1)
    if stop_after == "nsa_cmp":
        return

    cmask = A.alloc("cmask", [128, 8, 128], F32)
    addmask = A.alloc("addmask", [128, 8, 32], F32)
    kvalid = A.alloc("kvalid", [128, 16], BF16)
    Emat = A.alloc("Emat", [128, 2048], BF16)
    causneg = A.alloc("causneg", [128, 512], BF16)
    winneg = A.alloc("winneg", [128, 512], BF16)
    g3 = A.alloc("g3", [128, 8, 48], F32)
    P.dma("sp", cmask, D["cmask"][:, :, :])
    P.dma("sp", addmask, D["addmask"][:, :, :])
    P.dma("pool", kvalid, D["kvalid"][:, :])
    P.dma("pool", Emat, D["Emat"][:, :])
    P.dma("pool", causneg, D["causneg"][:, :])
    P.dma("pool", winneg, D["winneg"][:, :])
    tiles = [("wg3",) + wtile_cols(w_in, OG, 48)]
    for g in range(4):
        tiles.append(("wq%d" % g,) + wtile_cols(w_in, OQ + 512 * g, 512))
        tiles.append(("wkv%d" % g,) + wtile_cols(w_in, OKV + 512 * g, 512))
    wt = WS.get(tiles, 0)
    for tb in range(NB):
        ps = next_ps()
        for kc in range(KCH):
            P.mm(ps[:, 0:48], nT_own[:, kc, tb * 128:(tb + 1) * 128], wt[:, kc, 0:48], kc == 0, kc == KCH - 1)
        P.act(g3[:, tb, :], ps[:, 0:48], AF.Sigmoid)

    qT = A.alloc("qT", [128, NB, 4, 128], BF16)
    ksT = A.alloc("ksT", [128, 2048], BF16)
    kwT = A.alloc("kwT", [128, 1536], BF16)
    vs = A.alloc("vs", [128, 16, 130], BF16)
    vw = A.alloc("vw", [128, 12, 130], BF16)
    e32 = A.alloc("e32", [128, 4, 128], F32)
    p32 = A.alloc("p32", [128, 4, 128], F32)
    p16 = A.alloc("p16", [128, 4, 128], BF16)
    pT = A.alloc("pT", [128, 4, 128], BF16)
    Pg = A.alloc("Pg", [128, 128], F32)
    imp = A.alloc("imp", [128, 32], F32)
    imp2 = A.alloc("imp2", [128, 32], F32)
    m8 = A.alloc("m8", [128, 8], F32)
    sm4 = A.alloc("sm4", [128, 16], F32)
    selneg = A.alloc("selneg", [128, 32], BF16)
    negT = [A.alloc("negT%d" % i, [128, 4, 128], BF16) for i in range(2)]
    for nb_ in negT:
        P.memset("dve", nb_, 0.0)
    oacc = [A.alloc("oacc%d" % i, [128, 4, 128], F32) for i in range(2)]
    o16 = A.alloc("o16", [128, 4, 128], BF16)
    PT = [A.alloc("PT%d" % i, [128, 512], BF16) for i in range(2)]
    coef = A.alloc("coef", [128, 8], F32)
    P.memset("dve", vs, 0.0)
    P.memset("dve", vw, 0.0)

    def bc_heads(b):
        return b.v(b.ap.unsqueeze(1).to_broadcast([b.ap.shape[0], 4, 128]))

    for g in range(4):
        wq = WS.get(tiles, 1 + 2 * g)
        for hh in range(4):
            for tch in range(2):
                proj_T(wq, hh * 128, nT_own, tch * 512, 512, qT[:, 4 * tch:4 * tch + 4, hh, :], scale=QSCALE,
                       out_view3=True)
        wkv = WS.get(tiles, 2 + 2 * g)
        for (nT, t0, u0) in uchunks:
            proj_T(wkv, 0, nT, t0, 512, ksT[:, u0:u0 + 512])
        for (nT, t0, u0) in uchunks[1:]:
            proj_T(wkv, 256, nT, t0, 512, kwT[:, u0 - 512:u0])
        for (vbuf, c0, ub0) in ((vs, 128, 0), (vw, 384, 4)):
            for q4 in range(ub0 // 4, 4):
                ps = next_ps()
                for j in range(4):
                    ub = 4 * q4 + j
                    nT = nT_ctx if ub < 8 else nT_own
                    tb = ub % 8
                    for kc in range(KCH):
                        P.mm(ps[:, j * 128:(j + 1) * 128], nT[:, kc, tb * 128:(tb + 1) * 128],
                             wkv[:, kc, c0:c0 + 128], kc == 0, kc == KCH - 1)
                evac(vbuf[:, 4 * q4 - ub0:4 * q4 - ub0 + 4, 0:128],
                     ps.v(ps.ap.rearrange("p (a b) -> p a b", b=128)))
            P.copy("dve", vbuf[:, :, 128:129], kvalid.v(kvalid.ap[:, ub0:16].unsqueeze(2)))

        def cmp_stage(qb):
            par = qb % 2
            for hh in range(4):
                P.mm(psC[:, hh * 128:(hh + 1) * 128], qT[:, qb, hh, :], kcmpT[:, g, :], True, True)
            psC3 = psC.v(psC.ap.rearrange("p (a b) -> p a b", b=128))
            P.reduce("dve", sm4[:, 0:4], psC3, ALU.max)
            P.ts("dve", sm4[:, 4:8], sm4[:, 0:4], -1.0, None, ALU.mult)
            for hh in range(4):
                P.act(e32[:, hh, :], psC[:, hh * 128:(hh + 1) * 128], AF.Exp, bias=sm4[:, 4 + hh:5 + hh])
            P.tt("dve", e32, e32, bc_heads(cmask[:, qb, :]), ALU.mult)
            P.reduce("dve", sm4[:, 8:12], e32, ALU.add)
            P.ts("dve", sm4[:, 8:12], sm4[:, 8:12], 1e-30, None, ALU.max)
            P.recip(sm4[:, 12:16], sm4[:, 8:12])
            rb = sm4.v(sm4.ap[:, 12:16].unsqueeze(2).to_broadcast([128, 4, 128]))
            P.tt("dve", p32, e32, rb, ALU.mult)
            P.copy("act", p16, p32)
            P.reduce("dve", Pg, p32.v(p32.ap.rearrange("p h c -> p c h")), ALU.add)
            P.reduce("dve", imp, Pg.v(Pg.ap.rearrange("p (j f) -> p j f", f=4)), ALU.add)
            P.tt("dve", imp2[:, 1:32], imp[:, 1:32], Pg[:, 3:124:4], ALU.add)
            P.copy("dve", imp2[:, 0:1], imp[:, 0:1])
            P.tt("dve", imp, imp2, addmask[:, qb, :], ALU.add)
            P.op("dve", lambda e: e.max(out=m8.ap, in_=imp.ap), [imp], [m8])
            P.op("dve", lambda e: e.match_replace(out=imp2.ap, in_to_replace=m8.ap, in_values=imp.ap,
                                                   imm_value=-3e38), [m8, imp], [imp2])
            P.op("dve", lambda e: e.max(out=m8.ap, in_=imp2.ap), [imp2], [m8])
            P.ts("dve", selneg, imp, m8[:, 7:8], NEGB, ALU.is_lt, ALU.mult)
            if "imp" in k.dbg_d and g == 0:
                k.dbg_store("imp", imp, k.dbg_d["imp"][qb])
                k.dbg_store("Pg", Pg, k.dbg_d["Pg"][qb])

        def cmp_stage_b(qb):
            par = qb % 2
            P.transpose(psT[0:32, 0:128], selneg, ident)
            evac(negT[par][0:32, :, :], psT.v(psT.ap[0:32, 0:128].unsqueeze(1).to_broadcast([32, 4, 128])))
            for hh in range(4):
                P.transpose(psT[:, (1 + hh) * 128:(2 + hh) * 128], p16[:, hh, :], ident)
            evac(pT, psT.v(psT.ap[:, 128:640].rearrange("p (a b) -> p a b", b=128)))
            for hh in range(4):
                P.mm(psO[:, hh * 128:(hh + 1) * 128], pT[:, hh, :], vcmp[:, g, :], True, True)
            for hh in range(4):
                col = 3 * (4 * g + hh)
                P.act(oacc[par][:, hh, :], psO[:, hh * 128:(hh + 1) * 128], AF.Copy, scale=g3[:, qb, col:col + 1])

        def attn(qb, kT, kofs, vbuf, vofs, kbs, masks, gcol, final):
            par = qb % 2
            n = len(kbs)
            q3 = qT.v(qT.ap[:, qb, :, :].rearrange("p h t -> p (h t)"))
            pss = {}

            def stA(i):
                kb = kbs[i]
                ps = psS[i % 3]
                pss[i] = ps
                mk = masks.get(kb)
                P.mm(ps, kT[:, (kb - kofs) * 128:(kb - kofs + 1) * 128], q3, True, mk is None)
                if mk is not None:
                    P.mm(ps, mk[0], mk[1], False, True)
                P.act(PT[i % 2], ps, AF.Exp)

            def stB(i):
                kb = kbs[i]
                for hh in range(4):
                    acc = psV[hh // 2]
                    o = (hh % 2) * 130
                    P.mm(acc[:, o:o + 129], PT[i % 2][:, hh * 128:(hh + 1) * 128], vbuf[:, kb - vofs, 0:129],
                         i == 0 and hh % 2 == 0, i == n - 1, sgc=True)

            stA(0)
            for i in range(n):
                if i + 1 < n:
                    stA(i + 1)
                stB(i)
            for hh in range(4):
                acc = psV[hh // 2]
                o = (hh % 2) * 130
                P.copy("dve", coef[:, hh:hh + 1], acc[:, o + 128:o + 129])
            P.ts("dve", coef[:, 0:4], coef[:, 0:4], 1e-30, None, ALU.max)
            P.recip(coef[:, 4:8], coef[:, 0:4])
            base = 3 * 4 * g + gcol
            P.tt("dve", coef[:, 4:8], coef[:, 4:8], g3[:, qb, base:base + 10:3], ALU.mult)
            for hh in range(4):
                acc = psV[hh // 2]
                o = (hh % 2) * 130
                dst = o16[:, hh, :] if final else oacc[par][:, hh, :]
                P.stt("dve", dst, acc[:, o:o + 128], coef[:, 4 + hh:5 + hh], oacc[par][:, hh, :], ALU.mult, ALU.add)

        cmp_stage(0)
        cmp_stage_b(0)
        for qb in range(NB):
            if qb + 1 < NB:
                cmp_stage(qb + 1)
            ub = 8 + qb
            par = qb % 2
            negbc = negT[par].v(negT[par].ap.rearrange("p h t -> p (h t)"))
            masks = {kb: (Emat[:, kb * 128:(kb + 1) * 128], negbc) for kb in range(0, ub)}
            masks[ub] = (ident, causneg)
            attn(qb, ksT, 0, vs, 0, list(range(0, ub + 1)), masks, 1, False)
            if qb + 1 < NB:
                cmp_stage_b(qb + 1)
            masks = {ub - 4: (ident, winneg), ub: (ident, causneg)}
            attn(qb, kwT, 4, vw, 4, list(range(ub - 4, ub + 1)), masks, 2, True)
            for hh in range(4):
                P.transpose(psT[:, (5 + hh % 2) * 128:(6 + hh % 2) * 128], o16[:, hh, :], ident)
                if hh % 2 == 1:
                    evac(o_nsaT[:, 4 * g + hh - 1:4 * g + hh + 1, qb * 128:(qb + 1) * 128],
                         psT.v(psT.ap[:, 640:896].rearrange("p (a b) -> p a b", b=128)))
        if stop_after == "nsa_g0":
            break
    if "o_nsaT" in k.dbg_d:
        tmpf = A.alloc("dbgtmp", [128, TOK], F32)
        for hh in range(4):
            P.copy("dve", tmpf, o_nsaT[:, hh, :])
            k.dbg_store("o_nsaT", tmpf, k.dbg_d["o_nsaT"][:, hh, :])


def _rotary(k, ps128, cos, sin, out_even_odd, tmp):
    P = k.P
    x1 = ps128.v(ps128.ap.rearrange("p (i two) -> p i two", two=2)[:, :, 0])
    x2 = ps128.v(ps128.ap.rearrange("p (i two) -> p i two", two=2)[:, :, 1])
    o1 = out_even_odd.v(out_even_odd.ap.rearrange("p (i two) -> p i two", two=2)[:, :, 0])
    o2 = out_even_odd.v(out_even_odd.ap.rearrange("p (i two) -> p i two", two=2)[:, :, 1])
    P.tt("dve", tmp[:, 0, :], x1, cos, ALU.mult)
    P.tt("dve", tmp[:, 1, :], x2, sin, ALU.mult)
    P.tt("dve", tmp[:, 2, :], x1, sin, ALU.mult)
    P.tt("dve", tmp[:, 3, :], x2, cos, ALU.mult)
    P.tt(POOL, o1, tmp[:, 0, :], tmp[:, 1, :], ALU.subtract)
    P.tt(POOL, o2, tmp[:, 2, :], tmp[:, 3, :], ALU.add)


def build_ret_ctx(k):
    P, A, D, WS = k.P, k.R34, k.D, k.WS
    nT_ctx = k.nT_ctx
    state8 = k.state8
    ropek = A.alloc("ropek", [128, 16, 128], F32)
    wkc = A.alloc("wkc", [128, 8, 8], F32)
    P.dma("sp", ropek, D["ropek"][:, :, :])
    P.dma("sp", wkc, D["wkc"][:, :, :])
    Kr = [A.alloc("Kr%d" % i, [128, 128], F32) for i in range(2)]
    Ks = [A.alloc("Ks%d" % i, [128, 128], BF16) for i in range(2)]
    V = [A.alloc("V%d" % i, [128, 256], BF16) for i in range(2)]
    tmp = [A.alloc("rtmp%d" % i, [128, 4, 64], F32) for i in range(2)]
    tiles = [("wr_c%d" % h,) + wtile_cols(D["w_in"], OR_ + 512 * h, 512) for h in range(8)]
    for h in range(8):
        wt = WS.get(tiles, h)
        for cb in range(NB):
            par = cb % 2
            ps = k.next_ps()
            for kc in range(KCH):
                P.mm(ps[:, 0:384], nT_ctx[:, kc, cb * 128:(cb + 1) * 128], wt[:, kc, 128:512], kc == 0, kc == KCH - 1)
            _rotary(k, ps[:, 0:128], ropek[:, cb, 0:64], ropek[:, cb, 64:128], Kr[par], tmp[par])
            P.ts(POOL, Ks[par], Kr[par], wkc[:, cb, h:h + 1], None, ALU.mult)
            P.copy("act", V[par], ps[:, 128:384])
            P.mm(k.psV[h % 2][:, 0:256], Ks[par], V[par], cb == 0, cb == NB - 1)
        P.copy("act", state8[:, h, :], k.psV[h % 2][:, 0:256])


def build_ret_own(k):
    P, A, D, WS = k.P, k.R34, k.D, k.WS
    nT_own = k.nT_own
    state8 = k.state8
    psT = k.psT
    ident = k.ident
    o_retT = A.alloc("o_retT", [128, 16, TOK], BF16)
    k.o_retT = o_retT
    k.m_after_o = A.mark()
    ropek = A.alloc("ropek", [128, 16, 128], F32)
    ropeq = A.alloc("ropeq", [128, 8, 128], F32)
    decayT = A.alloc("decayT", [128, 8, 128], F32)
    wqB = A.alloc("wqB", [128, 8, 128], F32)
    wk = A.alloc("wk", [128, 8], F32)
    gnB = k.load_gain("ret_gn_w", A)
    P.dma("sp", ropek, D["ropek"][:, :, :])
    P.dma("sp", ropeq, D["ropeq"][:, :, :])
    P.dma("sp", decayT, D["decayT"][:, :, :])
    P.dma("sp", wqB, D["wqB"][:, :, :])
    P.dma("sp", wk, D["wk"][:, :])
    def mk_bufs(j):
        B = {}
        for nm, shp, dt in (("Qr", [128, 128], BF16), ("Kr", [128, 128], F32), ("Kb", [128, 128], BF16),
                            ("Ks", [128, 128], BF16), ("V", [128, 256], BF16), ("sg", [128, 256], F32),
                            ("tmp", [128, 4, 64], F32)):
            B[nm] = [A.alloc("%s%d_%d" % (nm, j, i), shp, dt) for i in range(2)]
        for nm, shp, dt in (("QT", [128, 128], BF16), ("KT", [128, 128], BF16), ("QsT", [128, 128], BF16),
                            ("SdT", [128, 128], BF16), ("stbf", [128, 256], BF16), ("osb", [128, 256], F32),
                            ("junk", [128, 256], BF16), ("y", [128, 256], F32), ("y16", [128, 256], BF16),
                            ("gs", [128, 8], F32)):
            B[nm] = A.alloc("%s%d" % (nm, j), shp, dt)
        return B

    bufs = [mk_bufs(0), mk_bufs(1)]
    tiles = []
    for p in range(4):
        tiles.append(("wr_o%d" % (2 * p),) + wtile_cols(D["w_in"], OR_ + 512 * (2 * p), 512))
        tiles.append(("wr_o%d" % (2 * p + 1),) + wtile_cols(D["w_in"], OR_ + 512 * (2 * p + 1), 512))
        tiles.append(("wgr%d" % p,) + wtile_cols(D["w_in"], OGR + 512 * p, 512))

    def run2(sa, sb):
        n = max(len(sa), len(sb))
        for i in range(n):
            if i < len(sa):
                sa[i]()
            if i < len(sb):
                sb[i]()

    for p in range(4):
        wts = [WS.get(tiles, 3 * p + j, depth=0) for j in range(2)]
        wg = WS.get(tiles, 3 * p + 2, depth=0)

        def stage1_steps(j, ob):
            B = bufs[j]
            h = 2 * p + j
            wt = wts[j]
            par = ob % 2
            ub = 8 + ob
            st = {}
            steps = []

            def s_proj():
                st["ps"] = k.next_ps()
                for kc in range(KCH):
                    P.mm(st["ps"], nT_own[:, kc, ob * 128:(ob + 1) * 128], wt[:, kc, 0:512], kc == 0, kc == KCH - 1)
            steps.append(s_proj)

            def s_projg():
                st["psg"] = k.next_ps()
                for kc in range(KCH):
                    P.mm(st["psg"][:, 0:256], nT_own[:, kc, ob * 128:(ob + 1) * 128],
                         wg[:, kc, j * 256:j * 256 + 256], kc == 0, kc == KCH - 1)
            steps.append(s_projg)
            steps.append(lambda: _rotary(k, st["ps"][:, 0:128], ropeq[:, ob, 0:64], ropeq[:, ob, 64:128],
                                         B["Qr"][par], B["tmp"][par]))
            steps.append(lambda: P.copy("act", B["V"][par], st["ps"][:, 256:512]))
            steps.append(lambda: _rotary(k, st["ps"][:, 128:256], ropek[:, ub, 0:64], ropek[:, ub, 64:128],
                                         B["Kr"][par], B["tmp"][par]))
            steps.append(lambda: P.act(B["sg"][par], st["psg"][:, 0:256], AF.Silu))
            steps.append(lambda: P.copy(POOL, B["Kb"][par], B["Kr"][par]))
            steps.append(lambda: P.ts(POOL, B["Ks"][par], B["Kr"][par], wk[:, h:h + 1], None, ALU.mult))
            return steps

        def stage2_steps(j, ob):
            B = bufs[j]
            h = 2 * p + j
            par = ob % 2
            c0 = j * 512
            st = {}
            gs = B["gs"]
            steps = []

            def s_tr():
                P.transpose(psT[:, c0:c0 + 128], B["Qr"][par], ident)
                P.transpose(psT[:, c0 + 128:c0 + 256], B["Kb"][par], ident)
            steps.append(s_tr)
            steps.append(lambda: P.copy("act", B["QT"], psT[:, c0:c0 + 128]))
            steps.append(lambda: P.copy("act", B["KT"], psT[:, c0 + 128:c0 + 256]))
            steps.append(lambda: P.tt("dve", B["QsT"], psT[:, c0:c0 + 128], wqB[:, h, :], ALU.mult))

            def s_S():
                st["ps"] = k.next_ps()
                P.mm(st["ps"][:, 0:128], B["KT"], B["QT"], True, True)
            steps.append(s_S)
            steps.append(lambda: P.tt("dve", B["SdT"], st["ps"][:, 0:128], decayT[:, h, :], ALU.mult))
            steps.append(lambda: P.copy("act", B["stbf"], state8[:, h, :]))

            def s_o():
                st["po"] = k.next_ps()
                P.mm(st["po"][:, 0:256], B["SdT"], B["V"][par], True, False)
                P.mm(st["po"][:, 0:256], B["QsT"], B["stbf"], False, True)
            steps.append(s_o)
            steps.append(lambda: P.memset("dve", gs, 0.0))
            steps.append(lambda: P.act(B["osb"], st["po"][:, 0:256], AF.Copy, accum_out=gs[:, 0:1]))
            steps.append(lambda: P.act(B["junk"], st["po"][:, 0:256], AF.Square, accum_out=gs[:, 1:2]))

            def s_state():
                st["ps3"] = k.next_ps()
                P.mm(st["ps3"][:, 0:256], B["Ks"][par], B["V"][par], True, True)
            steps.append(s_state)
            steps.append(lambda: P.stt("dve", state8[:, h, :], state8[:, h, :], _G_CHUNK[h], st["ps3"][:, 0:256],
                                       ALU.mult, ALU.add))
            steps.append(lambda: P.ts("dve", gs[:, 2:3], gs[:, 0:1], 1.0 / 256, None, ALU.mult))
            steps.append(lambda: P.tt("dve", gs[:, 3:4], gs[:, 2:3], gs[:, 2:3], ALU.mult))
            steps.append(lambda: P.stt("dve", gs[:, 4:5], gs[:, 1:2], 1.0 / 256, gs[:, 3:4], ALU.mult, ALU.subtract))
            steps.append(lambda: P.ts("dve", gs[:, 4:5], gs[:, 4:5], EPS, None, ALU.add))
            steps.append(lambda: P.act(gs[:, 5:6], gs[:, 4:5], AF.Sqrt))
            steps.append(lambda: P.recip(gs[:, 6:7], gs[:, 5:6]))
            steps.append(lambda: P.ts("dve", B["y"], B["osb"], gs[:, 2:3], gs[:, 6:7], ALU.subtract, ALU.mult))
            steps.append(lambda: P.tt(POOL, B["y"], B["y"], gnB[:, h * 256:(h + 1) * 256], ALU.mult))
            steps.append(lambda: P.tt(POOL, B["y16"], B["y"], B["sg"][par], ALU.mult))

            def s_tr2():
                for jj in range(2):
                    P.transpose(psT[:, c0 + (2 + jj) * 128:c0 + (3 + jj) * 128], B["y16"][:, jj * 128:(jj + 1) * 128], ident)
            steps.append(s_tr2)
            steps.append(lambda: k.evac(o_retT[:, 2 * h:2 * h + 2, ob * 128:(ob + 1) * 128],
                                        psT.v(psT.ap[:, c0 + 256:c0 + 512].rearrange("p (a b) -> p a b", b=128))))
            return steps

        run2(stage1_steps(0, 0), stage1_steps(1, 0))
        for ob in range(NB):
            if ob + 1 < NB:
                run2(stage1_steps(0, ob + 1), stage1_steps(1, ob + 1))
            run2(stage2_steps(0, ob), stage2_steps(1, ob))
    if "o_retT" in k.dbg_d:
        tmpf = A.alloc("dbgtmp", [128, 2, TOK], F32)
        P.copy("dve", tmpf, o_retT[:, 0:2, :])
        k.dbg_store("o_retT", tmpf)


def build_merge(k, which, srcT):
    P, A, D, WS = k.P, k.R34, k.D, k.WS
    nT_own, mergedT = k.nT_own, k.mergedT
    W = D["w_a"] if which == "a" else D["w_b"]
    og = OGA if which == "a" else OGB
    sig = [A.alloc("sig%d" % i, [128, 512], F32) for i in range(2)]
    tmp = [A.alloc("mtmp%d" % i, [128, 512], F32) for i in range(2)]
    tiles = []
    for i in range(4):
        tiles.append(("wm%s%d" % (which, i),) + wtile_cols(W, 512 * i, 512))
        tiles.append(("wgt%s%d" % (which, i),) + wtile_cols(D["w_in"], og + 512 * i, 512))
    n = 0
    for i in range(4):
        wm = WS.get(tiles, 2 * i, depth=1)
        wg = WS.get(tiles, 2 * i + 1, depth=1)
        for cc in range(4):
            for tch in range(2):
                psA = k.next_ps()
                for kc in range(KCH):
                    P.mm(psA, wm[:, kc, cc * 128:(cc + 1) * 128], srcT[:, kc, tch * 512:(tch + 1) * 512],
                         kc == 0, kc == KCH - 1)
                psG = k.next_ps()
                for kc in range(KCH):
                    P.mm(psG, wg[:, kc, cc * 128:(cc + 1) * 128], nT_own[:, kc, tch * 512:(tch + 1) * 512],
                         kc == 0, kc == KCH - 1)
                par = n % 2
                n += 1
                P.act(sig[par], psG, AF.Sigmoid)
                dst = mergedT[:, 4 * i + cc, tch * 512:(tch + 1) * 512]
                if which == "a":
                    P.tt("dve", dst, sig[par], psA, ALU.mult)
                else:
                    P.tt("dve", tmp[par], sig[par], psA, ALU.mult)
                    P.tt(POOL, dst, tmp[par], dst, ALU.add)
    if ("mergedT_" + which) in k.dbg_d:
        tmpf = A.alloc("dbgtmp", [128, 4, TOK], F32)
        P.copy("dve", tmpf, mergedT[:, 0:4, :])
        k.dbg_store("mergedT_" + which, tmpf)


def build_tail(k, stop_after):
    P, D, WS = k.P, k.D, k.WS
    R1, R2, R34 = k.R1, k.R2, k.R34
    psS, psV, psT, ident = k.psS, k.psV, k.psT, k.ident
    mergedT = k.mergedT
    hb = [R34.alloc("h%d" % tb, [128, DM], F32) for tb in range(NB)]
    for tb in range(NB):
        P.dma("sp", hb[tb], D["xo"][tb * 128:(tb + 1) * 128, :])
    tiles = [("wout%d" % i,) + wtile_cols(D["w_out"], 512 * i, 512) for i in range(4)]
    for i in range(4):
        wt = WS.get(tiles, i)
        for tb in range(NB):
            ps = k.next_ps()
            for kc in range(KCH):
                P.mm(ps, mergedT[:, kc, tb * 128:(tb + 1) * 128], wt[:, kc, :], kc == 0, kc == KCH - 1)
            hs = hb[tb][:, 512 * i:512 * (i + 1)]
            P.tt("dve", hs, hs, ps, ALU.add)
    if "h1" in k.dbg_d:
        for tb in range(NB):
            k.dbg_store("h1", hb[tb], k.dbg_d["h1"][tb * 128:(tb + 1) * 128, :])
    R2.release()
    if stop_after == "h1":
        return

    nxT = R1.alloc("nxT", [128, KCH, TOK], BF16)
    gainX = k.load_gain("x_norm_w", R2)
    xn = R2.alloc("xn", [128, DM], BF16)
    stat = R2.alloc("stat", [128, 16], F32)
    k.norm_T(lambda tb: hb[tb], NB, gainX, nxT, xn, stat)
    P.dma("sp", gainX, D["mem_norm_w"].to_broadcast([128, DM]))
    mh = R34.mark()
    mts = [R34.alloc("mt%d" % i, [128, DM], F32) for i in range(2)]
    mT = R2.alloc("mT", [128, KCH, 256], BF16)
    k.norm_T(k.mk_get(D["mem"], mts), 2, gainX, mT, xn, stat)
    R34.release(mh)
    qxT = R2.alloc("qxT", [128, 4, TOK], BF16)
    kxT = R34.alloc("kxT", [128, 4, 256], BF16)
    vx = R34.alloc("vx", [128, 2, 4, 130], BF16)
    PTx = [R34.alloc("PTx%d" % i, [128, 512], BF16) for i in range(2)]
    xc = R34.alloc("xcoef", [128, 4], F32)
    tiles = [("wqx",) + wtile_cols(D["wq_x"], 0, 512), ("wkx",) + wtile_cols(D["wk_x"], 0, 512),
             ("wvx",) + wtile_cols(D["wv_x"], 0, 512), ("wox",) + wtile_rows(D["wo_x"], 0, 512)]
    wq = WS.get(tiles, 0)
    for hh in range(4):
        for tch in range(2):
            k.proj_T(wq, hh * 128, nxT, tch * 512, 512, qxT[:, hh, tch * 512:(tch + 1) * 512], scale=QSCALE)
    wk_ = WS.get(tiles, 1)
    for hh in range(4):
        k.proj_T(wk_, hh * 128, mT, 0, 256, kxT[:, hh, :])
    wv = WS.get(tiles, 2)
    P.memset("dve", vx, 1.0)
    for mb in range(2):
        ps = k.next_ps()
        for kc in range(KCH):
            P.mm(ps, mT[:, kc, mb * 128:(mb + 1) * 128], wv[:, kc, :], kc == 0, kc == KCH - 1)
        k.evac(vx[:, mb, :, 0:128], ps.v(ps.ap.rearrange("p (a b) -> p a b", b=128)))
    R1.release()
    ox16 = R1.alloc("ox16", [128, NB, 512], BF16)
    oxT = R1.alloc("oxT", [128, 4, TOK], BF16)
    for hh in range(4):
        for tch in range(2):
            for mb in range(2):
                ps = psS[mb]
                P.mm(ps, kxT[:, hh, mb * 128:(mb + 1) * 128], qxT[:, hh, tch * 512:(tch + 1) * 512], True, True)
                P.act(PTx[mb], ps, AF.Exp)
            for tq in range(4):
                tb = tch * 4 + tq
                acc = psV[tq % 2]
                for mb in range(2):
                    P.mm(acc[:, 0:129], PTx[mb][:, tq * 128:(tq + 1) * 128], vx[:, mb, hh, 0:129], mb == 0, mb == 1)
                P.copy("dve", xc[:, 0:1], acc[:, 128:129])
                P.recip(xc[:, 1:2], xc[:, 0:1])
                P.ts("dve", ox16[:, tb, hh * 128:(hh + 1) * 128], acc[:, 0:128], xc[:, 1:2], None, ALU.mult)
    for tb in range(NB):
        for j in range(4):
            P.transpose(psT[:, j * 128:(j + 1) * 128], ox16[:, tb, j * 128:(j + 1) * 128], ident)
        k.evac(oxT[:, :, tb * 128:(tb + 1) * 128], psT.v(psT.ap[:, 0:512].rearrange("p (a b) -> p a b", b=128)))
    wo = WS.get(tiles, 3)
    for tb in range(NB):
        for cc in range(4):
            ps = k.next_ps()
            for kc in range(4):
                P.mm(ps, oxT[:, kc, tb * 128:(tb + 1) * 128], wo[:, kc, cc * 512:(cc + 1) * 512], kc == 0, kc == 3)
            hs = hb[tb][:, 512 * cc:512 * (cc + 1)]
            P.tt("dve", hs, hs, ps, ALU.add)
    if "h2" in k.dbg_d:
        for tb in range(NB):
            k.dbg_store("h2", hb[tb], k.dbg_d["h2"][tb * 128:(tb + 1) * 128, :])
    R1.release()
    R2.release()
    R34.release(mh)
    if stop_after == "h2":
        return

    nmT = R1.alloc("nmT", [128, KCH, TOK], BF16)
    gainM = k.load_gain("mlp_norm_w", R2)
    xn = R2.alloc("xn", [128, DM], BF16)
    stat = R2.alloc("stat", [128, 16], F32)
    k.norm_T(lambda tb: hb[tb], NB, gainM, nmT, xn, stat)
    aT = R2.alloc("aT", [128, 4, TOK], BF16)
    rl = [R2.alloc("rl%d" % i, [128, 512], F32) for i in range(2)]
    tiles = []
    for f in range(16):
        tiles.append(("wup%d" % f,) + wtile_cols(D["w_up"], 512 * f, 512))
        tiles.append(("wdn%d" % f,) + wtile_rows(D["w_down"], 512 * f, 512))
    n = 0
    for f in range(16):
        wu = WS.get(tiles, 2 * f)
        for cc in range(4):
            for tch in range(2):
                ps = k.next_ps()
                for kc in range(KCH):
                    P.mm(ps, wu[:, kc, cc * 128:(cc + 1) * 128], nmT[:, kc, tch * 512:(tch + 1) * 512],
                         kc == 0, kc == KCH - 1)
                par = n % 2
                n += 1
                P.act(rl[par], ps, AF.Relu)
                P.tt(POOL, aT[:, cc, tch * 512:(tch + 1) * 512], rl[par], rl[par], ALU.mult)
        wd = WS.get(tiles, 2 * f + 1)
        for tb in range(NB):
            for cc in range(4):
                ps = k.next_ps()
                for kc in range(4):
                    P.mm(ps, aT[:, kc, tb * 128:(tb + 1) * 128], wd[:, kc, cc * 512:(cc + 1) * 512], kc == 0, kc == 3)
                hs = hb[tb][:, 512 * cc:512 * (cc + 1)]
                P.tt("dve", hs, hs, ps, ALU.add)
    if "h3" in k.dbg_d:
        for tb in range(NB):
            k.dbg_store("h3", hb[tb], k.dbg_d["h3"][tb * 128:(tb + 1) * 128, :])
    R1.release()
    R2.release()

    gainF = k.load_gain("final_norm_w", R2)
    outt = [R2.alloc("outt%d" % i, [128, DM], F32) for i in range(2)]
    junk = R2.alloc("junkf", [128, DM], BF16)
    stat = R2.alloc("statf", [128, 16], F32)
    P.memset("dve", stat, 0.0)
    for tb in range(NB):
        P.act(junk, hb[tb], AF.Square, accum_out=stat[:, tb:tb + 1])
        P.ts("dve", stat[:, tb:tb + 1], stat[:, tb:tb + 1], 1.0 / DM, EPS, ALU.mult, ALU.add)
        P.act(stat[:, tb:tb + 1], stat[:, tb:tb + 1], AF.Sqrt)
        P.recip(stat[:, tb:tb + 1], stat[:, tb:tb + 1])
        P.stt("dve", outt[tb % 2], hb[tb], stat[:, tb:tb + 1], gainF, ALU.mult, ALU.mult)
        P.dma("sp", k.out_d[tb * 128:(tb + 1) * 128, :], outt[tb % 2], semkey="dma:store%d" % (tb % 2))


def _w_in_perm():
    q = np.arange(0, 2048)
    parts = [q]
    for g in range(4):
        for base in (3072, 3584, 4096, 4608):
            parts.append(base + 128 * g + np.arange(128))
    parts.append(2048 + np.arange(512))
    parts.append(2560 + np.arange(512))
    for h in range(8):
        parts.append(5168 + 128 * h + np.arange(128))
        parts.append(6192 + 128 * h + np.arange(128))
        parts.append(7216 + 256 * h + np.arange(256))
    parts.append(np.arange(9264, 15408))
    parts.append(5120 + np.arange(48))
    perm = np.concatenate(parts)
    assert perm.shape[0] == IN_WIDTH and len(set(perm.tolist())) == IN_WIDTH
    return perm


def _tables(s):
    f32 = np.float32
    T = {}
    T["ident"] = np.eye(128, dtype=f32)
    u = np.arange(2048)
    t = u - 1024 + 1024 * s
    inv = (10000.0 ** (-np.arange(0, 128, 2, dtype=f32) / f32(128))).astype(f32)
    ang = t.astype(f32)[:, None] * inv[None, :]
    cos, sin = np.cos(ang).astype(f32), np.sin(ang).astype(f32)
    rk = np.concatenate([cos, sin], axis=1) * f32(128 ** -0.5)
    T["ropek"] = np.ascontiguousarray(rk.reshape(16, 128, 128).transpose(1, 0, 2)).astype(f32)
    rq = np.concatenate([cos, sin], axis=1)[1024:]
    T["ropeq"] = np.ascontiguousarray(rq.reshape(8, 128, 128).transpose(1, 0, 2)).astype(f32)
    H = 8
    log_g = np.log1p(-np.exp2(-5.0 - np.arange(H, dtype=f32))).astype(f32)
    i = np.arange(128, dtype=f32)
    rel = i[:, None] - i[None, :]
    decay = np.where(rel >= 0, np.exp(log_g[:, None, None] * np.maximum(rel, 0.0)), 0.0).astype(f32)
    T["decayT"] = np.ascontiguousarray(decay.transpose(2, 0, 1))
    w_q = np.exp(log_g[:, None] * (i + 1.0)[None, :]).astype(f32)
    T["wqB"] = np.ascontiguousarray(np.broadcast_to(w_q[None], (128, H, 128))).astype(f32)
    w_k = np.exp(log_g[:, None] * (127.0 - i)[None, :]).astype(f32)
    T["wk"] = np.ascontiguousarray(w_k.T)
    T["g_chunk"] = np.exp(log_g * f32(128.0)).astype(f32)
    gpow = np.stack([T["g_chunk"] ** f32(7 - blk) for blk in range(8)], 0).astype(f32)
    T["wkc"] = np.ascontiguousarray((w_k.T[:, None, :] * gpow[None, :, :]).astype(f32))
    uq = 1024 + np.arange(1024)
    lc = np.arange(128)
    vis = (16 * lc[None, :] + 31 <= uq[:, None]) & (lc[None, :] < 127)
    if s == 0:
        vis &= (lc[None, :] >= 64)
    T["cmask"] = np.ascontiguousarray(vis.astype(f32).reshape(8, 128, 128).transpose(1, 0, 2))
    lj = np.arange(32)
    lcur = (uq // 64)[:, None]
    first = 0 if s == 1 else 16
    am = np.zeros((1024, 32), f32)
    am[np.broadcast_to(lj[None, :] == first, am.shape)] = 1e9
    prev = (lj[None, :] == lcur - 1) & (lj[None, :] >= first)
    am[prev] = 2e9
    am[np.broadcast_to(lj[None, :], am.shape) == lcur] = 3e9
    am[(lj[None, :] > lcur) | (lj[None, :] < first)] = -1e9
    T["addmask"] = np.ascontiguousarray(am.reshape(8, 128, 32).transpose(1, 0, 2))
    kval = (t >= 0).astype(f32)
    T["kvalid"] = np.ascontiguousarray(kval.reshape(16, 128).T)
    kk = np.arange(2048)
    T["Emat"] = (kk[None, :] // 64 == np.arange(128)[:, None]).astype(f32)
    kq = np.arange(128)
    T["causneg"] = np.tile(np.where(kq[:, None] > kq[None, :], NEGB, 0.0).astype(f32), (1, 4))
    T["winneg"] = np.tile(np.where(kq[:, None] <= kq[None, :], NEGB, 0.0).astype(f32), (1, 4))
    return T


def make_in_maps(inputs):
    f32 = np.float32
    x = np.asarray(inputs["x"], f32)
    mem = np.asarray(inputs["mem"], f32)
    perm = _w_in_perm()
    shared = {}
    shared["w_in"] = np.ascontiguousarray(np.asarray(inputs["w_in"], f32)[0][:, perm])
    for nm in ("cmp_w1_k", "cmp_w1_v", "cmp_w2_k", "cmp_w2_v", "w_a", "w_b", "w_out", "wq_x", "wk_x", "wv_x",
               "wo_x", "w_up", "w_down"):
        shared[nm] = np.ascontiguousarray(np.asarray(inputs[nm], f32)[0])
    shared["peT_k"] = np.ascontiguousarray(np.asarray(inputs["cmp_pe_k"], f32)[0].T)
    shared["peT_v"] = np.ascontiguousarray(np.asarray(inputs["cmp_pe_v"], f32)[0].T)
    for nm in ("attn_norm_w", "ret_gn_w", "x_norm_w", "mem_norm_w", "mlp_norm_w"):
        shared[nm] = np.ascontiguousarray(np.asarray(inputs[nm], f32).reshape(1, DM))
    shared["final_norm_w"] = np.ascontiguousarray(np.asarray(inputs["final_norm_w"], f32).reshape(1, DM))
    tabs = [_tables(0), _tables(1)]
    zeros = np.zeros((TOK, DM), f32)
    in_maps = []
    for c in range(8):
        b, s = c // 2, c % 2
        m = dict(shared)
        m["xo"] = np.ascontiguousarray(x[b, 1024 * s:1024 * (s + 1)])
        m["xc"] = np.ascontiguousarray(x[b, 0:1024]) if s == 1 else zeros
        m["mem"] = np.ascontiguousarray(mem[b])
        for kk, v in tabs[s].items():
            if kk != "g_chunk":
                m[kk] = v
        in_maps.append(m)
    return in_maps


_G_CHUNK = [float(v) for v in np.exp(np.log1p(-np.exp2(-5.0 - np.arange(8, dtype=np.float32))).astype(np.float32)
                                     * np.float32(128.0)).astype(np.float32)]


def kernel(**inputs):
    in_maps = make_in_maps(inputs)
    nc, st = build_program()
    res = run_bass_kernel_spmd(nc, in_maps, core_ids=list(range(8)))
    out = np.zeros((4, 2048, DM), np.float32)
    for c in range(8):
        b, s = c // 2, c % 2
        out[b, 1024 * s:1024 * (s + 1)] = res.results[c]["out"]
    return out
```

```python
import numpy as np
from concourse.bass_utils import run_bass_kernel_spmd
import concourse.bass as bass
import concourse.mybir as mybir

F32 = mybir.dt.float32
BF16 = mybir.dt.bfloat16
AF = mybir.ActivationFunctionType
ALU = mybir.AluOpType
AX = mybir.AxisListType

_DT_SIZE = {F32: 4, BF16: 2}


class Buf:
    def __init__(self, key, ap):
        self.key = key
        self.ap = ap

    def __getitem__(self, idx):
        return Buf(self.key, self.ap[idx])

    def v(self, ap):
        return Buf(self.key, ap)


class Op:
    __slots__ = ("eng", "fn", "reads", "writes", "is_dma", "semkey", "deps", "dma_deps",
                 "signal", "signum", "pos", "accum", "idx")


class Prog:
    ENGS = ("pe", "act", "dve", "pool", "sp")

    def __init__(self, nc):
        self.nc = nc
        self.ops = []
        self.eng_obj = {"pe": nc.tensor, "act": nc.scalar, "dve": nc.vector, "pool": nc.gpsimd, "sp": nc.sync}
        self.sync_same_engine_war = False

    def _add(self, eng, fn, reads, writes, is_dma=False, semkey=None, accum=False):
        o = Op()
        o.eng = eng
        o.fn = fn
        o.reads = [b.key for b in reads if b is not None]
        o.writes = [b.key for b in writes if b is not None]
        for kk in o.reads:
            if kk.startswith("ps") and kk not in o.writes:
                o.writes.append(kk)
        o.is_dma = is_dma
        o.semkey = semkey
        o.accum = accum
        o.idx = len(self.ops)
        self.ops.append(o)
        return o

    def op(self, eng, fn, reads=(), writes=()):
        return self._add(eng, fn, reads, writes)

    def barrier(self):
        o = Op()
        o.eng = None
        o.idx = len(self.ops)
        self.ops.append(o)

    def dma(self, eng, out, in_, semkey=None):
        reads, writes = [], []
        if isinstance(in_, Buf):
            reads.append(in_)
            in_ap = in_.ap
        else:
            in_ap = in_
        if isinstance(out, Buf):
            writes.append(out)
            out_ap = out.ap
            if semkey is None:
                semkey = "dma:" + out.key
        else:
            out_ap = out
            if semkey is None:
                semkey = "dma:store"

        def fn(e):
            return e.dma_start(out=out_ap, in_=in_ap)

        return self._add(eng, fn, reads, writes, is_dma=True, semkey=semkey)

    def mm(self, out, lhsT, rhs, start, stop, extra_reads=(), sgc=False):
        def fn(e):
            if sgc:
                return e.matmul(out.ap, lhsT.ap, rhs.ap, start=start, stop=stop, skip_group_check=True)
            return e.matmul(out.ap, lhsT.ap, rhs.ap, start=start, stop=stop)
        return self._add("pe", fn, [lhsT, rhs] + list(extra_reads), [out], accum=not start)

    def transpose(self, out, in_, ident):
        def fn(e):
            return e.transpose(out.ap, in_.ap, ident.ap)
        return self._add("pe", fn, [in_, ident], [out])

    def act(self, out, in_, func, bias=None, scale=1.0, accum_out=None, eng="act"):
        reads = [in_]
        kw = {}
        if isinstance(bias, Buf):
            reads.append(bias)
            kw["bias"] = bias.ap
        elif bias is not None:
            kw["bias"] = bias
        if isinstance(scale, Buf):
            reads.append(scale)
            kw["scale"] = scale.ap
        else:
            kw["scale"] = scale
        writes = [out]
        if accum_out is not None:
            writes.append(accum_out)
            kw["accum_out"] = accum_out.ap

        def fn(e):
            return e.activation(out=out.ap, in_=in_.ap, func=func, **kw)
        return self._add(eng, fn, reads, writes)

    def tt(self, eng, out, in0, in1, op):
        def fn(e):
            return e.tensor_tensor(out=out.ap, in0=in0.ap, in1=in1.ap, op=op)
        return self._add(eng, fn, [in0, in1], [out])

    def ts(self, eng, out, in0, s1, s2, op0, op1=None, accum_out=None):
        reads = [in0]
        a1 = s1
        a2 = s2
        if isinstance(s1, Buf):
            reads.append(s1)
            a1 = s1.ap
        if isinstance(s2, Buf):
            reads.append(s2)
            a2 = s2.ap
        writes = [out]
        kw = {}
        if op1 is not None:
            kw["op1"] = op1
        if accum_out is not None:
            writes.append(accum_out)
            kw["accum_out"] = accum_out.ap

        def fn(e):
            return e.tensor_scalar(out=out.ap, in0=in0.ap, scalar1=a1, scalar2=a2, op0=op0, **kw)
        return self._add(eng, fn, reads, writes)

    def stt(self, eng, out, in0, scalar, in1, op0, op1):
        reads = [in0, in1]
        a = scalar
        if isinstance(scalar, Buf):
            reads.append(scalar)
            a = scalar.ap

        def fn(e):
            return e.scalar_tensor_tensor(out=out.ap, in0=in0.ap, scalar=a, in1=in1.ap, op0=op0, op1=op1)
        return self._add(eng, fn, reads, [out])

    def copy(self, eng, out, in_):
        if eng == "act":
            def fn(e):
                return e.copy(out=out.ap, in_=in_.ap)
        else:
            def fn(e):
                return e.tensor_copy(out=out.ap, in_=in_.ap)
        return self._add(eng, fn, [in_], [out])

    def reduce(self, eng, out, in_, op, axis=AX.X):
        def fn(e):
            return e.tensor_reduce(out=out.ap, in_=in_.ap, axis=axis, op=op)
        return self._add(eng, fn, [in_], [out])

    def memset(self, eng, out, val):
        def fn(e):
            return e.memset(out.ap, val)
        return self._add(eng, fn, [], [out])

    def recip(self, out, in_):
        def fn(e):
            return e.reciprocal(out=out.ap, in_=in_.ap)
        return self._add("dve", fn, [in_], [out])

    def emit(self, final_wait_eng="sp"):
        nc = self.nc
        ops = self.ops
        last_writer = {}
        readers = {}
        pos_ctr = {e: 0 for e in self.ENGS}
        waited = {f: {e: -1 for e in self.ENGS} for f in self.ENGS}
        waited_dma = {f: {} for f in self.ENGS}
        dma_count = {}
        last_op_on = {e: None for e in self.ENGS}
        pending_barrier = {e: [] for e in self.ENGS}
        outstanding_dma = []

        for o in ops:
            if o.eng is None:
                for f in self.ENGS:
                    pending_barrier[f] = [last_op_on[e] for e in self.ENGS if e != f and last_op_on[e] is not None]
                continue
            f = o.eng
            deps = set()
            for k in o.reads:
                w = last_writer.get(k)
                if w is not None:
                    deps.add(w)
            for k in o.writes:
                w = last_writer.get(k)
                if w is not None:
                    deps.add(w)
                for r in readers.get(k, ()):
                    deps.add(r)
            for b in pending_barrier[f]:
                deps.add(b)
            pending_barrier[f] = []
            o.pos = pos_ctr[f]
            pos_ctr[f] += 1
            o.deps = []
            o.dma_deps = []
            o.signal = False
            for di in sorted(deps):
                d = ops[di]
                if d.idx == o.idx:
                    continue
                if d.is_dma:
                    cnt = d.signum
                    if waited_dma[f].get(d.semkey, 0) >= cnt:
                        continue
                    waited_dma[f][d.semkey] = cnt
                    o.dma_deps.append((d.semkey, cnt))
                else:
                    if d.eng == "pe" and f == "pe" and not o.is_dma:
                        continue
                    if d.eng == f and not self.sync_same_engine_war and not o.is_dma:
                        is_raw_waw = any(last_writer.get(k) == di for k in o.reads + o.writes)
                        if not is_raw_waw:
                            continue
                    if waited[f][d.eng] >= d.pos:
                        continue
                    waited[f][d.eng] = d.pos
                    d.signal = True
                    o.deps.append(di)
            if o.is_dma:
                dma_count[o.semkey] = dma_count.get(o.semkey, 0) + 16
                o.signum = dma_count[o.semkey]
                outstanding_dma.append(o)
            for k in o.reads:
                readers.setdefault(k, []).append(o.idx)
            for k in o.writes:
                last_writer[k] = o.idx
                readers[k] = []
            last_op_on[f] = o.idx

        tail_deps = []
        for e in self.ENGS:
            li = last_op_on[e]
            if li is not None and not ops[li].is_dma:
                ops[li].signal = True
                tail_deps.append(li)
        sig_ctr = {e: 0 for e in self.ENGS}
        for o in ops:
            if o.eng is None or o.is_dma:
                continue
            if o.signal:
                sig_ctr[o.eng] += 1
                o.signum = sig_ctr[o.eng]
        import contextlib
        es = contextlib.ExitStack()
        self._es = es
        sems = {e: es.enter_context(nc.semaphore("s_" + e)) for e in self.ENGS}
        dsems = {}
        for k in dma_count:
            dsems[k] = es.enter_context(nc.semaphore("d%d" % len(dsems)))
        n_wait = 0
        for o in ops:
            if o.eng is None:
                continue
            e = self.eng_obj[o.eng]
            for di in o.deps:
                d = ops[di]
                e.wait_ge(sems[d.eng], d.signum)
                n_wait += 1
            for (k, cnt) in o.dma_deps:
                e.wait_ge(dsems[k], cnt)
                n_wait += 1
            ins = o.fn(e)
            if o.is_dma:
                ins.then_inc(dsems[o.semkey], 16)
            elif o.signal:
                ins.then_inc(sems[o.eng], 1)
        fe = self.eng_obj[final_wait_eng]
        for di in tail_deps:
            d = ops[di]
            fe.wait_ge(sems[d.eng], d.signum)
        for k, cnt in dma_count.items():
            fe.wait_ge(dsems[k], cnt)
        self.stats = dict(n_ops=len(ops), n_wait=n_wait, sig=dict(sig_ctr), n_dsems=len(dsems))
        return self.stats


TOK = 1024
NB = 8
DM = 2048
KCH = 16
EPS = 1e-6
OQ, OKV, OC, OR_, OGR, OGA, OGB, OG = 0, 2048, 4096, 5120, 9216, 11264, 13312, 15360
IN_WIDTH = 15408
NEGB = -30000.0
QSCALE = 128 ** -0.5
POOL = "dve"


def _prod(xs):
    r = 1
    for x in xs:
        r *= int(x)
    return r


class Arena:
    uid = 0

    def __init__(self, full_ap, P, base, nel, name):
        self.ap = full_ap
        self.base = base
        self.top = 0
        self.nel = nel
        self.P = P
        self.peak = 0
        self.name = name

    def alloc(self, name, shape, dt):
        inner = _prod(shape[1:])
        n = inner * (2 if dt == F32 else 1)
        npad = (n + 15) // 16 * 16
        off = self.base + self.top
        self.top += npad
        self.peak = max(self.peak, self.top)
        assert self.top <= self.nel, ("SBUF arena overflow", self.name, name, self.top, self.nel)
        ap = self.ap[:shape[0], off:off + n]
        if dt == F32:
            ap = ap.bitcast(F32)
        if len(shape) == 3:
            ap = ap.rearrange("p (a b) -> p a b", b=shape[2])
        elif len(shape) == 4:
            ap = ap.rearrange("p (a b c) -> p a b c", b=shape[2], c=shape[3])
        Arena.uid += 1
        return Buf("%s#%d" % (name, Arena.uid), ap)

    def mark(self):
        return self.top

    def release(self, mark=0):
        self.top = mark
        self.P.barrier()


class WStream:
    def __init__(self, P, arena, nslots, slot_elems):
        self.P = P
        self.nslots = nslots
        self.slot_elems = slot_elems
        self.slots = [arena.alloc("wslot%d" % i, [128, slot_elems], BF16) for i in range(nslots)]
        self.ctr = 0
        self.loaded = {}

    def _load(self, item):
        key, src, shape = item
        if key in self.loaded:
            return
        s = self.slots[self.ctr % self.nslots]
        self.ctr += 1
        n = _prod(shape[1:])
        ap = s.ap[:, 0:n]
        if len(shape) == 3:
            ap = ap.rearrange("p (a b) -> p a b", b=shape[2])
        b = Buf(s.key, ap)
        self.P.dma("pool", b, src)
        self.loaded[key] = b

    def get(self, lst, i, depth=None):
        depth = self.nslots - 1 if depth is None else depth
        for j in range(i, min(len(lst), i + depth + 1)):
            self._load(lst[j])
        b = self.loaded.pop(lst[i][0])
        return b


def wtile_cols(w2d, c0, ncols):
    return w2d[:, c0:c0 + ncols].rearrange("(kc p) c -> p kc c", p=128), [128, 16, ncols]


def wtile_rows(w2d, r0, nrows):
    return w2d[r0:r0 + nrows, :].rearrange("(kc p) c -> p kc c", p=128), [128, nrows // 128, 2048]


class K:
    pass


def build_program(dbg=None, stop_after=None):
    nc = bass.Bass("TRN2", target_bir_lowering=False)
    P = Prog(nc)
    k = K()
    k.nc, k.P = nc, P

    def din(name, shape):
        return nc.dram_tensor(name, list(shape), F32, kind="ExternalInput").ap()

    D = {}
    D["xo"] = din("xo", [TOK, DM])
    D["xc"] = din("xc", [TOK, DM])
    D["mem"] = din("mem", [256, DM])
    D["w_in"] = din("w_in", [DM, IN_WIDTH])
    for nm in ("cmp_w1_k", "cmp_w1_v"):
        D[nm] = din(nm, [4096, 1024])
    for nm in ("cmp_w2_k", "cmp_w2_v"):
        D[nm] = din(nm, [1024, 128])
    for nm in ("peT_k", "peT_v"):
        D[nm] = din(nm, [128, 32])
    for nm in ("w_a", "w_b", "w_out"):
        D[nm] = din(nm, [DM, DM])
    for nm in ("wq_x", "wk_x", "wv_x"):
        D[nm] = din(nm, [DM, 512])
    D["wo_x"] = din("wo_x", [512, DM])
    D["w_up"] = din("w_up", [DM, 8192])
    D["w_down"] = din("w_down", [8192, DM])
    for nm in ("attn_norm_w", "ret_gn_w", "x_norm_w", "mem_norm_w", "mlp_norm_w", "final_norm_w"):
        D[nm] = din(nm, [1, DM])
    D["ident"] = din("ident", [128, 128])
    D["ropeq"] = din("ropeq", [128, 8, 128])
    D["ropek"] = din("ropek", [128, 16, 128])
    D["decayT"] = din("decayT", [128, 8, 128])
    D["wqB"] = din("wqB", [128, 8, 128])
    D["wk"] = din("wk", [128, 8])
    D["wkc"] = din("wkc", [128, 8, 8])
    D["cmask"] = din("cmask", [128, 8, 128])
    D["addmask"] = din("addmask", [128, 8, 32])
    D["kvalid"] = din("kvalid", [128, 16])
    D["Emat"] = din("Emat", [128, 2048])
    D["causneg"] = din("causneg", [128, 512])
    D["winneg"] = din("winneg", [128, 512])
    out_d = nc.dram_tensor("out", [TOK, DM], F32, kind="ExternalOutput").ap()
    dbg_d = {}
    if dbg:
        for nm, shp in dbg.items():
            dbg_d[nm] = nc.dram_tensor("dbg_" + nm, list(shp), F32, kind="ExternalOutput").ap()
    k.D, k.out_d, k.dbg_d = D, out_d, dbg_d

    NEL = 106000
    full = nc.alloc_sbuf_tensor("arena", [128, NEL], BF16).ap()
    R0 = Arena(full, P, 0, 30000, "R0")
    R1 = Arena(full, P, 30000, 16384, "R1")
    R2 = Arena(full, P, 46384, 16384, "R2")
    R34 = Arena(full, P, 62768, NEL - 62768, "R34")
    k.R0, k.R1, k.R2, k.R34 = R0, R1, R2, R34
    A = R0
    k.psS = [Buf("psS%d" % i, nc.alloc_psum_tensor("psS%d" % i, [128, 512], F32).ap()) for i in range(3)]
    k.psC = Buf("psC", nc.alloc_psum_tensor("psC", [128, 512], F32).ap())
    k.psO = Buf("psO", nc.alloc_psum_tensor("psO", [128, 512], F32).ap())
    k.psV = [Buf("psV%d" % i, nc.alloc_psum_tensor("psV%d" % i, [128, 512], F32).ap()) for i in range(2)]
    k.psT = Buf("psT", nc.alloc_psum_tensor("psT", [128, 1024], BF16).ap())
    k.rot5 = [k.psS[0], k.psS[1], k.psS[2], k.psC, k.psO]
    k.rot_i = 0
    k.ev_i = 0

    k.ident = A.alloc("ident", [128, 128], BF16)
    P.dma("pool", k.ident, D["ident"][:, :])
    k.WS = WStream(P, A, 3, 16 * 512)

    def finish():
        st = P.emit()
        st["arena_peak_el"] = [R0.peak, R1.peak, R2.peak, R34.peak]
        return nc, st

    def dbg_store(name, buf, dram_view=None):
        if name in dbg_d:
            dv = dbg_d[name] if dram_view is None else dram_view
            P.dma("sp", dv, buf, semkey="dma:dbg")

    k.dbg_store = dbg_store

    def next_ps():
        b = k.rot5[k.rot_i % 5]
        k.rot_i += 1
        return b

    def evac(out, in_, scale=None):
        e = k.ev_i % 2
        k.ev_i += 1
        if e == 0:
            if scale is None:
                P.copy("act", out, in_)
            else:
                P.act(out, in_, AF.Copy, scale=scale)
        else:
            if scale is None:
                P.copy("dve", out, in_)
            else:
                P.ts("dve", out, in_, scale, None, ALU.mult)

    k.next_ps, k.evac = next_ps, evac

    def load_gain(name, reg):
        g = reg.alloc("gain_" + name, [128, DM], F32)
        P.dma("sp", g, D[name].to_broadcast([128, DM]))
        return g

    def norm_T(get_block, nblk, gain, nT, xns, stat):
        P.memset("dve", stat, 0.0)
        if not isinstance(xns, (list, tuple)):
            xns = [xns]
        for tb in range(nblk):
            xn = xns[tb % len(xns)]
            xt = get_block(tb)
            P.act(xn, xt, AF.Square, accum_out=stat[:, tb:tb + 1])
            P.ts("dve", stat[:, tb:tb + 1], stat[:, tb:tb + 1], 1.0 / DM, EPS, ALU.mult, ALU.add)
            P.act(stat[:, tb:tb + 1], stat[:, tb:tb + 1], AF.Sqrt)
            P.recip(stat[:, tb:tb + 1], stat[:, tb:tb + 1])
            P.stt("dve", xn, xt, stat[:, tb:tb + 1], gain, ALU.mult, ALU.mult)
            for half in range(2):
                for j in range(8):
                    kc = half * 8 + j
                    P.transpose(k.psT[:, j * 128:(j + 1) * 128], xn[:, kc * 128:(kc + 1) * 128], k.ident)
                src = k.psT.v(k.psT.ap.rearrange("p (a b) -> p a b", b=128))
                evac(nT[:, half * 8:half * 8 + 8, tb * 128:(tb + 1) * 128], src)

    def proj_T(wt, c0, nT, t0, ntok, out, scale=None, nk=KCH, out_view3=False):
        ps = next_ps()
        for kc in range(nk):
            P.mm(ps[:, 0:ntok], wt[:, kc, c0:c0 + 128], nT[:, kc, t0:t0 + ntok], kc == 0, kc == nk - 1)
        src = ps[:, 0:ntok]
        if out_view3:
            src = ps.v(ps.ap[:, 0:ntok].rearrange("p (a b) -> p a b", b=128))
        evac(out, src, scale)

    k.norm_T, k.proj_T, k.load_gain = norm_T, proj_T, load_gain

    nT_own = R1.alloc("nT_own", [128, KCH, TOK], BF16)
    nT_ctx = R2.alloc("nT_ctx", [128, KCH, TOK], BF16)
    k.state8 = R0.alloc("state8", [128, 8, 256], F32)
    k.kcmpT = R0.alloc("kcmpT", [128, 4, 128], BF16)
    k.vcmp = R0.alloc("vcmp", [128, 4, 128], BF16)
    gainA = load_gain("attn_norm_w", R34)
    xts = [R34.alloc("xt%d" % i, [128, DM], F32) for i in range(2)]
    xn = [R34.alloc("xn%d" % i, [128, DM], BF16) for i in range(2)]
    stat = R34.alloc("stat", [128, 16], F32)

    def mk_get(src, xts):
        def get(tb):
            xt = xts[tb % 2]
            P.dma("sp", xt, src[tb * 128:(tb + 1) * 128, :])
            return xt
        return get

    k.mk_get = mk_get
    norm_T(mk_get(D["xc"], xts), NB, gainA, nT_ctx, xn, stat)
    norm_T(mk_get(D["xo"], xts), NB, gainA, nT_own, xn, stat)
    if "nT_own" in dbg_d:
        tmpf = R34.alloc("dbgtmp", [128, KCH, TOK // 4], F32)
        P.copy("dve", tmpf, nT_own[:, :, 0:TOK // 4])
        dbg_store("nT_own", tmpf)
    R34.release()
    if stop_after == "norm":
        return finish()
    k.nT_own, k.nT_ctx = nT_own, nT_ctx
    k.finish = finish

    build_ret_ctx(k)
    R34.release()
    if stop_after == "ret_ctx":
        return finish()
    build_nsa(k, stop_after)
    if stop_after and stop_after.startswith("nsa"):
        return finish()
    R34.release(k.m_after_o)
    R2.release()
    k.mergedT = R2.alloc("mergedT", [128, KCH, TOK], BF16)
    build_merge(k, "a", k.o_nsaT)
    R34.release()
    if stop_after == "merge_a":
        return finish()
    build_ret_own(k)
    if stop_after == "ret":
        return finish()
    R34.release(k.m_after_o)
    build_merge(k, "b", k.o_retT)
    R34.release()
    R1.release()
    if stop_after == "merge_b":
        return finish()
    build_tail(k, stop_after)
    return finish()


def build_nsa(k, stop_after):
    P, A, D, WS = k.P, k.R34, k.D, k.WS
    nT_own, nT_ctx = k.nT_own, k.nT_ctx
    psS, psC, psO, psV, psT = k.psS, k.psC, k.psO, k.psV, k.psT
    evac, next_ps, proj_T = k.evac, k.next_ps, k.proj_T
    ident = k.ident
    w_in = D["w_in"]
    uchunks = [(nT_ctx, 0, 0), (nT_ctx, 512, 512), (nT_own, 0, 1024), (nT_own, 512, 1536)]

    o_nsaT = A.alloc("o_nsaT", [128, 16, TOK], BF16)
    k.o_nsaT = o_nsaT
    k.m_after_o = A.mark()
    kcmpT, vcmp = k.kcmpT, k.vcmp
    P.memset("dve", kcmpT, 0.0)
    P.memset("dve", vcmp, 0.0)
    m1 = A.mark()
    kvT = A.alloc("kvcT", [128, 4, 2048], BF16)
    kvD = A.alloc("kvcD", [128, 4, 16, 128], BF16)
    hidT = A.alloc("hidT", [128, 8, 4, 128], BF16)
    peT = A.alloc("peT", [128, 32], BF16)
    w2 = A.alloc("w2", [128, 8, 128], BF16)
    cbias = A.alloc("cbias", [128, 8], F32)
    for kind in range(2):
        sfx = "_k" if kind == 0 else "_v"
        tiles = [("wc%d" % kind,) + wtile_cols(w_in, OC + 512 * kind, 512)]
        for hh in range(2):
            for lh in range(2):
                src = D["cmp_w1" + sfx][2048 * lh:2048 * (lh + 1), 512 * hh:512 * (hh + 1)].rearrange(
                    "(l p) c -> p l c", p=128)
                tiles.append(("w1%d_%d_%d" % (kind, hh, lh), src, [128, 16, 512]))
        P.dma("pool", peT, D["peT" + sfx][:, :])
        P.dma("pool", w2, D["cmp_w2" + sfx].rearrange("(hc p) c -> p hc c", p=128))
        wt = WS.get(tiles, 0)
        for g in range(4):
            for (nT, t0, u0) in uchunks:
                proj_T(wt, g * 128, nT, t0, 512, kvT[:, g, u0:u0 + 512])
        for g in range(4):
            src = kvT.v(kvT.ap[:, g, :].rearrange("p (j r) -> p r j", r=16))
            if g % 2 == 0:
                P.copy("act", kvD[:, g, :, :], src)
            else:
                P.copy("dve", kvD[:, g, :, :], src)
        ti = 1
        for hh in range(2):
            accs = [psS[0], psS[1], psS[2], psC]
            for lh in range(2):
                wt = WS.get(tiles, ti)
                ti += 1
                for li in range(16):
                    l = lh * 16 + li
                    for hc in range(4):
                        lhsT = wt[:, li, hc * 128:(hc + 1) * 128]
                        for g in range(4):
                            rhs = kvD[:, g, l, 0:127] if l < 16 else kvD[:, g, l - 16, 1:128]
                            P.mm(accs[hc][:, g * 127:(g + 1) * 127], lhsT, rhs, l == 0 and g == 0, l == 31, sgc=True)
                        P.mm(psO[:, hc:hc + 1], lhsT, peT[:, l:l + 1], l == 0 and hc == 0, l == 31, sgc=True)
            for hc in range(4):
                hcg = hh * 4 + hc
                P.copy("dve", cbias[:, hcg:hcg + 1], psO[:, hc:hc + 1])
                src = accs[hc].v(accs[hc].ap[:, 0:508].rearrange("p (g c) -> p g c", c=127))
                P.act(hidT[:, hcg, :, 0:127], src, AF.Silu, bias=cbias[:, hcg:hcg + 1])
        if kind == 0:
            ps = next_ps()
            for g in range(4):
                for hc in range(8):
                    P.mm(ps[:, g * 127:(g + 1) * 127], w2[:, hc, :], hidT[:, hc, g, 0:127], hc == 0, hc == 7)
            evac(kcmpT[:, :, 0:127], ps.v(ps.ap[:, 0:508].rearrange("p (g c) -> p g c", c=127)))
        else:
            ps = next_ps()
            for g in range(4):
                for hc in range(8):
                    P.mm(ps[0:127, g * 128:(g + 1) * 128], hidT[:, hc, g, 0:127], w2[:, hc, :], hc == 0, hc == 7)
            evac(vcmp[0:127, :, :], ps.v(ps.ap[0:127, :].rearrange("p (g c) -> p g c", c=128)))
    if "kcmpT" in k.dbg_d:
        tmpf = A.alloc("dbgtmp", [128, 4, 128], F32)
        P.copy("dve", tmpf, kcmpT)
        k.dbg_store("kcmpT", tmpf)
        tmpf2 = A.alloc("dbgtmp2", [128, 4, 128], F32)
        P.copy("dve", tmpf2, vcmp)
        k.dbg_store("vcmp", tmpf2)
    A.release(m1)
    if stop_after == "nsa_cmp":
        return

    cmask = A.alloc("cmask", [128, 8, 128], F32)
    addmask = A.alloc("addmask", [128, 8, 32], F32)
    kvalid = A.alloc("kvalid", [128, 16], BF16)
    Emat = A.alloc("Emat", [128, 2048], BF16)
    causneg = A.alloc("causneg", [128, 512], BF16)
    winneg = A.alloc("winneg", [128, 512], BF16)
    g3 = A.alloc("g3", [128, 8, 48], F32)
    P.dma("sp", cmask, D["cmask"][:, :, :])
    P.dma("sp", addmask, D["addmask"][:, :, :])
    P.dma("pool", kvalid, D["kvalid"][:, :])
    P.dma("pool", Emat, D["Emat"][:, :])
    P.dma("pool", causneg, D["causneg"][:, :])
    P.dma("pool", winneg, D["winneg"][:, :])
    tiles = [("wg3",) + wtile_cols(w_in, OG, 48)]
    for g in range(4):
        tiles.append(("wq%d" % g,) + wtile_cols(w_in, OQ + 512 * g, 512))
        tiles.append(("wkv%d" % g,) + wtile_cols(w_in, OKV + 512 * g, 512))
    wt = WS.get(tiles, 0)
    for tb in range(NB):
        ps = next_ps()
        for kc in range(KCH):
            P.mm(ps[:, 0:48], nT_own[:, kc, tb * 128:(tb + 1) * 128], wt[:, kc, 0:48], kc == 0, kc == KCH - 1)
        P.act(g3[:, tb, :], ps[:, 0:48], AF.Sigmoid)

    qT = A.alloc("qT", [128, NB, 4, 128], BF16)
    ksT = A.alloc("ksT", [128, 2048], BF16)
    kwT = A.alloc("kwT", [128, 1536], BF16)
    vs = A.alloc("vs", [128, 16, 130], BF16)
    vw = A.alloc("vw", [128, 12, 130], BF16)
    e32 = A.alloc("e32", [128, 4, 128], F32)
    p32 = A.alloc("p32", [128, 4, 128], F32)
    p16 = A.alloc("p16", [128, 4, 128], BF16)
    pT = A.alloc("pT", [128, 4, 128], BF16)
    Pg = A.alloc("Pg", [128, 128], F32)
    imp = A.alloc("imp", [128, 32], F32)
    imp2 = A.alloc("imp2", [128, 32], F32)
    m8 = A.alloc("m8", [128, 8], F32)
    sm4 = A.alloc("sm4", [128, 16], F32)
    selneg = A.alloc("selneg", [128, 32], BF16)
    negT = [A.alloc("negT%d" % i, [128, 4, 128], BF16) for i in range(2)]
    for nb_ in negT:
        P.memset("dve", nb_, 0.0)
    oacc = [A.alloc("oacc%d" % i, [128, 4, 128], F32) for i in range(2)]
    o16 = A.alloc("o16", [128, 4, 128], BF16)
    PT = [A.alloc("PT%d" % i, [128, 512], BF16) for i in range(2)]
    coef = A.alloc("coef", [128, 8], F32)
    P.memset("dve", vs, 0.0)
    P.memset("dve", vw, 0.0)

    def bc_heads(b):
        return b.v(b.ap.unsqueeze(1).to_broadcast([b.ap.shape[0], 4, 128]))

    for g in range(4):
        wq = WS.get(tiles, 1 + 2 * g)
        for hh in range(4):
            for tch in range(2):
                proj_T(wq, hh * 128, nT_own, tch * 512, 512, qT[:, 4 * tch:4 * tch + 4, hh, :], scale=QSCALE,
                       out_view3=True)
        wkv = WS.get(tiles, 2 + 2 * g)
        for (nT, t0, u0) in uchunks:
            proj_T(wkv, 0, nT, t0, 512, ksT[:, u0:u0 + 512])
        for (nT, t0, u0) in uchunks[1:]:
            proj_T(wkv, 256, nT, t0, 512, kwT[:, u0 - 512:u0])
        for (vbuf, c0, ub0) in ((vs, 128, 0), (vw, 384, 4)):
            for q4 in range(ub0 // 4, 4):
                ps = next_ps()
                for j in range(4):
                    ub = 4 * q4 + j
                    nT = nT_ctx if ub < 8 else nT_own
                    tb = ub % 8
                    for kc in range(KCH):
                        P.mm(ps[:, j * 128:(j + 1) * 128], nT[:, kc, tb * 128:(tb + 1) * 128],
                             wkv[:, kc, c0:c0 + 128], kc == 0, kc == KCH - 1)
                evac(vbuf[:, 4 * q4 - ub0:4 * q4 - ub0 + 4, 0:128],
                     ps.v(ps.ap.rearrange("p (a b) -> p a b", b=128)))
            P.copy("dve", vbuf[:, :, 128:129], kvalid.v(kvalid.ap[:, ub0:16].unsqueeze(2)))

        def cmp_stage(qb):
            par = qb % 2
            for hh in range(4):
                P.mm(psC[:, hh * 128:(hh + 1) * 128], qT[:, qb, hh, :], kcmpT[:, g, :], True, True)
            psC3 = psC.v(psC.ap.rearrange("p (a b) -> p a b", b=128))
            P.reduce("dve", sm4[:, 0:4], psC3, ALU.max)
            P.ts("dve", sm4[:, 4:8], sm4[:, 0:4], -1.0, None, ALU.mult)
            for hh in range(4):
                P.act(e32[:, hh, :], psC[:, hh * 128:(hh + 1) * 128], AF.Exp, bias=sm4[:, 4 + hh:5 + hh])
            P.tt("dve", e32, e32, bc_heads(cmask[:, qb, :]), ALU.mult)
            P.reduce("dve", sm4[:, 8:12], e32, ALU.add)
            P.ts("dve", sm4[:, 8:12], sm4[:, 8:12], 1e-30, None, ALU.max)
            P.recip(sm4[:, 12:16], sm4[:, 8:12])
            rb = sm4.v(sm4.ap[:, 12:16].unsqueeze(2).to_broadcast([128, 4, 128]))
            P.tt("dve", p32, e32, rb, ALU.mult)
            P.copy("act", p16, p32)
            P.reduce("dve", Pg, p32.v(p32.ap.rearrange("p h c -> p c h")), ALU.add)
            P.reduce("dve", imp, Pg.v(Pg.ap.rearrange("p (j f) -> p j f", f=4)), ALU.add)
            P.tt("dve", imp2[:, 1:32], imp[:, 1:32], Pg[:, 3:124:4], ALU.add)
            P.copy("dve", imp2[:, 0:1], imp[:, 0:1])
            P.tt("dve", imp, imp2, addmask[:, qb, :], ALU.add)
            P.op("dve", lambda e: e.max(out=m8.ap, in_=imp.ap), [imp], [m8])
            P.op("dve", lambda e: e.match_replace(out=imp2.ap, in_to_replace=m8.ap, in_values=imp.ap,
                                                   imm_value=-3e38), [m8, imp], [imp2])
            P.op("dve", lambda e: e.max(out=m8.ap, in_=imp2.ap), [imp2], [m8])
            P.ts("dve", selneg, imp, m8[:, 7:8], NEGB, ALU.is_lt, ALU.mult)
            if "imp" in k.dbg_d and g == 0:
                k.dbg_store("imp", imp, k.dbg_d["imp"][qb])
                k.dbg_store("Pg", Pg, k.dbg_d["Pg"][qb])

        def cmp_stage_b(qb):
            par = qb % 2
            P.transpose(psT[0:32, 0:128], selneg, ident)
            evac(negT[par][0:32, :, :], psT.v(psT.ap[0:32, 0:128].unsqueeze(1).to_broadcast([32, 4, 128])))
            for hh in range(4):
                P.transpose(psT[:, (1 + hh) * 128:(2 + hh) * 128], p16[:, hh, :], ident)
            evac(pT, psT.v(psT.ap[:, 128:640].rearrange("p (a b) -> p a b", b=128)))
            for hh in range(4):
                P.mm(psO[:, hh * 128:(hh + 1) * 128], pT[:, hh, :], vcmp[:, g, :], True, True)
            for hh in range(4):
                col = 3 * (4 * g + hh)
                P.act(oacc[par][:, hh, :], psO[:, hh * 128:(hh + 1) * 128], AF.Copy, scale=g3[:, qb, col:col + 1])

        def attn(qb, kT, kofs, vbuf, vofs, kbs, masks, gcol, final):
            par = qb % 2
            n = len(kbs)
            q3 = qT.v(qT.ap[:, qb, :, :].rearrange("p h t -> p (h t)"))
            pss = {}

            def stA(i):
                kb = kbs[i]
                ps = psS[i % 3]
                pss[i] = ps
                mk = masks.get(kb)
                P.mm(ps, kT[:, (kb - kofs) * 128:(kb - kofs + 1) * 128], q3, True, mk is None)
                if mk is not None:
                    P.mm(ps, mk[0], mk[1], False, True)
                P.act(PT[i % 2], ps, AF.Exp)

            def stB(i):
                kb = kbs[i]
                for hh in range(4):
                    acc = psV[hh // 2]
                    o = (hh % 2) * 130
                    P.mm(acc[:, o:o + 129], PT[i % 2][:, hh * 128:(hh + 1) * 128], vbuf[:, kb - vofs, 0:129],
                         i == 0 and hh % 2 == 0, i == n - 1, sgc=True)

            stA(0)
            for i in range(n):
                if i + 1 < n:
                    stA(i + 1)
                stB(i)
            for hh in range(4):
                acc = psV[hh // 2]
                o = (hh % 2) * 130
                P.copy("dve", coef[:, hh:hh + 1], acc[:, o + 128:o + 129])
            P.ts("dve", coef[:, 0:4], coef[:, 0:4], 1e-30, None, ALU.max)
            P.recip(coef[:, 4:8], coef[:, 0:4])
            base = 3 * 4 * g + gcol
            P.tt("dve", coef[:, 4:8], coef[:, 4:8], g3[:, qb, base:base + 10:3], ALU.mult)
            for hh in range(4):
                acc = psV[hh // 2]
                o = (hh % 2) * 130
                dst = o16[:, hh, :] if final else oacc[par][:, hh, :]
                P.stt("dve", dst, acc[:, o:o + 128], coef[:, 4 + hh:5 + hh], oacc[par][:, hh, :], ALU.mult, ALU.add)

        cmp_stage(0)
        cmp_stage_b(0)
        for qb in range(NB):
            if qb + 1 < NB:
                cmp_stage(qb + 1)
            ub = 8 + qb
            par = qb % 2
            negbc = negT[par].v(negT[par].ap.rearrange("p h t -> p (h t)"))
            masks = {kb: (Emat[:, kb * 128:(kb + 1) * 128], negbc) for kb in range(0, ub)}
            masks[ub] = (ident, causneg)
            attn(qb, ksT, 0, vs, 0, list(range(0, ub + 1)), masks, 1, False)
            if qb + 1 < NB:
                cmp_stage_b(qb + 1)
            masks = {ub - 4: (ident, winneg), ub: (ident, causneg)}
            attn(qb, kwT, 4, vw, 4, list(range(ub - 4, ub + 1)), masks, 2, True)
            for hh in range(4):
                P.transpose(psT[:, (5 + hh % 2) * 128:(6 + hh % 2) * 128], o16[:, hh, :], ident)
                if hh % 2 == 1:
                    evac(o_nsaT[:, 4 * g + hh - 1:4 * g + hh + 1, qb * 128:(qb + 1) * 128],
                         psT.v(psT.ap[:, 640:896].rearrange("p (a b) -> p a b", b=128)))
        if stop_after == "nsa_g0":
            break
    if "o_nsaT" in k.dbg_d:
        tmpf = A.alloc("dbgtmp", [128, TOK], F32)
        for hh in range(4):
            P.copy("dve", tmpf, o_nsaT[:, hh, :])
            k.dbg_store("o_nsaT", tmpf, k.dbg_d["o_nsaT"][:, hh, :])


def _rotary(k, ps128, cos, sin, out_even_odd, tmp):
    P = k.P
    x3 = ps128.v(ps128.ap.rearrange("p (i two) -> p i two", two=2))
    c3 = cos.v(cos.ap.unsqueeze(2).to_broadcast([128, 64, 2]))
    s3 = sin.v(sin.ap.unsqueeze(2).to_broadcast([128, 64, 2]))
    tA = tmp.v(tmp.ap[:, 0:2, :].rearrange("p a (i two) -> p (a i) two", two=2))
    tB = tmp.v(tmp.ap[:, 2:4, :].rearrange("p a (i two) -> p (a i) two", two=2))
    o3 = out_even_odd.v(out_even_odd.ap.rearrange("p (i two) -> p i two", two=2))
    P.tt("dve", tA, x3, c3, ALU.mult)
    P.tt("dve", tB, x3, s3, ALU.mult)
    P.tt("dve", o3[:, :, 0], tA[:, :, 0], tB[:, :, 1], ALU.subtract)
    P.tt("dve", o3[:, :, 1], tB[:, :, 0], tA[:, :, 1], ALU.add)


def build_ret_ctx(k):
    P, A, D, WS = k.P, k.R34, k.D, k.WS
    nT_ctx = k.nT_ctx
    state8 = k.state8
    ropek = A.alloc("ropek", [128, 16, 128], F32)
    wkc = A.alloc("wkc", [128, 8, 8], F32)
    P.dma("sp", ropek, D["ropek"][:, :, :])
    P.dma("sp", wkc, D["wkc"][:, :, :])
    Kr = [A.alloc("Kr%d" % i, [128, 128], F32) for i in range(2)]
    Ks = [A.alloc("Ks%d" % i, [128, 128], BF16) for i in range(2)]
    V = [A.alloc("V%d" % i, [128, 256], BF16) for i in range(2)]
    tmp = [A.alloc("rtmp%d" % i, [128, 4, 64], F32) for i in range(2)]
    tiles = [("wr_c%d" % h,) + wtile_cols(D["w_in"], OR_ + 512 * h, 512) for h in range(8)]
    for h in range(8):
        wt = WS.get(tiles, h)

        def proj(cb):
            ps = k.next_ps()
            for kc in range(KCH):
                P.mm(ps[:, 0:384], nT_ctx[:, kc, cb * 128:(cb + 1) * 128], wt[:, kc, 128:512], kc == 0, kc == KCH - 1)
            return ps

        ps_next = proj(0)
        for cb in range(NB):
            par = cb % 2
            ps = ps_next
            if cb + 1 < NB:
                ps_next = proj(cb + 1)
            _rotary(k, ps[:, 0:128], ropek[:, cb, 0:64], ropek[:, cb, 64:128], Kr[par], tmp[par])
            P.copy("act", V[par], ps[:, 128:384])
            P.act(Ks[par], Kr[par], AF.Copy, scale=wkc[:, cb, h:h + 1])
            P.mm(k.psV[h % 2][:, 0:256], Ks[par], V[par], cb == 0, cb == NB - 1)
        P.copy("act", state8[:, h, :], k.psV[h % 2][:, 0:256])


def build_ret_own(k):
    P, A, D, WS = k.P, k.R34, k.D, k.WS
    nT_own = k.nT_own
    state8 = k.state8
    psT = k.psT
    ident = k.ident
    o_retT = A.alloc("o_retT", [128, 16, TOK], BF16)
    k.o_retT = o_retT
    k.m_after_o = A.mark()
    ropek = A.alloc("ropek", [128, 16, 128], F32)
    ropeq = A.alloc("ropeq", [128, 8, 128], F32)
    decayT = A.alloc("decayT", [128, 8, 128], F32)
    wqB = A.alloc("wqB", [128, 8, 128], F32)
    wk = A.alloc("wk", [128, 8], F32)
    gnB = k.load_gain("ret_gn_w", A)
    P.dma("sp", ropek, D["ropek"][:, :, :])
    P.dma("sp", ropeq, D["ropeq"][:, :, :])
    P.dma("sp", decayT, D["decayT"][:, :, :])
    P.dma("sp", wqB, D["wqB"][:, :, :])
    P.dma("sp", wk, D["wk"][:, :])
    def mk_bufs(j):
        B = {}
        for nm, shp, dt in (("Qr", [128, 128], BF16), ("Kr", [128, 128], F32), ("Kb", [128, 128], BF16),
                            ("Ks", [128, 128], BF16), ("V", [128, 256], BF16), ("sg", [128, 256], F32),
                            ("tmp", [128, 4, 64], F32)):
            B[nm] = [A.alloc("%s%d_%d" % (nm, j, i), shp, dt) for i in range(2)]
        for nm, shp, dt in (("QT", [128, 128], BF16), ("KT", [128, 128], BF16), ("QsT", [128, 128], BF16),
                            ("SdT", [128, 128], BF16), ("stbf", [128, 256], BF16), ("osb", [128, 256], F32),
                            ("junk", [128, 256], BF16), ("y", [128, 256], F32), ("y16", [128, 256], BF16),
                            ("gs", [128, 8], F32)):
            B[nm] = A.alloc("%s%d" % (nm, j), shp, dt)
        return B

    bufs = [mk_bufs(0), mk_bufs(1)]
    tiles = []
    for p in range(4):
        tiles.append(("wr_o%d" % (2 * p),) + wtile_cols(D["w_in"], OR_ + 512 * (2 * p), 512))
        tiles.append(("wr_o%d" % (2 * p + 1),) + wtile_cols(D["w_in"], OR_ + 512 * (2 * p + 1), 512))
        tiles.append(("wgr%d" % p,) + wtile_cols(D["w_in"], OGR + 512 * p, 512))

    def run2(sa, sb):
        n = max(len(sa), len(sb))
        for i in range(n):
            if i < len(sa):
                sa[i]()
            if i < len(sb):
                sb[i]()

    for p in range(4):
        wts = [WS.get(tiles, 3 * p + j, depth=0) for j in range(2)]
        wg = WS.get(tiles, 3 * p + 2, depth=0)

        def stage1_steps(j, ob):
            B = bufs[j]
            h = 2 * p + j
            wt = wts[j]
            par = ob % 2
            ub = 8 + ob
            st = {}
            steps = []

            def s_proj():
                st["ps"] = k.next_ps()
                for kc in range(KCH):
                    P.mm(st["ps"], nT_own[:, kc, ob * 128:(ob + 1) * 128], wt[:, kc, 0:512], kc == 0, kc == KCH - 1)
            steps.append(s_proj)

            def s_projg():
                st["psg"] = k.next_ps()
                for kc in range(KCH):
                    P.mm(st["psg"][:, 0:256], nT_own[:, kc, ob * 128:(ob + 1) * 128],
                         wg[:, kc, j * 256:j * 256 + 256], kc == 0, kc == KCH - 1)
            steps.append(s_projg)
            steps.append(lambda: _rotary(k, st["ps"][:, 0:128], ropeq[:, ob, 0:64], ropeq[:, ob, 64:128],
                                         B["Qr"][par], B["tmp"][par]))
            steps.append(lambda: P.copy("act", B["V"][par], st["ps"][:, 256:512]))
            steps.append(lambda: _rotary(k, st["ps"][:, 128:256], ropek[:, ub, 0:64], ropek[:, ub, 64:128],
                                         B["Kr"][par], B["tmp"][par]))
            steps.append(lambda: P.act(B["sg"][par], st["psg"][:, 0:256], AF.Silu))
            steps.append(lambda: P.copy("act", B["Kb"][par], B["Kr"][par]))
            steps.append(lambda: P.act(B["Ks"][par], B["Kr"][par], AF.Copy, scale=wk[:, h:h + 1]))
            return steps

        def stage2_steps(j, ob):
            B = bufs[j]
            h = 2 * p + j
            par = ob % 2
            c0 = j * 512
            st = {}
            gs = B["gs"]
            steps = []

            def s_tr():
                P.transpose(psT[:, c0:c0 + 128], B["Qr"][par], ident)
                P.transpose(psT[:, c0 + 128:c0 + 256], B["Kb"][par], ident)
            steps.append(s_tr)
            steps.append(lambda: P.copy("act", B["QT"], psT[:, c0:c0 + 128]))
            steps.append(lambda: P.copy("act", B["KT"], psT[:, c0 + 128:c0 + 256]))
            steps.append(lambda: P.tt("dve", B["QsT"], psT[:, c0:c0 + 128], wqB[:, h, :], ALU.mult))

            def s_S():
                st["ps"] = k.next_ps()
                P.mm(st["ps"][:, 0:128], B["KT"], B["QT"], True, True)
            steps.append(s_S)
            steps.append(lambda: P.tt("dve", B["SdT"], st["ps"][:, 0:128], decayT[:, h, :], ALU.mult))
            steps.append(lambda: P.copy("act", B["stbf"], state8[:, h, :]))

            def s_o():
                st["po"] = k.next_ps()
                P.mm(st["po"][:, 0:256], B["SdT"], B["V"][par], True, False)
                P.mm(st["po"][:, 0:256], B["QsT"], B["stbf"], False, True)
            steps.append(s_o)
            steps.append(lambda: P.memset("dve", gs, 0.0))
            steps.append(lambda: P.act(B["osb"], st["po"][:, 0:256], AF.Copy, accum_out=gs[:, 0:1]))
            steps.append(lambda: P.act(B["junk"], st["po"][:, 0:256], AF.Square, accum_out=gs[:, 1:2]))

            def s_state():
                st["ps3"] = k.next_ps()
                P.mm(st["ps3"][:, 0:256], B["Ks"][par], B["V"][par], True, True)
            steps.append(s_state)
            steps.append(lambda: P.stt("dve", state8[:, h, :], state8[:, h, :], _G_CHUNK[h], st["ps3"][:, 0:256],
                                       ALU.mult, ALU.add))
            steps.append(lambda: P.ts("dve", gs[:, 2:3], gs[:, 0:1], 1.0 / 256, None, ALU.mult))
            steps.append(lambda: P.tt("dve", gs[:, 3:4], gs[:, 2:3], gs[:, 2:3], ALU.mult))
            steps.append(lambda: P.stt("dve", gs[:, 4:5], gs[:, 1:2], 1.0 / 256, gs[:, 3:4], ALU.mult, ALU.subtract))
            steps.append(lambda: P.ts("dve", gs[:, 4:5], gs[:, 4:5], EPS, None, ALU.add))
            steps.append(lambda: P.act(gs[:, 5:6], gs[:, 4:5], AF.Sqrt))
            steps.append(lambda: P.recip(gs[:, 6:7], gs[:, 5:6]))
            steps.append(lambda: P.ts("dve", B["y"], B["osb"], gs[:, 2:3], gs[:, 6:7], ALU.subtract, ALU.mult))
            steps.append(lambda: P.tt(POOL, B["y"], B["y"], gnB[:, h * 256:(h + 1) * 256], ALU.mult))
            steps.append(lambda: P.tt(POOL, B["y16"], B["y"], B["sg"][par], ALU.mult))

            def s_tr2():
                for jj in range(2):
                    P.transpose(psT[:, c0 + (2 + jj) * 128:c0 + (3 + jj) * 128], B["y16"][:, jj * 128:(jj + 1) * 128], ident)
            steps.append(s_tr2)
            steps.append(lambda: k.evac(o_retT[:, 2 * h:2 * h + 2, ob * 128:(ob + 1) * 128],
                                        psT.v(psT.ap[:, c0 + 256:c0 + 512].rearrange("p (a b) -> p a b", b=128))))
            return steps

        run2(stage1_steps(0, 0), stage1_steps(1, 0))
        for ob in range(NB):
            if ob + 1 < NB:
                run2(stage1_steps(0, ob + 1), stage1_steps(1, ob + 1))
            run2(stage2_steps(0, ob), stage2_steps(1, ob))
    if "o_retT" in k.dbg_d:
        tmpf = A.alloc("dbgtmp", [128, 2, TOK], F32)
        P.copy("dve", tmpf, o_retT[:, 0:2, :])
        k.dbg_store("o_retT", tmpf)


def build_merge(k, which, srcT):
    P, A, D, WS = k.P, k.R34, k.D, k.WS
    nT_own, mergedT = k.nT_own, k.mergedT
    W = D["w_a"] if which == "a" else D["w_b"]
    og = OGA if which == "a" else OGB
    sig = [A.alloc("sig%d" % i, [128, 512], F32) for i in range(2)]
    tmp = [A.alloc("mtmp%d" % i, [128, 512], F32) for i in range(2)]
    tiles = []
    for i in range(4):
        tiles.append(("wm%s%d" % (which, i),) + wtile_cols(W, 512 * i, 512))
        tiles.append(("wgt%s%d" % (which, i),) + wtile_cols(D["w_in"], og + 512 * i, 512))
    n = 0
    for i in range(4):
        wm = WS.get(tiles, 2 * i, depth=1)
        wg = WS.get(tiles, 2 * i + 1, depth=1)
        for cc in range(4):
            for tch in range(2):
                psA = k.next_ps()
                for kc in range(KCH):
                    P.mm(psA, wm[:, kc, cc * 128:(cc + 1) * 128], srcT[:, kc, tch * 512:(tch + 1) * 512],
                         kc == 0, kc == KCH - 1)
                psG = k.next_ps()
                for kc in range(KCH):
                    P.mm(psG, wg[:, kc, cc * 128:(cc + 1) * 128], nT_own[:, kc, tch * 512:(tch + 1) * 512],
                         kc == 0, kc == KCH - 1)
                par = n % 2
                n += 1
                P.act(sig[par], psG, AF.Sigmoid)
                dst = mergedT[:, 4 * i + cc, tch * 512:(tch + 1) * 512]
                if which == "a":
                    P.tt("dve", dst, sig[par], psA, ALU.mult)
                else:
                    P.tt("dve", tmp[par], sig[par], psA, ALU.mult)
                    P.tt(POOL, dst, tmp[par], dst, ALU.add)
    if ("mergedT_" + which) in k.dbg_d:
        tmpf = A.alloc("dbgtmp", [128, 4, TOK], F32)
        P.copy("dve", tmpf, mergedT[:, 0:4, :])
        k.dbg_store("mergedT_" + which, tmpf)


def build_tail(k, stop_after):
    P, D, WS = k.P, k.D, k.WS
    R1, R2, R34 = k.R1, k.R2, k.R34
    psS, psV, psT, ident = k.psS, k.psV, k.psT, k.ident
    mergedT = k.mergedT
    hb = [R34.alloc("h%d" % tb, [128, DM], F32) for tb in range(NB)]
    for tb in range(NB):
        P.dma("sp", hb[tb], D["xo"][tb * 128:(tb + 1) * 128, :])
    tiles = [("wout%d" % i,) + wtile_cols(D["w_out"], 512 * i, 512) for i in range(4)]
    for i in range(4):
        wt = WS.get(tiles, i)
        for tb in range(NB):
            ps = k.next_ps()
            for kc in range(KCH):
                P.mm(ps, mergedT[:, kc, tb * 128:(tb + 1) * 128], wt[:, kc, :], kc == 0, kc == KCH - 1)
            hs = hb[tb][:, 512 * i:512 * (i + 1)]
            P.tt("dve", hs, hs, ps, ALU.add)
    if "h1" in k.dbg_d:
        for tb in range(NB):
            k.dbg_store("h1", hb[tb], k.dbg_d["h1"][tb * 128:(tb + 1) * 128, :])
    R2.release()
    if stop_after == "h1":
        return

    nxT = R1.alloc("nxT", [128, KCH, TOK], BF16)
    gainX = k.load_gain("x_norm_w", R2)
    xn = R2.alloc("xn", [128, DM], BF16)
    stat = R2.alloc("stat", [128, 16], F32)
    mh0 = R34.mark()
    xn2 = R34.alloc("xn2", [128, DM], BF16)
    k.norm_T(lambda tb: hb[tb], NB, gainX, nxT, [xn, xn2], stat)
    R34.release(mh0)
    P.dma("sp", gainX, D["mem_norm_w"].to_broadcast([128, DM]))
    mh = R34.mark()
    mts = [R34.alloc("mt%d" % i, [128, DM], F32) for i in range(2)]
    mT = R2.alloc("mT", [128, KCH, 256], BF16)
    k.norm_T(k.mk_get(D["mem"], mts), 2, gainX, mT, xn, stat)
    R34.release(mh)
    qxT = R2.alloc("qxT", [128, 4, TOK], BF16)
    kxT = R34.alloc("kxT", [128, 4, 256], BF16)
    vx = R34.alloc("vx", [128, 2, 4, 130], BF16)
    PTx = [R34.alloc("PTx%d" % i, [128, 512], BF16) for i in range(2)]
    xc = R34.alloc("xcoef", [128, 4], F32)
    tiles = [("wqx",) + wtile_cols(D["wq_x"], 0, 512), ("wkx",) + wtile_cols(D["wk_x"], 0, 512),
             ("wvx",) + wtile_cols(D["wv_x"], 0, 512), ("wox",) + wtile_rows(D["wo_x"], 0, 512)]
    wq = WS.get(tiles, 0)
    for hh in range(4):
        for tch in range(2):
            k.proj_T(wq, hh * 128, nxT, tch * 512, 512, qxT[:, hh, tch * 512:(tch + 1) * 512], scale=QSCALE)
    wk_ = WS.get(tiles, 1)
    for hh in range(4):
        k.proj_T(wk_, hh * 128, mT, 0, 256, kxT[:, hh, :])
    wv = WS.get(tiles, 2)
    P.memset("dve", vx, 1.0)
    for mb in range(2):
        ps = k.next_ps()
        for kc in range(KCH):
            P.mm(ps, mT[:, kc, mb * 128:(mb + 1) * 128], wv[:, kc, :], kc == 0, kc == KCH - 1)
        k.evac(vx[:, mb, :, 0:128], ps.v(ps.ap.rearrange("p (a b) -> p a b", b=128)))
    R1.release()
    ox16 = R1.alloc("ox16", [128, NB, 512], BF16)
    oxT = R1.alloc("oxT", [128, 4, TOK], BF16)
    for hh in range(4):
        for tch in range(2):
            for mb in range(2):
                ps = psS[mb]
                P.mm(ps, kxT[:, hh, mb * 128:(mb + 1) * 128], qxT[:, hh, tch * 512:(tch + 1) * 512], True, True)
                P.act(PTx[mb], ps, AF.Exp)
            for tq in range(4):
                tb = tch * 4 + tq
                acc = psV[tq % 2]
                for mb in range(2):
                    P.mm(acc[:, 0:129], PTx[mb][:, tq * 128:(tq + 1) * 128], vx[:, mb, hh, 0:129], mb == 0, mb == 1)
                P.copy("dve", xc[:, 0:1], acc[:, 128:129])
                P.recip(xc[:, 1:2], xc[:, 0:1])
                P.ts("dve", ox16[:, tb, hh * 128:(hh + 1) * 128], acc[:, 0:128], xc[:, 1:2], None, ALU.mult)
    for tb in range(NB):
        for j in range(4):
            P.transpose(psT[:, j * 128:(j + 1) * 128], ox16[:, tb, j * 128:(j + 1) * 128], ident)
        k.evac(oxT[:, :, tb * 128:(tb + 1) * 128], psT.v(psT.ap[:, 0:512].rearrange("p (a b) -> p a b", b=128)))
    wo = WS.get(tiles, 3)
    for tb in range(NB):
        for cc in range(4):
            ps = k.next_ps()
            for kc in range(4):
                P.mm(ps, oxT[:, kc, tb * 128:(tb + 1) * 128], wo[:, kc, cc * 512:(cc + 1) * 512], kc == 0, kc == 3)
            hs = hb[tb][:, 512 * cc:512 * (cc + 1)]
            P.tt("dve", hs, hs, ps, ALU.add)
    if "h2" in k.dbg_d:
        for tb in range(NB):
            k.dbg_store("h2", hb[tb], k.dbg_d["h2"][tb * 128:(tb + 1) * 128, :])
    R1.release()
    R2.release()
    R34.release(mh)
    if stop_after == "h2":
        return

    nmT = R1.alloc("nmT", [128, KCH, TOK], BF16)
    gainM = k.load_gain("mlp_norm_w", R2)
    xn = [R2.alloc("xn%d" % i, [128, DM], BF16) for i in range(2)]
    stat = R2.alloc("stat", [128, 16], F32)
    k.norm_T(lambda tb: hb[tb], NB, gainM, nmT, xn, stat)
    aT = R2.alloc("aT", [128, 4, TOK], BF16)
    rl = [R2.alloc("rl%d" % i, [128, 512], F32) for i in range(2)]
    tiles = []
    for f in range(16):
        tiles.append(("wup%d" % f,) + wtile_cols(D["w_up"], 512 * f, 512))
        tiles.append(("wdn%d" % f,) + wtile_rows(D["w_down"], 512 * f, 512))
    n = 0
    for f in range(16):
        wu = WS.get(tiles, 2 * f)
        for cc in range(4):
            for tch in range(2):
                ps = k.next_ps()
                for kc in range(KCH):
                    P.mm(ps, wu[:, kc, cc * 128:(cc + 1) * 128], nmT[:, kc, tch * 512:(tch + 1) * 512],
                         kc == 0, kc == KCH - 1)
                par = n % 2
                n += 1
                P.act(rl[par], ps, AF.Relu)
                P.tt(POOL, aT[:, cc, tch * 512:(tch + 1) * 512], rl[par], rl[par], ALU.mult)
        wd = WS.get(tiles, 2 * f + 1)
        for tb in range(NB):
            for cc in range(4):
                ps = k.next_ps()
                for kc in range(4):
                    P.mm(ps, aT[:, kc, tb * 128:(tb + 1) * 128], wd[:, kc, cc * 512:(cc + 1) * 512], kc == 0, kc == 3)
                hs = hb[tb][:, 512 * cc:512 * (cc + 1)]
                P.tt("dve", hs, hs, ps, ALU.add)
    if "h3" in k.dbg_d:
        for tb in range(NB):
            k.dbg_store("h3", hb[tb], k.dbg_d["h3"][tb * 128:(tb + 1) * 128, :])
    R1.release()
    R2.release()

    gainF = k.load_gain("final_norm_w", R2)
    outt = [R2.alloc("outt%d" % i, [128, DM], F32) for i in range(2)]
    junk = R2.alloc("junkf", [128, DM], BF16)
    stat = R2.alloc("statf", [128, 16], F32)
    P.memset("dve", stat, 0.0)
    for tb in range(NB):
        P.act(junk, hb[tb], AF.Square, accum_out=stat[:, tb:tb + 1])
        P.ts("dve", stat[:, tb:tb + 1], stat[:, tb:tb + 1], 1.0 / DM, EPS, ALU.mult, ALU.add)
        P.act(stat[:, tb:tb + 1], stat[:, tb:tb + 1], AF.Sqrt)
        P.recip(stat[:, tb:tb + 1], stat[:, tb:tb + 1])
        P.stt("dve", outt[tb % 2], hb[tb], stat[:, tb:tb + 1], gainF, ALU.mult, ALU.mult)
        P.dma("sp", k.out_d[tb * 128:(tb + 1) * 128, :], outt[tb % 2], semkey="dma:store%d" % (tb % 2))


def _w_in_perm():
    q = np.arange(0, 2048)
    parts = [q]
    for g in range(4):
        for base in (3072, 3584, 4096, 4608):
            parts.append(base + 128 * g + np.arange(128))
    parts.append(2048 + np.arange(512))
    parts.append(2560 + np.arange(512))
    for h in range(8):
        parts.append(5168 + 128 * h + np.arange(128))
        parts.append(6192 + 128 * h + np.arange(128))
        parts.append(7216 + 256 * h + np.arange(256))
    parts.append(np.arange(9264, 15408))
    parts.append(5120 + np.arange(48))
    perm = np.concatenate(parts)
    assert perm.shape[0] == IN_WIDTH and len(set(perm.tolist())) == IN_WIDTH
    return perm


def _tables(s):
    f32 = np.float32
    T = {}
    T["ident"] = np.eye(128, dtype=f32)
    u = np.arange(2048)
    t = u - 1024 + 1024 * s
    inv = (10000.0 ** (-np.arange(0, 128, 2, dtype=f32) / f32(128))).astype(f32)
    ang = t.astype(f32)[:, None] * inv[None, :]
    cos, sin = np.cos(ang).astype(f32), np.sin(ang).astype(f32)
    rk = np.concatenate([cos, sin], axis=1) * f32(128 ** -0.5)
    T["ropek"] = np.ascontiguousarray(rk.reshape(16, 128, 128).transpose(1, 0, 2)).astype(f32)
    rq = np.concatenate([cos, sin], axis=1)[1024:]
    T["ropeq"] = np.ascontiguousarray(rq.reshape(8, 128, 128).transpose(1, 0, 2)).astype(f32)
    H = 8
    log_g = np.log1p(-np.exp2(-5.0 - np.arange(H, dtype=f32))).astype(f32)
    i = np.arange(128, dtype=f32)
    rel = i[:, None] - i[None, :]
    decay = np.where(rel >= 0, np.exp(log_g[:, None, None] * np.maximum(rel, 0.0)), 0.0).astype(f32)
    T["decayT"] = np.ascontiguousarray(decay.transpose(2, 0, 1))
    w_q = np.exp(log_g[:, None] * (i + 1.0)[None, :]).astype(f32)
    T["wqB"] = np.ascontiguousarray(np.broadcast_to(w_q[None], (128, H, 128))).astype(f32)
    w_k = np.exp(log_g[:, None] * (127.0 - i)[None, :]).astype(f32)
    T["wk"] = np.ascontiguousarray(w_k.T)
    T["g_chunk"] = np.exp(log_g * f32(128.0)).astype(f32)
    gpow = np.stack([T["g_chunk"] ** f32(7 - blk) for blk in range(8)], 0).astype(f32)
    T["wkc"] = np.ascontiguousarray((w_k.T[:, None, :] * gpow[None, :, :]).astype(f32))
    uq = 1024 + np.arange(1024)
    lc = np.arange(128)
    vis = (16 * lc[None, :] + 31 <= uq[:, None]) & (lc[None, :] < 127)
    if s == 0:
        vis &= (lc[None, :] >= 64)
    T["cmask"] = np.ascontiguousarray(vis.astype(f32).reshape(8, 128, 128).transpose(1, 0, 2))
    lj = np.arange(32)
    lcur = (uq // 64)[:, None]
    first = 0 if s == 1 else 16
    am = np.zeros((1024, 32), f32)
    am[np.broadcast_to(lj[None, :] == first, am.shape)] = 1e9
    prev = (lj[None, :] == lcur - 1) & (lj[None, :] >= first)
    am[prev] = 2e9
    am[np.broadcast_to(lj[None, :], am.shape) == lcur] = 3e9
    am[(lj[None, :] > lcur) | (lj[None, :] < first)] = -1e9
    T["addmask"] = np.ascontiguousarray(am.reshape(8, 128, 32).transpose(1, 0, 2))
    kval = (t >= 0).astype(f32)
    T["kvalid"] = np.ascontiguousarray(kval.reshape(16, 128).T)
    kk = np.arange(2048)
    T["Emat"] = (kk[None, :] // 64 == np.arange(128)[:, None]).astype(f32)
    kq = np.arange(128)
    T["causneg"] = np.tile(np.where(kq[:, None] > kq[None, :], NEGB, 0.0).astype(f32), (1, 4))
    T["winneg"] = np.tile(np.where(kq[:, None] <= kq[None, :], NEGB, 0.0).astype(f32), (1, 4))
    return T


def make_in_maps(inputs):
    f32 = np.float32
    x = np.asarray(inputs["x"], f32)
    mem = np.asarray(inputs["mem"], f32)
    perm = _w_in_perm()
    shared = {}
    shared["w_in"] = np.ascontiguousarray(np.asarray(inputs["w_in"], f32)[0][:, perm])
    for nm in ("cmp_w1_k", "cmp_w1_v", "cmp_w2_k", "cmp_w2_v", "w_a", "w_b", "w_out", "wq_x", "wk_x", "wv_x",
               "wo_x", "w_up", "w_down"):
        shared[nm] = np.ascontiguousarray(np.asarray(inputs[nm], f32)[0])
    shared["peT_k"] = np.ascontiguousarray(np.asarray(inputs["cmp_pe_k"], f32)[0].T)
    shared["peT_v"] = np.ascontiguousarray(np.asarray(inputs["cmp_pe_v"], f32)[0].T)
    for nm in ("attn_norm_w", "ret_gn_w", "x_norm_w", "mem_norm_w", "mlp_norm_w"):
        shared[nm] = np.ascontiguousarray(np.asarray(inputs[nm], f32).reshape(1, DM))
    shared["final_norm_w"] = np.ascontiguousarray(np.asarray(inputs["final_norm_w"], f32).reshape(1, DM))
    tabs = [_tables(0), _tables(1)]
    zeros = np.zeros((TOK, DM), f32)
    in_maps = []
    for c in range(8):
        b, s = c // 2, c % 2
        m = dict(shared)
        m["xo"] = np.ascontiguousarray(x[b, 1024 * s:1024 * (s + 1)])
        m["xc"] = np.ascontiguousarray(x[b, 0:1024]) if s == 1 else zeros
        m["mem"] = np.ascontiguousarray(mem[b])
        for kk, v in tabs[s].items():
            if kk != "g_chunk":
                m[kk] = v
        in_maps.append(m)
    return in_maps


_G_CHUNK = [float(v) for v in np.exp(np.log1p(-np.exp2(-5.0 - np.arange(8, dtype=np.float32))).astype(np.float32)
                                     * np.float32(128.0)).astype(np.float32)]


def kernel(**inputs):
    in_maps = make_in_maps(inputs)
    nc, st = build_program()
    res = run_bass_kernel_spmd(nc, in_maps, core_ids=list(range(8)))
    out = np.zeros((4, 2048, DM), np.float32)
    for c in range(8):
        b, s = c // 2, c % 2
        out[b, 1024 * s:1024 * (s + 1)] = res.results[c]["out"]
    return out
```

```python
import numpy as np
from concourse.bass_utils import run_bass_kernel_spmd
import concourse.bass as bass
import concourse.mybir as mybir

F32 = mybir.dt.float32
BF16 = mybir.dt.bfloat16
AF = mybir.ActivationFunctionType
ALU = mybir.AluOpType
AX = mybir.AxisListType

_DT_SIZE = {F32: 4, BF16: 2}


class Buf:
    def __init__(self, key, ap):
        self.key = key
        self.ap = ap

    def __getitem__(self, idx):
        return Buf(self.key, self.ap[idx])

    def v(self, ap):
        return Buf(self.key, ap)


class Op:
    __slots__ = ("eng", "fn", "reads", "writes", "is_dma", "semkey", "deps", "dma_deps",
                 "signal", "signum", "pos", "accum", "idx")


class Prog:
    ENGS = ("pe", "act", "dve", "pool", "sp")

    def __init__(self, nc):
        self.nc = nc
        self.ops = []
        self.eng_obj = {"pe": nc.tensor, "act": nc.scalar, "dve": nc.vector, "pool": nc.gpsimd, "sp": nc.sync}
        self.sync_same_engine_war = False

    def _add(self, eng, fn, reads, writes, is_dma=False, semkey=None, accum=False):
        o = Op()
        o.eng = eng
        o.fn = fn
        o.reads = [b.key for b in reads if b is not None]
        o.writes = [b.key for b in writes if b is not None]
        for kk in o.reads:
            if kk.startswith("ps") and kk not in o.writes:
                o.writes.append(kk)
        o.is_dma = is_dma
        o.semkey = semkey
        o.accum = accum
        o.idx = len(self.ops)
        self.ops.append(o)
        return o

    def op(self, eng, fn, reads=(), writes=()):
        return self._add(eng, fn, reads, writes)

    def barrier(self):
        o = Op()
        o.eng = None
        o.idx = len(self.ops)
        self.ops.append(o)

    def dma(self, eng, out, in_, semkey=None):
        reads, writes = [], []
        if isinstance(in_, Buf):
            reads.append(in_)
            in_ap = in_.ap
        else:
            in_ap = in_
        if isinstance(out, Buf):
            writes.append(out)
            out_ap = out.ap
            if semkey is None:
                semkey = "dma:" + out.key
        else:
            out_ap = out
            if semkey is None:
                semkey = "dma:store"

        def fn(e):
            return e.dma_start(out=out_ap, in_=in_ap)

        return self._add(eng, fn, reads, writes, is_dma=True, semkey=semkey)

    def mm(self, out, lhsT, rhs, start, stop, extra_reads=(), sgc=False):
        def fn(e):
            if sgc:
                return e.matmul(out.ap, lhsT.ap, rhs.ap, start=start, stop=stop, skip_group_check=True)
            return e.matmul(out.ap, lhsT.ap, rhs.ap, start=start, stop=stop)
        return self._add("pe", fn, [lhsT, rhs] + list(extra_reads), [out], accum=not start)

    def transpose(self, out, in_, ident):
        def fn(e):
            return e.transpose(out.ap, in_.ap, ident.ap)
        return self._add("pe", fn, [in_, ident], [out])

    def act(self, out, in_, func, bias=None, scale=1.0, accum_out=None, eng="act"):
        reads = [in_]
        kw = {}
        if isinstance(bias, Buf):
            reads.append(bias)
            kw["bias"] = bias.ap
        elif bias is not None:
            kw["bias"] = bias
        if isinstance(scale, Buf):
            reads.append(scale)
            kw["scale"] = scale.ap
        else:
            kw["scale"] = scale
        writes = [out]
        if accum_out is not None:
            writes.append(accum_out)
            kw["accum_out"] = accum_out.ap

        def fn(e):
            return e.activation(out=out.ap, in_=in_.ap, func=func, **kw)
        return self._add(eng, fn, reads, writes)

    def tt(self, eng, out, in0, in1, op):
        def fn(e):
            return e.tensor_tensor(out=out.ap, in0=in0.ap, in1=in1.ap, op=op)
        return self._add(eng, fn, [in0, in1], [out])

    def ts(self, eng, out, in0, s1, s2, op0, op1=None, accum_out=None):
        reads = [in0]
        a1 = s1
        a2 = s2
        if isinstance(s1, Buf):
            reads.append(s1)
            a1 = s1.ap
        if isinstance(s2, Buf):
            reads.append(s2)
            a2 = s2.ap
        writes = [out]
        kw = {}
        if op1 is not None:
            kw["op1"] = op1
        if accum_out is not None:
            writes.append(accum_out)
            kw["accum_out"] = accum_out.ap

        def fn(e):
            return e.tensor_scalar(out=out.ap, in0=in0.ap, scalar1=a1, scalar2=a2, op0=op0, **kw)
        return self._add(eng, fn, reads, writes)

    def stt(self, eng, out, in0, scalar, in1, op0, op1):
        reads = [in0, in1]
        a = scalar
        if isinstance(scalar, Buf):
            reads.append(scalar)
            a = scalar.ap

        def fn(e):
            return e.scalar_tensor_tensor(out=out.ap, in0=in0.ap, scalar=a, in1=in1.ap, op0=op0, op1=op1)
        return self._add(eng, fn, reads, [out])

    def copy(self, eng, out, in_):
        if eng == "act":
            def fn(e):
                return e.copy(out=out.ap, in_=in_.ap)
        else:
            def fn(e):
                return e.tensor_copy(out=out.ap, in_=in_.ap)
        return self._add(eng, fn, [in_], [out])

    def reduce(self, eng, out, in_, op, axis=AX.X):
        def fn(e):
            return e.tensor_reduce(out=out.ap, in_=in_.ap, axis=axis, op=op)
        return self._add(eng, fn, [in_], [out])

    def memset(self, eng, out, val):
        def fn(e):
            return e.memset(out.ap, val)
        return self._add(eng, fn, [], [out])

    def recip(self, out, in_):
        def fn(e):
            return e.reciprocal(out=out.ap, in_=in_.ap)
        return self._add("dve", fn, [in_], [out])

    def emit(self, final_wait_eng="sp"):
        nc = self.nc
        ops = self.ops
        last_writer = {}
        readers = {}
        pos_ctr = {e: 0 for e in self.ENGS}
        waited = {f: {e: -1 for e in self.ENGS} for f in self.ENGS}
        waited_dma = {f: {} for f in self.ENGS}
        dma_count = {}
        last_op_on = {e: None for e in self.ENGS}
        pending_barrier = {e: [] for e in self.ENGS}
        outstanding_dma = []

        for o in ops:
            if o.eng is None:
                for f in self.ENGS:
                    pending_barrier[f] = [last_op_on[e] for e in self.ENGS if e != f and last_op_on[e] is not None]
                continue
            f = o.eng
            deps = set()
            for k in o.reads:
                w = last_writer.get(k)
                if w is not None:
                    deps.add(w)
            for k in o.writes:
                w = last_writer.get(k)
                if w is not None:
                    deps.add(w)
                for r in readers.get(k, ()):
                    deps.add(r)
            for b in pending_barrier[f]:
                deps.add(b)
            pending_barrier[f] = []
            o.pos = pos_ctr[f]
            pos_ctr[f] += 1
            o.deps = []
            o.dma_deps = []
            o.signal = False
            for di in sorted(deps):
                d = ops[di]
                if d.idx == o.idx:
                    continue
                if d.is_dma:
                    cnt = d.signum
                    if waited_dma[f].get(d.semkey, 0) >= cnt:
                        continue
                    waited_dma[f][d.semkey] = cnt
                    o.dma_deps.append((d.semkey, cnt))
                else:
                    if d.eng == "pe" and f == "pe" and not o.is_dma:
                        continue
                    if d.eng == f and not self.sync_same_engine_war and not o.is_dma:
                        is_raw_waw = any(last_writer.get(k) == di for k in o.reads + o.writes)
                        if not is_raw_waw:
                            continue
                    if waited[f][d.eng] >= d.pos:
                        continue
                    waited[f][d.eng] = d.pos
                    d.signal = True
                    o.deps.append(di)
            if o.is_dma:
                dma_count[o.semkey] = dma_count.get(o.semkey, 0) + 16
                o.signum = dma_count[o.semkey]
                outstanding_dma.append(o)
            for k in o.reads:
                readers.setdefault(k, []).append(o.idx)
            for k in o.writes:
                last_writer[k] = o.idx
                readers[k] = []
            last_op_on[f] = o.idx

        tail_deps = []
        for e in self.ENGS:
            li = last_op_on[e]
            if li is not None and not ops[li].is_dma:
                ops[li].signal = True
                tail_deps.append(li)
        sig_ctr = {e: 0 for e in self.ENGS}
        for o in ops:
            if o.eng is None or o.is_dma:
                continue
            if o.signal:
                sig_ctr[o.eng] += 1
                o.signum = sig_ctr[o.eng]
        import contextlib
        es = contextlib.ExitStack()
        self._es = es
        sems = {e: es.enter_context(nc.semaphore("s_" + e)) for e in self.ENGS}
        dsems = {}
        for k in dma_count:
            dsems[k] = es.enter_context(nc.semaphore("d%d" % len(dsems)))
        n_wait = 0
        for o in ops:
            if o.eng is None:
                continue
            e = self.eng_obj[o.eng]
            for di in o.deps:
                d = ops[di]
                e.wait_ge(sems[d.eng], d.signum)
                n_wait += 1
            for (k, cnt) in o.dma_deps:
                e.wait_ge(dsems[k], cnt)
                n_wait += 1
            ins = o.fn(e)
            if o.is_dma:
                ins.then_inc(dsems[o.semkey], 16)
            elif o.signal:
                ins.then_inc(sems[o.eng], 1)
        fe = self.eng_obj[final_wait_eng]
        for di in tail_deps:
            d = ops[di]
            fe.wait_ge(sems[d.eng], d.signum)
        for k, cnt in dma_count.items():
            fe.wait_ge(dsems[k], cnt)
        self.stats = dict(n_ops=len(ops), n_wait=n_wait, sig=dict(sig_ctr), n_dsems=len(dsems))
        return self.stats


TOK = 1024
NB = 8
DM = 2048
KCH = 16
EPS = 1e-6
OQ, OKV, OC, OR_, OGR, OGA, OGB, OG = 0, 2048, 4096, 5120, 9216, 11264, 13312, 15360
IN_WIDTH = 15408
NEGB = -30000.0
QSCALE = 128 ** -0.5
POOL = "dve"


def _prod(xs):
    r = 1
    for x in xs:
        r *= int(x)
    return r


class Arena:
    uid = 0

    def __init__(self, full_ap, P, base, nel, name):
        self.ap = full_ap
        self.base = base
        self.top = 0
        self.nel = nel
        self.P = P
        self.peak = 0
        self.name = name

    def alloc(self, name, shape, dt):
        inner = _prod(shape[1:])
        n = inner * (2 if dt == F32 else 1)
        npad = (n + 15) // 16 * 16
        off = self.base + self.top
        self.top += npad
        self.peak = max(self.peak, self.top)
        assert self.top <= self.nel, ("SBUF arena overflow", self.name, name, self.top, self.nel)
        ap = self.ap[:shape[0], off:off + n]
        if dt == F32:
            ap = ap.bitcast(F32)
        if len(shape) == 3:
            ap = ap.rearrange("p (a b) -> p a b", b=shape[2])
        elif len(shape) == 4:
            ap = ap.rearrange("p (a b c) -> p a b c", b=shape[2], c=shape[3])
        Arena.uid += 1
        return Buf("%s#%d" % (name, Arena.uid), ap)

    def mark(self):
        return self.top

    def release(self, mark=0):
        self.top = mark
        self.P.barrier()


class WStream:
    def __init__(self, P, arena, nslots, slot_elems):
        self.P = P
        self.nslots = nslots
        self.slot_elems = slot_elems
        self.slots = [arena.alloc("wslot%d" % i, [128, slot_elems], BF16) for i in range(nslots)]
        self.ctr = 0
        self.loaded = {}

    def _load(self, item):
        key, src, shape = item
        if key in self.loaded:
            return
        s = self.slots[self.ctr % self.nslots]
        self.ctr += 1
        n = _prod(shape[1:])
        ap = s.ap[:, 0:n]
        if len(shape) == 3:
            ap = ap.rearrange("p (a b) -> p a b", b=shape[2])
        b = Buf(s.key, ap)
        self.P.dma("pool", b, src)
        self.loaded[key] = b

    def get(self, lst, i, depth=None):
        depth = self.nslots - 1 if depth is None else depth
        for j in range(i, min(len(lst), i + depth + 1)):
            self._load(lst[j])
        b = self.loaded.pop(lst[i][0])
        return b


def wtile_cols(w2d, c0, ncols):
    return w2d[:, c0:c0 + ncols].rearrange("(kc p) c -> p kc c", p=128), [128, 16, ncols]


def wtile_rows(w2d, r0, nrows):
    return w2d[r0:r0 + nrows, :].rearrange("(kc p) c -> p kc c", p=128), [128, nrows // 128, 2048]


class K:
    pass


def build_program(dbg=None, stop_after=None):
    nc = bass.Bass("TRN2", target_bir_lowering=False)
    P = Prog(nc)
    k = K()
    k.nc, k.P = nc, P

    def din(name, shape):
        return nc.dram_tensor(name, list(shape), F32, kind="ExternalInput").ap()

    D = {}
    D["xo"] = din("xo", [TOK, DM])
    D["xc"] = din("xc", [TOK, DM])
    D["mem"] = din("mem", [256, DM])
    D["w_in"] = din("w_in", [DM, IN_WIDTH])
    for nm in ("cmp_w1_k", "cmp_w1_v"):
        D[nm] = din(nm, [4096, 1024])
    for nm in ("cmp_w2_k", "cmp_w2_v"):
        D[nm] = din(nm, [1024, 128])
    for nm in ("peT_k", "peT_v"):
        D[nm] = din(nm, [128, 32])
    for nm in ("w_a", "w_b", "w_out"):
        D[nm] = din(nm, [DM, DM])
    for nm in ("wq_x", "wk_x", "wv_x"):
        D[nm] = din(nm, [DM, 512])
    D["wo_x"] = din("wo_x", [512, DM])
    D["w_up"] = din("w_up", [DM, 8192])
    D["w_down"] = din("w_down", [8192, DM])
    for nm in ("attn_norm_w", "ret_gn_w", "x_norm_w", "mem_norm_w", "mlp_norm_w", "final_norm_w"):
        D[nm] = din(nm, [1, DM])
    D["ident"] = din("ident", [128, 128])
    D["ropeq"] = din("ropeq", [128, 8, 128])
    D["ropek"] = din("ropek", [128, 16, 128])
    D["decayT"] = din("decayT", [128, 8, 128])
    D["wqB"] = din("wqB", [128, 8, 128])
    D["wk"] = din("wk", [128, 8])
    D["wkc"] = din("wkc", [128, 8, 8])
    D["cmask"] = din("cmask", [128, 8, 128])
    D["addmask"] = din("addmask", [128, 8, 32])
    D["kvalid"] = din("kvalid", [128, 16])
    D["Emat"] = din("Emat", [128, 2048])
    D["causneg"] = din("causneg", [128, 512])
    D["winneg"] = din("winneg", [128, 512])
    out_d = nc.dram_tensor("out", [TOK, DM], F32, kind="ExternalOutput").ap()
    dbg_d = {}
    if dbg:
        for nm, shp in dbg.items():
            dbg_d[nm] = nc.dram_tensor("dbg_" + nm, list(shp), F32, kind="ExternalOutput").ap()
    k.D, k.out_d, k.dbg_d = D, out_d, dbg_d

    NEL = 106000
    full = nc.alloc_sbuf_tensor("arena", [128, NEL], BF16).ap()
    R0 = Arena(full, P, 0, 30000, "R0")
    R1 = Arena(full, P, 30000, 16384, "R1")
    R2 = Arena(full, P, 46384, 16384, "R2")
    R34 = Arena(full, P, 62768, NEL - 62768, "R34")
    k.R0, k.R1, k.R2, k.R34 = R0, R1, R2, R34
    A = R0
    k.psS = [Buf("psS%d" % i, nc.alloc_psum_tensor("psS%d" % i, [128, 512], F32).ap()) for i in range(3)]
    k.psC = Buf("psC", nc.alloc_psum_tensor("psC", [128, 512], F32).ap())
    k.psO = Buf("psO", nc.alloc_psum_tensor("psO", [128, 512], F32).ap())
    k.psV = [Buf("psV%d" % i, nc.alloc_psum_tensor("psV%d" % i, [128, 512], F32).ap()) for i in range(2)]
    k.psT = Buf("psT", nc.alloc_psum_tensor("psT", [128, 1024], BF16).ap())
    k.rot5 = [k.psS[0], k.psS[1], k.psS[2], k.psC, k.psO]
    k.rot_i = 0
    k.ev_i = 0

    k.ident = A.alloc("ident", [128, 128], BF16)
    P.dma("pool", k.ident, D["ident"][:, :])
    k.WS = WStream(P, A, 3, 16 * 512)

    def finish():
        st = P.emit()
        st["arena_peak_el"] = [R0.peak, R1.peak, R2.peak, R34.peak]
        return nc, st

    def dbg_store(name, buf, dram_view=None):
        if name in dbg_d:
            dv = dbg_d[name] if dram_view is None else dram_view
            P.dma("sp", dv, buf, semkey="dma:dbg")

    k.dbg_store = dbg_store

    def next_ps():
        b = k.rot5[k.rot_i % 5]
        k.rot_i += 1
        return b

    def evac(out, in_, scale=None):
        e = k.ev_i % 2
        k.ev_i += 1
        if e == 0:
            if scale is None:
                P.copy("act", out, in_)
            else:
                P.act(out, in_, AF.Copy, scale=scale)
        else:
            if scale is None:
                P.copy("dve", out, in_)
            else:
                P.ts("dve", out, in_, scale, None, ALU.mult)

    k.next_ps, k.evac = next_ps, evac

    def load_gain(name, reg):
        g = reg.alloc("gain_" + name, [128, DM], F32)
        P.dma("sp", g, D[name].to_broadcast([128, DM]))
        return g

    def norm_T(get_block, nblk, gain, nT, xns, stat):
        P.memset("dve", stat, 0.0)
        if not isinstance(xns, (list, tuple)):
            xns = [xns]
        for tb in range(nblk):
            xn = xns[tb % len(xns)]
            xt = get_block(tb)
            P.act(xn, xt, AF.Square, accum_out=stat[:, tb:tb + 1])
            P.ts("dve", stat[:, tb:tb + 1], stat[:, tb:tb + 1], 1.0 / DM, EPS, ALU.mult, ALU.add)
            P.act(stat[:, tb:tb + 1], stat[:, tb:tb + 1], AF.Sqrt)
            P.recip(stat[:, tb:tb + 1], stat[:, tb:tb + 1])
            P.stt("dve", xn, xt, stat[:, tb:tb + 1], gain, ALU.mult, ALU.mult)
            for half in range(2):
                for j in range(8):
                    kc = half * 8 + j
                    P.transpose(k.psT[:, j * 128:(j + 1) * 128], xn[:, kc * 128:(kc + 1) * 128], k.ident)
                src = k.psT.v(k.psT.ap.rearrange("p (a b) -> p a b", b=128))
                evac(nT[:, half * 8:half * 8 + 8, tb * 128:(tb + 1) * 128], src)

    def proj_T(wt, c0, nT, t0, ntok, out, scale=None, nk=KCH, out_view3=False):
        ps = next_ps()
        for kc in range(nk):
            P.mm(ps[:, 0:ntok], wt[:, kc, c0:c0 + 128], nT[:, kc, t0:t0 + ntok], kc == 0, kc == nk - 1)
        src = ps[:, 0:ntok]
        if out_view3:
            src = ps.v(ps.ap[:, 0:ntok].rearrange("p (a b) -> p a b", b=128))
        evac(out, src, scale)

    k.norm_T, k.proj_T, k.load_gain = norm_T, proj_T, load_gain

    nT_own = R1.alloc("nT_own", [128, KCH, TOK], BF16)
    nT_ctx = R2.alloc("nT_ctx", [128, KCH, TOK], BF16)
    k.state8 = R0.alloc("state8", [128, 8, 256], F32)
    k.kcmpT = R0.alloc("kcmpT", [128, 4, 128], BF16)
    k.vcmp = R0.alloc("vcmp", [128, 4, 128], BF16)
    gainA = load_gain("attn_norm_w", R34)
    xts = [R34.alloc("xt%d" % i, [128, DM], F32) for i in range(2)]
    xn = [R34.alloc("xn%d" % i, [128, DM], BF16) for i in range(2)]
    stat = R34.alloc("stat", [128, 16], F32)

    def mk_get(src, xts):
        def get(tb):
            xt = xts[tb % 2]
            P.dma("sp", xt, src[tb * 128:(tb + 1) * 128, :])
            return xt
        return get

    k.mk_get = mk_get
    norm_T(mk_get(D["xc"], xts), NB, gainA, nT_ctx, xn, stat)
    norm_T(mk_get(D["xo"], xts), NB, gainA, nT_own, xn, stat)
    if "nT_own" in dbg_d:
        tmpf = R34.alloc("dbgtmp", [128, KCH, TOK // 4], F32)
        P.copy("dve", tmpf, nT_own[:, :, 0:TOK // 4])
        dbg_store("nT_own", tmpf)
    R34.release()
    if stop_after == "norm":
        return finish()
    k.nT_own, k.nT_ctx = nT_own, nT_ctx
    k.finish = finish

    build_ret_ctx(k)
    R34.release()
    if stop_after == "ret_ctx":
        return finish()
    build_nsa(k, stop_after)
    if stop_after and stop_after.startswith("nsa"):
        return finish()
    R34.release(k.m_after_o)
    R2.release()
    k.mergedT = R2.alloc("mergedT", [128, KCH, TOK], BF16)
    build_merge(k, "a", k.o_nsaT)
    R34.release()
    if stop_after == "merge_a":
        return finish()
    build_ret_own(k)
    if stop_after == "ret":
        return finish()
    R34.release(k.m_after_o)
    build_merge(k, "b", k.o_retT)
    R34.release()
    R1.release()
    if stop_after == "merge_b":
        return finish()
    build_tail(k, stop_after)
    return finish()


def build_nsa(k, stop_after):
    P, A, D, WS = k.P, k.R34, k.D, k.WS
    nT_own, nT_ctx = k.nT_own, k.nT_ctx
    psS, psC, psO, psV, psT = k.psS, k.psC, k.psO, k.psV, k.psT
    evac, next_ps, proj_T = k.evac, k.next_ps, k.proj_T
    ident = k.ident
    w_in = D["w_in"]
    uchunks = [(nT_ctx, 0, 0), (nT_ctx, 512, 512), (nT_own, 0, 1024), (nT_own, 512, 1536)]

    o_nsaT = A.alloc("o_nsaT", [128, 16, TOK], BF16)
    k.o_nsaT = o_nsaT
    k.m_after_o = A.mark()
    kcmpT, vcmp = k.kcmpT, k.vcmp
    P.memset("dve", kcmpT, 0.0)
    P.memset("dve", vcmp, 0.0)
    m1 = A.mark()
    kvT = A.alloc("kvcT", [128, 4, 2048], BF16)
    kvD = A.alloc("kvcD", [128, 4, 16, 128], BF16)
    hidT = A.alloc("hidT", [128, 8, 4, 128], BF16)
    peT = A.alloc("peT", [128, 32], BF16)
    w2 = A.alloc("w2", [128, 8, 128], BF16)
    cbias = A.alloc("cbias", [128, 8], F32)
    for kind in range(2):
        sfx = "_k" if kind == 0 else "_v"
        tiles = [("wc%d" % kind,) + wtile_cols(w_in, OC + 512 * kind, 512)]
        for hh in range(2):
            for lh in range(2):
                src = D["cmp_w1" + sfx][2048 * lh:2048 * (lh + 1), 512 * hh:512 * (hh + 1)].rearrange(
                    "(l p) c -> p l c", p=128)
                tiles.append(("w1%d_%d_%d" % (kind, hh, lh), src, [128, 16, 512]))
        P.dma("pool", peT, D["peT" + sfx][:, :])
        P.dma("pool", w2, D["cmp_w2" + sfx].rearrange("(hc p) c -> p hc c", p=128))
        wt = WS.get(tiles, 0)
        for g in range(4):
            for (nT, t0, u0) in uchunks:
                proj_T(wt, g * 128, nT, t0, 512, kvT[:, g, u0:u0 + 512])
        for g in range(4):
            src = kvT.v(kvT.ap[:, g, :].rearrange("p (j r) -> p r j", r=16))
            if g % 2 == 0:
                P.copy("act", kvD[:, g, :, :], src)
            else:
                P.copy("dve", kvD[:, g, :, :], src)
        ti = 1
        for hh in range(2):
            accs = [psS[0], psS[1], psS[2], psC]
            for lh in range(2):
                wt = WS.get(tiles, ti)
                ti += 1
                for li in range(16):
                    l = lh * 16 + li
                    for hc in range(4):
                        lhsT = wt[:, li, hc * 128:(hc + 1) * 128]
                        for g in range(4):
                            rhs = kvD[:, g, l, 0:127] if l < 16 else kvD[:, g, l - 16, 1:128]
                            P.mm(accs[hc][:, g * 127:(g + 1) * 127], lhsT, rhs, l == 0 and g == 0, l == 31, sgc=True)
                        P.mm(psO[:, hc:hc + 1], lhsT, peT[:, l:l + 1], l == 0 and hc == 0, l == 31, sgc=True)
            for hc in range(4):
                hcg = hh * 4 + hc
                P.copy("dve", cbias[:, hcg:hcg + 1], psO[:, hc:hc + 1])
                src = accs[hc].v(accs[hc].ap[:, 0:508].rearrange("p (g c) -> p g c", c=127))
                P.act(hidT[:, hcg, :, 0:127], src, AF.Silu, bias=cbias[:, hcg:hcg + 1])
        if kind == 0:
            ps = next_ps()
            for g in range(4):
                for hc in range(8):
                    P.mm(ps[:, g * 127:(g + 1) * 127], w2[:, hc, :], hidT[:, hc, g, 0:127], hc == 0, hc == 7)
            evac(kcmpT[:, :, 0:127], ps.v(ps.ap[:, 0:508].rearrange("p (g c) -> p g c", c=127)))
        else:
            ps = next_ps()
            for g in range(4):
                for hc in range(8):
                    P.mm(ps[0:127, g * 128:(g + 1) * 128], hidT[:, hc, g, 0:127], w2[:, hc, :], hc == 0, hc == 7)
            evac(vcmp[0:127, :, :], ps.v(ps.ap[0:127, :].rearrange("p (g c) -> p g c", c=128)))
    if "kcmpT" in k.dbg_d:
        tmpf = A.alloc("dbgtmp", [128, 4, 128], F32)
        P.copy("dve", tmpf, kcmpT)
        k.dbg_store("kcmpT", tmpf)
        tmpf2 = A.alloc("dbgtmp2", [128, 4, 128], F32)
        P.copy("dve", tmpf2, vcmp)
        k.dbg_store("vcmp", tmpf2)
    A.release(m1)
    if stop_after == "nsa_cmp":
        return

    cmask = A.alloc("cmask", [128, 8, 128], BF16)
    addmask = A.alloc("addmask", [128, 8, 32], F32)
    kvalid = A.alloc("kvalid", [128, 16], BF16)
    Emat = A.alloc("Emat", [128, 2048], BF16)
    causneg = A.alloc("causneg", [128, 512], BF16)
    winneg = A.alloc("winneg", [128, 512], BF16)
    g3 = A.alloc("g3", [128, 8, 48], F32)
    P.dma("pool", cmask, D["cmask"][:, :, :])
    P.dma("sp", addmask, D["addmask"][:, :, :])
    P.dma("pool", kvalid, D["kvalid"][:, :])
    P.dma("pool", Emat, D["Emat"][:, :])
    P.dma("pool", causneg, D["causneg"][:, :])
    P.dma("pool", winneg, D["winneg"][:, :])
    tiles = [("wg3",) + wtile_cols(w_in, OG, 48)]
    for g in range(4):
        tiles.append(("wq%d" % g,) + wtile_cols(w_in, OQ + 512 * g, 512))
        tiles.append(("wkv%d" % g,) + wtile_cols(w_in, OKV + 512 * g, 512))
    wt = WS.get(tiles, 0)
    for tb in range(NB):
        ps = next_ps()
        for kc in range(KCH):
            P.mm(ps[:, 0:48], nT_own[:, kc, tb * 128:(tb + 1) * 128], wt[:, kc, 0:48], kc == 0, kc == KCH - 1)
        P.act(g3[:, tb, :], ps[:, 0:48], AF.Sigmoid)

    qT = A.alloc("qT", [128, NB, 4, 128], BF16)
    ksT = A.alloc("ksT", [128, 2048], BF16)
    kwT = A.alloc("kwT", [128, 1536], BF16)
    vs = A.alloc("vs", [128, 16, 130], BF16)
    vw = A.alloc("vw", [128, 12, 130], BF16)
    e32 = A.alloc("e32", [128, 4, 128], F32)
    p32 = A.alloc("p32", [128, 4, 128], F32)
    p16 = A.alloc("p16", [128, 4, 128], BF16)
    pT = A.alloc("pT", [128, 4, 128], BF16)
    Pg = A.alloc("Pg", [128, 128], F32)
    imp = A.alloc("imp", [128, 32], F32)
    imp2 = A.alloc("imp2", [128, 32], F32)
    m8 = A.alloc("m8", [128, 8], F32)
    sm4 = A.alloc("sm4", [128, 16], F32)
    selneg = A.alloc("selneg", [128, 32], BF16)
    negT = [A.alloc("negT%d" % i, [128, 4, 128], BF16) for i in range(2)]
    for nb_ in negT:
        P.memset("dve", nb_, 0.0)
    oacc = [A.alloc("oacc%d" % i, [128, 4, 128], F32) for i in range(2)]
    o16 = A.alloc("o16", [128, 4, 128], BF16)
    PT = [A.alloc("PT%d" % i, [128, 512], BF16) for i in range(3)]
    coef = A.alloc("coef", [128, 8], F32)
    P.memset("dve", vs, 0.0)
    P.memset("dve", vw, 0.0)

    def bc_heads(b):
        return b.v(b.ap.unsqueeze(1).to_broadcast([b.ap.shape[0], 4, 128]))

    for g in range(4):
        wq = WS.get(tiles, 1 + 2 * g)
        for hh in range(4):
            for tch in range(2):
                proj_T(wq, hh * 128, nT_own, tch * 512, 512, qT[:, 4 * tch:4 * tch + 4, hh, :], scale=QSCALE,
                       out_view3=True)
        wkv = WS.get(tiles, 2 + 2 * g)
        for (nT, t0, u0) in uchunks:
            proj_T(wkv, 0, nT, t0, 512, ksT[:, u0:u0 + 512])
        for (nT, t0, u0) in uchunks[1:]:
            proj_T(wkv, 256, nT, t0, 512, kwT[:, u0 - 512:u0])
        for (vbuf, c0, ub0) in ((vs, 128, 0), (vw, 384, 4)):
            for q4 in range(ub0 // 4, 4):
                ps = next_ps()
                for j in range(4):
                    ub = 4 * q4 + j
                    nT = nT_ctx if ub < 8 else nT_own
                    tb = ub % 8
                    for kc in range(KCH):
                        P.mm(ps[:, j * 128:(j + 1) * 128], nT[:, kc, tb * 128:(tb + 1) * 128],
                             wkv[:, kc, c0:c0 + 128], kc == 0, kc == KCH - 1)
                evac(vbuf[:, 4 * q4 - ub0:4 * q4 - ub0 + 4, 0:128],
                     ps.v(ps.ap.rearrange("p (a b) -> p a b", b=128)))
            P.copy("dve", vbuf[:, :, 128:129], kvalid.v(kvalid.ap[:, ub0:16].unsqueeze(2)))

        def cmp_stage(qb):
            par = qb % 2
            for hh in range(4):
                P.mm(psC[:, hh * 128:(hh + 1) * 128], qT[:, qb, hh, :], kcmpT[:, g, :], True, True)
            psC3 = psC.v(psC.ap.rearrange("p (a b) -> p a b", b=128))
            P.reduce("dve", sm4[:, 0:4], psC3, ALU.max)
            P.ts("dve", sm4[:, 4:8], sm4[:, 0:4], -1.0, None, ALU.mult)
            for hh in range(4):
                P.act(e32[:, hh, :], psC[:, hh * 128:(hh + 1) * 128], AF.Exp, bias=sm4[:, 4 + hh:5 + hh])
            P.tt("dve", e32, e32, bc_heads(cmask[:, qb, :]), ALU.mult)
            P.reduce("dve", sm4[:, 8:12], e32, ALU.add)
            P.ts("dve", sm4[:, 8:12], sm4[:, 8:12], 1e-30, None, ALU.max)
            P.recip(sm4[:, 12:16], sm4[:, 8:12])
            rb = sm4.v(sm4.ap[:, 12:16].unsqueeze(2).to_broadcast([128, 4, 128]))
            P.tt("dve", p32, e32, rb, ALU.mult)
            P.copy("act", p16, p32)
            P.reduce("dve", Pg, p32.v(p32.ap.rearrange("p h c -> p c h")), ALU.add)
            P.reduce("dve", imp, Pg.v(Pg.ap.rearrange("p (j f) -> p j f", f=4)), ALU.add)
            P.tt("dve", imp2[:, 1:32], imp[:, 1:32], Pg[:, 3:124:4], ALU.add)
            P.copy("dve", imp2[:, 0:1], imp[:, 0:1])
            P.tt("dve", imp, imp2, addmask[:, qb, :], ALU.add)
            P.op("dve", lambda e: e.max(out=m8.ap, in_=imp.ap), [imp], [m8])
            P.op("dve", lambda e: e.match_replace(out=imp2.ap, in_to_replace=m8.ap, in_values=imp.ap,
                                                   imm_value=-3e38), [m8, imp], [imp2])
            P.op("dve", lambda e: e.max(out=m8.ap, in_=imp2.ap), [imp2], [m8])
            P.ts("dve", selneg, imp, m8[:, 7:8], NEGB, ALU.is_lt, ALU.mult)
            if "imp" in k.dbg_d and g == 0:
                k.dbg_store("imp", imp, k.dbg_d["imp"][qb])
                k.dbg_store("Pg", Pg, k.dbg_d["Pg"][qb])

        def cmp_stage_b(qb):
            par = qb % 2
            P.transpose(psT[0:32, 0:128], selneg, ident)
            evac(negT[par][0:32, :, :], psT.v(psT.ap[0:32, 0:128].unsqueeze(1).to_broadcast([32, 4, 128])))
            for hh in range(4):
                P.transpose(psT[:, (1 + hh) * 128:(2 + hh) * 128], p16[:, hh, :], ident)
            evac(pT, psT.v(psT.ap[:, 128:640].rearrange("p (a b) -> p a b", b=128)))
            for hh in range(4):
                P.mm(psO[:, hh * 128:(hh + 1) * 128], pT[:, hh, :], vcmp[:, g, :], True, True)
            for hh in range(4):
                col = 3 * (4 * g + hh)
                P.act(oacc[par][:, hh, :], psO[:, hh * 128:(hh + 1) * 128], AF.Copy, scale=g3[:, qb, col:col + 1])

        def attn(qb, kT, kofs, vbuf, vofs, kbs, masks, gcol, final):
            par = qb % 2
            n = len(kbs)
            q3 = qT.v(qT.ap[:, qb, :, :].rearrange("p h t -> p (h t)"))
            pss = {}

            def stA(i):
                kb = kbs[i]
                ps = psS[i % 3]
                pss[i] = ps
                mk = masks.get(kb)
                P.mm(ps, kT[:, (kb - kofs) * 128:(kb - kofs + 1) * 128], q3, True, mk is None)
                if mk is not None:
                    P.mm(ps, mk[0], mk[1], False, True)
                P.act(PT[i % 3], ps, AF.Exp)

            def stB(i):
                kb = kbs[i]
                for hh in range(4):
                    acc = psV[hh // 2]
                    o = (hh % 2) * 130
                    P.mm(acc[:, o:o + 129], PT[i % 3][:, hh * 128:(hh + 1) * 128], vbuf[:, kb - vofs, 0:129],
                         i == 0 and hh % 2 == 0, i == n - 1, sgc=True)

            stA(0)
            if n > 1:
                stA(1)
            for i in range(n):
                if i + 2 < n:
                    stA(i + 2)
                stB(i)
            for hh in range(4):
                acc = psV[hh // 2]
                o = (hh % 2) * 130
                P.copy("dve", coef[:, hh:hh + 1], acc[:, o + 128:o + 129])
            P.ts("dve", coef[:, 0:4], coef[:, 0:4], 1e-30, None, ALU.max)
            P.recip(coef[:, 4:8], coef[:, 0:4])
            base = 3 * 4 * g + gcol
            P.tt("dve", coef[:, 4:8], coef[:, 4:8], g3[:, qb, base:base + 10:3], ALU.mult)
            for hh in range(4):
                acc = psV[hh // 2]
                o = (hh % 2) * 130
                dst = o16[:, hh, :] if final else oacc[par][:, hh, :]
                P.stt("dve", dst, acc[:, o:o + 128], coef[:, 4 + hh:5 + hh], oacc[par][:, hh, :], ALU.mult, ALU.add)

        cmp_stage(0)
        cmp_stage_b(0)
        for qb in range(NB):
            if qb + 1 < NB:
                cmp_stage(qb + 1)
            ub = 8 + qb
            par = qb % 2
            negbc = negT[par].v(negT[par].ap.rearrange("p h t -> p (h t)"))
            masks = {kb: (Emat[:, kb * 128:(kb + 1) * 128], negbc) for kb in range(0, ub)}
            masks[ub] = (ident, causneg)
            attn(qb, ksT, 0, vs, 0, list(range(0, ub + 1)), masks, 1, False)
            if qb + 1 < NB:
                cmp_stage_b(qb + 1)
            masks = {ub - 4: (ident, winneg), ub: (ident, causneg)}
            attn(qb, kwT, 4, vw, 4, list(range(ub - 4, ub + 1)), masks, 2, True)
            for hh in range(4):
                P.transpose(psT[:, (5 + hh % 2) * 128:(6 + hh % 2) * 128], o16[:, hh, :], ident)
                if hh % 2 == 1:
                    evac(o_nsaT[:, 4 * g + hh - 1:4 * g + hh + 1, qb * 128:(qb + 1) * 128],
                         psT.v(psT.ap[:, 640:896].rearrange("p (a b) -> p a b", b=128)))
        if stop_after == "nsa_g0":
            break
    if "o_nsaT" in k.dbg_d:
        tmpf = A.alloc("dbgtmp", [128, TOK], F32)
        for hh in range(4):
            P.copy("dve", tmpf, o_nsaT[:, hh, :])
            k.dbg_store("o_nsaT", tmpf, k.dbg_d["o_nsaT"][:, hh, :])


def _rotary(k, ps128, cos, sin, out_even_odd, tmp):
    P = k.P
    x3 = ps128.v(ps128.ap.rearrange("p (i two) -> p i two", two=2))
    c3 = cos.v(cos.ap.unsqueeze(2).to_broadcast([128, 64, 2]))
    s3 = sin.v(sin.ap.unsqueeze(2).to_broadcast([128, 64, 2]))
    tA = tmp.v(tmp.ap[:, 0:2, :].rearrange("p a (i two) -> p (a i) two", two=2))
    tB = tmp.v(tmp.ap[:, 2:4, :].rearrange("p a (i two) -> p (a i) two", two=2))
    o3 = out_even_odd.v(out_even_odd.ap.rearrange("p (i two) -> p i two", two=2))
    P.tt("dve", tA, x3, c3, ALU.mult)
    P.tt("dve", tB, x3, s3, ALU.mult)
    P.tt("dve", o3[:, :, 0], tA[:, :, 0], tB[:, :, 1], ALU.subtract)
    P.tt("dve", o3[:, :, 1], tB[:, :, 0], tA[:, :, 1], ALU.add)


def build_ret_ctx(k):
    P, A, D, WS = k.P, k.R34, k.D, k.WS
    nT_ctx = k.nT_ctx
    state8 = k.state8
    ropek = A.alloc("ropek", [128, 16, 128], F32)
    wkc = A.alloc("wkc", [128, 8, 8], F32)
    P.dma("sp", ropek, D["ropek"][:, :, :])
    P.dma("sp", wkc, D["wkc"][:, :, :])
    Kr = [A.alloc("Kr%d" % i, [128, 128], F32) for i in range(2)]
    Ks = [A.alloc("Ks%d" % i, [128, 128], BF16) for i in range(2)]
    V = [A.alloc("V%d" % i, [128, 256], BF16) for i in range(2)]
    tmp = [A.alloc("rtmp%d" % i, [128, 4, 64], F32) for i in range(2)]
    tiles = [("wr_c%d" % h,) + wtile_cols(D["w_in"], OR_ + 512 * h, 512) for h in range(8)]
    for h in range(8):
        wt = WS.get(tiles, h)

        def proj(cb):
            ps = k.next_ps()
            for kc in range(KCH):
                P.mm(ps[:, 0:384], nT_ctx[:, kc, cb * 128:(cb + 1) * 128], wt[:, kc, 128:512], kc == 0, kc == KCH - 1)
            return ps

        ps_next = proj(0)
        for cb in range(NB):
            par = cb % 2
            ps = ps_next
            if cb + 1 < NB:
                ps_next = proj(cb + 1)
            _rotary(k, ps[:, 0:128], ropek[:, cb, 0:64], ropek[:, cb, 64:128], Kr[par], tmp[par])
            P.copy("act", V[par], ps[:, 128:384])
            P.act(Ks[par], Kr[par], AF.Copy, scale=wkc[:, cb, h:h + 1])
            P.mm(k.psV[h % 2][:, 0:256], Ks[par], V[par], cb == 0, cb == NB - 1)
        P.copy("act", state8[:, h, :], k.psV[h % 2][:, 0:256])


def build_ret_own(k):
    P, A, D, WS = k.P, k.R34, k.D, k.WS
    nT_own = k.nT_own
    state8 = k.state8
    psT = k.psT
    ident = k.ident
    o_retT = A.alloc("o_retT", [128, 16, TOK], BF16)
    k.o_retT = o_retT
    k.m_after_o = A.mark()
    ropek = A.alloc("ropek", [128, 16, 128], F32)
    ropeq = A.alloc("ropeq", [128, 8, 128], F32)
    decayT = A.alloc("decayT", [128, 8, 128], F32)
    wqB = A.alloc("wqB", [128, 8, 128], F32)
    wk = A.alloc("wk", [128, 8], F32)
    gnB = k.load_gain("ret_gn_w", A)
    P.dma("sp", ropek, D["ropek"][:, :, :])
    P.dma("sp", ropeq, D["ropeq"][:, :, :])
    P.dma("sp", decayT, D["decayT"][:, :, :])
    P.dma("sp", wqB, D["wqB"][:, :, :])
    P.dma("sp", wk, D["wk"][:, :])
    def mk_bufs(j):
        B = {}
        for nm, shp, dt in (("Qr", [128, 128], BF16), ("Kr", [128, 128], F32), ("Kb", [128, 128], BF16),
                            ("Ks", [128, 128], BF16), ("V", [128, 256], BF16), ("sg", [128, 256], F32),
                            ("tmp", [128, 4, 64], F32)):
            B[nm] = [A.alloc("%s%d_%d" % (nm, j, i), shp, dt) for i in range(2)]
        for nm, shp, dt in (("QT", [128, 128], BF16), ("KT", [128, 128], BF16), ("QsT", [128, 128], BF16),
                            ("SdT", [128, 128], BF16), ("stbf", [128, 256], BF16), ("osb", [128, 256], F32),
                            ("junk", [128, 256], BF16), ("y", [128, 256], F32), ("y16", [128, 256], BF16),
                            ("gs", [128, 8], F32)):
            B[nm] = A.alloc("%s%d" % (nm, j), shp, dt)
        return B

    bufs = [mk_bufs(0), mk_bufs(1)]
    tiles = []
    for p in range(4):
        tiles.append(("wr_o%d" % (2 * p),) + wtile_cols(D["w_in"], OR_ + 512 * (2 * p), 512))
        tiles.append(("wr_o%d" % (2 * p + 1),) + wtile_cols(D["w_in"], OR_ + 512 * (2 * p + 1), 512))
        tiles.append(("wgr%d" % p,) + wtile_cols(D["w_in"], OGR + 512 * p, 512))

    def run2(sa, sb):
        n = max(len(sa), len(sb))
        for i in range(n):
            if i < len(sa):
                sa[i]()
            if i < len(sb):
                sb[i]()

    for p in range(4):
        wts = [WS.get(tiles, 3 * p + j, depth=0) for j in range(2)]
        wg = WS.get(tiles, 3 * p + 2, depth=0)

        def stage1_steps(j, ob):
            B = bufs[j]
            h = 2 * p + j
            wt = wts[j]
            par = ob % 2
            ub = 8 + ob
            st = {}
            steps = []

            def s_proj():
                st["ps"] = k.next_ps()
                for kc in range(KCH):
                    P.mm(st["ps"], nT_own[:, kc, ob * 128:(ob + 1) * 128], wt[:, kc, 0:512], kc == 0, kc == KCH - 1)
            steps.append(s_proj)

            def s_projg():
                st["psg"] = k.next_ps()
                for kc in range(KCH):
                    P.mm(st["psg"][:, 0:256], nT_own[:, kc, ob * 128:(ob + 1) * 128],
                         wg[:, kc, j * 256:j * 256 + 256], kc == 0, kc == KCH - 1)
            steps.append(s_projg)
            steps.append(lambda: _rotary(k, st["ps"][:, 0:128], ropeq[:, ob, 0:64], ropeq[:, ob, 64:128],
                                         B["Qr"][par], B["tmp"][par]))
            steps.append(lambda: P.copy("act", B["V"][par], st["ps"][:, 256:512]))
            steps.append(lambda: _rotary(k, st["ps"][:, 128:256], ropek[:, ub, 0:64], ropek[:, ub, 64:128],
                                         B["Kr"][par], B["tmp"][par]))
            steps.append(lambda: P.act(B["sg"][par], st["psg"][:, 0:256], AF.Silu))
            steps.append(lambda: P.copy("act", B["Kb"][par], B["Kr"][par]))
            steps.append(lambda: P.act(B["Ks"][par], B["Kr"][par], AF.Copy, scale=wk[:, h:h + 1]))
            return steps

        def stage2_steps(j, ob):
            B = bufs[j]
            h = 2 * p + j
            par = ob % 2
            c0 = j * 512
            st = {}
            gs = B["gs"]
            steps = []

            def s_tr():
                P.transpose(psT[:, c0:c0 + 128], B["Qr"][par], ident)
                P.transpose(psT[:, c0 + 128:c0 + 256], B["Kb"][par], ident)
            steps.append(s_tr)
            steps.append(lambda: P.copy("act", B["QT"], psT[:, c0:c0 + 128]))
            steps.append(lambda: P.copy("act", B["KT"], psT[:, c0 + 128:c0 + 256]))
            steps.append(lambda: P.tt("dve", B["QsT"], psT[:, c0:c0 + 128], wqB[:, h, :], ALU.mult))

            def s_S():
                st["ps"] = k.next_ps()
                P.mm(st["ps"][:, 0:128], B["KT"], B["QT"], True, True)
            steps.append(s_S)
            steps.append(lambda: P.tt("dve", B["SdT"], st["ps"][:, 0:128], decayT[:, h, :], ALU.mult))
            steps.append(lambda: P.copy("act", B["stbf"], state8[:, h, :]))

            def s_o():
                st["po"] = k.next_ps()
                P.mm(st["po"][:, 0:256], B["SdT"], B["V"][par], True, False)
                P.mm(st["po"][:, 0:256], B["QsT"], B["stbf"], False, True)
            steps.append(s_o)
            steps.append(lambda: P.memset("dve", gs, 0.0))
            steps.append(lambda: P.act(B["osb"], st["po"][:, 0:256], AF.Copy, accum_out=gs[:, 0:1]))
            steps.append(lambda: P.act(B["junk"], st["po"][:, 0:256], AF.Square, accum_out=gs[:, 1:2]))

            def s_state():
                st["ps3"] = k.next_ps()
                P.mm(st["ps3"][:, 0:256], B["Ks"][par], B["V"][par], True, True)
            steps.append(s_state)
            steps.append(lambda: P.stt("dve", state8[:, h, :], state8[:, h, :], _G_CHUNK[h], st["ps3"][:, 0:256],
                                       ALU.mult, ALU.add))
            steps.append(lambda: P.ts("dve", gs[:, 2:3], gs[:, 0:1], 1.0 / 256, None, ALU.mult))
            steps.append(lambda: P.tt("dve", gs[:, 3:4], gs[:, 2:3], gs[:, 2:3], ALU.mult))
            steps.append(lambda: P.stt("dve", gs[:, 4:5], gs[:, 1:2], 1.0 / 256, gs[:, 3:4], ALU.mult, ALU.subtract))
            steps.append(lambda: P.ts("dve", gs[:, 4:5], gs[:, 4:5], EPS, None, ALU.add))
            steps.append(lambda: P.act(gs[:, 5:6], gs[:, 4:5], AF.Sqrt))
            steps.append(lambda: P.recip(gs[:, 6:7], gs[:, 5:6]))
            steps.append(lambda: P.ts("dve", B["y"], B["osb"], gs[:, 2:3], gs[:, 6:7], ALU.subtract, ALU.mult))
            steps.append(lambda: P.tt(POOL, B["y"], B["y"], gnB[:, h * 256:(h + 1) * 256], ALU.mult))
            steps.append(lambda: P.tt(POOL, B["y16"], B["y"], B["sg"][par], ALU.mult))

            def s_tr2():
                for jj in range(2):
                    P.transpose(psT[:, c0 + (2 + jj) * 128:c0 + (3 + jj) * 128], B["y16"][:, jj * 128:(jj + 1) * 128], ident)
            steps.append(s_tr2)
            steps.append(lambda: k.evac(o_retT[:, 2 * h:2 * h + 2, ob * 128:(ob + 1) * 128],
                                        psT.v(psT.ap[:, c0 + 256:c0 + 512].rearrange("p (a b) -> p a b", b=128))))
            return steps

        run2(stage1_steps(0, 0), stage1_steps(1, 0))
        for ob in range(NB):
            if ob + 1 < NB:
                run2(stage1_steps(0, ob + 1), stage1_steps(1, ob + 1))
            run2(stage2_steps(0, ob), stage2_steps(1, ob))
    if "o_retT" in k.dbg_d:
        tmpf = A.alloc("dbgtmp", [128, 2, TOK], F32)
        P.copy("dve", tmpf, o_retT[:, 0:2, :])
        k.dbg_store("o_retT", tmpf)


def build_merge(k, which, srcT):
    P, A, D, WS = k.P, k.R34, k.D, k.WS
    nT_own, mergedT = k.nT_own, k.mergedT
    W = D["w_a"] if which == "a" else D["w_b"]
    og = OGA if which == "a" else OGB
    sig = [A.alloc("sig%d" % i, [128, 512], F32) for i in range(2)]
    tmp = [A.alloc("mtmp%d" % i, [128, 512], F32) for i in range(2)]
    tiles = []
    for i in range(4):
        tiles.append(("wm%s%d" % (which, i),) + wtile_cols(W, 512 * i, 512))
        tiles.append(("wgt%s%d" % (which, i),) + wtile_cols(D["w_in"], og + 512 * i, 512))
    n = 0
    for i in range(4):
        wm = WS.get(tiles, 2 * i, depth=1)
        wg = WS.get(tiles, 2 * i + 1, depth=1)
        for cc in range(4):
            for tch in range(2):
                psA = k.next_ps()
                for kc in range(KCH):
                    P.mm(psA, wm[:, kc, cc * 128:(cc + 1) * 128], srcT[:, kc, tch * 512:(tch + 1) * 512],
                         kc == 0, kc == KCH - 1)
                psG = k.next_ps()
                for kc in range(KCH):
                    P.mm(psG, wg[:, kc, cc * 128:(cc + 1) * 128], nT_own[:, kc, tch * 512:(tch + 1) * 512],
                         kc == 0, kc == KCH - 1)
                par = n % 2
                n += 1
                P.act(sig[par], psG, AF.Sigmoid)
                dst = mergedT[:, 4 * i + cc, tch * 512:(tch + 1) * 512]
                if which == "a":
                    P.tt("dve", dst, sig[par], psA, ALU.mult)
                else:
                    P.tt("dve", tmp[par], sig[par], psA, ALU.mult)
                    P.tt(POOL, dst, tmp[par], dst, ALU.add)
    if ("mergedT_" + which) in k.dbg_d:
        tmpf = A.alloc("dbgtmp", [128, 4, TOK], F32)
        P.copy("dve", tmpf, mergedT[:, 0:4, :])
        k.dbg_store("mergedT_" + which, tmpf)


def build_tail(k, stop_after):
    P, D, WS = k.P, k.D, k.WS
    R1, R2, R34 = k.R1, k.R2, k.R34
    psS, psV, psT, ident = k.psS, k.psV, k.psT, k.ident
    mergedT = k.mergedT
    hb = [R34.alloc("h%d" % tb, [128, DM], F32) for tb in range(NB)]
    for tb in range(NB):
        P.dma("sp", hb[tb], D["xo"][tb * 128:(tb + 1) * 128, :])
    tiles = [("wout%d" % i,) + wtile_cols(D["w_out"], 512 * i, 512) for i in range(4)]
    for i in range(4):
        wt = WS.get(tiles, i)
        for tb in range(NB):
            ps = k.next_ps()
            for kc in range(KCH):
                P.mm(ps, mergedT[:, kc, tb * 128:(tb + 1) * 128], wt[:, kc, :], kc == 0, kc == KCH - 1)
            hs = hb[tb][:, 512 * i:512 * (i + 1)]
            P.tt("dve", hs, hs, ps, ALU.add)
    if "h1" in k.dbg_d:
        for tb in range(NB):
            k.dbg_store("h1", hb[tb], k.dbg_d["h1"][tb * 128:(tb + 1) * 128, :])
    R2.release()
    if stop_after == "h1":
        return

    nxT = R1.alloc("nxT", [128, KCH, TOK], BF16)
    gainX = k.load_gain("x_norm_w", R2)
    xn = R2.alloc("xn", [128, DM], BF16)
    stat = R2.alloc("stat", [128, 16], F32)
    mh0 = R34.mark()
    xn2 = R34.alloc("xn2", [128, DM], BF16)
    k.norm_T(lambda tb: hb[tb], NB, gainX, nxT, [xn, xn2], stat)
    R34.release(mh0)
    P.dma("sp", gainX, D["mem_norm_w"].to_broadcast([128, DM]))
    mh = R34.mark()
    mts = [R34.alloc("mt%d" % i, [128, DM], F32) for i in range(2)]
    mT = R2.alloc("mT", [128, KCH, 256], BF16)
    k.norm_T(k.mk_get(D["mem"], mts), 2, gainX, mT, xn, stat)
    R34.release(mh)
    qxT = R2.alloc("qxT", [128, 4, TOK], BF16)
    kxT = R34.alloc("kxT", [128, 4, 256], BF16)
    vx = R34.alloc("vx", [128, 2, 4, 130], BF16)
    PTx = [R34.alloc("PTx%d" % i, [128, 512], BF16) for i in range(2)]
    xc = R34.alloc("xcoef", [128, 4], F32)
    tiles = [("wqx",) + wtile_cols(D["wq_x"], 0, 512), ("wkx",) + wtile_cols(D["wk_x"], 0, 512),
             ("wvx",) + wtile_cols(D["wv_x"], 0, 512), ("wox",) + wtile_rows(D["wo_x"], 0, 512)]
    wq = WS.get(tiles, 0)
    for hh in range(4):
        for tch in range(2):
            k.proj_T(wq, hh * 128, nxT, tch * 512, 512, qxT[:, hh, tch * 512:(tch + 1) * 512], scale=QSCALE)
    wk_ = WS.get(tiles, 1)
    for hh in range(4):
        k.proj_T(wk_, hh * 128, mT, 0, 256, kxT[:, hh, :])
    wv = WS.get(tiles, 2)
    P.memset("dve", vx, 1.0)
    for mb in range(2):
        ps = k.next_ps()
        for kc in range(KCH):
            P.mm(ps, mT[:, kc, mb * 128:(mb + 1) * 128], wv[:, kc, :], kc == 0, kc == KCH - 1)
        k.evac(vx[:, mb, :, 0:128], ps.v(ps.ap.rearrange("p (a b) -> p a b", b=128)))
    R1.release()
    ox16 = R1.alloc("ox16", [128, NB, 512], BF16)
    oxT = R1.alloc("oxT", [128, 4, TOK], BF16)
    for hh in range(4):
        for tch in range(2):
            for mb in range(2):
                ps = psS[mb]
                P.mm(ps, kxT[:, hh, mb * 128:(mb + 1) * 128], qxT[:, hh, tch * 512:(tch + 1) * 512], True, True)
                P.act(PTx[mb], ps, AF.Exp)
            for tq in range(4):
                tb = tch * 4 + tq
                acc = psV[tq % 2]
                for mb in range(2):
                    P.mm(acc[:, 0:129], PTx[mb][:, tq * 128:(tq + 1) * 128], vx[:, mb, hh, 0:129], mb == 0, mb == 1)
                P.copy("dve", xc[:, 0:1], acc[:, 128:129])
                P.recip(xc[:, 1:2], xc[:, 0:1])
                P.ts("dve", ox16[:, tb, hh * 128:(hh + 1) * 128], acc[:, 0:128], xc[:, 1:2], None, ALU.mult)
    for tb in range(NB):
        for j in range(4):
            P.transpose(psT[:, j * 128:(j + 1) * 128], ox16[:, tb, j * 128:(j + 1) * 128], ident)
        k.evac(oxT[:, :, tb * 128:(tb + 1) * 128], psT.v(psT.ap[:, 0:512].rearrange("p (a b) -> p a b", b=128)))
    wo = WS.get(tiles, 3)
    for tb in range(NB):
        for cc in range(4):
            ps = k.next_ps()
            for kc in range(4):
                P.mm(ps, oxT[:, kc, tb * 128:(tb + 1) * 128], wo[:, kc, cc * 512:(cc + 1) * 512], kc == 0, kc == 3)
            hs = hb[tb][:, 512 * cc:512 * (cc + 1)]
            P.tt("dve", hs, hs, ps, ALU.add)
    if "h2" in k.dbg_d:
        for tb in range(NB):
            k.dbg_store("h2", hb[tb], k.dbg_d["h2"][tb * 128:(tb + 1) * 128, :])
    R1.release()
    R2.release()
    R34.release(mh)
    if stop_after == "h2":
        return

    nmT = R1.alloc("nmT", [128, KCH, TOK], BF16)
    gainM = k.load_gain("mlp_norm_w", R2)
    xn = [R2.alloc("xn%d" % i, [128, DM], BF16) for i in range(2)]
    stat = R2.alloc("stat", [128, 16], F32)
    k.norm_T(lambda tb: hb[tb], NB, gainM, nmT, xn, stat)
    aT = R2.alloc("aT", [128, 4, TOK], BF16)
    rl = [R2.alloc("rl%d" % i, [128, 512], F32) for i in range(2)]
    tiles = []
    for f in range(16):
        tiles.append(("wup%d" % f,) + wtile_cols(D["w_up"], 512 * f, 512))
        tiles.append(("wdn%d" % f,) + wtile_rows(D["w_down"], 512 * f, 512))
    n = 0
    for f in range(16):
        wu = WS.get(tiles, 2 * f)
        for cc in range(4):
            for tch in range(2):
                ps = k.next_ps()
                for kc in range(KCH):
                    P.mm(ps, wu[:, kc, cc * 128:(cc + 1) * 128], nmT[:, kc, tch * 512:(tch + 1) * 512],
                         kc == 0, kc == KCH - 1)
                par = n % 2
                n += 1
                P.act(rl[par], ps, AF.Relu)
                P.tt(POOL, aT[:, cc, tch * 512:(tch + 1) * 512], rl[par], rl[par], ALU.mult)
        wd = WS.get(tiles, 2 * f + 1)
        for tb in range(NB):
            for cc in range(4):
                ps = k.next_ps()
                for kc in range(4):
                    P.mm(ps, aT[:, kc, tb * 128:(tb + 1) * 128], wd[:, kc, cc * 512:(cc + 1) * 512], kc == 0, kc == 3)
                hs = hb[tb][:, 512 * cc:512 * (cc + 1)]
                P.tt("dve", hs, hs, ps, ALU.add)
    if "h3" in k.dbg_d:
        for tb in range(NB):
            k.dbg_store("h3", hb[tb], k.dbg_d["h3"][tb * 128:(tb + 1) * 128, :])
    R1.release()
    R2.release()

    gainF = k.load_gain("final_norm_w", R2)
    outt = [R2.alloc("outt%d" % i, [128, DM], F32) for i in range(2)]
    junk = R2.alloc("junkf", [128, DM], BF16)
    stat = R2.alloc("statf", [128, 16], F32)
    P.memset("dve", stat, 0.0)
    for tb in range(NB):
        P.act(junk, hb[tb], AF.Square, accum_out=stat[:, tb:tb + 1])
        P.ts("dve", stat[:, tb:tb + 1], stat[:, tb:tb + 1], 1.0 / DM, EPS, ALU.mult, ALU.add)
        P.act(stat[:, tb:tb + 1], stat[:, tb:tb + 1], AF.Sqrt)
        P.recip(stat[:, tb:tb + 1], stat[:, tb:tb + 1])
        P.stt("dve", outt[tb % 2], hb[tb], stat[:, tb:tb + 1], gainF, ALU.mult, ALU.mult)
        P.dma("sp", k.out_d[tb * 128:(tb + 1) * 128, :], outt[tb % 2], semkey="dma:store%d" % (tb % 2))


def _w_in_perm():
    q = np.arange(0, 2048)
    parts = [q]
    for g in range(4):
        for base in (3072, 3584, 4096, 4608):
            parts.append(base + 128 * g + np.arange(128))
    parts.append(2048 + np.arange(512))
    parts.append(2560 + np.arange(512))
    for h in range(8):
        parts.append(5168 + 128 * h + np.arange(128))
        parts.append(6192 + 128 * h + np.arange(128))
        parts.append(7216 + 256 * h + np.arange(256))
    parts.append(np.arange(9264, 15408))
    parts.append(5120 + np.arange(48))
    perm = np.concatenate(parts)
    assert perm.shape[0] == IN_WIDTH and len(set(perm.tolist())) == IN_WIDTH
    return perm


def _tables(s):
    f32 = np.float32
    T = {}
    T["ident"] = np.eye(128, dtype=f32)
    u = np.arange(2048)
    t = u - 1024 + 1024 * s
    inv = (10000.0 ** (-np.arange(0, 128, 2, dtype=f32) / f32(128))).astype(f32)
    ang = t.astype(f32)[:, None] * inv[None, :]
    cos, sin = np.cos(ang).astype(f32), np.sin(ang).astype(f32)
    rk = np.concatenate([cos, sin], axis=1) * f32(128 ** -0.5)
    T["ropek"] = np.ascontiguousarray(rk.reshape(16, 128, 128).transpose(1, 0, 2)).astype(f32)
    rq = np.concatenate([cos, sin], axis=1)[1024:]
    T["ropeq"] = np.ascontiguousarray(rq.reshape(8, 128, 128).transpose(1, 0, 2)).astype(f32)
    H = 8
    log_g = np.log1p(-np.exp2(-5.0 - np.arange(H, dtype=f32))).astype(f32)
    i = np.arange(128, dtype=f32)
    rel = i[:, None] - i[None, :]
    decay = np.where(rel >= 0, np.exp(log_g[:, None, None] * np.maximum(rel, 0.0)), 0.0).astype(f32)
    T["decayT"] = np.ascontiguousarray(decay.transpose(2, 0, 1))
    w_q = np.exp(log_g[:, None] * (i + 1.0)[None, :]).astype(f32)
    T["wqB"] = np.ascontiguousarray(np.broadcast_to(w_q[None], (128, H, 128))).astype(f32)
    w_k = np.exp(log_g[:, None] * (127.0 - i)[None, :]).astype(f32)
    T["wk"] = np.ascontiguousarray(w_k.T)
    T["g_chunk"] = np.exp(log_g * f32(128.0)).astype(f32)
    gpow = np.stack([T["g_chunk"] ** f32(7 - blk) for blk in range(8)], 0).astype(f32)
    T["wkc"] = np.ascontiguousarray((w_k.T[:, None, :] * gpow[None, :, :]).astype(f32))
    uq = 1024 + np.arange(1024)
    lc = np.arange(128)
    vis = (16 * lc[None, :] + 31 <= uq[:, None]) & (lc[None, :] < 127)
    if s == 0:
        vis &= (lc[None, :] >= 64)
    T["cmask"] = np.ascontiguousarray(vis.astype(f32).reshape(8, 128, 128).transpose(1, 0, 2))
    lj = np.arange(32)
    lcur = (uq // 64)[:, None]
    first = 0 if s == 1 else 16
    am = np.zeros((1024, 32), f32)
    am[np.broadcast_to(lj[None, :] == first, am.shape)] = 1e9
    prev = (lj[None, :] == lcur - 1) & (lj[None, :] >= first)
    am[prev] = 2e9
    am[np.broadcast_to(lj[None, :], am.shape) == lcur] = 3e9
    am[(lj[None, :] > lcur) | (lj[None, :] < first)] = -1e9
    T["addmask"] = np.ascontiguousarray(am.reshape(8, 128, 32).transpose(1, 0, 2))
    kval = (t >= 0).astype(f32)
    T["kvalid"] = np.ascontiguousarray(kval.reshape(16, 128).T)
    kk = np.arange(2048)
    T["Emat"] = (kk[None, :] // 64 == np.arange(128)[:, None]).astype(f32)
    kq = np.arange(128)
    T["causneg"] = np.tile(np.where(kq[:, None] > kq[None, :], NEGB, 0.0).astype(f32), (1, 4))
    T["winneg"] = np.tile(np.where(kq[:, None] <= kq[None, :], NEGB, 0.0).astype(f32), (1, 4))
    return T


def make_in_maps(inputs):
    f32 = np.float32
    x = np.asarray(inputs["x"], f32)
    mem = np.asarray(inputs["mem"], f32)
    perm = _w_in_perm()
    shared = {}
    shared["w_in"] = np.ascontiguousarray(np.asarray(inputs["w_in"], f32)[0][:, perm])
    for nm in ("cmp_w1_k", "cmp_w1_v", "cmp_w2_k", "cmp_w2_v", "w_a", "w_b", "w_out", "wq_x", "wk_x", "wv_x",
               "wo_x", "w_up", "w_down"):
        shared[nm] = np.ascontiguousarray(np.asarray(inputs[nm], f32)[0])
    shared["peT_k"] = np.ascontiguousarray(np.asarray(inputs["cmp_pe_k"], f32)[0].T)
    shared["peT_v"] = np.ascontiguousarray(np.asarray(inputs["cmp_pe_v"], f32)[0].T)
    for nm in ("attn_norm_w", "ret_gn_w", "x_norm_w", "mem_norm_w", "mlp_norm_w"):
        shared[nm] = np.ascontiguousarray(np.asarray(inputs[nm], f32).reshape(1, DM))
    shared["final_norm_w"] = np.ascontiguousarray(np.asarray(inputs["final_norm_w"], f32).reshape(1, DM))
    tabs = [_tables(0), _tables(1)]
    zeros = np.zeros((TOK, DM), f32)
    in_maps = []
    for c in range(8):
        b, s = c // 2, c % 2
        m = dict(shared)
        m["xo"] = np.ascontiguousarray(x[b, 1024 * s:1024 * (s + 1)])
        m["xc"] = np.ascontiguousarray(x[b, 0:1024]) if s == 1 else zeros
        m["mem"] = np.ascontiguousarray(mem[b])
        for kk, v in tabs[s].items():
            if kk != "g_chunk":
                m[kk] = v
        in_maps.append(m)
    return in_maps


_G_CHUNK = [float(v) for v in np.exp(np.log1p(-np.exp2(-5.0 - np.arange(8, dtype=np.float32))).astype(np.float32)
                                     * np.float32(128.0)).astype(np.float32)]


def kernel(**inputs):
    in_maps = make_in_maps(inputs)
    nc, st = build_program()
    res = run_bass_kernel_spmd(nc, in_maps, core_ids=list(range(8)))
    out = np.zeros((4, 2048, DM), np.float32)
    for c in range(8):
        b, s = c // 2, c % 2
        out[b, 1024 * s:1024 * (s + 1)] = res.results[c]["out"]
    return out
```
